# Optimizing a Trainium2 kernel written in Bass

```python
import math
import jax
import jax.numpy as jnp
from jax import lax
import numpy as np

D_MODEL = 1024
BATCH = 4
SEQ = 8192
DEPTH = 4

GRID_W = 64
CTX_LEN = 256
EPS = 1e-6
NEG_INF = -1e30
ROPE_BASE = 10000.0
BLOCK = 128
F32 = jnp.float32

S5_CH = 256
S5_GROUP = 16
S5_GROUPS = S5_CH // S5_GROUP
S5_STATE = 64
SWA_HEADS = 4
SWA_KV_HEADS = 2
SWA_HEAD_DIM = 64
SWA_WINDOW = 128
HG_HEADS = 4
HG_DK = 64
HG_DV = 64
HG_CHUNK = 32
MLA_HEADS = 4
MLA_Q_RANK = 256
MLA_KV_RANK = 128
MLA_NOPE = 64
MLA_ROPE = 32
MLA_V = 64
MLA_SCALE = (MLA_NOPE + MLA_ROPE) ** -0.5
FFN_HIDDEN = -(-8 * D_MODEL // (3 * 256)) * 256

IN_SIZES = (
    S5_CH,
    SWA_HEADS * SWA_HEAD_DIM,
    SWA_KV_HEADS * SWA_HEAD_DIM,
    SWA_KV_HEADS * SWA_HEAD_DIM,
    HG_HEADS * HG_DK,
    HG_HEADS * HG_DK,
    HG_HEADS * HG_DK,
    HG_HEADS * HG_DV,
    HG_HEADS * HG_DV,
    MLA_Q_RANK,
    MLA_KV_RANK,
    MLA_ROPE,
)
N_IN = sum(IN_SIZES)
D_MIX = S5_CH + SWA_HEADS * SWA_HEAD_DIM + HG_HEADS * HG_DV + MLA_HEADS * MLA_V

kernel_name = 'hybrid_prefix_dit_block'


def rms_norm(x, g):
    xf = x.astype(F32)
    y = xf * lax.rsqrt(jnp.mean(xf * xf, axis=-1, keepdims=True) + EPS)
    return (y * g.astype(F32)).astype(x.dtype)


def split_cols(p):
    parts, start = [], 0
    for n in IN_SIZES:
        parts.append(p[..., start:start + n])
        start += n
    return parts


def swiglu(h, w_up, w_down):
    gate, up = jnp.split(h @ w_up, 2, axis=-1)
    return (jax.nn.silu(gate) * up) @ w_down


def axial_rope_tables(length, dim):
    rows = length // GRID_W
    row = jnp.repeat(jnp.arange(rows, dtype=F32), GRID_W)
    col = jnp.tile(jnp.arange(GRID_W, dtype=F32), rows)
    n_freq = dim // 4
    inv = ROPE_BASE ** (-jnp.arange(n_freq, dtype=F32) / n_freq)
    ang = jnp.stack([row[:, None] * inv, col[:, None] * inv], axis=1)
    return jnp.cos(ang), jnp.sin(ang)


def apply_axial_rope(x, cos, sin):
    b, l, h, dim = x.shape
    xr = x.astype(F32).reshape(b, l, h, 2, 2, dim // 4)
    x1, x2 = xr[..., 0, :], xr[..., 1, :]
    c = cos[None, :, None]
    s = sin[None, :, None]
    out = jnp.stack([x1 * c - x2 * s, x2 * c + x1 * s], axis=-2)
    return out.reshape(b, l, h, dim).astype(x.dtype)


def s5_linear_scan(a_bar, bu, s0):
    bu = bu.at[:, 0].add(a_bar * s0)
    a = jnp.broadcast_to(a_bar, bu.shape)

    def combine(e1, e2):
        a1, b1 = e1
        a2, b2 = e2
        return a1 * a2, a2 * b1 + b2

    _, states = lax.associative_scan(combine, (a, bu), axis=1)
    return states


def s5_mixer(u, uc, lam_re, lam_im, log_dt, b_re, b_im, c_re, c_im, d_skip, w_glu, b_glu, need_ctx):
    lam = lax.complex(lam_re.astype(F32), lam_im.astype(F32))
    a_bar = jnp.exp(lam * jnp.exp(log_dt.astype(F32)))
    b_bar = ((a_bar - 1.0) / lam)[..., None] * lax.complex(b_re.astype(F32), b_im.astype(F32))
    c_mat = lax.complex(c_re.astype(F32), c_im.astype(F32))
    d_g = d_skip.astype(F32).reshape(S5_GROUPS, S5_GROUP)

    def grouped(t):
        return t.astype(F32).reshape(t.shape[0], t.shape[1], S5_GROUPS, S5_GROUP)

    def drive(ug, k):
        return jnp.einsum('btgh,gph->btgp', ug.astype(jnp.complex64), b_bar[k])

    def readout(s_f, s_b, ug):
        y = jnp.real(jnp.einsum('btgp,ghp->btgh', s_f, c_mat[0])
                     + jnp.einsum('btgp,ghp->btgh', s_b, c_mat[1])) + d_g * ug
        y = jax.nn.gelu(y.reshape(y.shape[0], y.shape[1], S5_CH))
        return y * jax.nn.sigmoid(y @ w_glu.astype(F32) + b_glu.astype(F32))

    ug, ucg = grouped(u), grouped(uc)
    zero = jnp.zeros((u.shape[0], S5_GROUPS, S5_STATE), jnp.complex64)
    sc_f = s5_linear_scan(a_bar[0], drive(ucg, 0), zero)
    sc_b = jnp.flip(s5_linear_scan(a_bar[1], jnp.flip(drive(ucg, 1), 1), zero), 1)
    sx_f = s5_linear_scan(a_bar[0], drive(ug, 0), sc_f[:, -1])
    sx_b = jnp.flip(s5_linear_scan(a_bar[1], jnp.flip(drive(ug, 1), 1), sc_b[:, 0]), 1)
    y = readout(sx_f, sx_b, ug).astype(u.dtype)
    y_c = readout(sc_f, sc_b, ucg).astype(u.dtype) if need_ctx else None
    return y, y_c


def softmax_with_sink(logits, sink):
    full = jnp.concatenate([logits, jnp.broadcast_to(sink, logits.shape[:-1] + (1,))], axis=-1)
    return jax.nn.softmax(full, axis=-1)[..., :-1]


def window_attention(q, k, v, kc, vc, sink):
    b, l, hq, dh = q.shape
    hkv = k.shape[2]
    grp = hq // hkv
    nb = l // BLOCK
    scale = dh ** -0.5
    qb = q.reshape(b, nb, BLOCK, hkv, grp, dh)

    def band(t):
        tp = jnp.pad(t, ((0, 0), (BLOCK, BLOCK), (0, 0), (0, 0))).reshape(b, nb + 2, BLOCK, hkv, dh)
        return jnp.concatenate([tp[:, :-2], tp[:, 1:-1], tp[:, 2:]], axis=2)

    kw, vw = band(k), band(v)
    qpos = jnp.arange(l).reshape(nb, BLOCK)
    kpos = (jnp.arange(nb) * BLOCK - BLOCK)[:, None] + jnp.arange(3 * BLOCK)[None, :]
    valid = ((jnp.abs(qpos[:, :, None] - kpos[:, None, :]) <= SWA_WINDOW)
             & (kpos >= 0)[:, None, :] & (kpos < l)[:, None, :])
    s_win = jnp.einsum('bnqhgd,bnkhd->bnhgqk', qb, kw).astype(F32) * scale
    s_win = jnp.where(valid[None, :, None, None], s_win, NEG_INF)
    s_ctx = jnp.einsum('bnqhgd,bchd->bnhgqc', qb, kc).astype(F32) * scale
    sink_b = sink.astype(F32).reshape(hkv, grp)[:, :, None, None]
    p = softmax_with_sink(jnp.concatenate([s_win, s_ctx], axis=-1), sink_b).astype(v.dtype)
    o = (jnp.einsum('bnhgqk,bnkhd->bnqhgd', p[..., :3 * BLOCK], vw)
         + jnp.einsum('bnhgqc,bchd->bnqhgd', p[..., 3 * BLOCK:], vc))
    return o.reshape(b, l, hq * dh)


def context_gqa(qc, kc, vc, sink):
    b, lc, hq, dh = qc.shape
    hkv = kc.shape[2]
    grp = hq // hkv
    qg = qc.reshape(b, lc, hkv, grp, dh)
    s = jnp.einsum('bqhgd,bkhd->bhgqk', qg, kc).astype(F32) * dh ** -0.5
    p = softmax_with_sink(s, sink.astype(F32).reshape(hkv, grp)[:, :, None, None]).astype(vc.dtype)
    return jnp.einsum('bhgqk,bkhd->bqhgd', p, vc).reshape(b, lc, hq * dh)


def swa_mixer(q, k, v, qc, kc, vc, sink, rope, need_ctx):
    b, l, _ = q.shape
    lc = kc.shape[1]
    q = apply_axial_rope(q.reshape(b, l, SWA_HEADS, SWA_HEAD_DIM), *rope)
    k = apply_axial_rope(k.reshape(b, l, SWA_KV_HEADS, SWA_HEAD_DIM), *rope)
    v = v.reshape(b, l, SWA_KV_HEADS, SWA_HEAD_DIM)
    kc = kc.reshape(b, lc, SWA_KV_HEADS, SWA_HEAD_DIM)
    vc = vc.reshape(b, lc, SWA_KV_HEADS, SWA_HEAD_DIM)
    y = window_attention(q, k, v, kc, vc, sink)
    y_c = context_gqa(qc.reshape(b, lc, SWA_HEADS, SWA_HEAD_DIM), kc, vc, sink) if need_ctx else None
    return y, y_c


def hgrn2_gates(z, lb):
    z = z.astype(F32)
    log_f = jnp.logaddexp(jnp.log(lb), jnp.log1p(-lb) + jax.nn.log_sigmoid(z))
    k = (1.0 - lb) * jax.nn.sigmoid(-z)
    shape = z.shape[:2] + (HG_HEADS, HG_DK)
    return log_f.reshape(shape), k.reshape(shape)


def hgrn2_chunked(q, log_f, k, v, s0):
    b, t, h, _ = q.shape
    dv = v.shape[-1]
    n = t // HG_CHUNK

    def chunks(a):
        return a.reshape(b, n, HG_CHUNK, h, a.shape[-1])

    q, log_f, k, v = chunks(q), chunks(log_f), chunks(k), chunks(v)
    cum = jnp.cumsum(log_f, axis=2)
    last = cum[:, :, -1]
    q_dec = q * jnp.exp(cum)
    k_inv = k * jnp.exp(-cum)
    k_end = k * jnp.exp(last[:, :, None] - cum)
    lower_tri = jnp.tril(jnp.ones((HG_CHUNK, HG_CHUNK), dtype=bool))
    att = jnp.where(lower_tri, jnp.einsum('bnthd,bnshd->bnhts', q_dec, k_inv), 0.0)
    o_intra = jnp.einsum('bnhts,bnshv->bnthv', att, v)
    kv = jnp.einsum('bnshd,bnshv->bnhdv', k_end, v)

    def step(state, inp):
        dec, kv_c = inp
        return dec[..., None] * state + kv_c, state

    s_final, s_prev = lax.scan(step, s0, (jnp.exp(last).swapaxes(0, 1), kv.swapaxes(0, 1)))
    o_inter = jnp.einsum('bnthd,bnhdv->bnthv', q_dec, s_prev.swapaxes(0, 1))
    return (o_intra + o_inter).reshape(b, t, h, dv), s_final


def hgrn2_final_state(log_f, k, v):
    cum = jnp.cumsum(log_f, axis=1)
    return jnp.einsum('bthd,bthv->bhdv', k * jnp.exp(cum[:, -1:] - cum), v)


def hgrn2_mixer(q, z_f, z_b, i_in, g, q_c, z_f_c, z_b_c, i_c, g_c, lb_f, lb_b, norm_g, need_ctx):
    def heads(a, d):
        return a.astype(F32).reshape(a.shape[0], a.shape[1], HG_HEADS, d)

    def flip(a):
        return jnp.flip(a, 1)

    def gate_out(o, gate):
        o = rms_norm(o, norm_g)
        return o.reshape(o.shape[0], o.shape[1], HG_HEADS * HG_DV) * jax.nn.silu(gate.astype(F32))

    lf_f, k_f = hgrn2_gates(z_f, lb_f)
    lf_b, k_b = hgrn2_gates(z_b, lb_b)
    lfc_f, kc_f = hgrn2_gates(z_f_c, lb_f)
    lfc_b, kc_b = hgrn2_gates(z_b_c, lb_b)
    qh, vh = heads(q, HG_DK), heads(i_in, HG_DV)
    vch = heads(i_c, HG_DV)
    if need_ctx:
        qch = heads(q_c, HG_DK)
        zero = jnp.zeros((q.shape[0], HG_HEADS, HG_DK, HG_DV), F32)
        oc_f, sc_f = hgrn2_chunked(qch, lfc_f, kc_f, vch, zero)
        oc_b, sc_b = hgrn2_chunked(flip(qch), flip(lfc_b), flip(kc_b), flip(vch), zero)
        y_c = gate_out(oc_f + flip(oc_b), g_c).astype(q_c.dtype)
    else:
        sc_f = hgrn2_final_state(lfc_f, kc_f, vch)
        sc_b = hgrn2_final_state(flip(lfc_b), flip(kc_b), flip(vch))
        y_c = None
    o_f, _ = hgrn2_chunked(qh, lf_f, k_f, vh, sc_f)
    o_b, _ = hgrn2_chunked(flip(qh), flip(lf_b), flip(k_b), flip(vh), sc_b)
    y = gate_out(o_f + flip(o_b), g).astype(q.dtype)
    return y, y_c


def mla_queries(cq, q_norm_g, w_qb, rope):
    b, t, _ = cq.shape
    q = (rms_norm(cq, q_norm_g) @ w_qb).reshape(b, t, MLA_HEADS, MLA_NOPE + MLA_ROPE)
    q_nope, q_rope = q[..., :MLA_NOPE], q[..., MLA_NOPE:]
    if rope is not None:
        q_rope = apply_axial_rope(q_rope, *rope)
    return q_nope, q_rope


def mla_keys(ckv, kr, kv_norm_g, w_kvb, rope):
    b, t, _ = ckv.shape
    kv = (rms_norm(ckv, kv_norm_g) @ w_kvb).reshape(b, t, MLA_HEADS, MLA_NOPE + MLA_V)
    k_rope = kr[:, :, None, :]
    if rope is not None:
        k_rope = apply_axial_rope(k_rope, *rope)
    return kv[..., :MLA_NOPE], k_rope[:, :, 0], kv[..., MLA_NOPE:]


def mla_attend(q_nope, q_rope, k_nope, k_rope, v):
    s = (jnp.einsum('bqhd,bkhd->bhqk', q_nope, k_nope)
         + jnp.einsum('bqhd,bkd->bhqk', q_rope, k_rope))
    p = jax.nn.softmax(s.astype(F32) * MLA_SCALE, axis=-1).astype(v.dtype)
    return jnp.einsum('bhqk,bkhd->bqhd', p, v)


def mla_mixer(cq, ckv, kr, cq_c, ckv_c, kr_c, q_norm_g, w_qb, kv_norm_g, w_kvb, rope, need_ctx):
    b, l, _ = cq.shape
    nb = l // BLOCK
    qn, qr = mla_queries(cq, q_norm_g, w_qb, rope)
    kn, krr, v = mla_keys(ckv, kr, kv_norm_g, w_kvb, rope)
    kn_c, krr_c, v_c = mla_keys(ckv_c, kr_c, kv_norm_g, w_kvb, None)
    kn_all = jnp.concatenate([kn_c, kn], axis=1)
    kr_all = jnp.concatenate([krr_c, krr], axis=1)
    v_all = jnp.concatenate([v_c, v], axis=1)

    def to_blocks(a):
        return a.reshape((b, nb, BLOCK) + a.shape[2:]).swapaxes(0, 1)

    o = lax.map(lambda qs: mla_attend(qs[0], qs[1], kn_all, kr_all, v_all),
                (to_blocks(qn), to_blocks(qr)))
    y = o.swapaxes(0, 1).reshape(b, l, MLA_HEADS * MLA_V)
    y_c = None
    if need_ctx:
        qn_c, qr_c = mla_queries(cq_c, q_norm_g, w_qb, None)
        y_c = mla_attend(qn_c, qr_c, kn_c, krr_c, v_c).reshape(b, -1, MLA_HEADS * MLA_V)
    return y, y_c


def hybrid_mixer(hx, hc, w_in, w_out,
                 s5_lam_re, s5_lam_im, s5_log_dt, s5_b_re, s5_b_im, s5_c_re, s5_c_im,
                 s5_d, s5_w_glu, s5_b_glu, swa_sink, lb_f, lb_b, hg_norm_g,
                 mla_q_norm_g, mla_w_qb, mla_kv_norm_g, mla_w_kvb,
                 rope_attn, rope_mla, need_ctx):
    px = split_cols(hx @ w_in)
    pc = split_cols(hc @ w_in)
    ya, ya_c = s5_mixer(px[0], pc[0], s5_lam_re, s5_lam_im, s5_log_dt, s5_b_re, s5_b_im,
                        s5_c_re, s5_c_im, s5_d, s5_w_glu, s5_b_glu, need_ctx)
    yb, yb_c = swa_mixer(px[1], px[2], px[3], pc[1], pc[2], pc[3], swa_sink, rope_attn, need_ctx)
    yc, yc_c = hgrn2_mixer(px[4], px[5], px[6], px[7], px[8], pc[4], pc[5], pc[6], pc[7], pc[8],
                           lb_f, lb_b, hg_norm_g, need_ctx)
    yd, yd_c = mla_mixer(px[9], px[10], px[11], pc[9], pc[10], pc[11],
                         mla_q_norm_g, mla_w_qb, mla_kv_norm_g, mla_w_kvb, rope_mla, need_ctx)
    dt = hx.dtype
    y = jnp.concatenate([ya.astype(dt), yb.astype(dt), yc.astype(dt), yd.astype(dt)], axis=-1) @ w_out
    y_c = None
    if need_ctx:
        y_c = jnp.concatenate([ya_c.astype(dt), yb_c.astype(dt), yc_c.astype(dt), yd_c.astype(dt)],
                              axis=-1) @ w_out
    return y, y_c


def setup_inputs(seed: int = 0) -> dict:
    key = jax.random.key(seed)
    ks = iter(jax.random.split(key, 32))

    def nrm(shape, scale):
        return jax.random.normal(next(ks), shape, F32) * scale

    L = DEPTH
    G, P = S5_GROUPS, S5_STATE
    n_idx = jnp.arange(P, dtype=F32)
    return {
        'x': nrm((BATCH, SEQ, D_MODEL), 1.0),
        'c': nrm((BATCH, D_MODEL), 1.0),
        'ctx': nrm((BATCH, CTX_LEN, D_MODEL), 1.0),
        'c_ctx': nrm((D_MODEL,), 1.0),
        'w_mod': nrm((L, D_MODEL, 6 * D_MODEL), 0.5 * D_MODEL ** -0.5),
        'b_mod': nrm((L, 6 * D_MODEL), 0.01),
        'norm1_g': 1.0 + nrm((L, D_MODEL), 0.02),
        'norm2_g': 1.0 + nrm((L, D_MODEL), 0.02),
        'w_in': nrm((L, D_MODEL, N_IN), D_MODEL ** -0.5),
        'w_out': nrm((L, D_MIX, D_MODEL), D_MIX ** -0.5),
        's5_lam_re': -0.5 + nrm((L, 2, G, P), 0.01),
        's5_lam_im': math.pi * n_idx + nrm((L, 2, G, P), 0.01),
        's5_log_dt': jax.random.uniform(next(ks), (L, 2, G, P), F32, math.log(1e-3), math.log(1e-1)),
        's5_b_re': nrm((L, 2, G, P, S5_GROUP), (2 * S5_GROUP) ** -0.5),
        's5_b_im': nrm((L, 2, G, P, S5_GROUP), (2 * S5_GROUP) ** -0.5),
        's5_c_re': nrm((L, 2, G, S5_GROUP, P), (2 * P) ** -0.5),
        's5_c_im': nrm((L, 2, G, S5_GROUP, P), (2 * P) ** -0.5),
        's5_d': nrm((L, S5_CH), 1.0),
        's5_w_glu': nrm((L, S5_CH, S5_CH), S5_CH ** -0.5),
        's5_b_glu': nrm((L, S5_CH), 0.01),
        'swa_sink': nrm((L, SWA_HEADS), 0.5),
        'hg_lb': nrm((2, L, HG_HEADS * HG_DK), 1.0),
        'hg_norm_g': 1.0 + nrm((L, HG_DV), 0.02),
        'mla_q_norm_g': 1.0 + nrm((L, MLA_Q_RANK), 0.02),
        'mla_w_qb': nrm((L, MLA_Q_RANK, MLA_HEADS * (MLA_NOPE + MLA_ROPE)), MLA_Q_RANK ** -0.5),
        'mla_kv_norm_g': 1.0 + nrm((L, MLA_KV_RANK), 0.02),
        'mla_w_kvb': nrm((L, MLA_KV_RANK, MLA_HEADS * (MLA_NOPE + MLA_V)), MLA_KV_RANK ** -0.5),
        'ffn_w_up': nrm((L, D_MODEL, 2 * FFN_HIDDEN), D_MODEL ** -0.5),
        'ffn_w_down': nrm((L, FFN_HIDDEN, D_MODEL), FFN_HIDDEN ** -0.5),
        'final_norm_g': 1.0 + nrm((D_MODEL,), 0.02),
    }


def reference(x, c, ctx, c_ctx, w_mod, b_mod, norm1_g, norm2_g, w_in, w_out,
              s5_lam_re, s5_lam_im, s5_log_dt, s5_b_re, s5_b_im, s5_c_re, s5_c_im,
              s5_d, s5_w_glu, s5_b_glu, swa_sink, hg_lb, hg_norm_g,
              mla_q_norm_g, mla_w_qb, mla_kv_norm_g, mla_w_kvb,
              ffn_w_up, ffn_w_down, final_norm_g):
    length = x.shape[1]
    rope_attn = axial_rope_tables(length, SWA_HEAD_DIM)
    rope_mla = axial_rope_tables(length, MLA_ROPE)
    lb_cum = jnp.cumsum(jax.nn.softmax(hg_lb.astype(F32), axis=1), axis=1)
    lb = lb_cum - lb_cum[:, :1]
    silu_c = jax.nn.silu(c)
    silu_cc = jax.nn.silu(c_ctx)
    for i in range(DEPTH):
        need_ctx = i < DEPTH - 1
        mod = (silu_c @ w_mod[i] + b_mod[i])[:, None, :]
        mod_c = silu_cc @ w_mod[i] + b_mod[i]
        sh1, sc1, g1, sh2, sc2, g2 = jnp.split(mod, 6, axis=-1)
        csh1, csc1, cg1, csh2, csc2, cg2 = jnp.split(mod_c, 6, axis=-1)
        hx = rms_norm(x, norm1_g[i]) * (1.0 + sc1) + sh1
        hc = rms_norm(ctx, norm1_g[i]) * (1.0 + csc1) + csh1
        y, y_c = hybrid_mixer(hx, hc, w_in[i], w_out[i],
                              s5_lam_re[i], s5_lam_im[i], s5_log_dt[i], s5_b_re[i], s5_b_im[i],
                              s5_c_re[i], s5_c_im[i], s5_d[i], s5_w_glu[i], s5_b_glu[i],
                              swa_sink[i], lb[0, i], lb[1, i], hg_norm_g[i],
                              mla_q_norm_g[i], mla_w_qb[i], mla_kv_norm_g[i], mla_w_kvb[i],
                              rope_attn, rope_mla, need_ctx)
        x = x + g1 * y
        hx = rms_norm(x, norm2_g[i]) * (1.0 + sc2) + sh2
        x = x + g2 * swiglu(hx, ffn_w_up[i], ffn_w_down[i])
        if need_ctx:
            ctx = ctx + cg1 * y_c
            hc = rms_norm(ctx, norm2_g[i]) * (1.0 + csc2) + csh2
            ctx = ctx + cg2 * swiglu(hc, ffn_w_up[i], ffn_w_down[i])
    return rms_norm(x, final_norm_g)
```

```python
import contextlib
import math
import os
import numpy as np
import concourse.bass as bass
import concourse.mybir as mybir
from concourse.bass_utils import run_bass_kernel_spmd

F32 = mybir.dt.float32
BF16 = mybir.dt.bfloat16
ALU = mybir.AluOpType
AF = mybir.ActivationFunctionType

D = 1024
SEQ = 8192
CTX = 256
TALL = SEQ + CTX
DEPTH = 4
NCORES = 4
FFH = 2816
EPS = 1e-6
GRID_W = 64
MLA_SCALE = 96 ** -0.5
SWA_SCALE = 64 ** -0.5
SAME_ENGINE_SYNC = True
OVERLAP_SWA = os.environ.get("KOVS", "0") == "1"
OVERLAP_S5 = os.environ.get("KOVL", "1") == "1"
ATTACH_WAIT = os.environ.get("KATTACH", "1") == "1"


class T:
    _n = 0

    def __init__(self, t, name, psum=False):
        self.t = t
        self.psum = psum
        T._n += 1
        self.key = (name, T._n)

    def __getitem__(self, idx):
        return self.t[idx]


class PV(T):
    def __init__(self, base, off, name):
        T.__init__(self, base.t, name, psum=True)
        self.off = off

    def _c(self, c):
        if isinstance(c, slice):
            a = self.off + (c.start or 0)
            b = self.off + (512 if c.stop is None else c.stop)
            return slice(a, b, c.step)
        return self.off + c

    def __getitem__(self, idx):
        if isinstance(idx, tuple):
            return self.t[(idx[0], self._c(idx[1])) + tuple(idx[2:])]
        return self.t[idx, self.off:self.off + 512]


class KB:
    def __init__(self, nc):
        self.nc = nc
        self.es = contextlib.ExitStack()
        self.eng = {"pe": nc.tensor, "act": nc.scalar, "dve": nc.vector, "pool": nc.gpsimd, "sp": nc.sync}
        self.sem = {e: self.es.enter_context(nc.semaphore("s_" + e)) for e in self.eng}
        self.cnt = {e: 0 for e in self.eng}
        self.lanes = {}
        self.lane_val = {}
        self.lane_rr = {}
        for q, n in (("sp", int(os.environ.get("KLANES", "12"))), ("pool", 8), ("act", 4)):
            self.lanes[q] = [self.es.enter_context(nc.semaphore(f"l_{q}{i}")) for i in range(n)]
            self.lane_rr[q] = 0
            for i in range(n):
                self.lane_val[(q, i)] = 0
        self.seen = {e: {} for e in self.eng}
        self.res = {}
        self.ninst = 0
        self.nwait = 0
        self.uid = 0

    def sb(self, es, name, shape, dtype):
        self.uid += 1
        t = es.enter_context(self.nc.sbuf_tensor(f"{name}_{self.uid}", list(shape), dtype))
        return T(t, name)

    def ps(self, es, name, shape, dtype=F32):
        self.uid += 1
        t = es.enter_context(self.nc.psum_tensor(f"{name}_{self.uid}", list(shape), dtype))
        return T(t, name, psum=True)

    def _semof(self, src):
        if src[0] == "eng":
            return self.sem[src[1]]
        return self.lanes[src[1]][src[2]]

    def _wait(self, engine, dep):
        src, val = dep
        if val <= 0:
            return
        if src[0] == "eng" and src[1] == engine:
            if engine == "pe" or not SAME_ENGINE_SYNC:
                return
        if self.seen[engine].get(src, 0) >= val:
            return
        self.eng[engine].wait_ge(self._semof(src), val)
        self.nwait += 1
        self.seen[engine][src] = val

    def _deps(self, r, w, me=None):
        deps = []
        for t in r:
            st = self.res.get(t.key)
            if st and st["w"]:
                deps.append(st["w"])
            if st and t.psum:
                deps.extend((src, v) for src, v in st["r"].items() if src != me)
        for t in w:
            st = self.res.get(t.key)
            if st:
                if st["w"]:
                    deps.append(st["w"])
                deps.extend(st["r"].items())
        return deps

    def _update(self, r, w, src, val):
        for t in r:
            st = self.res.setdefault(t.key, {"w": None, "r": {}})
            st["r"][src] = val
        for t in w:
            self.res[t.key] = {"w": (src, val), "r": {}}

    def _need(self, engine, dep):
        src, val = dep
        if val <= 0:
            return False
        if src[0] == "eng" and src[1] == engine and (engine == "pe" or not SAME_ENGINE_SYNC):
            return False
        return self.seen[engine].get(src, 0) < val

    def op(self, engine, fn, r=(), w=()):
        deps = [d for d in self._deps(r, w, ("eng", engine))]
        best = {}
        for src, val in deps:
            if self._need(engine, (src, val)) and val > best.get(src, 0):
                best[src] = val
        items = list(best.items())
        attach = None
        if ATTACH_WAIT and items:
            attach = items.pop()
        for dep in items:
            self._wait(engine, dep)
        ins = fn(self.eng[engine])
        if attach is not None:
            ins._wait_ge(self._semof(attach[0]), attach[1])
            self.seen[engine][attach[0]] = attach[1]
            self.nwait += 1
        self.cnt[engine] += 1
        ins.then_inc(self.sem[engine], 1)
        self.ninst += 1
        self._update(r, w, ("eng", engine), self.cnt[engine])
        return ins

    def dma(self, out, in_, r=(), w=(), q="sp", **kw):
        i = self.lane_rr[q]
        self.lane_rr[q] = (i + 1) % len(self.lanes[q])
        src = ("dma", q, i)
        deps = self._deps(r, w)
        deps.append((src, self.lane_val[(q, i)]))
        for dep in deps:
            self._wait(q, dep)
        ins = self.eng[q].dma_start(out=out, in_=in_, **kw)
        self.lane_val[(q, i)] += 16
        ins.then_inc(self.lanes[q][i], 16)
        self.ninst += 1
        self._update(r, w, src, self.lane_val[(q, i)])

    def barrier(self):
        for e in self.eng:
            for e2 in self.eng:
                self._wait(e, (("eng", e2), self.cnt[e2]))
            for (q, i), v in self.lane_val.items():
                self._wait(e, (("dma", q, i), v))
        self.res = {}

    def finish(self):
        for (q, i), v in self.lane_val.items():
            self._wait("sp", (("dma", q, i), v))
        for e2 in self.eng:
            if e2 != "sp":
                self._wait("sp", (("eng", e2), self.cnt[e2]))


def _blocks():
    out = [(0, CTX)]
    for j in range(SEQ // 512):
        out.append((CTX + 512 * j, 512))
    return out


def _win_cols():
    cols = {}
    base = {"u": 0, "sq": 256, "sk": 512, "sv": 640, "hq": 768, "zf": 1024, "zb": 1280, "hi": 1536, "hg": 1792,
            "cq": 2048, "ckv": 2304, "kr": 2432}

    def swap64(off):
        return np.concatenate([off + np.arange(16, 32), off + np.arange(0, 16), off + np.arange(48, 64), off + np.arange(32, 48)])

    def swap32(off):
        return np.concatenate([off + np.arange(8, 16), off + np.arange(0, 8), off + np.arange(24, 32), off + np.arange(16, 24)])

    sq = base["sq"]
    hA = np.concatenate([sq + np.arange(0, 64), sq + np.arange(128, 192)])
    hB = np.concatenate([sq + np.arange(64, 128), sq + np.arange(192, 256)])
    hAs = np.concatenate([swap64(sq + 0), swap64(sq + 128)])
    hBs = np.concatenate([swap64(sq + 64), swap64(sq + 192)])
    sk = base["sk"]
    fm = [hA, hB, hAs, hBs, sk + np.arange(128), np.concatenate([swap64(sk), swap64(sk + 64)]),
          base["hq"] + np.arange(256), base["zf"] + np.arange(256), base["zb"] + np.arange(256),
          base["hg"] + np.arange(256), base["cq"] + np.arange(256), base["ckv"] + np.arange(128),
          base["kr"] + np.arange(32), swap32(base["kr"])]
    tm = [base["u"] + np.arange(256), base["sv"] + np.arange(128), base["hi"] + np.arange(256)]
    return np.concatenate(fm + tm)


C_SQ, C_SQS, C_SK, C_SKS, C_HQ, C_ZF, C_ZB, C_HG, C_CQ, C_CKV, C_KR, C_KRS = 0, 256, 512, 640, 768, 1024, 1280, 1536, 1792, 2048, 2176, 2208
C_TM = 2240
NWIN = C_TM + 640


def _wqb_cols():
    def swap32(off):
        return np.concatenate([off + np.arange(8, 16), off + np.arange(0, 8), off + np.arange(24, 32), off + np.arange(16, 24)])
    cols = []
    for h in range(4):
        cols += [h * 96 + np.arange(96), swap32(h * 96 + 64)]
    return np.concatenate(cols)


def _swa_mask():
    kk = np.arange(128)[:, None]
    qq = np.arange(128)[None, :]
    lo = (qq <= kk).astype(np.float32)
    hi = (kk <= qq).astype(np.float32)
    m = np.stack([np.stack([lo] * 4, 1), np.stack([hi] * 4, 1)], 1)
    return np.ascontiguousarray(m.astype(np.float32))


def _hg_consts():
    t = np.arange(2048)
    rm = np.broadcast_to((t % 32 != 0).astype(np.float32)[None, :], (64, 2048))
    s_ = np.arange(128)[:, None]
    t_ = np.arange(128)[None, :]
    same = (s_ // 32) == (t_ // 32)
    fw = (same & (s_ <= t_)).astype(np.float32)
    bw = (same & (s_ >= t_)).astype(np.float32)
    am = np.stack([np.stack([fw] * 4, 1), np.stack([bw] * 4, 1)], 1)
    cm = (np.arange(128)[:, None] // 32 == np.arange(4)[None, :]).astype(np.float32)
    return {"hg_rmask": np.ascontiguousarray(rm), "hg_amask": np.ascontiguousarray(am.astype(np.float32)), "hg_cmask": cm}


def _s5_tmask():
    j = np.arange(128)[:, None] // 16
    t = np.arange(128)[None, :] // 16
    return np.ascontiguousarray(np.stack([(t >= j), (t <= j)], 1).astype(np.float32))


def _rope_tables():
    def tab(dim):
        rows = SEQ // GRID_W
        row = np.repeat(np.arange(rows, dtype=np.float64), GRID_W)
        col = np.tile(np.arange(GRID_W, dtype=np.float64), rows)
        nf = dim // 4
        inv = 10000.0 ** (-np.arange(nf, dtype=np.float64) / nf)
        ar = row[None, :] * inv[:, None]
        ac = col[None, :] * inv[:, None]
        C = np.concatenate([np.cos(ar), np.cos(ar), np.cos(ac), np.cos(ac)], 0)
        S = np.concatenate([-np.sin(ar), np.sin(ar), -np.sin(ac), np.sin(ac)], 0)
        C = np.concatenate([np.ones((dim, CTX)), C], 1)
        S = np.concatenate([np.zeros((dim, CTX)), S], 1)
        return C.astype(np.float32), S.astype(np.float32)
    c64, s64 = tab(64)
    c32, s32 = tab(32)
    rs = np.stack([np.concatenate([c64, c64], 0), np.concatenate([s64, s64], 0)])
    z = np.zeros((64, TALL), np.float32)
    rm = np.stack([np.concatenate([z, c32], 0), np.concatenate([z, s32], 0)])
    return rs, rm


class Builder:
    def __init__(self, nlayers=DEPTH, debug=None, stop_after=None, only=None):
        self.stop_after = stop_after
        self.only = only
        self.nl = nlayers
        self.debug = debug
        nc = bass.Bass("TRN2", target_bir_lowering=False)
        self.nc = nc
        self.k = KB(nc)
        dt = nc.dram_tensor

        def ext(name, shape, dtype=F32):
            return dt(name, list(shape), dtype, kind="ExternalInput").ap()

        def internal(name, shape, dtype=F32):
            return dt(name, list(shape), dtype, kind="Internal").ap()

        self.xin = ext("xin", [D, TALL])
        self.cc = ext("cc", [2, D])
        self.w_mod = ext("w_mod", [DEPTH, D, 6 * D])
        self.b_mod = ext("b_mod", [DEPTH, 6 * D])
        self.norm1_g = ext("norm1_g", [DEPTH, D])
        self.norm2_g = ext("norm2_g", [DEPTH, D])
        self.w_in = ext("w_in", [DEPTH, D, NWIN])
        self.w_out = ext("w_out", [DEPTH, D, D])
        self.w_qb = ext("w_qb", [DEPTH, 256, 512])
        self.w_kvb = ext("w_kvb", [DEPTH, 128, 512])
        self.qn_g = ext("qn_g", [DEPTH, 256])
        self.kvn_g = ext("kvn_g", [DEPTH, 128])
        self.w_up = ext("w_up", [DEPTH, D, 2 * FFH])
        self.w_down = ext("w_down", [DEPTH, FFH, D])
        self.final_g = ext("final_g", [D])
        self.rope_s = ext("rope_s", [2, 128, TALL])
        self.rope_m = ext("rope_m", [2, 96, TALL])
        self.ident = ext("ident", [128, 128])
        self.swa_mask = ext("swa_mask", [128, 2, 4, 128])
        self.swa_sink = ext("swa_sink", [DEPTH, 4])
        self.hg_lb = ext("hg_lb", [2, DEPTH, 256])
        self.s5_lam_re = ext("s5_lam_re", [DEPTH, 2, 16, 64])
        self.s5_lam_im = ext("s5_lam_im", [DEPTH, 2, 16, 64])
        self.s5_log_dt = ext("s5_log_dt", [DEPTH, 2, 16, 64])
        self.s5_b_re = ext("s5_b_re", [DEPTH, 2, 16, 64, 16])
        self.s5_b_im = ext("s5_b_im", [DEPTH, 2, 16, 64, 16])
        self.s5_c_re = ext("s5_c_re", [DEPTH, 2, 16, 16, 64])
        self.s5_c_im = ext("s5_c_im", [DEPTH, 2, 16, 16, 64])
        self.s5_d = ext("s5_d", [DEPTH, 256])
        self.s5_w_glu = ext("s5_w_glu", [DEPTH, 256, 256])
        self.s5_b_glu = ext("s5_b_glu", [DEPTH, 256])
        self.s5_tmask = ext("s5_tmask", [128, 2, 128])
        self.hg_norm_g = ext("hg_norm_g", [DEPTH, 64])
        self.hg_rmask = ext("hg_rmask", [64, 2048])
        self.hg_amask = ext("hg_amask", [128, 2, 4, 128])
        self.hg_cmask = ext("hg_cmask", [128, 4])
        self.out = dt("out", [D, SEQ], F32, kind="ExternalOutput").ap()
        self.xres = internal("xres", [D, TALL])
        self.s_sq = internal("s_sq", [256, TALL], BF16)
        self.s_sk = internal("s_sk", [128, TALL], BF16)
        self.s_sv = internal("s_sv", [TALL, 130], BF16)
        self.s_u = internal("s_u", [TALL, 256], F32)
        self.s_hq = internal("s_hq", [256, TALL], BF16)
        self.s_zf = internal("s_zf", [256, TALL], F32)
        self.s_zb = internal("s_zb", [256, TALL], F32)
        self.s_hg = internal("s_hg", [256, TALL], F32)
        self.s_hi = internal("s_hi", [TALL, 256], BF16)
        self.s_mq = internal("s_mq", [4, 96, TALL], BF16)
        self.s_mk = internal("s_mk", [4, 96, TALL], BF16)
        self.s_mv = internal("s_mv", [TALL, 260], BF16)
        self.ymix = internal("ymix", [D, TALL], BF16)
        self.s_ob = internal("s_ob", [64, 4, TALL], F32)
        self.s_y = [internal(f"s_y{d}", [128, 16, TALL // 8], F32) for d in range(2)]
        if debug:
            self.dbg = {n: dt("dbg_" + n, list(v[0]), v[1], kind="ExternalOutput").ap() for n, v in debug.items()}
        self.build()

    def build(self):
        k = self.k
        nc = self.nc
        es = k.es
        self.ones_bf = k.sb(es, "ones_bf", [128, 128], BF16)
        self.ident_bf = k.sb(es, "ident_bf", [128, 128], BF16)
        self.ident_f = k.sb(es, "ident_f", [128, 128], F32)
        self.mod = k.sb(es, "mod", [128, DEPTH, 48, 2], F32)
        self.gs = k.sb(es, "gs", [128, DEPTH, 2, 8, 2], F32)
        self.epsb = k.sb(es, "epsb", [128, 1], F32)
        k.op("dve", lambda e: e.memset(self.ones_bf[:], 1.0), w=[self.ones_bf])
        k.op("dve", lambda e: e.memset(self.epsb[:], EPS), w=[self.epsb])
        k.dma(self.ident_f[:], self.ident, w=[self.ident_f])
        k.dma(self.ident_bf[:], self.ident, w=[self.ident_bf], q="pool")
        self.psw = [k.ps(es, f"psw{i}", [128, 1024], F32) for i in range(4)]
        self.psb = [PV(self.psw[i // 2], 512 * (i % 2), f"psb{i}") for i in range(8)]
        self.setup_mod()
        k.barrier()
        self.setup_hglb()
        for l in range(self.nl):
            self.layer(l)
        if self.debug:
            k.barrier()
            for n in self.debug:
                src = self.debug[n][2](self) if len(self.debug[n]) > 2 else getattr(self, n)
                nd = len(src.shape)
                pat = " ".join("abcd"[:nd])
                fl = lambda a: a.rearrange(f"{pat} -> ({pat})").rearrange("(p f) -> p f", p=16)
                k.dma(fl(self.dbg[n]), fl(src))
        k.finish()
        es.close()

    def load_fm(self, es, dst_ap, src2d, n, wkeys, wd=128):
        k = self.k
        stg = k.sb(es, "stg", [128, 128], F32)
        ps = self.psb[7]
        k.dma(stg[0:n, 0:wd], src2d, w=[stg])
        k.op("pe", lambda e: e.transpose(out=ps[0:wd, 0:n], in_=stg[0:n, 0:wd], identity=self.ident_f[0:n, 0:n]),
             r=[stg, self.ident_f], w=[ps])
        k.op("dve", lambda e: e.tensor_copy(out=dst_ap, in_=ps[0:wd, 0:n]), r=[ps], w=wkeys)

    def setup_mod(self):
        k = self.k
        with contextlib.ExitStack() as es:
            craw = k.sb(es, "craw", [128, 8, 2], F32)
            csil = k.sb(es, "csil", [128, 8, 2], F32)
            bm = k.sb(es, "bm", [128, DEPTH, 48], F32)
            ng = k.sb(es, "ng", [128, 2, DEPTH, 8], F32)
            wm = [k.sb(es, f"wm{i}", [128, 8, 768], F32) for i in range(2)]
            self.load_fm(es, craw[:, :, 0], self.cc[0].rearrange("(c p) -> c p", p=128), 8, [craw])
            self.load_fm(es, craw[:, :, 1], self.cc[1].rearrange("(c p) -> c p", p=128), 8, [craw])
            for l in range(DEPTH):
                self.load_fm(es, bm[:, l, :], self.b_mod[l].rearrange("(j p) -> j p", p=128), 48, [bm])
            self.load_fm(es, ng[:, 0], self.norm1_g.rearrange("l (c p) -> (l c) p", p=128), 32, [ng])
            self.load_fm(es, ng[:, 1], self.norm2_g.rearrange("l (c p) -> (l c) p", p=128), 32, [ng])
            k.op("act", lambda e: e.activation(out=csil[:], in_=craw[:], func=AF.Silu), r=[craw], w=[csil])
            it = 0
            for l in range(self.nl):
                for grp in range(8):
                    wt = wm[it % 2]
                    it += 1
                    k.dma(wt[:], self.w_mod[l].rearrange("(c p) n -> p c n", p=128)[:, :, grp * 768:(grp + 1) * 768], w=[wt])
                    for n in range(6):
                        ps = self.psb[n % 4]
                        for kc in range(8):
                            k.op("pe", lambda e: e.matmul(ps[:, 0:2], lhsT=wt[:, kc, n * 128:(n + 1) * 128], rhs=csil[:, kc, :],
                                                          start=(kc == 0), stop=(kc == 7)), r=[wt, csil], w=[ps])
                        j = grp * 6 + n
                        k.op("dve", lambda e: e.tensor_scalar(out=self.mod[:, l, j, :], in0=ps[:, 0:2], scalar1=bm[:, l, j:j + 1],
                                                              scalar2=None, op0=ALU.add), r=[ps, bm], w=[self.mod])
                for which, sci in ((0, 1), (1, 4)):
                    for j in range(2):
                        k.op("dve", lambda e: e.scalar_tensor_tensor(out=self.gs[:, l, which, :, j], in0=self.mod[:, l, sci * 8:(sci + 1) * 8, j],
                                                                     scalar=1.0, in1=ng[:, which, l, :], op0=ALU.add, op1=ALU.mult),
                             r=[self.mod, ng], w=[self.gs])
            k.barrier()

    def layer(self, l):
        k = self.k
        xsrc = self.xin if l == 0 else self.xres
        if self.stop_after == "mod":
            return
        self.phase_proj(l, xsrc)
        k.barrier()
        if self.stop_after == "proj":
            return
        if self.only is None and OVERLAP_S5:
            gen = self.phase_s5_gen(l)
            next(gen)
            need_ctx = l < DEPTH - 1
            npi = (16 * 4 * 33) + (4 if need_ctx else 0)
            rate = self.s5_nitems / float(npi)
            acc = [0.0]

            def filler():
                acc[0] += rate
                while acc[0] >= 1.0:
                    acc[0] -= 1.0
                    next(gen, None)
            self.phase_mla(l, filler=filler)
            for _ in gen:
                pass
        else:
            if self.only in (None, "mla"):
                self.phase_mla(l)
        if self.only is None and OVERLAP_SWA:
            sgen = self.phase_swa_gen(l, corun=True)
            next(sgen)
            self.phase_hg(l, filler=lambda: next(sgen, None))
            for _ in sgen:
                pass
        elif self.only in (None, "swa"):
            for _ in self.phase_swa_gen(l):
                pass
        if self.only == "hg" or (self.only is None and not OVERLAP_SWA):
            self.phase_hg(l)
        if self.only == "s5" or (self.only is None and not OVERLAP_S5):
            for _ in self.phase_s5_gen(l):
                pass
        if self.stop_after == "mix":
            return
        self.phase_ffn(l, xsrc)

    def norm_mod(self, es_tiles, xsrc, t0, n, l, which, ctxflag, load=True, gain=None, bias=None, out_f32=None):
        k = self.k
        xt, sq, rstd, tmps, ht, ps = es_tiles
        shi = 0 if which == 0 else 3
        if load:
            k.dma(xt[:, :, :n], xsrc.rearrange("(c p) t -> p c t", p=128)[:, :, t0:t0 + n], w=[xt])
        for c in range(8):
            k.op("pool", lambda e: e.tensor_tensor(out=sq[:, c, :n], in0=xt[:, c, :n], in1=xt[:, c, :n], op=ALU.mult), r=[xt], w=[sq])
        for c in range(8):
            k.op("pe", lambda e: e.matmul(ps[:, :n], lhsT=self.ones_bf[:], rhs=sq[:, c, :n], start=(c == 0), stop=(c == 7)),
                 r=[sq, self.ones_bf], w=[ps])
        k.op("act", lambda e: e.activation(out=rstd[:, :n], in_=ps[:, :n], func=AF.Sqrt, bias=self.epsb[:], scale=1.0 / D),
             r=[ps, self.epsb], w=[rstd])
        k.op("dve", lambda e: e.reciprocal(out=rstd[:, :n], in_=rstd[:, :n]), r=[rstd], w=[rstd])
        for c in range(8):
            g_ap = gain[:, c:c + 1] if gain is not None else self.gs[:, l, which, c, ctxflag:ctxflag + 1]
            if out_f32 is not None:
                tmp = tmps[c % len(tmps)]
                k.op("dve", lambda e: e.scalar_tensor_tensor(out=tmp[:, :n], in0=xt[:, c, :n], scalar=g_ap,
                                                             in1=rstd[:, :n], op0=ALU.mult, op1=ALU.mult), r=[xt, rstd, self.gs], w=[tmp])
                k.dma(out_f32(c), tmp[:, :n], r=[tmp])
                continue
            tmp = tmps[c % len(tmps)]
            k.op("dve", lambda e: e.scalar_tensor_tensor(out=tmp[:, :n], in0=xt[:, c, :n], scalar=g_ap,
                                                         in1=rstd[:, :n], op0=ALU.mult, op1=ALU.mult), r=[xt, rstd, self.gs], w=[tmp])
            k.op("act", lambda e: e.activation(out=ht[:, c, :n], in_=tmp[:, :n], func=AF.Identity,
                                               bias=self.mod[:, l, shi * 8 + c, ctxflag:ctxflag + 1], scale=1.0),
                 r=[tmp, self.mod], w=[ht])

    def phase_proj(self, l, xsrc):
        k = self.k
        with contextlib.ExitStack() as es:
            win = k.sb(es, "win", [128, 8, NWIN], BF16)
            wqb = k.sb(es, "wqb", [128, 2, 512], BF16)
            wkvb = k.sb(es, "wkvb", [128, 512], BF16)
            qng = k.sb(es, "qng", [128, 2], F32)
            kvng = k.sb(es, "kvng", [128, 1], F32)
            for c in range(8):
                k.dma(win[:, c, :], self.w_in[l, c * 128:(c + 1) * 128, :], w=[win], q="pool")
            k.dma(wqb[:], self.w_qb[l].rearrange("(c p) n -> p c n", p=128), w=[wqb], q="pool")
            k.dma(wkvb[:], self.w_kvb[l], w=[wkvb], q="pool")
            self.load_fm(es, qng[:], self.qn_g[l].rearrange("(c p) -> c p", p=128), 2, [qng])
            self.load_fm(es, kvng[:], self.kvn_g[l].rearrange("(c p) -> c p", p=128), 1, [kvng])
            NB = 2
            xt = [k.sb(es, f"xt{i}", [128, 8, 512], F32) for i in range(NB)]
            sq = [k.sb(es, f"sq{i}", [128, 8, 512], BF16) for i in range(NB)]
            rstd = [k.sb(es, f"rstd{i}", [128, 512], F32) for i in range(NB)]
            tmp = [k.sb(es, f"tmp{i}", [128, 512], F32) for i in range(2)]
            ht = [k.sb(es, f"ht{i}", [128, 8, 512], BF16) for i in range(NB)]
            rs = [k.sb(es, f"rs{i}", [128, 2, 512], F32) for i in range(NB)]
            rm = [k.sb(es, f"rm{i}", [96, 2, 512], F32) for i in range(NB)]
            ob = [k.sb(es, f"ob{i}", [128, 512], BF16) for i in range(4)]
            of = [k.sb(es, f"of{i}", [128, 512], F32) for i in range(4)]
            r1 = [k.sb(es, f"r1{i}", [128, 512], F32) for i in range(2)]
            r2 = [k.sb(es, f"r2{i}", [128, 512], F32) for i in range(2)]
            cqn = [k.sb(es, f"cqn{i}", [128, 2, 512], BF16) for i in range(NB)]
            ckvn = [k.sb(es, f"ckvn{i}", [128, 512], BF16) for i in range(NB)]
            nsq = [k.sb(es, f"nsq{i}", [128, 512], BF16) for i in range(2)]
            nrs = [k.sb(es, f"nrs{i}", [128, 512], F32) for i in range(2)]
            tmo = [k.sb(es, f"tmo{i}", [128, 640], F32) for i in range(2)]
            tmb = [k.sb(es, f"tmb{i}", [128, 256], BF16) for i in range(2)]
            svb = [k.sb(es, f"svb{i}", [128, 2, 65], BF16) for i in range(2)]
            mvb = [k.sb(es, f"mvb{i}", [128, 4, 65], BF16) for i in range(2)]
            for i in range(2):
                k.op("pool", lambda e: e.memset(svb[i][:, :, 64:65], 1.0), w=[svb[i]])
                k.op("pool", lambda e: e.memset(mvb[i][:, :, 64:65], 1.0), w=[mvb[i]])
            state = {"ob": 0, "of": 0, "ps": 0, "r": 0, "n": 0}

            def nxt(lst, key):
                i = state[key]
                state[key] = (i + 1) % len(lst)
                return lst[i]

            def fm_mm(ps, col0, m, hT, n, lhs_w=None, p0=0):
                for kc in range(8):
                    k.op("pe", lambda e: e.matmul(ps[0:m, :n], lhsT=win[:, kc, col0:col0 + m], rhs=hT[:, kc, :n],
                                                  start=(kc == 0), stop=(kc == 7)), r=[win, hT], w=[ps])

            for bi, (t0, n) in enumerate(_blocks()):
                ctxflag = 1 if bi == 0 else 0
                b = bi % NB
                self.norm_mod((xt[b], sq[b], rstd[b], tmp, ht[b], self.psb[7]), xsrc, t0, n, l, 0, ctxflag)
                hT = ht[b]
                CUT = float(os.environ.get("KCUT", "99"))
                if bi >= int(os.environ.get("KBLK", "99")):
                    break
                if CUT < 1:
                    continue
                k.dma(rs[b][:, :, :n], self.rope_s[:, :, t0:t0 + n].rearrange("a p t -> p a t"), w=[rs[b]])
                k.dma(rm[b][64:96, :, :n], self.rope_m[:, 64:96, t0:t0 + n].rearrange("a p t -> p a t"), w=[rm[b]])
                for ci, (c_a, c_b, dst) in enumerate(((C_SQ, C_SQS, self.s_sq[0:128]), (C_SQ + 128, C_SQS + 128, self.s_sq[128:256]),
                                                      (C_SK, C_SKS, self.s_sk))):
                    pa = nxt(self.psb[0:6], "ps")
                    fm_mm(pa, c_a, 128, hT, n)
                    pb = nxt(self.psb[0:6], "ps")
                    fm_mm(pb, c_b, 128, hT, n)
                    a1 = nxt(r1, "r")
                    a2 = r2[r1.index(a1)]
                    o = nxt(ob, "ob")
                    k.op("dve", lambda e: e.tensor_tensor(out=a1[:, :n], in0=pa[:, :n], in1=rs[b][:, 0, :n], op=ALU.mult), r=[pa, rs[b]], w=[a1])
                    k.op("dve", lambda e: e.tensor_tensor(out=a2[:, :n], in0=pb[:, :n], in1=rs[b][:, 1, :n], op=ALU.mult), r=[pb, rs[b]], w=[a2])
                    k.op("pool", lambda e: e.tensor_tensor(out=o[:, :n], in0=a1[:, :n], in1=a2[:, :n], op=ALU.add), r=[a1, a2], w=[o])
                    k.dma(dst[:, t0:t0 + n], o[:, :n], r=[o])
                if CUT < 2:
                    continue
                for c in range(2):
                    pa = nxt(self.psb[0:6], "ps")
                    fm_mm(pa, C_HQ + 128 * c, 128, hT, n)
                    o = nxt(ob, "ob")
                    k.op("act", lambda e: e.copy(out=o[:, :n], in_=pa[:, :n]), r=[pa], w=[o])
                    k.dma(self.s_hq[128 * c:128 * c + 128, t0:t0 + n], o[:, :n], r=[o])
                for (c0, dst, fn) in ((C_ZF, self.s_zf, AF.Copy), (C_ZB, self.s_zb, AF.Copy), (C_HG, self.s_hg, AF.Silu)):
                    for c in range(2):
                        pa = nxt(self.psb[0:6], "ps")
                        fm_mm(pa, c0 + 128 * c, 128, hT, n)
                        o = nxt(of, "of")
                        k.op("act", lambda e: e.activation(out=o[:, :n], in_=pa[:, :n], func=fn), r=[pa], w=[o])
                        k.dma(dst[128 * c:128 * c + 128, t0:t0 + n], o[:, :n], r=[o])
                if CUT < 3:
                    continue
                pcq = [nxt(self.psb[0:6], "ps") for _ in range(2)]
                for c in range(2):
                    fm_mm(pcq[c], C_CQ + 128 * c, 128, hT, n)
                pss = self.psb[6]
                for c in range(2):
                    s_ = nxt(nsq, "n")
                    k.op("act", lambda e: e.activation(out=s_[:, :n], in_=pcq[c][:, :n], func=AF.Square), r=[pcq[c]], w=[s_])
                    k.op("pe", lambda e: e.matmul(pss[:, :n], lhsT=self.ones_bf[:], rhs=s_[:, :n], start=(c == 0), stop=(c == 1)),
                         r=[s_, self.ones_bf], w=[pss])
                nr = nrs[0]
                k.op("act", lambda e: e.activation(out=nr[:, :n], in_=pss[:, :n], func=AF.Sqrt, bias=self.epsb[:], scale=1.0 / 256),
                     r=[pss, self.epsb], w=[nr])
                k.op("dve", lambda e: e.reciprocal(out=nr[:, :n], in_=nr[:, :n]), r=[nr], w=[nr])
                for c in range(2):
                    k.op("dve", lambda e: e.scalar_tensor_tensor(out=cqn[b][:, c, :n], in0=pcq[c][:, :n], scalar=qng[:, c:c + 1], in1=nr[:, :n],
                                                                 op0=ALU.mult, op1=ALU.mult), r=[pcq[c], qng, nr], w=[cqn[b]])
                for h in range(4):
                    pa = nxt(self.psb[0:6], "ps")
                    pb = nxt(self.psb[0:6], "ps")
                    for c in range(2):
                        k.op("pe", lambda e: e.matmul(pa[0:96, :n], lhsT=wqb[:, c, h * 128:h * 128 + 96], rhs=cqn[b][:, c, :n],
                                                      start=(c == 0), stop=(c == 1)), r=[wqb, cqn[b]], w=[pa])
                    for c in range(2):
                        k.op("pe", lambda e: e.matmul(pb[0:96, :n], lhsT=wqb[:, c, h * 128 + 32:h * 128 + 128], rhs=cqn[b][:, c, :n],
                                                      start=(c == 0), stop=(c == 1)), r=[wqb, cqn[b]], w=[pb])
                    o = nxt(ob, "ob")
                    a1 = nxt(r1, "r")
                    a2 = r2[r1.index(a1)]
                    k.op("act", lambda e: e.copy(out=o[0:64, :n], in_=pa[0:64, :n]), r=[pa], w=[o])
                    k.op("dve", lambda e: e.tensor_tensor(out=a1[64:96, :n], in0=pa[64:96, :n], in1=rm[b][64:96, 0, :n], op=ALU.mult),
                         r=[pa, rm[b]], w=[a1])
                    k.op("dve", lambda e: e.tensor_tensor(out=a2[64:96, :n], in0=pb[64:96, :n], in1=rm[b][64:96, 1, :n], op=ALU.mult),
                         r=[pb, rm[b]], w=[a2])
                    k.op("pool", lambda e: e.tensor_tensor(out=o[64:96, :n], in0=a1[64:96, :n], in1=a2[64:96, :n], op=ALU.add),
                         r=[a1, a2, o], w=[o])
                    k.dma(self.s_mq[h, :, t0:t0 + n], o[0:96, :n], r=[o])
                if CUT < 4:
                    continue
                pkv = nxt(self.psb[0:6], "ps")
                fm_mm(pkv, C_CKV, 128, hT, n)
                s_ = nxt(nsq, "n")
                k.op("act", lambda e: e.activation(out=s_[:, :n], in_=pkv[:, :n], func=AF.Square), r=[pkv], w=[s_])
                k.op("pe", lambda e: e.matmul(pss[:, :n], lhsT=self.ones_bf[:], rhs=s_[:, :n], start=True, stop=True),
                     r=[s_, self.ones_bf], w=[pss])
                nr = nrs[1]
                k.op("act", lambda e: e.activation(out=nr[:, :n], in_=pss[:, :n], func=AF.Sqrt, bias=self.epsb[:], scale=1.0 / 128),
                     r=[pss, self.epsb], w=[nr])
                k.op("dve", lambda e: e.reciprocal(out=nr[:, :n], in_=nr[:, :n]), r=[nr], w=[nr])
                k.op("dve", lambda e: e.scalar_tensor_tensor(out=ckvn[b][:, :n], in0=pkv[:, :n], scalar=kvng[:, 0:1], in1=nr[:, :n],
                                                             op0=ALU.mult, op1=ALU.mult), r=[pkv, kvng, nr], w=[ckvn[b]])
                if CUT < 4.2:
                    continue
                pa = nxt(self.psb[0:6], "ps")
                pb = nxt(self.psb[0:6], "ps")
                fm_mm(pa, C_KR - 64, 96, hT, n)
                fm_mm(pb, C_KRS - 64, 96, hT, n)
                a1 = nxt(r1, "r")
                a2 = r2[r1.index(a1)]
                kro = nxt(ob, "ob")
                k.op("dve", lambda e: e.tensor_tensor(out=a1[64:96, :n], in0=pa[64:96, :n], in1=rm[b][64:96, 0, :n], op=ALU.mult),
                     r=[pa, rm[b]], w=[a1])
                k.op("dve", lambda e: e.tensor_tensor(out=a2[64:96, :n], in0=pb[64:96, :n], in1=rm[b][64:96, 1, :n], op=ALU.mult),
                     r=[pb, rm[b]], w=[a2])
                k.op("pool", lambda e: e.tensor_tensor(out=kro[64:96, :n], in0=a1[64:96, :n], in1=a2[64:96, :n], op=ALU.add),
                     r=[a1, a2], w=[kro])
                if CUT < 4.4:
                    continue
                for h in range(4):
                    k.dma(self.s_mk[h, 64:96, t0:t0 + n], kro[64:96, :n], r=[kro])
                    if CUT < 4.6:
                        continue
                    pa = nxt(self.psb[0:6], "ps")
                    if os.environ.get("KVAR") == "A":
                        k.op("pe", lambda e: e.matmul(pa[:, :n], lhsT=wkvb[:, h * 128:h * 128 + 128], rhs=ckvn[b][:, :n], start=True, stop=True),
                             r=[wkvb, ckvn[b]], w=[pa])
                    else:
                        k.op("pe", lambda e: e.matmul(pa[0:64, :n], lhsT=wkvb[:, h * 128:h * 128 + 64], rhs=ckvn[b][:, :n], start=True, stop=True),
                             r=[wkvb, ckvn[b]], w=[pa])
                    if CUT < 4.7:
                        continue
                    o = nxt(ob, "ob")
                    k.op("act", lambda e: e.copy(out=o[0:64, :n], in_=pa[0:64, :n]), r=[pa], w=[o])
                    if CUT < 4.8:
                        continue
                    k.dma(self.s_mk[h, 0:64, t0:t0 + n], o[0:64, :n], r=[o])
                if CUT < 5:
                    continue
                for st in range(n // 128):
                    ts_ = slice(st * 128, st * 128 + 128)
                    pa = nxt(self.psb[0:6], "ps")
                    pb = nxt(self.psb[0:6], "ps")
                    for kc in range(8):
                        k.op("pe", lambda e: e.matmul(pa[:, 0:512], lhsT=hT[:, kc, ts_], rhs=win[:, kc, C_TM:C_TM + 512],
                                                      start=(kc == 0), stop=(kc == 7)), r=[win, hT], w=[pa])
                    for kc in range(8):
                        k.op("pe", lambda e: e.matmul(pb[:, 0:128], lhsT=hT[:, kc, ts_], rhs=win[:, kc, C_TM + 512:C_TM + 640],
                                                      start=(kc == 0), stop=(kc == 7)), r=[win, hT], w=[pb])
                    k.op("pe", lambda e: e.matmul(pb[:, 128:384], lhsT=ckvn[b][:, ts_],
                                                  rhs=wkvb[:].rearrange("p (h x) -> p h x", x=128)[:, :, 64:128],
                                                  start=True, stop=True), r=[wkvb, ckvn[b]], w=[pb])
                    uo = tmo[st % 2]
                    bo = tmb[st % 2]
                    so = svb[st % 2]
                    mo = mvb[st % 2]
                    k.op("act", lambda e: e.copy(out=uo[:, 0:256], in_=pa[:, 0:256]), r=[pa], w=[uo])
                    k.op("dve", lambda e: e.tensor_copy(out=bo[:, 0:128], in_=pa[:, 384:512]), r=[pa], w=[bo])
                    k.op("dve", lambda e: e.tensor_copy(out=bo[:, 128:256], in_=pb[:, 0:128]), r=[pb, bo], w=[bo])
                    k.op("dve", lambda e: e.tensor_copy(out=so[:, :, 0:64], in_=pa[:, 256:384].rearrange("p (g x) -> p g x", g=2)), r=[pa, so], w=[so])
                    k.op("act", lambda e: e.copy(out=mo[:, :, 0:64], in_=pb[:, 128:384].rearrange("p (g x) -> p g x", g=4)), r=[pb, mo], w=[mo])
                    tt = t0 + st * 128
                    k.dma(self.s_u[tt:tt + 128, :], uo[:, 0:256], r=[uo])
                    k.dma(self.s_sv[tt:tt + 128, :], so[:].rearrange("p g x -> p (g x)"), r=[so])
                    k.dma(self.s_hi[tt:tt + 128, :], bo[:, 0:256], r=[bo])
                    k.dma(self.s_mv[tt:tt + 128, :], mo[:].rearrange("p g x -> p (g x)"), r=[mo])
            k.barrier()


    def phase_ffn(self, l, xsrc):
        k = self.k
        last = (l == DEPTH - 1)
        NB = 256
        with contextlib.ExitStack() as es:
            wout = k.sb(es, "wout", [128, 8, D], BF16)
            wup = k.sb(es, "wup", [128, 8, 2 * FFH], BF16)
            wdn = k.sb(es, "wdn", [128, 22, D], BF16)
            for c in range(8):
                k.dma(wout[:, c, :], self.w_out[l, c * 128:(c + 1) * 128, :], w=[wout], q="pool")
                k.dma(wup[:, c, :], self.w_up[l, c * 128:(c + 1) * 128, :], w=[wup], q="pool")
            for j in range(22):
                k.dma(wdn[:, j, :], self.w_down[l, j * 128:(j + 1) * 128, :], w=[wdn], q="pool")
            fg = None
            if last:
                fg = k.sb(es, "fg", [128, 8], F32)
                self.load_fm(es, fg[:], self.final_g.rearrange("(c p) -> c p", p=128), 8, [fg])
            xt = [k.sb(es, f"fxt{i}", [128, 8, NB], F32) for i in range(2)]
            ym = [k.sb(es, f"fym{i}", [128, 8, NB], BF16) for i in range(2)]
            sq = k.sb(es, "fsq", [128, 8, NB], BF16)
            tmp = [k.sb(es, f"ftmp{i}", [128, NB], F32) for i in range(2)]
            ht = [k.sb(es, f"fht{i}", [128, 8, NB], BF16) for i in range(2)]
            rstd = k.sb(es, "frstd", [128, NB], F32)
            sq2, rstd2, tmp2 = sq, rstd, tmp
            aT = k.sb(es, "faT", [128, 22, NB], BF16)
            sg = [k.sb(es, f"fsg{i}", [128, NB], F32) for i in range(2)]
            psi = [0]

            def nps():
                psi[0] = (psi[0] + 1) % 6
                return self.psb[psi[0]]

            t_start = CTX if last else 0
            t0s = list(range(t_start, TALL, NB))
            n = NB

            def stage_a(bi):
                t0 = t0s[bi]
                flag = 1 if t0 < CTX else 0
                b = bi % 2
                k.dma(xt[b][:, :, :n], xsrc.rearrange("(c p) t -> p c t", p=128)[:, :, t0:t0 + n], w=[xt[b]])
                k.dma(ym[b][:, :, :n], self.ymix.rearrange("(c p) t -> p c t", p=128)[:, :, t0:t0 + n], w=[ym[b]])
                for oc in range(8):
                    ps = nps()
                    for kc in range(8):
                        k.op("pe", lambda e: e.matmul(ps[:, :n], lhsT=wout[:, kc, oc * 128:(oc + 1) * 128], rhs=ym[b][:, kc, :n],
                                                      start=(kc == 0), stop=(kc == 7)), r=[wout, ym[b]], w=[ps])
                    k.op("dve", lambda e: e.scalar_tensor_tensor(out=xt[b][:, oc, :n], in0=ps[:, :n], scalar=self.mod[:, l, 16 + oc, flag:flag + 1],
                                                                 in1=xt[b][:, oc, :n], op0=ALU.mult, op1=ALU.add), r=[ps, xt[b], self.mod], w=[xt[b]])
                self.norm_mod((xt[b], sq, rstd, tmp, ht[b], self.psb[7]), None, t0, n, l, 1, flag, load=False)

            stage_a(0)
            for bi, t0 in enumerate(t0s):
                flag = 1 if t0 < CTX else 0
                b = bi % 2
                for j in range(22):
                    if j == 4 and bi + 1 < len(t0s):
                        stage_a(bi + 1)
                    pg = nps()
                    pu = nps()
                    for kc in range(8):
                        k.op("pe", lambda e: e.matmul(pg[:, :n], lhsT=wup[:, kc, j * 128:(j + 1) * 128], rhs=ht[b][:, kc, :n],
                                                      start=(kc == 0), stop=(kc == 7)), r=[wup, ht[b]], w=[pg])
                    for kc in range(8):
                        k.op("pe", lambda e: e.matmul(pu[:, :n], lhsT=wup[:, kc, FFH + j * 128:FFH + (j + 1) * 128], rhs=ht[b][:, kc, :n],
                                                      start=(kc == 0), stop=(kc == 7)), r=[wup, ht[b]], w=[pu])
                    s_ = sg[j % 2]
                    k.op("act", lambda e: e.activation(out=s_[:, :n], in_=pg[:, :n], func=AF.Silu), r=[pg], w=[s_])
                    k.op("dve", lambda e: e.tensor_tensor(out=aT[:, j, :n], in0=s_[:, :n], in1=pu[:, :n], op=ALU.mult), r=[s_, pu], w=[aT])
                for oc in range(8):
                    ps = nps()
                    for j in range(22):
                        k.op("pe", lambda e: e.matmul(ps[:, :n], lhsT=wdn[:, j, oc * 128:(oc + 1) * 128], rhs=aT[:, j, :n],
                                                      start=(j == 0), stop=(j == 21)), r=[wdn, aT], w=[ps])
                    k.op("dve", lambda e: e.scalar_tensor_tensor(out=xt[b][:, oc, :n], in0=ps[:, :n], scalar=self.mod[:, l, 40 + oc, flag:flag + 1],
                                                                 in1=xt[b][:, oc, :n], op0=ALU.mult, op1=ALU.add), r=[ps, xt[b], self.mod], w=[xt[b]])
                if not last:
                    k.dma(self.xres.rearrange("(c p) t -> p c t", p=128)[:, :, t0:t0 + n], xt[b][:, :, :n], r=[xt[b]])
                else:
                    self.norm_mod((xt[b], sq2, rstd2, tmp2, None, self.psb[7]), None, t0, n, l, 1, flag, load=False, gain=fg,
                                  out_f32=lambda c: self.out[c * 128:(c + 1) * 128, t0 - CTX:t0 - CTX + n])
            k.barrier()

    def attn_finish(self, OT, n, Osb, rec, yo, sel, dst_aps, nh=1, bc=None):
        k = self.k
        bc = self.psb[6] if bc is None else bc
        k.op("act", lambda e: e.copy(out=Osb[0:65, :n], in_=OT[0:65, :n]), r=[OT], w=[Osb])
        k.op("pe", lambda e: e.matmul(bc[0:64, :n], lhsT=sel[0:65, :], rhs=Osb[0:65, :n], start=True, stop=True), r=[sel, Osb], w=[bc])
        k.op("dve", lambda e: e.reciprocal(out=rec[0:64, :n], in_=bc[0:64, :n]), r=[bc], w=[rec])
        k.op("dve", lambda e: e.tensor_tensor(out=yo[0:64, :n], in0=Osb[0:64, :n], in1=rec[0:64, :n], op=ALU.mult), r=[Osb, rec], w=[yo])
        w = n // nh
        for j, dst in enumerate(dst_aps):
            k.dma(dst, yo[0:64, j * w:(j + 1) * w], r=[yo])

    def make_sel(self, es):
        k = self.k
        sel = k.sb(es, "sel", [65, 64], F32)
        k.op("dve", lambda e: e.memset(sel[0:64, :], 0.0), w=[sel])
        k.op("dve", lambda e: e.memset(sel[64:65, :], 1.0), r=[sel], w=[sel])
        return sel

    def phase_mla(self, l, filler=None):
        k = self.k
        need_ctx = l < DEPTH - 1
        NT = TALL // 128
        with contextlib.ExitStack() as es:
            KT = k.sb(es, "mKT", [96, 2, TALL], BF16)
            Va = k.sb(es, "mVa", [128, NT, 2, 65], BF16)
            sel = self.make_sel(es)
            QT = [k.sb(es, f"mQT{i}", [96, 2, 512], BF16) for i in range(2)]
            PT = [k.sb(es, f"mPT{i}", [128, 2, 512], BF16) for i in range(2)]
            Osb = [k.sb(es, f"mOsb{i}", [65, 512], F32) for i in range(2)]
            rec = [k.sb(es, f"mrec{i}", [64, 512], F32) for i in range(2)]
            yo = [k.sb(es, f"myo{i}", [64, 512], BF16) for i in range(2)]
            cnt = 0
            vsrc = self.s_mv.rearrange("(t p) (h x) -> p t h x", p=128, h=4)
            qi = 0
            for hp in range(2):
                for j in range(2):
                    k.dma(KT[:, j, :], self.s_mk[2 * hp + j], w=[KT])
                for t in range(0, NT, 6):
                    k.dma(Va[:, t:t + 6], vsrc[:, t:t + 6, 2 * hp:2 * hp + 2, :], w=[Va])
                for bi, (t0, n) in enumerate(_blocks()):
                    if bi == 0 and not need_ctx:
                        continue
                    ktiles = [0, 1] if bi == 0 else list(range(NT))
                    b = qi % 2
                    qi += 1
                    k.dma(QT[b][:, :, :n], self.s_mq[2 * hp:2 * hp + 2, :, t0:t0 + n].rearrange("h p t -> p h t"), w=[QT[b]])
                    for hj in range(2):
                        h = 2 * hp + hj
                        OT = self.psb[4 + cnt % 2]
                        npair = len(ktiles) // 2

                        def st_pair(ip):
                            STw = self.psw[ip % 2]
                            for j in range(2):
                                kt = ktiles[2 * ip + j]
                                k.op("pe", lambda e: e.matmul(STw[:, j * 512:j * 512 + n], lhsT=KT[:, hj, kt * 128:(kt + 1) * 128], rhs=QT[b][:, hj, :n],
                                                              start=True, stop=True), r=[KT, QT[b]], w=[STw])
                        st_pair(0)
                        for ip in range(npair):
                            if ip + 1 < npair:
                                st_pair(ip + 1)
                            STw = self.psw[ip % 2]
                            P = PT[ip % 2]
                            k.op("act", lambda e: e.activation(out=P[:, :, :n], in_=STw[:, :].rearrange("p (j x) -> p j x", j=2)[:, :, :n], func=AF.Exp,
                                                               scale=MLA_SCALE), r=[STw], w=[P])
                            for j in range(2):
                                kt = ktiles[2 * ip + j]
                                k.op("pe", lambda e: e.matmul(OT[0:65, :n], lhsT=Va[:, kt, hj, :], rhs=P[:, j, :n], start=(ip == 0 and j == 0),
                                                              stop=(ip == npair - 1 and j == 1)), r=[Va, P], w=[OT])
                            if filler is not None:
                                filler()
                        self.attn_finish(OT, n, Osb[cnt % 2], rec[cnt % 2], yo[cnt % 2], sel,
                                         [self.ymix[768 + h * 64:768 + (h + 1) * 64, t0:t0 + n]])
                        cnt += 1
            k.barrier()

    def phase_swa_gen(self, l, corun=False):
        k = self.k
        need_ctx = l < DEPTH - 1
        NT = TALL // 128
        with contextlib.ExitStack() as es:
            QT = k.sb(es, "sQT", [128, 2, TALL], BF16)
            KT = k.sb(es, "sKT", [128, TALL], BF16)
            Va = k.sb(es, "sVa", [128, NT, 2, 65], BF16)
            msk = k.sb(es, "smsk", [128, 2, 4, 128], BF16)
            sel = self.make_sel(es)
            sk = k.sb(es, "ssk", [1, 4], F32)
            skrow = k.sb(es, "sskrow", [1, 4, 128], BF16)
            e64 = k.sb(es, "se64", [1, 65], BF16)
            k.dma(msk[:], self.swa_mask, w=[msk], q="pool")
            k.dma(sk[:], self.swa_sink[l:l + 1, :], w=[sk])
            k.op("act", lambda e: e.activation(out=sk[:], in_=sk[:], func=AF.Exp), r=[sk], w=[sk])
            for h in range(4):
                k.op("dve", lambda e: e.tensor_scalar(out=skrow[:, h, :], in0=self.ones_bf[0:1, :], scalar1=sk[0:1, h:h + 1], scalar2=None,
                                                      op0=ALU.mult), r=[sk, self.ones_bf], w=[skrow])
            k.op("dve", lambda e: e.memset(e64[:, 0:64], 0.0), w=[e64])
            k.op("dve", lambda e: e.memset(e64[:, 64:65], 1.0), r=[e64], w=[e64])
            for c in range(2):
                k.dma(QT[:, c, :], self.s_sq[c * 128:(c + 1) * 128, :], w=[QT])
            k.dma(KT[:], self.s_sk, w=[KT])
            vsrc = self.s_sv.rearrange("(t p) f -> p t f", p=128)
            for t in range(0, NT, 6):
                k.dma(Va[:, t:t + 6].rearrange("p t h x -> p t (h x)"), vsrc[:, t:t + 6, :], w=[Va])
            PT = [k.sb(es, f"sPT{i}", [128, 4, 128], BF16) for i in range(3)]
            stb = [self.psb[6]] if corun else [self.psb[0], self.psb[1], self.psb[2]]
            otb = [self.psb[7]] if corun else [self.psb[4], self.psb[5]]
            Osb = [k.sb(es, f"sOsb{i}", [65, 512], F32) for i in range(2)]
            rec = [k.sb(es, f"srec{i}", [64, 512], F32) for i in range(2)]
            yo = [k.sb(es, f"syo{i}", [64, 512], BF16) for i in range(2)]
            cnt = 0
            yield "ready"
            for gt in range(NT):
                if gt < 2:
                    if not need_ctx:
                        continue
                    keys = [(0, None), (1, None)]
                else:
                    keys = []
                    if gt > 2:
                        keys.append((gt - 1, 0))
                    keys.append((gt, None))
                    if gt < NT - 1:
                        keys.append((gt + 1, 1))
                    keys += [(0, None), (1, None)]
                qs = slice(gt * 128, (gt + 1) * 128)
                OT = otb[cnt % len(otb)]
                nk = len(keys)

                assert not corun

                def st_mm(i):
                    STw = self.psw[i % 2]
                    kt = keys[i][0]
                    for g in range(2):
                        p0 = 64 * g
                        k.op("pe", lambda e: e.matmul(STw[:, g * 512:g * 512 + 256], lhsT=KT[p0:p0 + 64, kt * 128:(kt + 1) * 128], rhs=QT[p0:p0 + 64, :, qs],
                                                      start=True, stop=True), r=[KT, QT], w=[STw])
                st_mm(0)
                for i in range(nk):
                    if i + 1 < nk:
                        st_mm(i + 1)
                    ST = self.psw[i % 2]
                    P = PT[i % 3]
                    kt, mk = keys[i]
                    k.op("act", lambda e: e.activation(out=P[:].rearrange("p (g a) b -> p g (a b)", g=2),
                                                       in_=ST[:, :].rearrange("p (g x) -> p g x", g=2)[:, :, 0:256], func=AF.Exp, scale=SWA_SCALE),
                         r=[ST], w=[P])
                    if mk is not None:
                        k.op("dve", lambda e: e.tensor_tensor(out=P[:], in0=P[:], in1=msk[:, mk], op=ALU.mult), r=[P, msk], w=[P])
                    for g in range(2):
                        k.op("pe", lambda e: e.matmul(OT[0:65, g * 256:(g + 1) * 256], lhsT=Va[:, kt, g, :], rhs=P[:, 2 * g:2 * g + 2, :],
                                                      start=(i == 0 and g == 0), stop=False, skip_group_check=True), r=[Va, P], w=[OT])
                k.op("pe", lambda e: e.matmul(OT[0:65, 0:512], lhsT=e64[0:1, :], rhs=skrow[0:1, :, :], start=False, stop=True,
                                              skip_group_check=True), r=[e64, skrow], w=[OT])
                self.attn_finish(OT, 512, Osb[cnt % 2], rec[cnt % 2], yo[cnt % 2], sel,
                                 [self.ymix[256 + h * 64:256 + (h + 1) * 64, qs] for h in range(4)], nh=4,
                                 bc=(OT if corun else None))
                cnt += 1
                yield
            k.barrier()

    def setup_hglb(self):
        k = self.k
        es = k.es
        self.hglb = k.sb(es, "hglb", [64, 2, DEPTH, 4], F32)
        self.hgoml = k.sb(es, "hgoml", [64, 2, DEPTH, 4], F32)
        self.hgnoml = k.sb(es, "hgnoml", [64, 2, DEPTH, 4], F32)
        with contextlib.ExitStack() as es2:
            e_ = k.sb(es2, "lbe", [64, 2, DEPTH, 4], F32)
            s_ = k.sb(es2, "lbs", [64, 2, 4], F32)
            self.load_fm(es2, e_[:].rearrange("p d l h -> p (d l h)"), self.hg_lb.rearrange("d l (h x) -> (d l h) x", x=64), 32, [e_], wd=64)
            k.op("act", lambda e: e.activation(out=e_[:], in_=e_[:], func=AF.Exp), r=[e_], w=[e_])
            k.op("dve", lambda e: e.tensor_tensor(out=s_[:], in0=e_[:, :, 0, :], in1=e_[:, :, 1, :], op=ALU.add), r=[e_], w=[s_])
            for l in (2, 3):
                k.op("dve", lambda e: e.tensor_tensor(out=s_[:], in0=s_[:], in1=e_[:, :, l, :], op=ALU.add), r=[e_, s_], w=[s_])
            k.op("dve", lambda e: e.reciprocal(out=s_[:], in_=s_[:]), r=[s_], w=[s_])
            for l in range(DEPTH):
                k.op("dve", lambda e: e.tensor_tensor(out=e_[:, :, l, :], in0=e_[:, :, l, :], in1=s_[:], op=ALU.mult), r=[e_, s_], w=[e_])
            k.op("dve", lambda e: e.memset(self.hglb[:, :, 0, :], 0.0), w=[self.hglb])
            k.op("dve", lambda e: e.tensor_copy(out=self.hglb[:, :, 1, :], in_=e_[:, :, 1, :]), r=[e_, self.hglb], w=[self.hglb])
            for l in (2, 3):
                k.op("dve", lambda e: e.tensor_tensor(out=self.hglb[:, :, l, :], in0=self.hglb[:, :, l - 1, :], in1=e_[:, :, l, :], op=ALU.add),
                     r=[e_, self.hglb], w=[self.hglb])
            k.op("dve", lambda e: e.tensor_scalar(out=self.hgoml[:], in0=self.hglb[:], scalar1=-1.0, scalar2=1.0, op0=ALU.mult, op1=ALU.add),
                 r=[self.hglb], w=[self.hgoml])
            k.op("dve", lambda e: e.tensor_scalar(out=self.hgnoml[:], in0=self.hglb[:], scalar1=-1.0, scalar2=None, op0=ALU.add),
                 r=[self.hglb], w=[self.hgnoml])
            k.barrier()

    def phase_hg(self, l, filler=None):
        k = self.k
        with contextlib.ExitStack() as es:
            rmask = k.sb(es, "hrm", [64, 2048], F32)
            amask = k.sb(es, "ham", [128, 2, 4, 128], BF16)
            cmask = k.sb(es, "hcm", [128, 4], F32)
            ng = k.sb(es, "hng", [64, 1], F32)
            k.dma(rmask[:], self.hg_rmask, w=[rmask])
            k.dma(amask[:], self.hg_amask, w=[amask], q="pool")
            k.dma(cmask[:], self.hg_cmask, w=[cmask])
            self.load_fm(es, ng[:], self.hg_norm_g[l:l + 1, :], 1, [ng], wd=64)
            S = k.sb(es, "hS", [64, 4, 64], F32)
            St = k.sb(es, "hSt", [64, 4, 64], F32)
            Sbf = [k.sb(es, f"hSbf{j}", [64, 4, 64], BF16) for j in range(8)]
            names = ["z", "sg", "lf", "kk", "P", "eP", "eN"]
            bt = {nm: k.sb(es, "hb_" + nm, [64, 2048], F32) for nm in names}
            bq = k.sb(es, "hb_q", [64, 2048], BF16)
            bqd = k.sb(es, "hb_qd", [64, 2048], BF16)
            bki = k.sb(es, "hb_ki", [64, 2048], BF16)
            dec = k.sb(es, "hdec", [64, 4, 16], F32)
            vt = [k.sb(es, f"hvt{i}", [128, 256], BF16) for i in range(2)]
            kim = k.sb(es, "hkim", [128, 4, 256], BF16)
            attm = k.sb(es, "hattm", [128, 4, 128], BF16)
            obt = [k.sb(es, f"hobt{i}", [64, 4, 128], F32) for i in range(2)]
            gsl = [k.sb(es, f"hgsl{i}", [64, 4, 128], F32) for i in range(2)]
            osum = k.sb(es, "hosum", [64, 4, 128], F32)
            sqo = k.sb(es, "hsqo", [64, 512], BF16)
            rst = k.sb(es, "hrst", [64, 512], F32)
            yo = [k.sb(es, f"hyo{i}", [64, 4, 128], BF16) for i in range(2)]
            Mps = [self.psb[0], self.psb[1]]
            attps = self.psb[2]
            ops = self.psb[3]
            trp = self.psb[4]
            ssps = self.psb[5]
            trp_bf = trp[:].bitcast(BF16)
            ymix_v = self.ymix[512:768, :].rearrange("(h d) t -> d h t", d=64)
            sbi = [0]
            vti = [0]
            bqd2 = [bqd, k.sb(es, "hb_qd2", [64, 2048], BF16)]
            bki2 = [bki, k.sb(es, "hb_ki2", [64, 2048], BF16)]
            dec2 = [dec, k.sb(es, "hdec2", [64, 4, 16], F32)]
            Sx = [S, k.sb(es, "hS2", [64, 4, 64], F32)]
            Stx = [St, k.sb(es, "hSt2", [64, 4, 64], F32)]
            sidx = [0]

            def prep_groups(d, t0, n, bs):
                zsrc = (self.s_zf if d == 0 else self.s_zb).rearrange("(h d) t -> d h t", d=64)
                qsrc = self.s_hq.rearrange("(h d) t -> d h t", d=64)
                bqd_, bki_, dec_ = bqd2[bs], bki2[bs], dec2[bs]

                def V(t):
                    return t[:, 0:4 * n].rearrange("p (h t) -> p h t", h=4)
                f2 = lambda t: t[:, 0:4 * n]
                z, sg, lf, kk, P, eP, eN = [bt[nm] for nm in names]

                def g1():
                    k.dma(V(z), zsrc[:, :, t0:t0 + n], w=[z])
                    k.dma(V(bq), qsrc[:, :, t0:t0 + n], w=[bq])
                    k.op("act", lambda e: e.activation(out=f2(sg), in_=f2(z), func=AF.Sigmoid), r=[z], w=[sg])

                def g2():
                    for h in range(4):
                        k.op("dve", lambda e: e.tensor_scalar(out=V(lf)[:, h, :], in0=V(sg)[:, h, :], scalar1=self.hgoml[:, d, l, h:h + 1],
                                                              scalar2=self.hglb[:, d, l, h:h + 1], op0=ALU.mult, op1=ALU.add),
                             r=[sg, self.hgoml, self.hglb], w=[lf])
                        k.op("pool", lambda e: e.tensor_scalar(out=V(kk)[:, h, :], in0=V(sg)[:, h, :], scalar1=self.hgnoml[:, d, l, h:h + 1],
                                                               scalar2=self.hgoml[:, d, l, h:h + 1], op0=ALU.mult, op1=ALU.add),
                             r=[sg, self.hgoml, self.hgnoml], w=[kk])
                    k.op("act", lambda e: e.activation(out=f2(lf), in_=f2(lf), func=AF.Ln), r=[lf], w=[lf])

                def g3():
                    k.op("dve", lambda e: e.tensor_tensor_scan(out=f2(P), data0=f2(rmask), data1=f2(lf), initial=0.0, op0=ALU.mult, op1=ALU.add),
                         r=[rmask, lf], w=[P])
                    k.op("act", lambda e: e.activation(out=dec_[:, :, 0:n // 32], in_=V(P)[:, :, 31:n:32], func=AF.Exp), r=[P], w=[dec_])
                    if d == 1:
                        k.op("pool", lambda e: e.tensor_tensor(out=f2(P), in0=f2(P), in1=f2(lf), op=ALU.subtract), r=[P, lf], w=[P])

                def g4():
                    k.op("act", lambda e: e.activation(out=f2(eP), in_=f2(P), func=AF.Exp), r=[P], w=[eP])
                    k.op("act", lambda e: e.activation(out=f2(eN), in_=f2(P), func=AF.Exp, scale=-1.0), r=[P], w=[eN])
                    eq, ek = (eP, eN) if d == 0 else (eN, eP)
                    k.op("dve", lambda e: e.tensor_tensor(out=f2(bqd_), in0=f2(bq), in1=f2(eq), op=ALU.mult), r=[bq, eq], w=[bqd_])
                    k.op("pool", lambda e: e.tensor_tensor(out=f2(bki_), in0=f2(kk), in1=f2(ek), op=ALU.mult), r=[kk, ek], w=[bki_])
                return [g1, g2, g3, g4]

            for d in (1, 0):
                S = Sx[sidx[0] % 2]
                k.op("dve", lambda e: e.memset(S[:], 0.0), r=[S], w=[S])
                blocks = _blocks()
                if d == 1:
                    blocks = [blocks[0]] + blocks[:0:-1]
                for g_ in prep_groups(d, blocks[0][0], blocks[0][1], 0):
                    g_()
                for bix, (t0, n) in enumerate(blocks):
                    bs = bix % 2
                    pending = prep_groups(d, blocks[bix + 1][0], blocks[bix + 1][1], (bix + 1) % 2) if bix + 1 < len(blocks) else []

                    def V(t):
                        return t[:, 0:4 * n].rearrange("p (h t) -> p h t", h=4)
                    bqd, bki, dec = bqd2[bs], bki2[bs], dec2[bs]
                    qd, ki, decv = V(bqd), V(bki), dec
                    ntile = n // 128
                    for tix, ti in enumerate(range(ntile) if d == 0 else range(ntile - 1, -1, -1)):
                        if tix > 0:
                            for _ in range(4 // ntile if ntile < 4 else 1):
                                if pending:
                                    pending.pop(0)()
                        if filler is not None:
                            filler()
                        cols = slice(ti * 128, ti * 128 + 128)
                        gt0 = t0 + ti * 128
                        v = vt[vti[0] % 2]
                        ob_ = obt[vti[0] % 2]
                        gs_ = gsl[vti[0] % 2]
                        yo_ = yo[vti[0] % 2]
                        vti[0] += 1
                        k.dma(v[:], self.s_hi[gt0:gt0 + 128, :], w=[v])
                        if d == 0:
                            k.dma(ob_[:], self.s_ob[:, :, gt0:gt0 + 128], w=[ob_])
                            k.dma(gs_[:], self.s_hg.rearrange("(h d) t -> d h t", d=64)[:, :, gt0:gt0 + 128], w=[gs_])
                        for h in range(4):
                            k.op("pe", lambda e: e.transpose(out=trp_bf[:, h * 64:(h + 1) * 64], in_=ki[:, h, cols], identity=self.ident_bf[0:64, 0:64]),
                                 r=[bki, self.ident_bf], w=[trp])
                        for j in range(4):
                            k.op("dve" if j % 2 == 0 else "act", (lambda e: e.tensor_scalar(out=kim[:, j, :], in0=trp_bf[:, 0:256], scalar1=cmask[:, j:j + 1], scalar2=None, op0=ALU.mult))
                                 if j % 2 == 0 else (lambda e: e.activation(out=kim[:, j, :], in_=trp_bf[:, 0:256], func=AF.Copy, scale=cmask[:, j:j + 1])),
                                 r=[trp, cmask], w=[kim])
                        for j in range(4):
                            for h in range(4):
                                k.op("pe", lambda e: e.matmul(Mps[j // 2][0:64, (j % 2) * 256 + h * 64:(j % 2) * 256 + (h + 1) * 64],
                                                              lhsT=kim[:, j, h * 64:(h + 1) * 64], rhs=v[:, h * 64:(h + 1) * 64], start=True, stop=True),
                                     r=[kim, v], w=[Mps[j // 2]])
                        for h in range(4):
                            k.op("pe", lambda e: e.matmul(attps[:, h * 128:(h + 1) * 128], lhsT=ki[:, h, cols], rhs=qd[:, h, cols], start=True, stop=True),
                                 r=[bki, bqd], w=[attps])
                        k.op("dve", lambda e: e.tensor_tensor(out=attm[:].rearrange("p a b -> p (a b)"), in0=attps[:, 0:512],
                                                              in1=amask[:, d].rearrange("p a b -> p (a b)"), op=ALU.mult), r=[attps, amask], w=[attm])
                        for h in range(4):
                            k.op("pe", lambda e: e.matmul(ops[0:64, h * 128:(h + 1) * 128], lhsT=v[:, h * 64:(h + 1) * 64], rhs=attm[:, h, :],
                                                          start=(h == 0), stop=False, skip_group_check=True), r=[v, attm], w=[ops])
                        order = range(4) if d == 0 else range(3, -1, -1)
                        for ji, j in enumerate(order):
                            ce = ti * 4 + j
                            Mj = Mps[j // 2][0:64, (j % 2) * 256:(j % 2) * 256 + 256].rearrange("p (h x) -> p h x", h=4)
                            dec_bc = decv[:, :, ce:ce + 1].to_broadcast([64, 4, 64])
                            sb_ = Sbf[sbi[0] % 8]
                            sbi[0] += 1
                            S = Sx[sidx[0] % 2]
                            Sn = Sx[(sidx[0] + 1) % 2]
                            St = Stx[sidx[0] % 2]
                            sidx[0] += 1
                            if d == 0:
                                k.op("act", lambda e: e.copy(out=sb_[:], in_=S[:]), r=[S], w=[sb_])
                                k.op("dve", lambda e: e.tensor_tensor(out=St[:], in0=S[:], in1=Mj, op=ALU.add), r=[S, Mps[j // 2]], w=[St])
                                k.op("dve", lambda e: e.tensor_tensor(out=Sn[:], in0=St[:], in1=dec_bc, op=ALU.mult), r=[St, dec], w=[Sn])
                            else:
                                k.op("dve", lambda e: e.tensor_tensor(out=St[:], in0=S[:], in1=dec_bc, op=ALU.mult), r=[S, dec], w=[St])
                                k.op("act", lambda e: e.copy(out=sb_[:], in_=St[:]), r=[St], w=[sb_])
                                k.op("dve", lambda e: e.tensor_tensor(out=Sn[:], in0=St[:], in1=Mj, op=ALU.add), r=[St, Mps[j // 2]], w=[Sn])
                            for h in range(4):
                                last = (ji == 3 and h == 3)
                                k.op("pe", lambda e: e.matmul(ops[0:64, h * 128 + j * 32:h * 128 + (j + 1) * 32], lhsT=sb_[:, h, :],
                                                              rhs=qd[:, h, ti * 128 + j * 32:ti * 128 + (j + 1) * 32], start=False, stop=last,
                                                              skip_group_check=True), r=[sb_, bqd], w=[ops])
                        opsv = ops[0:64, 0:512].rearrange("p (h t) -> p h t", h=4)
                        if d == 1:
                            k.op("act", lambda e: e.copy(out=ob_[:], in_=opsv), r=[ops], w=[ob_])
                            k.dma(self.s_ob[:, :, gt0:gt0 + 128], ob_[:], r=[ob_])
                        else:
                            k.op("dve", lambda e: e.tensor_tensor(out=osum[:], in0=opsv, in1=ob_[:], op=ALU.add), r=[ops, ob_], w=[osum])
                            o2 = osum[:].rearrange("p h t -> p (h t)")
                            k.op("pool", lambda e: e.tensor_tensor(out=sqo[:], in0=o2, in1=o2, op=ALU.mult), r=[osum], w=[sqo])
                            k.op("pe", lambda e: e.matmul(ssps[0:64, 0:512], lhsT=self.ones_bf[0:64, 0:64], rhs=sqo[:], start=True, stop=True),
                                 r=[sqo, self.ones_bf], w=[ssps])
                            k.op("act", lambda e: e.activation(out=rst[:], in_=ssps[0:64, 0:512], func=AF.Sqrt, bias=self.epsb[0:64, :], scale=1.0 / 64),
                                 r=[ssps, self.epsb], w=[rst])
                            k.op("dve", lambda e: e.reciprocal(out=rst[:], in_=rst[:]), r=[rst], w=[rst])
                            k.op("dve", lambda e: e.scalar_tensor_tensor(out=o2, in0=o2, scalar=ng[:, 0:1], in1=rst[:], op0=ALU.mult, op1=ALU.mult),
                                 r=[osum, ng, rst], w=[osum])
                            k.op("pool", lambda e: e.tensor_tensor(out=yo_[:], in0=osum[:], in1=gs_[:], op=ALU.mult), r=[osum, gs_], w=[yo_])
                            k.dma(ymix_v[:, :, gt0:gt0 + 128], yo_[:], r=[yo_])
                    while pending:
                        pending.pop(0)()
                k.barrier()

    def phase_s5_gen(self, l):
        k = self.k
        need_ctx = l < DEPTH - 1
        NC_ = TALL // 8
        mul, add, sub = ALU.mult, ALU.add, ALU.subtract
        with contextlib.ExitStack() as es:
            Tm = k.sb(es, "5Tm", [128, 32, 128], BF16)
            Gt = k.sb(es, "5Gt", [128, 32, 2, 64], BF16)
            Er = k.sb(es, "5Er", [64, 32, 128], BF16)
            Ei = k.sb(es, "5Ei", [64, 32, 128], BF16)
            A8 = k.sb(es, "5A8", [64, 4, 32], F32)
            U = k.sb(es, "5U", [128, 16, NC_], BF16)
            tmask = k.sb(es, "5tmask", [128, 2, 128], F32)
            k.dma(tmask[:], self.s5_tmask, w=[tmask])
            with contextlib.ExitStack() as e2:
                def t32(nm):
                    return k.sb(e2, nm, [64, 32], F32)
                lre, lim, ldt = t32("lre"), t32("lim"), t32("ldt")
                for dst, src in ((lre, self.s5_lam_re), (lim, self.s5_lam_im), (ldt, self.s5_log_dt)):
                    self.load_fm(e2, dst[:], src[l].rearrange("d g p -> (d g) p"), 32, [dst], wd=64)
                Bre = k.sb(e2, "Bre", [64, 32, 16], F32)
                Bim = k.sb(e2, "Bim", [64, 32, 16], F32)
                Cre = k.sb(e2, "Cre", [64, 32, 16], F32)
                Cim = k.sb(e2, "Cim", [64, 32, 16], F32)
                for dst, src in ((Bre, self.s5_b_re), (Bim, self.s5_b_im)):
                    for d in range(2):
                        k.dma(dst[:, d * 16:(d + 1) * 16, :], src[l, d].rearrange("g p h -> p g h"), w=[dst], allow_slow_non_contiguous=True)
                for dst, src in ((Cre, self.s5_c_re), (Cim, self.s5_c_im)):
                    for q4 in range(4):
                        self.load_fm(e2, dst[:].rearrange("p a h -> p (a h)")[:, q4 * 128:(q4 + 1) * 128],
                                     src[l].rearrange("d g h p -> (d g h) p")[q4 * 128:(q4 + 1) * 128, :], 128, [dst], wd=64)
                dt_, mag, th, c16, s16 = t32("dt"), t32("mag"), t32("th"), t32("c16"), t32("s16")
                t1, t2 = t32("t1"), t32("t2")
                k.op("act", lambda e: e.activation(out=dt_[:], in_=ldt[:], func=AF.Exp), r=[ldt], w=[dt_])
                k.op("dve", lambda e: e.tensor_tensor(out=mag[:], in0=lre[:], in1=dt_[:], op=mul), r=[lre, dt_], w=[mag])
                k.op("dve", lambda e: e.tensor_tensor(out=th[:], in0=lim[:], in1=dt_[:], op=mul), r=[lim, dt_], w=[th])
                k.op("act", lambda e: e.activation(out=mag[:], in_=mag[:], func=AF.Exp, scale=1.0 / 16), r=[mag], w=[mag])
                halfpi = k.sb(e2, "halfpi", [64, 1], F32)
                k.op("dve", lambda e: e.memset(halfpi[:], math.pi / 2), w=[halfpi])
                k.op("act", lambda e: e.activation(out=s16[:], in_=th[:], func=AF.Sin, scale=1.0 / 16), r=[th], w=[s16])
                k.op("act", lambda e: e.activation(out=c16[:], in_=th[:], func=AF.Sin, scale=1.0 / 16, bias=halfpi[:]), r=[th, halfpi], w=[c16])
                are, aim = t32("are"), t32("aim")
                k.op("dve", lambda e: e.tensor_tensor(out=are[:], in0=mag[:], in1=c16[:], op=mul), r=[mag, c16], w=[are])
                k.op("dve", lambda e: e.tensor_tensor(out=aim[:], in0=mag[:], in1=s16[:], op=mul), r=[mag, s16], w=[aim])

                def cmul(ore, oim, xr, xi, yr, yi, rk, wk, tA, tB):
                    k.op("dve", lambda e: e.tensor_tensor(out=tA, in0=xr, in1=yr, op=mul), r=rk, w=[wk[2]])
                    k.op("dve", lambda e: e.tensor_tensor(out=tB, in0=xi, in1=yi, op=mul), r=rk, w=[wk[3]])
                    k.op("dve", lambda e: e.tensor_tensor(out=tB, in0=tA, in1=tB, op=sub), r=[wk[2], wk[3]], w=[wk[3]])
                    k.op("dve", lambda e: e.tensor_tensor(out=tA, in0=xr, in1=yi, op=mul), r=rk, w=[wk[2]])
                    k.op("dve", lambda e: e.tensor_tensor(out=oim, in0=xi, in1=yr, op=mul), r=rk, w=[wk[1]])
                    k.op("dve", lambda e: e.tensor_tensor(out=oim, in0=oim, in1=tA, op=add), r=[wk[1], wk[2]], w=[wk[1]])
                    k.op("dve", lambda e: e.tensor_copy(out=ore, in_=tB), r=[wk[3]], w=[wk[0]])

                for _ in range(4):
                    cmul(are[:], aim[:], are[:], aim[:], are[:], aim[:], [are, aim], [are, aim, t1, t2], t1[:], t2[:])
                cfr, cfi, den = t32("cfr"), t32("cfi"), t32("den")
                k.op("dve", lambda e: e.tensor_tensor(out=den[:], in0=lre[:], in1=lre[:], op=mul), r=[lre], w=[den])
                k.op("dve", lambda e: e.tensor_tensor(out=t1[:], in0=lim[:], in1=lim[:], op=mul), r=[lim], w=[t1])
                k.op("dve", lambda e: e.tensor_tensor(out=den[:], in0=den[:], in1=t1[:], op=add), r=[den, t1], w=[den])
                k.op("dve", lambda e: e.reciprocal(out=den[:], in_=den[:]), r=[den], w=[den])
                am1 = t32("am1")
                nlim = t32("nlim")
                k.op("dve", lambda e: e.tensor_scalar(out=am1[:], in0=are[:], scalar1=-1.0, scalar2=None, op0=add), r=[are], w=[am1])
                k.op("dve", lambda e: e.tensor_scalar(out=nlim[:], in0=lim[:], scalar1=-1.0, scalar2=None, op0=mul), r=[lim], w=[nlim])
                cmul(cfr[:], cfi[:], am1[:], aim[:], lre[:], nlim[:], [am1, aim, lre, nlim], [cfr, cfi, t1, t2], t1[:], t2[:])
                k.op("dve", lambda e: e.tensor_tensor(out=cfr[:], in0=cfr[:], in1=den[:], op=mul), r=[cfr, den], w=[cfr])
                k.op("dve", lambda e: e.tensor_tensor(out=cfi[:], in0=cfi[:], in1=den[:], op=mul), r=[cfi, den], w=[cfi])
                ire, iim = t32("ire"), t32("iim")
                k.op("dve", lambda e: e.tensor_tensor(out=den[:], in0=are[:], in1=are[:], op=mul), r=[are], w=[den])
                k.op("dve", lambda e: e.tensor_tensor(out=t1[:], in0=aim[:], in1=aim[:], op=mul), r=[aim], w=[t1])
                k.op("dve", lambda e: e.tensor_tensor(out=den[:], in0=den[:], in1=t1[:], op=add), r=[den, t1], w=[den])
                k.op("dve", lambda e: e.reciprocal(out=den[:], in_=den[:]), r=[den], w=[den])
                k.op("dve", lambda e: e.tensor_tensor(out=ire[:], in0=are[:], in1=den[:], op=mul), r=[are, den], w=[ire])
                k.op("dve", lambda e: e.scalar_tensor_tensor(out=iim[:], in0=aim[:], scalar=-1.0, in1=den[:], op0=mul, op1=mul), r=[aim, den], w=[iim])
                def t512(nm):
                    return k.sb(e2, nm, [64, 32, 16], F32)
                Bbr, Bbi, u1, u2, xr, xi = t512("Bbr"), t512("Bbi"), t512("u1"), t512("u2"), t512("xr"), t512("xi")
                bc = lambda t: t[:].unsqueeze(2).to_broadcast([64, 32, 16])
                cmul(Bbr[:], Bbi[:], bc(cfr), bc(cfi), Bre[:], Bim[:], [cfr, cfi, Bre, Bim], [Bbr, Bbi, u1, u2], u1[:], u2[:])
                Lr = k.sb(e2, "Lr", [64, 32, 8, 16], F32)
                Li = k.sb(e2, "Li", [64, 32, 8, 16], F32)
                Rr = k.sb(e2, "Rr", [64, 32, 8, 16], F32)
                Ri = k.sb(e2, "Ri", [64, 32, 8, 16], F32)
                Gr = k.sb(e2, "Gr", [64, 32, 8, 16], F32)
                Gi = k.sb(e2, "Gi", [64, 32, 8, 16], F32)
                Erv = Er[:].rearrange("p a (t h) -> p a t h", h=16)
                Eiv = Ei[:].rearrange("p a (t h) -> p a t h", h=16)
                pr, pi_, qr, qi = t32("pr"), t32("pi"), t32("qr"), t32("qi")
                k.op("dve", lambda e: e.memset(pr[:], 1.0), w=[pr])
                k.op("dve", lambda e: e.memset(pi_[:], 0.0), w=[pi_])
                k.op("dve", lambda e: e.memset(qr[:], 1.0), w=[qr])
                k.op("dve", lambda e: e.memset(qi[:], 0.0), w=[qi])
                F_, Bk = slice(0, 16), slice(16, 32)
                for kk_ in range(9):
                    if kk_ > 0:
                        cmul(pr[:], pi_[:], pr[:], pi_[:], are[:], aim[:], [pr, pi_, are, aim], [pr, pi_, t1, t2], t1[:], t2[:])
                    if kk_ <= 7:
                        cmul(xr[:], xi[:], bc(pr), bc(pi_), Bbr[:], Bbi[:], [pr, pi_, Bbr, Bbi], [xr, xi, u1, u2], u1[:], u2[:])
                        for (dst, src) in ((Gr, xr), (Gi, xi)):
                            k.op("pool", lambda e: e.tensor_copy(out=dst[:, F_, 7 - kk_, :], in_=src[:, F_, :]), r=[src, dst], w=[dst])
                            k.op("pool", lambda e: e.tensor_copy(out=dst[:, Bk, kk_, :], in_=src[:, Bk, :]), r=[src, dst], w=[dst])
                    cmul(xr[:], xi[:], bc(pr), bc(pi_), Cre[:], Cim[:], [pr, pi_, Cre, Cim], [xr, xi, u1, u2], u1[:], u2[:])
                    if kk_ <= 7:
                        k.op("pool", lambda e: e.tensor_copy(out=Rr[:, F_, kk_, :], in_=xr[:, F_, :]), r=[xr, Rr], w=[Rr])
                        k.op("pool", lambda e: e.tensor_copy(out=Rr[:, Bk, 7 - kk_, :], in_=xr[:, Bk, :]), r=[xr, Rr], w=[Rr])
                        k.op("pool", lambda e: e.tensor_scalar(out=Ri[:, F_, kk_, :], in0=xi[:, F_, :], scalar1=-1.0, scalar2=None, op0=mul), r=[xi, Ri], w=[Ri])
                        k.op("pool", lambda e: e.tensor_scalar(out=Ri[:, Bk, 7 - kk_, :], in0=xi[:, Bk, :], scalar1=-1.0, scalar2=None, op0=mul), r=[xi, Ri], w=[Ri])
                    if kk_ >= 1:
                        k.op("act", lambda e: e.copy(out=Erv[:, F_, kk_ - 1, :], in_=xr[:, F_, :]), r=[xr, Er], w=[Er])
                        k.op("act", lambda e: e.copy(out=Erv[:, Bk, 8 - kk_, :], in_=xr[:, Bk, :]), r=[xr, Er], w=[Er])
                        k.op("act", lambda e: e.activation(out=Eiv[:, F_, kk_ - 1, :], in_=xi[:, F_, :], func=AF.Copy, scale=-1.0), r=[xi, Ei], w=[Ei])
                        k.op("act", lambda e: e.activation(out=Eiv[:, Bk, 8 - kk_, :], in_=xi[:, Bk, :], func=AF.Copy, scale=-1.0), r=[xi, Ei], w=[Ei])
                    if kk_ == 8:
                        k.op("dve", lambda e: e.tensor_copy(out=A8[:, 0, :], in_=pr[:]), r=[pr], w=[A8])
                        k.op("dve", lambda e: e.tensor_copy(out=A8[:, 1, :], in_=pr[:]), r=[pr, A8], w=[A8])
                        k.op("dve", lambda e: e.tensor_scalar(out=A8[:, 2, :], in0=pi_[:], scalar1=-1.0, scalar2=None, op0=mul), r=[pi_, A8], w=[A8])
                        k.op("dve", lambda e: e.tensor_copy(out=A8[:, 3, :], in_=pi_[:]), r=[pi_, A8], w=[A8])
                    if kk_ <= 7:
                        if kk_ > 0:
                            cmul(qr[:], qi[:], qr[:], qi[:], ire[:], iim[:], [qr, qi, ire, iim], [qr, qi, t1, t2], t1[:], t2[:])
                        cmul(xr[:], xi[:], bc(qr), bc(qi), Bbr[:], Bbi[:], [qr, qi, Bbr, Bbi], [xr, xi, u1, u2], u1[:], u2[:])
                        for (dst, src) in ((Lr, xr), (Li, xi)):
                            k.op("pool", lambda e: e.tensor_copy(out=dst[:, F_, kk_, :], in_=src[:, F_, :]), r=[src, dst], w=[dst])
                            k.op("pool", lambda e: e.tensor_copy(out=dst[:, Bk, 7 - kk_, :], in_=src[:, Bk, :]), r=[src, dst], w=[dst])
                fl = lambda t: t[:].rearrange("p a j h -> p a (j h)")
                for dg in range(32):
                    d = dg // 16
                    ps = self.psb[dg % 2]
                    k.op("pe", lambda e: e.matmul(ps[:, 0:128], lhsT=fl(Lr)[:, dg, :], rhs=fl(Rr)[:, dg, :], start=True, stop=False), r=[Lr, Rr], w=[ps])
                    k.op("pe", lambda e: e.matmul(ps[:, 0:128], lhsT=fl(Li)[:, dg, :], rhs=fl(Ri)[:, dg, :], start=False, stop=True), r=[Li, Ri], w=[ps])
                    k.op("dve", lambda e: e.tensor_tensor(out=Tm[:, dg, :], in0=ps[:, 0:128], in1=tmask[:, d, :], op=mul), r=[ps, tmask], w=[Tm])
                    ps2 = self.psb[2 + dg % 2]
                    k.op("pe", lambda e: e.transpose(out=ps2[:, 0:64], in_=fl(Gr)[:, dg, :], identity=self.ident_f[0:64, 0:64]), r=[Gr, self.ident_f], w=[ps2])
                    k.op("pe", lambda e: e.transpose(out=ps2[:, 64:128], in_=fl(Gi)[:, dg, :], identity=self.ident_f[0:64, 0:64]), r=[Gi, self.ident_f], w=[ps2])
                    k.op("act", lambda e: e.copy(out=Gt[:, dg].rearrange("p a b -> p (a b)"), in_=ps2[:, 0:128]), r=[ps2], w=[Gt])
                k.barrier()
            CBM = 64
            eu = contextlib.ExitStack()
            utok = k.sb(eu, "5utok", [128, 8, 256], F32)
            ub = k.sb(eu, "5ub", [128, 8, 256], BF16)
            ublocks = [(0, 32)] + [(32 + 128 * j, 128) for j in range(8)]
            trp = self.psb[4]
            trp_bf = trp[:].bitcast(BF16)
            usrc = self.s_u.rearrange("(c t) f -> c t f", t=8)
            for (c0, cb) in ublocks:
                k.dma(utok[0:cb], usrc[c0:c0 + cb], w=[utok])
                k.op("dve", lambda e: e.tensor_copy(out=ub[0:cb].rearrange("c a b -> c (a b)").rearrange("c (g t h) -> c g t h", g=16, t=8),
                                                    in_=utok[0:cb].rearrange("c t (g h) -> c g t h", g=16)), r=[utok], w=[ub])
                for g8 in range(2):
                    for gi in range(8):
                        g = g8 * 8 + gi
                        k.op("pe", lambda e: e.transpose(out=trp_bf[:, gi * 128:gi * 128 + cb], in_=ub[0:cb].rearrange("c a b -> c (a b)")[:, g * 128:(g + 1) * 128],
                                                         identity=self.ident_bf[0:cb, 0:cb]), r=[ub, self.ident_bf], w=[trp])
                    k.op("dve", lambda e: e.tensor_copy(out=U[:, g8 * 8:(g8 + 1) * 8, c0:c0 + cb],
                                                        in_=trp_bf[:, 0:1024].rearrange("p (g c) -> p g c", g=8)[:, :, 0:cb]), r=[trp], w=[U])
            k.barrier()
            eu.close()
            em = contextlib.ExitStack()
            Wt = k.sb(em, "5W", [64, 2, 32, CBM], F32)
            SP = [k.sb(em, f"5SP{d}", [64, 2, 16, CBM], BF16) for d in range(2)]
            Hist = [k.sb(em, f"5H{d}", [64, CBM + 1, 3, 16], F32) for d in range(2)]
            Tt = [k.sb(em, f"5T{d}", [64, 2, 16], F32) for d in range(2)]
            Vt = [k.sb(em, f"5V{d}", [64, 2, 16], F32) for d in range(2)]
            yev = [k.sb(em, f"5yev{i}", [128, 512], F32) for i in range(2)]
            blocks = [(0, 32)] + [(32 + CBM * j, CBM) for j in range((NC_ - 32) // CBM)]
            border = [blocks, [blocks[0]] + blocks[:0:-1]]
            A8v = [[A8[:, 0:2, d * 16:(d + 1) * 16], A8[:, 2:4, d * 16:(d + 1) * 16]] for d in range(2)]
            engs = ("dve", "pool")
            psS = self.psb[7]
            nbs = len(blocks)
            self.s5_nitems = sum(16 + 8 + b_[1] + 1 for b_ in blocks)
            yield "ready"

            def w_group(bs, d, g4, ri):
                cb = border[0][bs][1]
                cds = [border[0][bs][0], border[1][bs][0]]
                for gi in range(4):
                    g = g4 * 4 + gi
                    k.op("pe", lambda e: e.matmul(psS[0:64, gi * 128:gi * 128 + cb], lhsT=Gt[:, d * 16 + g, ri, :], rhs=U[:, g, cds[d]:cds[d] + cb],
                                                  start=True, stop=True), r=[Gt, U], w=[psS])
                k.op("dve", lambda e: e.tensor_copy(out=Wt[:, ri, d * 16 + g4 * 4:d * 16 + g4 * 4 + 4, 0:cb],
                                                    in_=psS[0:64, 0:512].rearrange("p (g c) -> p g c", g=4)[:, :, 0:cb]), r=[psS], w=[Wt])

            def y_group(bs, d, g4):
                cb = border[0][bs][1]
                cds = [border[0][bs][0], border[1][bs][0]]
                for gi in range(4):
                    g = g4 * 4 + gi
                    o = psS[:, gi * 128:gi * 128 + cb]
                    k.op("pe", lambda e: e.matmul(o, lhsT=Tm[:, d * 16 + g, :], rhs=U[:, g, cds[d]:cds[d] + cb], start=(gi == 0), stop=False,
                                                  skip_group_check=True), r=[Tm, U], w=[psS])
                    k.op("pe", lambda e: e.matmul(o, lhsT=Er[:, d * 16 + g, :], rhs=SP[d][:, 0, g, 0:cb], start=False, stop=False,
                                                  skip_group_check=True), r=[Er, SP[d]], w=[psS])
                    k.op("pe", lambda e: e.matmul(o, lhsT=Ei[:, d * 16 + g, :], rhs=SP[d][:, 1, g, 0:cb], start=False, stop=True,
                                                  skip_group_check=True), r=[Ei, SP[d]], w=[psS])
                ye = yev[g4 % 2]
                k.op("dve", lambda e: e.tensor_copy(out=ye[:], in_=psS[:, 0:512]), r=[psS], w=[ye])
                k.dma(self.s_y[d][:, g4 * 4:(g4 + 1) * 4, cds[d]:cds[d] + cb], ye[:].rearrange("p (g c) -> p g c", g=4)[:, :, 0:cb], r=[ye])

            prev_cb = None
            for bs in range(nbs):
                cb = border[0][bs][1]
                for d in range(2):
                    dst = Hist[d][:, 0] if d == 0 else Hist[d][:, cb]
                    if bs == 0:
                        k.op(engs[d], lambda e: e.memset(dst, 0.0), r=[Hist[d]], w=[Hist[d]])
                    else:
                        src = Hist[d][:, prev_cb] if d == 0 else Hist[d][:, 0]
                        k.op(engs[d], lambda e: e.tensor_copy(out=dst, in_=src), r=[Hist[d]], w=[Hist[d]])
                for d in range(2):
                    for g4 in range(4):
                        for ri in range(2):
                            w_group(bs, d, g4, ri)
                            yield
                if bs > 0:
                    for d in range(2):
                        for g4 in range(4):
                            y_group(bs - 1, d, g4)
                            yield
                prev_cb = cb
                for i in range(cb):
                    for d, eng in ((0, "dve"), (1, "pool")):
                        H, T_, V_ = Hist[d], Tt[d], Vt[d]
                        if d == 0:
                            col, pv, cu = i, i, i + 1
                        else:
                            col, pv, cu = cb - 1 - i, cb - i, cb - 1 - i
                        k.op(eng, lambda e: e.tensor_tensor(out=T_[:], in0=H[:, pv, 0:2, :], in1=A8v[d][0], op=mul), r=[H, A8], w=[T_])
                        k.op(eng, lambda e: e.tensor_tensor(out=V_[:], in0=H[:, pv, 1:3, :], in1=A8v[d][1], op=mul), r=[H, A8], w=[V_])
                        k.op(eng, lambda e: e.tensor_tensor(out=T_[:], in0=T_[:], in1=V_[:], op=add), r=[T_, V_], w=[T_])
                        k.op(eng, lambda e: e.tensor_tensor(out=H[:, cu, 0:2, :], in0=T_[:], in1=Wt[:, :, d * 16:(d + 1) * 16, col], op=add),
                             r=[T_, Wt, H], w=[H])
                        k.op(eng, lambda e: e.tensor_copy(out=H[:, cu, 2, :], in_=H[:, cu, 0, :]), r=[H], w=[H])
                    yield
                for d in range(2):
                    lo = 0 if d == 0 else 1
                    k.op("pool", lambda e: e.tensor_copy(out=SP[d][:, :, :, 0:cb].rearrange("p r g c -> p c r g"), in_=Hist[d][:, lo:lo + cb, 0:2, :]),
                         r=[Hist[d]], w=[SP[d]])
                yield
            for d in range(2):
                for g4 in range(4):
                    y_group(nbs - 1, d, g4)
                    yield
            k.barrier()
            em.close()
            with contextlib.ExitStack() as e3:
                utok = k.sb(e3, "5utok2", [128, 8, 256], F32)
                D8 = k.sb(e3, "5D8", [128, 8, 256], F32)
                wg = k.sb(e3, "5wg", [128, 2, 256], BF16)
                bg = k.sb(e3, "5bg", [128, 2], F32)
                for t in range(8):
                    k.dma(D8[:, t, :], self.s5_d[l:l + 1, :].partition_broadcast(128) if False else self.s5_d[l].partition_broadcast(128), w=[D8])
                k.dma(wg[:], self.s5_w_glu[l].rearrange("(c p) n -> p c n", p=128), w=[wg], q="pool")
                self.load_fm(e3, bg[:], self.s5_b_glu[l].rearrange("(c p) -> c p", p=128), 2, [bg])
                yf = k.sb(e3, "5yf", [128, 16, 128], F32)
                yb = k.sb(e3, "5yb", [128, 16, 128], F32)
                ytok = k.sb(e3, "5ytok", [128, 8, 256], F32)
                ygel = k.sb(e3, "5ygel", [128, 8, 256], BF16)
                yT = k.sb(e3, "5yT", [128, 2, 1024], BF16)
                sgm = [k.sb(e3, f"5sg{i}", [128, 512], F32) for i in range(2)]
                yao = [k.sb(e3, f"5ya{i}", [128, 512], BF16) for i in range(2)]
                for (c0, cb) in blocks:
                    if c0 == 0 and not need_ctx:
                        continue
                    ntok = cb * 8
                    k.dma(utok[0:cb], usrc[c0:c0 + cb], w=[utok])
                    k.dma(yf[:, :, 0:cb], self.s_y[0][:, :, c0:c0 + cb], w=[yf])
                    k.dma(yb[:, :, 0:cb], self.s_y[1][:, :, c0:c0 + cb], w=[yb])
                    k.op("pool", lambda e: e.tensor_tensor(out=yf[:, :, 0:cb], in0=yf[:, :, 0:cb], in1=yb[:, :, 0:cb], op=add), r=[yf, yb], w=[yf])
                    k.op("dve", lambda e: e.tensor_tensor(out=ytok[0:cb], in0=utok[0:cb], in1=D8[0:cb], op=mul), r=[utok, D8], w=[ytok])
                    for g4 in range(4):
                        ps = self.psb[g4 % 2]
                        for gi in range(4):
                            g = g4 * 4 + gi
                            k.op("pe", lambda e: e.transpose(out=ps[0:cb, gi * 128:(gi + 1) * 128], in_=yf[:, g, 0:cb], identity=self.ident_f[:, :]),
                                 r=[yf, self.ident_f], w=[ps])
                        yv = ytok[0:cb, :, g4 * 64:(g4 + 1) * 64].rearrange("c t (g h) -> c g t h", g=4)
                        pv = ps[0:cb, 0:512].rearrange("c (g t h) -> c g t h", g=4, t=8)
                        k.op("dve", lambda e: e.tensor_tensor(out=yv, in0=pv, in1=yv, op=add), r=[ps, ytok], w=[ytok])
                    k.op("act", lambda e: e.activation(out=ygel[0:cb], in_=ytok[0:cb], func=AF.Gelu), r=[ytok], w=[ygel])
                    for kc in range(2):
                        for t in range(8):
                            k.op("pe", lambda e: e.transpose(out=trp_bf[:, t * 128:t * 128 + cb], in_=ygel[0:cb, t, kc * 128:(kc + 1) * 128],
                                                             identity=self.ident_bf[0:cb, 0:cb]), r=[ygel, self.ident_bf], w=[trp])
                        k.op("dve", lambda e: e.tensor_copy(out=yT[:, kc, 0:ntok].rearrange("p (c t) -> p t c", t=8),
                                                            in_=trp_bf[:, 0:1024].rearrange("p (t c) -> p t c", t=8)[:, :, 0:cb]), r=[trp], w=[yT])
                    for r0 in range(0, ntok, 512):
                        n = min(512, ntok - r0)
                        for oc in range(2):
                            ps = self.psb[2 + oc]
                            for kc in range(2):
                                k.op("pe", lambda e: e.matmul(ps[:, 0:n], lhsT=wg[:, kc, oc * 128:(oc + 1) * 128], rhs=yT[:, kc, r0:r0 + n],
                                                              start=(kc == 0), stop=(kc == 1)), r=[wg, yT], w=[ps])
                            k.op("act", lambda e: e.activation(out=sgm[oc][:, 0:n], in_=ps[:, 0:n], func=AF.Sigmoid, bias=bg[:, oc:oc + 1]),
                                 r=[ps, bg], w=[sgm[oc]])
                            k.op("dve", lambda e: e.tensor_tensor(out=yao[oc][:, 0:n], in0=yT[:, oc, r0:r0 + n], in1=sgm[oc][:, 0:n], op=mul),
                                 r=[yT, sgm[oc]], w=[yao[oc]])
                            tok0 = c0 * 8 + r0
                            k.dma(self.ymix[oc * 128:(oc + 1) * 128, tok0:tok0 + n], yao[oc][:, 0:n], r=[yao[oc]])
                k.barrier()

def _prep_inputs(inputs):
    f = lambda a: np.ascontiguousarray(np.asarray(a, dtype=np.float32))
    cols = _win_cols()
    qcols = _wqb_cols()
    rs, rm = _rope_tables()
    shared = {
        "w_mod": f(inputs["w_mod"]), "b_mod": f(inputs["b_mod"]), "norm1_g": f(inputs["norm1_g"]), "norm2_g": f(inputs["norm2_g"]),
        "w_in": f(np.asarray(inputs["w_in"])[:, :, cols]), "w_out": f(inputs["w_out"]),
        "w_qb": f(np.asarray(inputs["mla_w_qb"])[:, :, qcols]), "w_kvb": f(inputs["mla_w_kvb"]),
        "qn_g": f(inputs["mla_q_norm_g"]), "kvn_g": f(inputs["mla_kv_norm_g"]),
        "w_up": f(inputs["ffn_w_up"]), "w_down": f(inputs["ffn_w_down"]), "final_g": f(inputs["final_norm_g"]),
        "rope_s": rs, "rope_m": rm, "ident": np.eye(128, dtype=np.float32),
        "swa_mask": _swa_mask(), "swa_sink": f(inputs["swa_sink"]),
        "hg_lb": f(inputs["hg_lb"]), "s5_tmask": _s5_tmask(),
        **{kk: f(inputs[kk]) for kk in ("s5_lam_re", "s5_lam_im", "s5_log_dt", "s5_b_re", "s5_b_im", "s5_c_re", "s5_c_im", "s5_d", "s5_w_glu", "s5_b_glu")}, "hg_norm_g": f(inputs["hg_norm_g"]), **_hg_consts(),
    }
    x = np.asarray(inputs["x"]); ctx = np.asarray(inputs["ctx"]); c = np.asarray(inputs["c"]); c_ctx = np.asarray(inputs["c_ctx"])
    maps = []
    for b in range(NCORES):
        m = dict(shared)
        m["xin"] = f(np.concatenate([ctx[b], x[b]], 0).T)
        m["cc"] = f(np.stack([c[b], c_ctx], 0))
        maps.append(m)
    return maps


def kernel(**inputs):
    bld = Builder()
    maps = _prep_inputs(inputs)
    res = run_bass_kernel_spmd(bld.nc, maps, core_ids=list(range(NCORES)))
    out = np.stack([np.ascontiguousarray(r["out"].T) for r in res.results], 0)
    return out.astype(np.float32)
```

```python
import contextlib
import math
import os
import numpy as np
import concourse.bass as bass
import concourse.mybir as mybir
from concourse.bass_utils import run_bass_kernel_spmd

F32 = mybir.dt.float32
BF16 = mybir.dt.bfloat16
ALU = mybir.AluOpType
AF = mybir.ActivationFunctionType

D = 1024
SEQ = 8192
CTX = 256
TALL = SEQ + CTX
DEPTH = 4
NCORES = 4
FFH = 2816
EPS = 1e-6
GRID_W = 64
MLA_SCALE = 96 ** -0.5
SWA_SCALE = 64 ** -0.5
SAME_ENGINE_SYNC = True
HG_PREP_ENG = os.environ.get("KHGP", "dve")
HG_SBF_ENG = os.environ.get("KHGS", "act")
OVERLAP_SWA = os.environ.get("KOVS", "0") == "1"
OVERLAP_S5 = os.environ.get("KOVL", "1") == "1"
ATTACH_WAIT = os.environ.get("KATTACH", "1") == "1"


class T:
    _n = 0

    def __init__(self, t, name, psum=False):
        self.t = t
        self.psum = psum
        T._n += 1
        self.key = (name, T._n)

    def __getitem__(self, idx):
        return self.t[idx]


class PV(T):
    def __init__(self, base, off, name):
        T.__init__(self, base.t, name, psum=True)
        self.off = off

    def _c(self, c):
        if isinstance(c, slice):
            a = self.off + (c.start or 0)
            b = self.off + (512 if c.stop is None else c.stop)
            return slice(a, b, c.step)
        return self.off + c

    def __getitem__(self, idx):
        if isinstance(idx, tuple):
            return self.t[(idx[0], self._c(idx[1])) + tuple(idx[2:])]
        return self.t[idx, self.off:self.off + 512]


class KB:
    def __init__(self, nc):
        self.nc = nc
        self.es = contextlib.ExitStack()
        self.eng = {"pe": nc.tensor, "act": nc.scalar, "dve": nc.vector, "pool": nc.gpsimd, "sp": nc.sync}
        self.sem = {e: self.es.enter_context(nc.semaphore("s_" + e)) for e in self.eng}
        self.cnt = {e: 0 for e in self.eng}
        self.lanes = {}
        self.lane_val = {}
        self.lane_rr = {}
        for q, n in (("sp", int(os.environ.get("KLANES", "12"))), ("pool", 8), ("act", 4)):
            self.lanes[q] = [self.es.enter_context(nc.semaphore(f"l_{q}{i}")) for i in range(n)]
            self.lane_rr[q] = 0
            for i in range(n):
                self.lane_val[(q, i)] = 0
        self.seen = {e: {} for e in self.eng}
        self.res = {}
        self.ninst = 0
        self.nwait = 0
        self.uid = 0

    def sb(self, es, name, shape, dtype):
        self.uid += 1
        t = es.enter_context(self.nc.sbuf_tensor(f"{name}_{self.uid}", list(shape), dtype))
        return T(t, name)

    def ps(self, es, name, shape, dtype=F32):
        self.uid += 1
        t = es.enter_context(self.nc.psum_tensor(f"{name}_{self.uid}", list(shape), dtype))
        return T(t, name, psum=True)

    def _semof(self, src):
        if src[0] == "eng":
            return self.sem[src[1]]
        return self.lanes[src[1]][src[2]]

    def _wait(self, engine, dep):
        src, val = dep
        if val <= 0:
            return
        if src[0] == "eng" and src[1] == engine:
            if engine == "pe" or not SAME_ENGINE_SYNC:
                return
        if self.seen[engine].get(src, 0) >= val:
            return
        self.eng[engine].wait_ge(self._semof(src), val)
        self.nwait += 1
        self.seen[engine][src] = val

    def _deps(self, r, w, me=None):
        deps = []
        for t in r:
            st = self.res.get(t.key)
            if st and st["w"]:
                deps.append(st["w"])
            if st and t.psum:
                deps.extend((src, v) for src, v in st["r"].items() if src != me)
        for t in w:
            st = self.res.get(t.key)
            if st:
                if st["w"]:
                    deps.append(st["w"])
                deps.extend(st["r"].items())
        return deps

    def _update(self, r, w, src, val):
        for t in r:
            st = self.res.setdefault(t.key, {"w": None, "r": {}})
            st["r"][src] = val
        for t in w:
            self.res[t.key] = {"w": (src, val), "r": {}}

    def _need(self, engine, dep):
        src, val = dep
        if val <= 0:
            return False
        if src[0] == "eng" and src[1] == engine and (engine == "pe" or not SAME_ENGINE_SYNC):
            return False
        return self.seen[engine].get(src, 0) < val

    def op(self, engine, fn, r=(), w=()):
        deps = [d for d in self._deps(r, w, ("eng", engine))]
        best = {}
        for src, val in deps:
            if self._need(engine, (src, val)) and val > best.get(src, 0):
                best[src] = val
        items = list(best.items())
        attach = None
        if ATTACH_WAIT and items:
            attach = items.pop()
        for dep in items:
            self._wait(engine, dep)
        ins = fn(self.eng[engine])
        if attach is not None:
            ins._wait_ge(self._semof(attach[0]), attach[1])
            self.seen[engine][attach[0]] = attach[1]
            self.nwait += 1
        self.cnt[engine] += 1
        ins.then_inc(self.sem[engine], 1)
        self.ninst += 1
        self._update(r, w, ("eng", engine), self.cnt[engine])
        return ins

    def dma(self, out, in_, r=(), w=(), q="sp", **kw):
        i = self.lane_rr[q]
        self.lane_rr[q] = (i + 1) % len(self.lanes[q])
        src = ("dma", q, i)
        deps = self._deps(r, w)
        deps.append((src, self.lane_val[(q, i)]))
        for dep in deps:
            self._wait(q, dep)
        ins = self.eng[q].dma_start(out=out, in_=in_, **kw)
        self.lane_val[(q, i)] += 16
        ins.then_inc(self.lanes[q][i], 16)
        self.ninst += 1
        self._update(r, w, src, self.lane_val[(q, i)])

    def barrier(self):
        for e in self.eng:
            for e2 in self.eng:
                self._wait(e, (("eng", e2), self.cnt[e2]))
            for (q, i), v in self.lane_val.items():
                self._wait(e, (("dma", q, i), v))
        self.res = {}

    def finish(self):
        for (q, i), v in self.lane_val.items():
            self._wait("sp", (("dma", q, i), v))
        for e2 in self.eng:
            if e2 != "sp":
                self._wait("sp", (("eng", e2), self.cnt[e2]))


def _blocks():
    out = [(0, CTX)]
    for j in range(SEQ // 512):
        out.append((CTX + 512 * j, 512))
    return out


def _win_cols():
    cols = {}
    base = {"u": 0, "sq": 256, "sk": 512, "sv": 640, "hq": 768, "zf": 1024, "zb": 1280, "hi": 1536, "hg": 1792,
            "cq": 2048, "ckv": 2304, "kr": 2432}

    def swap64(off):
        return np.concatenate([off + np.arange(16, 32), off + np.arange(0, 16), off + np.arange(48, 64), off + np.arange(32, 48)])

    def swap32(off):
        return np.concatenate([off + np.arange(8, 16), off + np.arange(0, 8), off + np.arange(24, 32), off + np.arange(16, 24)])

    sq = base["sq"]
    hA = np.concatenate([sq + np.arange(0, 64), sq + np.arange(128, 192)])
    hB = np.concatenate([sq + np.arange(64, 128), sq + np.arange(192, 256)])
    hAs = np.concatenate([swap64(sq + 0), swap64(sq + 128)])
    hBs = np.concatenate([swap64(sq + 64), swap64(sq + 192)])
    sk = base["sk"]
    fm = [hA, hB, hAs, hBs, sk + np.arange(128), np.concatenate([swap64(sk), swap64(sk + 64)]),
          base["hq"] + np.arange(256), base["zf"] + np.arange(256), base["zb"] + np.arange(256),
          base["hg"] + np.arange(256), base["cq"] + np.arange(256), base["ckv"] + np.arange(128),
          base["kr"] + np.arange(32), swap32(base["kr"])]
    tm = [base["u"] + np.arange(256), base["sv"] + np.arange(128), base["hi"] + np.arange(256)]
    return np.concatenate(fm + tm)


C_SQ, C_SQS, C_SK, C_SKS, C_HQ, C_ZF, C_ZB, C_HG, C_CQ, C_CKV, C_KR, C_KRS = 0, 256, 512, 640, 768, 1024, 1280, 1536, 1792, 2048, 2176, 2208
C_TM = 2240
NWIN = C_TM + 640


def _wqb_cols():
    def swap32(off):
        return np.concatenate([off + np.arange(8, 16), off + np.arange(0, 8), off + np.arange(24, 32), off + np.arange(16, 24)])
    cols = []
    for h in range(4):
        cols += [h * 96 + np.arange(96), swap32(h * 96 + 64)]
    return np.concatenate(cols)


def _swa_mask():
    kk = np.arange(128)[:, None]
    qq = np.arange(128)[None, :]
    lo = (qq <= kk).astype(np.float32)
    hi = (kk <= qq).astype(np.float32)
    m = np.stack([np.stack([lo] * 4, 1), np.stack([hi] * 4, 1)], 1)
    return np.ascontiguousarray(m.astype(np.float32))


def _hg_consts():
    t = np.arange(2048)
    rm = np.broadcast_to((t % 32 != 0).astype(np.float32)[None, :], (64, 2048))
    s_ = np.arange(128)[:, None]
    t_ = np.arange(128)[None, :]
    same = (s_ // 32) == (t_ // 32)
    fw = (same & (s_ <= t_)).astype(np.float32)
    bw = (same & (s_ >= t_)).astype(np.float32)
    am = np.stack([np.stack([fw] * 4, 1), np.stack([bw] * 4, 1)], 1)
    cm = (np.arange(128)[:, None] // 32 == np.arange(4)[None, :]).astype(np.float32)
    return {"hg_rmask": np.ascontiguousarray(rm), "hg_amask": np.ascontiguousarray(am.astype(np.float32)), "hg_cmask": cm}


def _s5_tmask():
    j = np.arange(128)[:, None] // 16
    t = np.arange(128)[None, :] // 16
    return np.ascontiguousarray(np.stack([(t >= j), (t <= j)], 1).astype(np.float32))


def _rope_tables():
    def tab(dim):
        rows = SEQ // GRID_W
        row = np.repeat(np.arange(rows, dtype=np.float64), GRID_W)
        col = np.tile(np.arange(GRID_W, dtype=np.float64), rows)
        nf = dim // 4
        inv = 10000.0 ** (-np.arange(nf, dtype=np.float64) / nf)
        ar = row[None, :] * inv[:, None]
        ac = col[None, :] * inv[:, None]
        C = np.concatenate([np.cos(ar), np.cos(ar), np.cos(ac), np.cos(ac)], 0)
        S = np.concatenate([-np.sin(ar), np.sin(ar), -np.sin(ac), np.sin(ac)], 0)
        C = np.concatenate([np.ones((dim, CTX)), C], 1)
        S = np.concatenate([np.zeros((dim, CTX)), S], 1)
        return C.astype(np.float32), S.astype(np.float32)
    c64, s64 = tab(64)
    c32, s32 = tab(32)
    rs = np.stack([np.concatenate([c64, c64], 0), np.concatenate([s64, s64], 0)])
    z = np.zeros((64, TALL), np.float32)
    rm = np.stack([np.concatenate([z, c32], 0), np.concatenate([z, s32], 0)])
    return rs, rm


class Builder:
    def __init__(self, nlayers=DEPTH, debug=None, stop_after=None, only=None):
        self.stop_after = stop_after
        self.only = only
        self.nl = nlayers
        self.debug = debug
        nc = bass.Bass("TRN2", target_bir_lowering=False)
        self.nc = nc
        self.k = KB(nc)
        dt = nc.dram_tensor

        def ext(name, shape, dtype=F32):
            return dt(name, list(shape), dtype, kind="ExternalInput").ap()

        def internal(name, shape, dtype=F32):
            return dt(name, list(shape), dtype, kind="Internal").ap()

        self.xin = ext("xin", [D, TALL])
        self.cc = ext("cc", [2, D])
        self.w_mod = ext("w_mod", [DEPTH, D, 6 * D])
        self.b_mod = ext("b_mod", [DEPTH, 6 * D])
        self.norm1_g = ext("norm1_g", [DEPTH, D])
        self.norm2_g = ext("norm2_g", [DEPTH, D])
        self.w_in = ext("w_in", [DEPTH, D, NWIN])
        self.w_out = ext("w_out", [DEPTH, D, D])
        self.w_qb = ext("w_qb", [DEPTH, 256, 512])
        self.w_kvb = ext("w_kvb", [DEPTH, 128, 512])
        self.qn_g = ext("qn_g", [DEPTH, 256])
        self.kvn_g = ext("kvn_g", [DEPTH, 128])
        self.w_up = ext("w_up", [DEPTH, D, 2 * FFH])
        self.w_down = ext("w_down", [DEPTH, FFH, D])
        self.final_g = ext("final_g", [D])
        self.rope_s = ext("rope_s", [2, 128, TALL])
        self.rope_m = ext("rope_m", [2, 96, TALL])
        self.ident = ext("ident", [128, 128])
        self.swa_mask = ext("swa_mask", [128, 2, 4, 128])
        self.swa_sink = ext("swa_sink", [DEPTH, 4])
        self.hg_lb = ext("hg_lb", [2, DEPTH, 256])
        self.s5_lam_re = ext("s5_lam_re", [DEPTH, 2, 16, 64])
        self.s5_lam_im = ext("s5_lam_im", [DEPTH, 2, 16, 64])
        self.s5_log_dt = ext("s5_log_dt", [DEPTH, 2, 16, 64])
        self.s5_b_re = ext("s5_b_re", [DEPTH, 2, 16, 64, 16])
        self.s5_b_im = ext("s5_b_im", [DEPTH, 2, 16, 64, 16])
        self.s5_c_re = ext("s5_c_re", [DEPTH, 2, 16, 16, 64])
        self.s5_c_im = ext("s5_c_im", [DEPTH, 2, 16, 16, 64])
        self.s5_d = ext("s5_d", [DEPTH, 256])
        self.s5_w_glu = ext("s5_w_glu", [DEPTH, 256, 256])
        self.s5_b_glu = ext("s5_b_glu", [DEPTH, 256])
        self.s5_tmask = ext("s5_tmask", [128, 2, 128])
        self.hg_norm_g = ext("hg_norm_g", [DEPTH, 64])
        self.hg_rmask = ext("hg_rmask", [64, 2048])
        self.hg_amask = ext("hg_amask", [128, 2, 4, 128])
        self.hg_cmask = ext("hg_cmask", [128, 4])
        self.out = dt("out", [D, SEQ], F32, kind="ExternalOutput").ap()
        self.xres = internal("xres", [D, TALL])
        self.s_sq = internal("s_sq", [256, TALL], BF16)
        self.s_sk = internal("s_sk", [128, TALL], BF16)
        self.s_sv = internal("s_sv", [TALL, 130], BF16)
        self.s_u = internal("s_u", [TALL, 256], F32)
        self.s_hq = internal("s_hq", [256, TALL], BF16)
        self.s_zf = internal("s_zf", [256, TALL], F32)
        self.s_zb = internal("s_zb", [256, TALL], F32)
        self.s_hg = internal("s_hg", [256, TALL], F32)
        self.s_hi = internal("s_hi", [TALL, 256], BF16)
        self.s_mq = internal("s_mq", [4, 96, TALL], BF16)
        self.s_mk = internal("s_mk", [4, 96, TALL], BF16)
        self.s_mv = internal("s_mv", [TALL, 260], BF16)
        self.ymix = internal("ymix", [D, TALL], BF16)
        self.s_ob = internal("s_ob", [64, 4, TALL], F32)
        self.s_y = [internal(f"s_y{d}", [128, 16, TALL // 8], F32) for d in range(2)]
        if debug:
            self.dbg = {n: dt("dbg_" + n, list(v[0]), v[1], kind="ExternalOutput").ap() for n, v in debug.items()}
        self.build()

    def build(self):
        k = self.k
        nc = self.nc
        es = k.es
        self.ones_bf = k.sb(es, "ones_bf", [128, 128], BF16)
        self.ident_bf = k.sb(es, "ident_bf", [128, 128], BF16)
        self.ident_f = k.sb(es, "ident_f", [128, 128], F32)
        self.mod = k.sb(es, "mod", [128, DEPTH, 48, 2], F32)
        self.gs = k.sb(es, "gs", [128, DEPTH, 2, 8, 2], F32)
        self.epsb = k.sb(es, "epsb", [128, 1], F32)
        k.op("dve", lambda e: e.memset(self.ones_bf[:], 1.0), w=[self.ones_bf])
        k.op("dve", lambda e: e.memset(self.epsb[:], EPS), w=[self.epsb])
        k.dma(self.ident_f[:], self.ident, w=[self.ident_f])
        k.dma(self.ident_bf[:], self.ident, w=[self.ident_bf], q="pool")
        self.psw = [k.ps(es, f"psw{i}", [128, 1024], F32) for i in range(4)]
        self.psb = [PV(self.psw[i // 2], 512 * (i % 2), f"psb{i}") for i in range(8)]
        self.setup_mod()
        k.barrier()
        self.setup_hglb()
        for l in range(self.nl):
            self.layer(l)
        if self.debug:
            k.barrier()
            for n in self.debug:
                src = self.debug[n][2](self) if len(self.debug[n]) > 2 else getattr(self, n)
                nd = len(src.shape)
                pat = " ".join("abcd"[:nd])
                fl = lambda a: a.rearrange(f"{pat} -> ({pat})").rearrange("(p f) -> p f", p=16)
                k.dma(fl(self.dbg[n]), fl(src))
        k.finish()
        es.close()

    def load_fm(self, es, dst_ap, src2d, n, wkeys, wd=128):
        k = self.k
        stg = k.sb(es, "stg", [128, 128], F32)
        ps = self.psb[7]
        k.dma(stg[0:n, 0:wd], src2d, w=[stg])
        k.op("pe", lambda e: e.transpose(out=ps[0:wd, 0:n], in_=stg[0:n, 0:wd], identity=self.ident_f[0:n, 0:n]),
             r=[stg, self.ident_f], w=[ps])
        k.op("dve", lambda e: e.tensor_copy(out=dst_ap, in_=ps[0:wd, 0:n]), r=[ps], w=wkeys)

    def setup_mod(self):
        k = self.k
        with contextlib.ExitStack() as es:
            craw = k.sb(es, "craw", [128, 8, 2], F32)
            csil = k.sb(es, "csil", [128, 8, 2], F32)
            bm = k.sb(es, "bm", [128, DEPTH, 48], F32)
            ng = k.sb(es, "ng", [128, 2, DEPTH, 8], F32)
            wm = [k.sb(es, f"wm{i}", [128, 8, 768], F32) for i in range(2)]
            self.load_fm(es, craw[:, :, 0], self.cc[0].rearrange("(c p) -> c p", p=128), 8, [craw])
            self.load_fm(es, craw[:, :, 1], self.cc[1].rearrange("(c p) -> c p", p=128), 8, [craw])
            for l in range(DEPTH):
                self.load_fm(es, bm[:, l, :], self.b_mod[l].rearrange("(j p) -> j p", p=128), 48, [bm])
            self.load_fm(es, ng[:, 0], self.norm1_g.rearrange("l (c p) -> (l c) p", p=128), 32, [ng])
            self.load_fm(es, ng[:, 1], self.norm2_g.rearrange("l (c p) -> (l c) p", p=128), 32, [ng])
            k.op("act", lambda e: e.activation(out=csil[:], in_=craw[:], func=AF.Silu), r=[craw], w=[csil])
            it = 0
            for l in range(self.nl):
                for grp in range(8):
                    wt = wm[it % 2]
                    it += 1
                    k.dma(wt[:], self.w_mod[l].rearrange("(c p) n -> p c n", p=128)[:, :, grp * 768:(grp + 1) * 768], w=[wt])
                    for n in range(6):
                        ps = self.psb[n % 4]
                        for kc in range(8):
                            k.op("pe", lambda e: e.matmul(ps[:, 0:2], lhsT=wt[:, kc, n * 128:(n + 1) * 128], rhs=csil[:, kc, :],
                                                          start=(kc == 0), stop=(kc == 7)), r=[wt, csil], w=[ps])
                        j = grp * 6 + n
                        k.op("dve", lambda e: e.tensor_scalar(out=self.mod[:, l, j, :], in0=ps[:, 0:2], scalar1=bm[:, l, j:j + 1],
                                                              scalar2=None, op0=ALU.add), r=[ps, bm], w=[self.mod])
                for which, sci in ((0, 1), (1, 4)):
                    for j in range(2):
                        k.op("dve", lambda e: e.scalar_tensor_tensor(out=self.gs[:, l, which, :, j], in0=self.mod[:, l, sci * 8:(sci + 1) * 8, j],
                                                                     scalar=1.0, in1=ng[:, which, l, :], op0=ALU.add, op1=ALU.mult),
                             r=[self.mod, ng], w=[self.gs])
            k.barrier()

    def layer(self, l):
        k = self.k
        xsrc = self.xin if l == 0 else self.xres
        if self.stop_after == "mod":
            return
        self.phase_proj(l, xsrc)
        k.barrier()
        if self.stop_after == "proj":
            return
        if self.only is None and OVERLAP_S5:
            gen = self.phase_s5_gen(l)
            next(gen)
            need_ctx = l < DEPTH - 1
            npi = (16 * 4 * 33) + (4 if need_ctx else 0)
            rate = self.s5_nitems / float(npi)
            acc = [0.0]

            def filler():
                acc[0] += rate
                while acc[0] >= 1.0:
                    acc[0] -= 1.0
                    next(gen, None)
            self.phase_mla(l, filler=filler)
            for _ in gen:
                pass
        else:
            if self.only in (None, "mla"):
                self.phase_mla(l)
        if self.only is None and OVERLAP_SWA:
            sgen = self.phase_swa_gen(l, corun=True)
            next(sgen)
            self.phase_hg(l, filler=lambda: next(sgen, None))
            for _ in sgen:
                pass
        elif self.only in (None, "swa"):
            for _ in self.phase_swa_gen(l):
                pass
        if self.only == "hg" or (self.only is None and not OVERLAP_SWA):
            self.phase_hg(l)
        if self.only == "s5" or (self.only is None and not OVERLAP_S5):
            for _ in self.phase_s5_gen(l):
                pass
        if self.stop_after == "mix":
            return
        self.phase_ffn(l, xsrc)

    def norm_mod(self, es_tiles, xsrc, t0, n, l, which, ctxflag, load=True, gain=None, bias=None, out_f32=None):
        k = self.k
        xt, sq, rstd, tmps, ht, ps = es_tiles
        shi = 0 if which == 0 else 3
        if load:
            k.dma(xt[:, :, :n], xsrc.rearrange("(c p) t -> p c t", p=128)[:, :, t0:t0 + n], w=[xt])
        for c in range(8):
            k.op("pool", lambda e: e.tensor_tensor(out=sq[:, c, :n], in0=xt[:, c, :n], in1=xt[:, c, :n], op=ALU.mult), r=[xt], w=[sq])
        for c in range(8):
            k.op("pe", lambda e: e.matmul(ps[:, :n], lhsT=self.ones_bf[:], rhs=sq[:, c, :n], start=(c == 0), stop=(c == 7)),
                 r=[sq, self.ones_bf], w=[ps])
        k.op("act", lambda e: e.activation(out=rstd[:, :n], in_=ps[:, :n], func=AF.Sqrt, bias=self.epsb[:], scale=1.0 / D),
             r=[ps, self.epsb], w=[rstd])
        k.op("dve", lambda e: e.reciprocal(out=rstd[:, :n], in_=rstd[:, :n]), r=[rstd], w=[rstd])
        for c in range(8):
            g_ap = gain[:, c:c + 1] if gain is not None else self.gs[:, l, which, c, ctxflag:ctxflag + 1]
            if out_f32 is not None:
                tmp = tmps[c % len(tmps)]
                k.op("dve", lambda e: e.scalar_tensor_tensor(out=tmp[:, :n], in0=xt[:, c, :n], scalar=g_ap,
                                                             in1=rstd[:, :n], op0=ALU.mult, op1=ALU.mult), r=[xt, rstd, self.gs], w=[tmp])
                k.dma(out_f32(c), tmp[:, :n], r=[tmp])
                continue
            tmp = tmps[c % len(tmps)]
            k.op("dve", lambda e: e.scalar_tensor_tensor(out=tmp[:, :n], in0=xt[:, c, :n], scalar=g_ap,
                                                         in1=rstd[:, :n], op0=ALU.mult, op1=ALU.mult), r=[xt, rstd, self.gs], w=[tmp])
            k.op("act", lambda e: e.activation(out=ht[:, c, :n], in_=tmp[:, :n], func=AF.Identity,
                                               bias=self.mod[:, l, shi * 8 + c, ctxflag:ctxflag + 1], scale=1.0),
                 r=[tmp, self.mod], w=[ht])

    def phase_proj(self, l, xsrc):
        k = self.k
        with contextlib.ExitStack() as es:
            win = k.sb(es, "win", [128, 8, NWIN], BF16)
            wqb = k.sb(es, "wqb", [128, 2, 512], BF16)
            wkvb = k.sb(es, "wkvb", [128, 512], BF16)
            qng = k.sb(es, "qng", [128, 2], F32)
            kvng = k.sb(es, "kvng", [128, 1], F32)
            for c in range(8):
                k.dma(win[:, c, :], self.w_in[l, c * 128:(c + 1) * 128, :], w=[win], q="pool")
            k.dma(wqb[:], self.w_qb[l].rearrange("(c p) n -> p c n", p=128), w=[wqb], q="pool")
            k.dma(wkvb[:], self.w_kvb[l], w=[wkvb], q="pool")
            self.load_fm(es, qng[:], self.qn_g[l].rearrange("(c p) -> c p", p=128), 2, [qng])
            self.load_fm(es, kvng[:], self.kvn_g[l].rearrange("(c p) -> c p", p=128), 1, [kvng])
            NB = 2
            xt = [k.sb(es, f"xt{i}", [128, 8, 512], F32) for i in range(NB)]
            sq = [k.sb(es, f"sq{i}", [128, 8, 512], BF16) for i in range(NB)]
            rstd = [k.sb(es, f"rstd{i}", [128, 512], F32) for i in range(NB)]
            tmp = [k.sb(es, f"tmp{i}", [128, 512], F32) for i in range(2)]
            ht = [k.sb(es, f"ht{i}", [128, 8, 512], BF16) for i in range(NB)]
            rs = [k.sb(es, f"rs{i}", [128, 2, 512], F32) for i in range(NB)]
            rm = [k.sb(es, f"rm{i}", [96, 2, 512], F32) for i in range(NB)]
            ob = [k.sb(es, f"ob{i}", [128, 512], BF16) for i in range(4)]
            of = [k.sb(es, f"of{i}", [128, 512], F32) for i in range(4)]
            r1 = [k.sb(es, f"r1{i}", [128, 512], F32) for i in range(2)]
            r2 = [k.sb(es, f"r2{i}", [128, 512], F32) for i in range(2)]
            cqn = [k.sb(es, f"cqn{i}", [128, 2, 512], BF16) for i in range(NB)]
            ckvn = [k.sb(es, f"ckvn{i}", [128, 512], BF16) for i in range(NB)]
            nsq = [k.sb(es, f"nsq{i}", [128, 512], BF16) for i in range(2)]
            nrs = [k.sb(es, f"nrs{i}", [128, 512], F32) for i in range(2)]
            tmo = [k.sb(es, f"tmo{i}", [128, 640], F32) for i in range(2)]
            tmb = [k.sb(es, f"tmb{i}", [128, 256], BF16) for i in range(2)]
            svb = [k.sb(es, f"svb{i}", [128, 2, 65], BF16) for i in range(2)]
            mvb = [k.sb(es, f"mvb{i}", [128, 4, 65], BF16) for i in range(2)]
            for i in range(2):
                k.op("pool", lambda e: e.memset(svb[i][:, :, 64:65], 1.0), w=[svb[i]])
                k.op("pool", lambda e: e.memset(mvb[i][:, :, 64:65], 1.0), w=[mvb[i]])
            state = {"ob": 0, "of": 0, "ps": 0, "r": 0, "n": 0}

            def nxt(lst, key):
                i = state[key]
                state[key] = (i + 1) % len(lst)
                return lst[i]

            def fm_mm(ps, col0, m, hT, n, lhs_w=None, p0=0):
                for kc in range(8):
                    k.op("pe", lambda e: e.matmul(ps[0:m, :n], lhsT=win[:, kc, col0:col0 + m], rhs=hT[:, kc, :n],
                                                  start=(kc == 0), stop=(kc == 7)), r=[win, hT], w=[ps])

            blks = _blocks()

            def do_norm(bj):
                t0_, n_ = blks[bj]
                self.norm_mod((xt[bj % NB], sq[bj % NB], rstd[bj % NB], tmp, ht[bj % NB], self.psb[7]), xsrc, t0_, n_, l, 0, 1 if bj == 0 else 0)
            do_norm(0)
            for bi, (t0, n) in enumerate(blks):
                ctxflag = 1 if bi == 0 else 0
                b = bi % NB
                hT = ht[b]
                CUT = float(os.environ.get("KCUT", "99"))
                if bi >= int(os.environ.get("KBLK", "99")):
                    break
                if CUT < 1:
                    continue
                k.dma(rs[b][:, :, :n], self.rope_s[:, :, t0:t0 + n].rearrange("a p t -> p a t"), w=[rs[b]])
                k.dma(rm[b][64:96, :, :n], self.rope_m[:, 64:96, t0:t0 + n].rearrange("a p t -> p a t"), w=[rm[b]])
                for ci, (c_a, c_b, dst) in enumerate(((C_SQ, C_SQS, self.s_sq[0:128]), (C_SQ + 128, C_SQS + 128, self.s_sq[128:256]),
                                                      (C_SK, C_SKS, self.s_sk))):
                    pa = nxt(self.psb[0:6], "ps")
                    fm_mm(pa, c_a, 128, hT, n)
                    pb = nxt(self.psb[0:6], "ps")
                    fm_mm(pb, c_b, 128, hT, n)
                    a1 = nxt(r1, "r")
                    a2 = r2[r1.index(a1)]
                    o = nxt(ob, "ob")
                    k.op("dve", lambda e: e.tensor_tensor(out=a1[:, :n], in0=pa[:, :n], in1=rs[b][:, 0, :n], op=ALU.mult), r=[pa, rs[b]], w=[a1])
                    k.op("dve", lambda e: e.tensor_tensor(out=a2[:, :n], in0=pb[:, :n], in1=rs[b][:, 1, :n], op=ALU.mult), r=[pb, rs[b]], w=[a2])
                    k.op("pool", lambda e: e.tensor_tensor(out=o[:, :n], in0=a1[:, :n], in1=a2[:, :n], op=ALU.add), r=[a1, a2], w=[o])
                    k.dma(dst[:, t0:t0 + n], o[:, :n], r=[o])
                if bi + 1 < len(blks) and bi + 1 < int(os.environ.get("KBLK", "99")):
                    do_norm(bi + 1)
                if CUT < 2:
                    continue
                for c in range(2):
                    pa = nxt(self.psb[0:6], "ps")
                    fm_mm(pa, C_HQ + 128 * c, 128, hT, n)
                    o = nxt(ob, "ob")
                    k.op("act", lambda e: e.copy(out=o[:, :n], in_=pa[:, :n]), r=[pa], w=[o])
                    k.dma(self.s_hq[128 * c:128 * c + 128, t0:t0 + n], o[:, :n], r=[o])
                for (c0, dst, fn) in ((C_ZF, self.s_zf, AF.Copy), (C_ZB, self.s_zb, AF.Copy), (C_HG, self.s_hg, AF.Silu)):
                    for c in range(2):
                        pa = nxt(self.psb[0:6], "ps")
                        fm_mm(pa, c0 + 128 * c, 128, hT, n)
                        o = nxt(of, "of")
                        k.op("act", lambda e: e.activation(out=o[:, :n], in_=pa[:, :n], func=fn), r=[pa], w=[o])
                        k.dma(dst[128 * c:128 * c + 128, t0:t0 + n], o[:, :n], r=[o])
                if CUT < 3:
                    continue
                pcq = [nxt(self.psb[0:6], "ps") for _ in range(2)]
                for c in range(2):
                    fm_mm(pcq[c], C_CQ + 128 * c, 128, hT, n)
                pss = self.psb[6]
                for c in range(2):
                    s_ = nxt(nsq, "n")
                    k.op("act", lambda e: e.activation(out=s_[:, :n], in_=pcq[c][:, :n], func=AF.Square), r=[pcq[c]], w=[s_])
                    k.op("pe", lambda e: e.matmul(pss[:, :n], lhsT=self.ones_bf[:], rhs=s_[:, :n], start=(c == 0), stop=(c == 1)),
                         r=[s_, self.ones_bf], w=[pss])
                nr = nrs[0]
                k.op("act", lambda e: e.activation(out=nr[:, :n], in_=pss[:, :n], func=AF.Sqrt, bias=self.epsb[:], scale=1.0 / 256),
                     r=[pss, self.epsb], w=[nr])
                k.op("dve", lambda e: e.reciprocal(out=nr[:, :n], in_=nr[:, :n]), r=[nr], w=[nr])
                for c in range(2):
                    k.op("dve", lambda e: e.scalar_tensor_tensor(out=cqn[b][:, c, :n], in0=pcq[c][:, :n], scalar=qng[:, c:c + 1], in1=nr[:, :n],
                                                                 op0=ALU.mult, op1=ALU.mult), r=[pcq[c], qng, nr], w=[cqn[b]])
                for h in range(4):
                    pa = nxt(self.psb[0:6], "ps")
                    pb = nxt(self.psb[0:6], "ps")
                    for c in range(2):
                        k.op("pe", lambda e: e.matmul(pa[0:96, :n], lhsT=wqb[:, c, h * 128:h * 128 + 96], rhs=cqn[b][:, c, :n],
                                                      start=(c == 0), stop=(c == 1)), r=[wqb, cqn[b]], w=[pa])
                    for c in range(2):
                        k.op("pe", lambda e: e.matmul(pb[0:96, :n], lhsT=wqb[:, c, h * 128 + 32:h * 128 + 128], rhs=cqn[b][:, c, :n],
                                                      start=(c == 0), stop=(c == 1)), r=[wqb, cqn[b]], w=[pb])
                    o = nxt(ob, "ob")
                    a1 = nxt(r1, "r")
                    a2 = r2[r1.index(a1)]
                    k.op("act", lambda e: e.copy(out=o[0:64, :n], in_=pa[0:64, :n]), r=[pa], w=[o])
                    k.op("dve", lambda e: e.tensor_tensor(out=a1[64:96, :n], in0=pa[64:96, :n], in1=rm[b][64:96, 0, :n], op=ALU.mult),
                         r=[pa, rm[b]], w=[a1])
                    k.op("dve", lambda e: e.tensor_tensor(out=a2[64:96, :n], in0=pb[64:96, :n], in1=rm[b][64:96, 1, :n], op=ALU.mult),
                         r=[pb, rm[b]], w=[a2])
                    k.op("pool", lambda e: e.tensor_tensor(out=o[64:96, :n], in0=a1[64:96, :n], in1=a2[64:96, :n], op=ALU.add),
                         r=[a1, a2, o], w=[o])
                    k.dma(self.s_mq[h, :, t0:t0 + n], o[0:96, :n], r=[o])
                if CUT < 4:
                    continue
                pkv = nxt(self.psb[0:6], "ps")
                fm_mm(pkv, C_CKV, 128, hT, n)
                s_ = nxt(nsq, "n")
                k.op("act", lambda e: e.activation(out=s_[:, :n], in_=pkv[:, :n], func=AF.Square), r=[pkv], w=[s_])
                k.op("pe", lambda e: e.matmul(pss[:, :n], lhsT=self.ones_bf[:], rhs=s_[:, :n], start=True, stop=True),
                     r=[s_, self.ones_bf], w=[pss])
                nr = nrs[1]
                k.op("act", lambda e: e.activation(out=nr[:, :n], in_=pss[:, :n], func=AF.Sqrt, bias=self.epsb[:], scale=1.0 / 128),
                     r=[pss, self.epsb], w=[nr])
                k.op("dve", lambda e: e.reciprocal(out=nr[:, :n], in_=nr[:, :n]), r=[nr], w=[nr])
                k.op("dve", lambda e: e.scalar_tensor_tensor(out=ckvn[b][:, :n], in0=pkv[:, :n], scalar=kvng[:, 0:1], in1=nr[:, :n],
                                                             op0=ALU.mult, op1=ALU.mult), r=[pkv, kvng, nr], w=[ckvn[b]])
                if CUT < 4.2:
                    continue
                pa = nxt(self.psb[0:6], "ps")
                pb = nxt(self.psb[0:6], "ps")
                fm_mm(pa, C_KR - 64, 96, hT, n)
                fm_mm(pb, C_KRS - 64, 96, hT, n)
                a1 = nxt(r1, "r")
                a2 = r2[r1.index(a1)]
                kro = nxt(ob, "ob")
                k.op("dve", lambda e: e.tensor_tensor(out=a1[64:96, :n], in0=pa[64:96, :n], in1=rm[b][64:96, 0, :n], op=ALU.mult),
                     r=[pa, rm[b]], w=[a1])
                k.op("dve", lambda e: e.tensor_tensor(out=a2[64:96, :n], in0=pb[64:96, :n], in1=rm[b][64:96, 1, :n], op=ALU.mult),
                     r=[pb, rm[b]], w=[a2])
                k.op("pool", lambda e: e.tensor_tensor(out=kro[64:96, :n], in0=a1[64:96, :n], in1=a2[64:96, :n], op=ALU.add),
                     r=[a1, a2], w=[kro])
                if CUT < 4.4:
                    continue
                for h in range(4):
                    k.dma(self.s_mk[h, 64:96, t0:t0 + n], kro[64:96, :n], r=[kro])
                    if CUT < 4.6:
                        continue
                    pa = nxt(self.psb[0:6], "ps")
                    if os.environ.get("KVAR") == "A":
                        k.op("pe", lambda e: e.matmul(pa[:, :n], lhsT=wkvb[:, h * 128:h * 128 + 128], rhs=ckvn[b][:, :n], start=True, stop=True),
                             r=[wkvb, ckvn[b]], w=[pa])
                    else:
                        k.op("pe", lambda e: e.matmul(pa[0:64, :n], lhsT=wkvb[:, h * 128:h * 128 + 64], rhs=ckvn[b][:, :n], start=True, stop=True),
                             r=[wkvb, ckvn[b]], w=[pa])
                    if CUT < 4.7:
                        continue
                    o = nxt(ob, "ob")
                    k.op("act", lambda e: e.copy(out=o[0:64, :n], in_=pa[0:64, :n]), r=[pa], w=[o])
                    if CUT < 4.8:
                        continue
                    k.dma(self.s_mk[h, 0:64, t0:t0 + n], o[0:64, :n], r=[o])
                if CUT < 5:
                    continue
                for st in range(n // 128):
                    ts_ = slice(st * 128, st * 128 + 128)
                    pa = nxt(self.psb[0:6], "ps")
                    pb = nxt(self.psb[0:6], "ps")
                    for kc in range(8):
                        k.op("pe", lambda e: e.matmul(pa[:, 0:512], lhsT=hT[:, kc, ts_], rhs=win[:, kc, C_TM:C_TM + 512],
                                                      start=(kc == 0), stop=(kc == 7)), r=[win, hT], w=[pa])
                    for kc in range(8):
                        k.op("pe", lambda e: e.matmul(pb[:, 0:128], lhsT=hT[:, kc, ts_], rhs=win[:, kc, C_TM + 512:C_TM + 640],
                                                      start=(kc == 0), stop=(kc == 7)), r=[win, hT], w=[pb])
                    k.op("pe", lambda e: e.matmul(pb[:, 128:384], lhsT=ckvn[b][:, ts_],
                                                  rhs=wkvb[:].rearrange("p (h x) -> p h x", x=128)[:, :, 64:128],
                                                  start=True, stop=True), r=[wkvb, ckvn[b]], w=[pb])
                    uo = tmo[st % 2]
                    bo = tmb[st % 2]
                    so = svb[st % 2]
                    mo = mvb[st % 2]
                    k.op("act", lambda e: e.copy(out=uo[:, 0:256], in_=pa[:, 0:256]), r=[pa], w=[uo])
                    k.op("dve", lambda e: e.tensor_copy(out=bo[:, 0:128], in_=pa[:, 384:512]), r=[pa], w=[bo])
                    k.op("dve", lambda e: e.tensor_copy(out=bo[:, 128:256], in_=pb[:, 0:128]), r=[pb, bo], w=[bo])
                    k.op("dve", lambda e: e.tensor_copy(out=so[:, :, 0:64], in_=pa[:, 256:384].rearrange("p (g x) -> p g x", g=2)), r=[pa, so], w=[so])
                    k.op("act", lambda e: e.copy(out=mo[:, :, 0:64], in_=pb[:, 128:384].rearrange("p (g x) -> p g x", g=4)), r=[pb, mo], w=[mo])
                    tt = t0 + st * 128
                    k.dma(self.s_u[tt:tt + 128, :], uo[:, 0:256], r=[uo])
                    k.dma(self.s_sv[tt:tt + 128, :], so[:].rearrange("p g x -> p (g x)"), r=[so])
                    k.dma(self.s_hi[tt:tt + 128, :], bo[:, 0:256], r=[bo])
                    k.dma(self.s_mv[tt:tt + 128, :], mo[:].rearrange("p g x -> p (g x)"), r=[mo])
            k.barrier()


    def phase_ffn(self, l, xsrc):
        k = self.k
        last = (l == DEPTH - 1)
        NB = 256
        with contextlib.ExitStack() as es:
            wout = k.sb(es, "wout", [128, 8, D], BF16)
            wup = k.sb(es, "wup", [128, 8, 2 * FFH], BF16)
            wdn = k.sb(es, "wdn", [128, 22, D], BF16)
            for c in range(8):
                k.dma(wout[:, c, :], self.w_out[l, c * 128:(c + 1) * 128, :], w=[wout], q="pool")
                k.dma(wup[:, c, :], self.w_up[l, c * 128:(c + 1) * 128, :], w=[wup], q="pool")
            for j in range(22):
                k.dma(wdn[:, j, :], self.w_down[l, j * 128:(j + 1) * 128, :], w=[wdn], q="pool")
            fg = None
            if last:
                fg = k.sb(es, "fg", [128, 8], F32)
                self.load_fm(es, fg[:], self.final_g.rearrange("(c p) -> c p", p=128), 8, [fg])
            xt = [k.sb(es, f"fxt{i}", [128, 8, NB], F32) for i in range(2)]
            ym = [k.sb(es, f"fym{i}", [128, 8, NB], BF16) for i in range(2)]
            sq = k.sb(es, "fsq", [128, 8, NB], BF16)
            tmp = [k.sb(es, f"ftmp{i}", [128, NB], F32) for i in range(2)]
            ht = [k.sb(es, f"fht{i}", [128, 8, NB], BF16) for i in range(2)]
            rstd = k.sb(es, "frstd", [128, NB], F32)
            sq2, rstd2, tmp2 = sq, rstd, tmp
            aT = k.sb(es, "faT", [128, 22, NB], BF16)
            sg = [k.sb(es, f"fsg{i}", [128, NB], F32) for i in range(2)]
            psi = [0]

            def nps():
                psi[0] = (psi[0] + 1) % 6
                return self.psb[psi[0]]

            t_start = CTX if last else 0
            t0s = list(range(t_start, TALL, NB))
            n = NB

            def stage_a(bi):
                t0 = t0s[bi]
                flag = 1 if t0 < CTX else 0
                b = bi % 2
                k.dma(xt[b][:, :, :n], xsrc.rearrange("(c p) t -> p c t", p=128)[:, :, t0:t0 + n], w=[xt[b]])
                k.dma(ym[b][:, :, :n], self.ymix.rearrange("(c p) t -> p c t", p=128)[:, :, t0:t0 + n], w=[ym[b]])
                for oc in range(8):
                    ps = nps()
                    for kc in range(8):
                        k.op("pe", lambda e: e.matmul(ps[:, :n], lhsT=wout[:, kc, oc * 128:(oc + 1) * 128], rhs=ym[b][:, kc, :n],
                                                      start=(kc == 0), stop=(kc == 7)), r=[wout, ym[b]], w=[ps])
                    k.op("dve", lambda e: e.scalar_tensor_tensor(out=xt[b][:, oc, :n], in0=ps[:, :n], scalar=self.mod[:, l, 16 + oc, flag:flag + 1],
                                                                 in1=xt[b][:, oc, :n], op0=ALU.mult, op1=ALU.add), r=[ps, xt[b], self.mod], w=[xt[b]])
                self.norm_mod((xt[b], sq, rstd, tmp, ht[b], self.psb[7]), None, t0, n, l, 1, flag, load=False)

            stage_a(0)
            for bi, t0 in enumerate(t0s):
                flag = 1 if t0 < CTX else 0
                b = bi % 2
                for j in range(22):
                    if j == 4 and bi + 1 < len(t0s):
                        stage_a(bi + 1)
                    pg = nps()
                    pu = nps()
                    for kc in range(8):
                        k.op("pe", lambda e: e.matmul(pg[:, :n], lhsT=wup[:, kc, j * 128:(j + 1) * 128], rhs=ht[b][:, kc, :n],
                                                      start=(kc == 0), stop=(kc == 7)), r=[wup, ht[b]], w=[pg])
                    for kc in range(8):
                        k.op("pe", lambda e: e.matmul(pu[:, :n], lhsT=wup[:, kc, FFH + j * 128:FFH + (j + 1) * 128], rhs=ht[b][:, kc, :n],
                                                      start=(kc == 0), stop=(kc == 7)), r=[wup, ht[b]], w=[pu])
                    s_ = sg[j % 2]
                    k.op("act", lambda e: e.activation(out=s_[:, :n], in_=pg[:, :n], func=AF.Silu), r=[pg], w=[s_])
                    k.op("dve", lambda e: e.tensor_tensor(out=aT[:, j, :n], in0=s_[:, :n], in1=pu[:, :n], op=ALU.mult), r=[s_, pu], w=[aT])
                for oc in range(8):
                    ps = nps()
                    for j in range(22):
                        k.op("pe", lambda e: e.matmul(ps[:, :n], lhsT=wdn[:, j, oc * 128:(oc + 1) * 128], rhs=aT[:, j, :n],
                                                      start=(j == 0), stop=(j == 21)), r=[wdn, aT], w=[ps])
                    k.op("dve", lambda e: e.scalar_tensor_tensor(out=xt[b][:, oc, :n], in0=ps[:, :n], scalar=self.mod[:, l, 40 + oc, flag:flag + 1],
                                                                 in1=xt[b][:, oc, :n], op0=ALU.mult, op1=ALU.add), r=[ps, xt[b], self.mod], w=[xt[b]])
                if not last:
                    k.dma(self.xres.rearrange("(c p) t -> p c t", p=128)[:, :, t0:t0 + n], xt[b][:, :, :n], r=[xt[b]])
                else:
                    self.norm_mod((xt[b], sq2, rstd2, tmp2, None, self.psb[7]), None, t0, n, l, 1, flag, load=False, gain=fg,
                                  out_f32=lambda c: self.out[c * 128:(c + 1) * 128, t0 - CTX:t0 - CTX + n])
            k.barrier()

    def attn_finish(self, OT, n, Osb, rec, yo, sel, dst_aps, nh=1, bc=None):
        k = self.k
        bc = self.psb[6] if bc is None else bc
        k.op("act", lambda e: e.copy(out=Osb[0:65, :n], in_=OT[0:65, :n]), r=[OT], w=[Osb])
        k.op("pe", lambda e: e.matmul(bc[0:64, :n], lhsT=sel[0:65, :], rhs=Osb[0:65, :n], start=True, stop=True), r=[sel, Osb], w=[bc])
        k.op("dve", lambda e: e.reciprocal(out=rec[0:64, :n], in_=bc[0:64, :n]), r=[bc], w=[rec])
        k.op("dve", lambda e: e.tensor_tensor(out=yo[0:64, :n], in0=Osb[0:64, :n], in1=rec[0:64, :n], op=ALU.mult), r=[Osb, rec], w=[yo])
        w = n // nh
        for j, dst in enumerate(dst_aps):
            k.dma(dst, yo[0:64, j * w:(j + 1) * w], r=[yo])

    def make_sel(self, es):
        k = self.k
        sel = k.sb(es, "sel", [65, 64], F32)
        k.op("dve", lambda e: e.memset(sel[0:64, :], 0.0), w=[sel])
        k.op("dve", lambda e: e.memset(sel[64:65, :], 1.0), r=[sel], w=[sel])
        return sel

    def phase_mla(self, l, filler=None):
        k = self.k
        need_ctx = l < DEPTH - 1
        NT = TALL // 128
        with contextlib.ExitStack() as es:
            KT = k.sb(es, "mKT", [96, 2, TALL], BF16)
            Va = k.sb(es, "mVa", [128, NT, 2, 65], BF16)
            sel = self.make_sel(es)
            QT = [k.sb(es, f"mQT{i}", [96, 2, 512], BF16) for i in range(2)]
            PT = [k.sb(es, f"mPT{i}", [128, 2, 512], BF16) for i in range(2)]
            Osb = [k.sb(es, f"mOsb{i}", [65, 512], F32) for i in range(2)]
            rec = [k.sb(es, f"mrec{i}", [64, 512], F32) for i in range(2)]
            yo = [k.sb(es, f"myo{i}", [64, 512], BF16) for i in range(2)]
            cnt = 0
            vsrc = self.s_mv.rearrange("(t p) (h x) -> p t h x", p=128, h=4)
            qi = 0
            for hp in range(2):
                for j in range(2):
                    k.dma(KT[:, j, :], self.s_mk[2 * hp + j], w=[KT])
                for t in range(0, NT, 6):
                    k.dma(Va[:, t:t + 6], vsrc[:, t:t + 6, 2 * hp:2 * hp + 2, :], w=[Va])
                for bi, (t0, n) in enumerate(_blocks()):
                    if bi == 0 and not need_ctx:
                        continue
                    ktiles = [0, 1] if bi == 0 else list(range(NT))
                    b = qi % 2
                    qi += 1
                    k.dma(QT[b][:, :, :n], self.s_mq[2 * hp:2 * hp + 2, :, t0:t0 + n].rearrange("h p t -> p h t"), w=[QT[b]])
                    for hj in range(2):
                        h = 2 * hp + hj
                        OT = self.psb[4 + cnt % 2]
                        npair = len(ktiles) // 2

                        def st_pair(ip):
                            STw = self.psw[ip % 2]
                            for j in range(2):
                                kt = ktiles[2 * ip + j]
                                k.op("pe", lambda e: e.matmul(STw[:, j * 512:j * 512 + n], lhsT=KT[:, hj, kt * 128:(kt + 1) * 128], rhs=QT[b][:, hj, :n],
                                                              start=True, stop=True), r=[KT, QT[b]], w=[STw])
                        st_pair(0)
                        for ip in range(npair):
                            if ip + 1 < npair:
                                st_pair(ip + 1)
                            STw = self.psw[ip % 2]
                            P = PT[ip % 2]
                            k.op("act", lambda e: e.activation(out=P[:, :, :n], in_=STw[:, :].rearrange("p (j x) -> p j x", j=2)[:, :, :n], func=AF.Exp,
                                                               scale=MLA_SCALE), r=[STw], w=[P])
                            for j in range(2):
                                kt = ktiles[2 * ip + j]
                                k.op("pe", lambda e: e.matmul(OT[0:65, :n], lhsT=Va[:, kt, hj, :], rhs=P[:, j, :n], start=(ip == 0 and j == 0),
                                                              stop=(ip == npair - 1 and j == 1)), r=[Va, P], w=[OT])
                            if filler is not None:
                                filler()
                        self.attn_finish(OT, n, Osb[cnt % 2], rec[cnt % 2], yo[cnt % 2], sel,
                                         [self.ymix[768 + h * 64:768 + (h + 1) * 64, t0:t0 + n]])
                        cnt += 1
            k.barrier()

    def phase_swa_gen(self, l, corun=False):
        k = self.k
        need_ctx = l < DEPTH - 1
        NT = TALL // 128
        with contextlib.ExitStack() as es:
            QT = k.sb(es, "sQT", [128, 2, TALL], BF16)
            KT = k.sb(es, "sKT", [128, TALL], BF16)
            Va = k.sb(es, "sVa", [128, NT, 2, 65], BF16)
            msk = k.sb(es, "smsk", [128, 2, 4, 128], BF16)
            sel = self.make_sel(es)
            sk = k.sb(es, "ssk", [1, 4], F32)
            skrow = k.sb(es, "sskrow", [1, 4, 128], BF16)
            e64 = k.sb(es, "se64", [1, 65], BF16)
            k.dma(msk[:], self.swa_mask, w=[msk], q="pool")
            k.dma(sk[:], self.swa_sink[l:l + 1, :], w=[sk])
            k.op("act", lambda e: e.activation(out=sk[:], in_=sk[:], func=AF.Exp), r=[sk], w=[sk])
            for h in range(4):
                k.op("dve", lambda e: e.tensor_scalar(out=skrow[:, h, :], in0=self.ones_bf[0:1, :], scalar1=sk[0:1, h:h + 1], scalar2=None,
                                                      op0=ALU.mult), r=[sk, self.ones_bf], w=[skrow])
            k.op("dve", lambda e: e.memset(e64[:, 0:64], 0.0), w=[e64])
            k.op("dve", lambda e: e.memset(e64[:, 64:65], 1.0), r=[e64], w=[e64])
            for c in range(2):
                k.dma(QT[:, c, :], self.s_sq[c * 128:(c + 1) * 128, :], w=[QT])
            k.dma(KT[:], self.s_sk, w=[KT])
            vsrc = self.s_sv.rearrange("(t p) f -> p t f", p=128)
            for t in range(0, NT, 6):
                k.dma(Va[:, t:t + 6].rearrange("p t h x -> p t (h x)"), vsrc[:, t:t + 6, :], w=[Va])
            PT = [k.sb(es, f"sPT{i}", [128, 4, 128], BF16) for i in range(3)]
            stb = [self.psb[6]] if corun else [self.psb[0], self.psb[1], self.psb[2]]
            otb = [self.psb[7]] if corun else [self.psb[4], self.psb[5]]
            Osb = [k.sb(es, f"sOsb{i}", [65, 512], F32) for i in range(2)]
            rec = [k.sb(es, f"srec{i}", [64, 512], F32) for i in range(2)]
            yo = [k.sb(es, f"syo{i}", [64, 512], BF16) for i in range(2)]
            cnt = 0
            yield "ready"
            for gt in range(NT):
                if gt < 2:
                    if not need_ctx:
                        continue
                    keys = [(0, None), (1, None)]
                else:
                    keys = []
                    if gt > 2:
                        keys.append((gt - 1, 0))
                    keys.append((gt, None))
                    if gt < NT - 1:
                        keys.append((gt + 1, 1))
                    keys += [(0, None), (1, None)]
                qs = slice(gt * 128, (gt + 1) * 128)
                OT = otb[cnt % len(otb)]
                nk = len(keys)

                assert not corun

                def st_mm(i):
                    STw = self.psw[i % 2]
                    kt = keys[i][0]
                    for g in range(2):
                        p0 = 64 * g
                        k.op("pe", lambda e: e.matmul(STw[:, g * 512:g * 512 + 256], lhsT=KT[p0:p0 + 64, kt * 128:(kt + 1) * 128], rhs=QT[p0:p0 + 64, :, qs],
                                                      start=True, stop=True), r=[KT, QT], w=[STw])
                st_mm(0)
                for i in range(nk):
                    if i + 1 < nk:
                        st_mm(i + 1)
                    ST = self.psw[i % 2]
                    P = PT[i % 3]
                    kt, mk = keys[i]
                    k.op("act", lambda e: e.activation(out=P[:].rearrange("p (g a) b -> p g (a b)", g=2),
                                                       in_=ST[:, :].rearrange("p (g x) -> p g x", g=2)[:, :, 0:256], func=AF.Exp, scale=SWA_SCALE),
                         r=[ST], w=[P])
                    if mk is not None:
                        k.op("dve", lambda e: e.tensor_tensor(out=P[:], in0=P[:], in1=msk[:, mk], op=ALU.mult), r=[P, msk], w=[P])
                    for g in range(2):
                        k.op("pe", lambda e: e.matmul(OT[0:65, g * 256:(g + 1) * 256], lhsT=Va[:, kt, g, :], rhs=P[:, 2 * g:2 * g + 2, :],
                                                      start=(i == 0 and g == 0), stop=False, skip_group_check=True), r=[Va, P], w=[OT])
                k.op("pe", lambda e: e.matmul(OT[0:65, 0:512], lhsT=e64[0:1, :], rhs=skrow[0:1, :, :], start=False, stop=True,
                                              skip_group_check=True), r=[e64, skrow], w=[OT])
                self.attn_finish(OT, 512, Osb[cnt % 2], rec[cnt % 2], yo[cnt % 2], sel,
                                 [self.ymix[256 + h * 64:256 + (h + 1) * 64, qs] for h in range(4)], nh=4,
                                 bc=(OT if corun else None))
                cnt += 1
                yield
            k.barrier()

    def setup_hglb(self):
        k = self.k
        es = k.es
        self.hglb = k.sb(es, "hglb", [64, 2, DEPTH, 4], F32)
        self.hgoml = k.sb(es, "hgoml", [64, 2, DEPTH, 4], F32)
        self.hgnoml = k.sb(es, "hgnoml", [64, 2, DEPTH, 4], F32)
        with contextlib.ExitStack() as es2:
            e_ = k.sb(es2, "lbe", [64, 2, DEPTH, 4], F32)
            s_ = k.sb(es2, "lbs", [64, 2, 4], F32)
            self.load_fm(es2, e_[:].rearrange("p d l h -> p (d l h)"), self.hg_lb.rearrange("d l (h x) -> (d l h) x", x=64), 32, [e_], wd=64)
            k.op("act", lambda e: e.activation(out=e_[:], in_=e_[:], func=AF.Exp), r=[e_], w=[e_])
            k.op("dve", lambda e: e.tensor_tensor(out=s_[:], in0=e_[:, :, 0, :], in1=e_[:, :, 1, :], op=ALU.add), r=[e_], w=[s_])
            for l in (2, 3):
                k.op("dve", lambda e: e.tensor_tensor(out=s_[:], in0=s_[:], in1=e_[:, :, l, :], op=ALU.add), r=[e_, s_], w=[s_])
            k.op("dve", lambda e: e.reciprocal(out=s_[:], in_=s_[:]), r=[s_], w=[s_])
            for l in range(DEPTH):
                k.op("dve", lambda e: e.tensor_tensor(out=e_[:, :, l, :], in0=e_[:, :, l, :], in1=s_[:], op=ALU.mult), r=[e_, s_], w=[e_])
            k.op("dve", lambda e: e.memset(self.hglb[:, :, 0, :], 0.0), w=[self.hglb])
            k.op("dve", lambda e: e.tensor_copy(out=self.hglb[:, :, 1, :], in_=e_[:, :, 1, :]), r=[e_, self.hglb], w=[self.hglb])
            for l in (2, 3):
                k.op("dve", lambda e: e.tensor_tensor(out=self.hglb[:, :, l, :], in0=self.hglb[:, :, l - 1, :], in1=e_[:, :, l, :], op=ALU.add),
                     r=[e_, self.hglb], w=[self.hglb])
            k.op("dve", lambda e: e.tensor_scalar(out=self.hgoml[:], in0=self.hglb[:], scalar1=-1.0, scalar2=1.0, op0=ALU.mult, op1=ALU.add),
                 r=[self.hglb], w=[self.hgoml])
            k.op("dve", lambda e: e.tensor_scalar(out=self.hgnoml[:], in0=self.hglb[:], scalar1=-1.0, scalar2=None, op0=ALU.add),
                 r=[self.hglb], w=[self.hgnoml])
            k.barrier()

    def phase_hg(self, l, filler=None):
        k = self.k
        with contextlib.ExitStack() as es:
            rmask = k.sb(es, "hrm", [64, 2048], F32)
            amask = k.sb(es, "ham", [128, 2, 4, 128], BF16)
            cmask = k.sb(es, "hcm", [128, 4], F32)
            ng = k.sb(es, "hng", [64, 1], F32)
            k.dma(rmask[:], self.hg_rmask, w=[rmask])
            k.dma(amask[:], self.hg_amask, w=[amask], q="pool")
            k.dma(cmask[:], self.hg_cmask, w=[cmask])
            self.load_fm(es, ng[:], self.hg_norm_g[l:l + 1, :], 1, [ng], wd=64)
            S = k.sb(es, "hS", [64, 4, 64], F32)
            St = k.sb(es, "hSt", [64, 4, 64], F32)
            Sbf = [k.sb(es, f"hSbf{j}", [64, 4, 64], BF16) for j in range(8)]
            names = ["z", "sg", "lf", "kk", "P", "eP", "eN"]
            bt = {nm: k.sb(es, "hb_" + nm, [64, 2048], F32) for nm in names}
            bq = k.sb(es, "hb_q", [64, 2048], BF16)
            bqd = k.sb(es, "hb_qd", [64, 2048], BF16)
            bki = k.sb(es, "hb_ki", [64, 2048], BF16)
            dec = k.sb(es, "hdec", [64, 4, 16], F32)
            vt = [k.sb(es, f"hvt{i}", [128, 256], BF16) for i in range(2)]
            kim = k.sb(es, "hkim", [128, 4, 256], BF16)
            attm = k.sb(es, "hattm", [128, 4, 128], BF16)
            obt = [k.sb(es, f"hobt{i}", [64, 4, 128], F32) for i in range(2)]
            gsl = [k.sb(es, f"hgsl{i}", [64, 4, 128], F32) for i in range(2)]
            osum = k.sb(es, "hosum", [64, 4, 128], F32)
            sqo = k.sb(es, "hsqo", [64, 512], BF16)
            rst = k.sb(es, "hrst", [64, 512], F32)
            yo = [k.sb(es, f"hyo{i}", [64, 4, 128], BF16) for i in range(2)]
            Mps = [self.psb[0], self.psb[1]]
            attps = self.psb[2]
            ops = self.psb[3]
            trp = self.psb[4]
            ssps = self.psb[5]
            trp_bf = trp[:].bitcast(BF16)
            ymix_v = self.ymix[512:768, :].rearrange("(h d) t -> d h t", d=64)
            sbi = [0]
            vti = [0]
            bqd2 = [bqd, k.sb(es, "hb_qd2", [64, 2048], BF16)]
            bki2 = [bki, k.sb(es, "hb_ki2", [64, 2048], BF16)]
            dec2 = [dec, k.sb(es, "hdec2", [64, 4, 16], F32)]
            Sx = [S, k.sb(es, "hS2", [64, 4, 64], F32)]
            Stx = [St, k.sb(es, "hSt2", [64, 4, 64], F32)]
            sidx = [0]

            def prep_groups(d, t0, n, bs):
                zsrc = (self.s_zf if d == 0 else self.s_zb).rearrange("(h d) t -> d h t", d=64)
                qsrc = self.s_hq.rearrange("(h d) t -> d h t", d=64)
                bqd_, bki_, dec_ = bqd2[bs], bki2[bs], dec2[bs]

                def V(t):
                    return t[:, 0:4 * n].rearrange("p (h t) -> p h t", h=4)
                f2 = lambda t: t[:, 0:4 * n]
                z, sg, lf, kk, P, eP, eN = [bt[nm] for nm in names]

                def g1():
                    k.dma(V(z), zsrc[:, :, t0:t0 + n], w=[z])
                    k.dma(V(bq), qsrc[:, :, t0:t0 + n], w=[bq])
                    k.op("act", lambda e: e.activation(out=f2(sg), in_=f2(z), func=AF.Sigmoid), r=[z], w=[sg])

                def g2():
                    for h in range(4):
                        k.op(HG_PREP_ENG, lambda e: e.tensor_scalar(out=V(lf)[:, h, :], in0=V(sg)[:, h, :], scalar1=self.hgoml[:, d, l, h:h + 1],
                                                              scalar2=self.hglb[:, d, l, h:h + 1], op0=ALU.mult, op1=ALU.add),
                             r=[sg, self.hgoml, self.hglb], w=[lf])
                        k.op("pool", lambda e: e.tensor_scalar(out=V(kk)[:, h, :], in0=V(sg)[:, h, :], scalar1=self.hgnoml[:, d, l, h:h + 1],
                                                               scalar2=self.hgoml[:, d, l, h:h + 1], op0=ALU.mult, op1=ALU.add),
                             r=[sg, self.hgoml, self.hgnoml], w=[kk])
                    k.op("act", lambda e: e.activation(out=f2(lf), in_=f2(lf), func=AF.Ln), r=[lf], w=[lf])

                def g3():
                    k.op("dve", lambda e: e.tensor_tensor_scan(out=f2(P), data0=f2(rmask), data1=f2(lf), initial=0.0, op0=ALU.mult, op1=ALU.add),
                         r=[rmask, lf], w=[P])
                    k.op("act", lambda e: e.activation(out=dec_[:, :, 0:n // 32], in_=V(P)[:, :, 31:n:32], func=AF.Exp), r=[P], w=[dec_])
                    if d == 1:
                        k.op("pool", lambda e: e.tensor_tensor(out=f2(P), in0=f2(P), in1=f2(lf), op=ALU.subtract), r=[P, lf], w=[P])

                def g4():
                    k.op("act", lambda e: e.activation(out=f2(eP), in_=f2(P), func=AF.Exp), r=[P], w=[eP])
                    k.op("act", lambda e: e.activation(out=f2(eN), in_=f2(P), func=AF.Exp, scale=-1.0), r=[P], w=[eN])
                    eq, ek = (eP, eN) if d == 0 else (eN, eP)
                    k.op(HG_PREP_ENG, lambda e: e.tensor_tensor(out=f2(bqd_), in0=f2(bq), in1=f2(eq), op=ALU.mult), r=[bq, eq], w=[bqd_])
                    k.op("pool", lambda e: e.tensor_tensor(out=f2(bki_), in0=f2(kk), in1=f2(ek), op=ALU.mult), r=[kk, ek], w=[bki_])
                return [g1, g2, g3, g4]

            for d in (1, 0):
                S = Sx[sidx[0] % 2]
                k.op("dve", lambda e: e.memset(S[:], 0.0), r=[S], w=[S])
                blocks = _blocks()
                if d == 1:
                    blocks = [blocks[0]] + blocks[:0:-1]
                for g_ in prep_groups(d, blocks[0][0], blocks[0][1], 0):
                    g_()
                for bix, (t0, n) in enumerate(blocks):
                    bs = bix % 2
                    pending = prep_groups(d, blocks[bix + 1][0], blocks[bix + 1][1], (bix + 1) % 2) if bix + 1 < len(blocks) else []

                    def V(t):
                        return t[:, 0:4 * n].rearrange("p (h t) -> p h t", h=4)
                    bqd, bki, dec = bqd2[bs], bki2[bs], dec2[bs]
                    qd, ki, decv = V(bqd), V(bki), dec
                    ntile = n // 128
                    for tix, ti in enumerate(range(ntile) if d == 0 else range(ntile - 1, -1, -1)):
                        if tix > 0:
                            for _ in range(4 // ntile if ntile < 4 else 1):
                                if pending:
                                    pending.pop(0)()
                        if filler is not None:
                            filler()
                        cols = slice(ti * 128, ti * 128 + 128)
                        gt0 = t0 + ti * 128
                        v = vt[vti[0] % 2]
                        ob_ = obt[vti[0] % 2]
                        gs_ = gsl[vti[0] % 2]
                        yo_ = yo[vti[0] % 2]
                        vti[0] += 1
                        k.dma(v[:], self.s_hi[gt0:gt0 + 128, :], w=[v])
                        if d == 0:
                            k.dma(ob_[:], self.s_ob[:, :, gt0:gt0 + 128], w=[ob_])
                            k.dma(gs_[:], self.s_hg.rearrange("(h d) t -> d h t", d=64)[:, :, gt0:gt0 + 128], w=[gs_])
                        for h in range(4):
                            k.op("pe", lambda e: e.transpose(out=trp_bf[:, h * 64:(h + 1) * 64], in_=ki[:, h, cols], identity=self.ident_bf[0:64, 0:64]),
                                 r=[bki, self.ident_bf], w=[trp])
                        for j in range(4):
                            k.op("dve" if j % 2 == 0 else "act", (lambda e: e.tensor_scalar(out=kim[:, j, :], in0=trp_bf[:, 0:256], scalar1=cmask[:, j:j + 1], scalar2=None, op0=ALU.mult))
                                 if j % 2 == 0 else (lambda e: e.activation(out=kim[:, j, :], in_=trp_bf[:, 0:256], func=AF.Copy, scale=cmask[:, j:j + 1])),
                                 r=[trp, cmask], w=[kim])
                        for j in range(4):
                            for h in range(4):
                                k.op("pe", lambda e: e.matmul(Mps[j // 2][0:64, (j % 2) * 256 + h * 64:(j % 2) * 256 + (h + 1) * 64],
                                                              lhsT=kim[:, j, h * 64:(h + 1) * 64], rhs=v[:, h * 64:(h + 1) * 64], start=True, stop=True),
                                     r=[kim, v], w=[Mps[j // 2]])
                        for h in range(4):
                            k.op("pe", lambda e: e.matmul(attps[:, h * 128:(h + 1) * 128], lhsT=ki[:, h, cols], rhs=qd[:, h, cols], start=True, stop=True),
                                 r=[bki, bqd], w=[attps])
                        k.op("dve", lambda e: e.tensor_tensor(out=attm[:].rearrange("p a b -> p (a b)"), in0=attps[:, 0:512],
                                                              in1=amask[:, d].rearrange("p a b -> p (a b)"), op=ALU.mult), r=[attps, amask], w=[attm])
                        for h in range(4):
                            k.op("pe", lambda e: e.matmul(ops[0:64, h * 128:(h + 1) * 128], lhsT=v[:, h * 64:(h + 1) * 64], rhs=attm[:, h, :],
                                                          start=(h == 0), stop=False, skip_group_check=True), r=[v, attm], w=[ops])
                        order = range(4) if d == 0 else range(3, -1, -1)
                        for ji, j in enumerate(order):
                            ce = ti * 4 + j
                            Mj = Mps[j // 2][0:64, (j % 2) * 256:(j % 2) * 256 + 256].rearrange("p (h x) -> p h x", h=4)
                            dec_bc = decv[:, :, ce:ce + 1].to_broadcast([64, 4, 64])
                            sb_ = Sbf[sbi[0] % 8]
                            sbi[0] += 1
                            S = Sx[sidx[0] % 2]
                            Sn = Sx[(sidx[0] + 1) % 2]
                            St = Stx[sidx[0] % 2]
                            sidx[0] += 1
                            if d == 0:
                                k.op(HG_SBF_ENG, (lambda e: e.copy(out=sb_[:], in_=S[:])) if HG_SBF_ENG == "act" else (lambda e: e.tensor_copy(out=sb_[:], in_=S[:])), r=[S], w=[sb_])
                                k.op("dve", lambda e: e.tensor_tensor(out=St[:], in0=S[:], in1=Mj, op=ALU.add), r=[S, Mps[j // 2]], w=[St])
                                k.op("dve", lambda e: e.tensor_tensor(out=Sn[:], in0=St[:], in1=dec_bc, op=ALU.mult), r=[St, dec], w=[Sn])
                            else:
                                k.op("dve", lambda e: e.tensor_tensor(out=St[:], in0=S[:], in1=dec_bc, op=ALU.mult), r=[S, dec], w=[St])
                                k.op(HG_SBF_ENG, (lambda e: e.copy(out=sb_[:], in_=St[:])) if HG_SBF_ENG == "act" else (lambda e: e.tensor_copy(out=sb_[:], in_=St[:])), r=[St], w=[sb_])
                                k.op("dve", lambda e: e.tensor_tensor(out=Sn[:], in0=St[:], in1=Mj, op=ALU.add), r=[St, Mps[j // 2]], w=[Sn])
                            for h in range(4):
                                last = (ji == 3 and h == 3)
                                k.op("pe", lambda e: e.matmul(ops[0:64, h * 128 + j * 32:h * 128 + (j + 1) * 32], lhsT=sb_[:, h, :],
                                                              rhs=qd[:, h, ti * 128 + j * 32:ti * 128 + (j + 1) * 32], start=False, stop=last,
                                                              skip_group_check=True), r=[sb_, bqd], w=[ops])
                        opsv = ops[0:64, 0:512].rearrange("p (h t) -> p h t", h=4)
                        if d == 1:
                            k.op("act", lambda e: e.copy(out=ob_[:], in_=opsv), r=[ops], w=[ob_])
                            k.dma(self.s_ob[:, :, gt0:gt0 + 128], ob_[:], r=[ob_])
                        else:
                            k.op("dve", lambda e: e.tensor_tensor(out=osum[:], in0=opsv, in1=ob_[:], op=ALU.add), r=[ops, ob_], w=[osum])
                            o2 = osum[:].rearrange("p h t -> p (h t)")
                            k.op("pool", lambda e: e.tensor_tensor(out=sqo[:], in0=o2, in1=o2, op=ALU.mult), r=[osum], w=[sqo])
                            k.op("pe", lambda e: e.matmul(ssps[0:64, 0:512], lhsT=self.ones_bf[0:64, 0:64], rhs=sqo[:], start=True, stop=True),
                                 r=[sqo, self.ones_bf], w=[ssps])
                            k.op("act", lambda e: e.activation(out=rst[:], in_=ssps[0:64, 0:512], func=AF.Sqrt, bias=self.epsb[0:64, :], scale=1.0 / 64),
                                 r=[ssps, self.epsb], w=[rst])
                            k.op("dve", lambda e: e.reciprocal(out=rst[:], in_=rst[:]), r=[rst], w=[rst])
                            k.op("dve", lambda e: e.scalar_tensor_tensor(out=o2, in0=o2, scalar=ng[:, 0:1], in1=rst[:], op0=ALU.mult, op1=ALU.mult),
                                 r=[osum, ng, rst], w=[osum])
                            k.op("pool", lambda e: e.tensor_tensor(out=yo_[:], in0=osum[:], in1=gs_[:], op=ALU.mult), r=[osum, gs_], w=[yo_])
                            k.dma(ymix_v[:, :, gt0:gt0 + 128], yo_[:], r=[yo_])
                    while pending:
                        pending.pop(0)()
                k.barrier()

    def phase_s5_gen(self, l):
        k = self.k
        need_ctx = l < DEPTH - 1
        NC_ = TALL // 8
        mul, add, sub = ALU.mult, ALU.add, ALU.subtract
        with contextlib.ExitStack() as es:
            Tm = k.sb(es, "5Tm", [128, 32, 128], BF16)
            Gt = k.sb(es, "5Gt", [128, 32, 2, 64], BF16)
            Er = k.sb(es, "5Er", [64, 32, 128], BF16)
            Ei = k.sb(es, "5Ei", [64, 32, 128], BF16)
            A8 = k.sb(es, "5A8", [64, 4, 32], F32)
            U = k.sb(es, "5U", [128, 16, NC_], BF16)
            tmask = k.sb(es, "5tmask", [128, 2, 128], F32)
            k.dma(tmask[:], self.s5_tmask, w=[tmask])
            with contextlib.ExitStack() as e2:
                def t32(nm):
                    return k.sb(e2, nm, [64, 32], F32)
                lre, lim, ldt = t32("lre"), t32("lim"), t32("ldt")
                for dst, src in ((lre, self.s5_lam_re), (lim, self.s5_lam_im), (ldt, self.s5_log_dt)):
                    self.load_fm(e2, dst[:], src[l].rearrange("d g p -> (d g) p"), 32, [dst], wd=64)
                Bre = k.sb(e2, "Bre", [64, 32, 16], F32)
                Bim = k.sb(e2, "Bim", [64, 32, 16], F32)
                Cre = k.sb(e2, "Cre", [64, 32, 16], F32)
                Cim = k.sb(e2, "Cim", [64, 32, 16], F32)
                for dst, src in ((Bre, self.s5_b_re), (Bim, self.s5_b_im)):
                    for d in range(2):
                        k.dma(dst[:, d * 16:(d + 1) * 16, :], src[l, d].rearrange("g p h -> p g h"), w=[dst], allow_slow_non_contiguous=True)
                for dst, src in ((Cre, self.s5_c_re), (Cim, self.s5_c_im)):
                    for q4 in range(4):
                        self.load_fm(e2, dst[:].rearrange("p a h -> p (a h)")[:, q4 * 128:(q4 + 1) * 128],
                                     src[l].rearrange("d g h p -> (d g h) p")[q4 * 128:(q4 + 1) * 128, :], 128, [dst], wd=64)
                dt_, mag, th, c16, s16 = t32("dt"), t32("mag"), t32("th"), t32("c16"), t32("s16")
                t1, t2 = t32("t1"), t32("t2")
                k.op("act", lambda e: e.activation(out=dt_[:], in_=ldt[:], func=AF.Exp), r=[ldt], w=[dt_])
                k.op("dve", lambda e: e.tensor_tensor(out=mag[:], in0=lre[:], in1=dt_[:], op=mul), r=[lre, dt_], w=[mag])
                k.op("dve", lambda e: e.tensor_tensor(out=th[:], in0=lim[:], in1=dt_[:], op=mul), r=[lim, dt_], w=[th])
                k.op("act", lambda e: e.activation(out=mag[:], in_=mag[:], func=AF.Exp, scale=1.0 / 16), r=[mag], w=[mag])
                halfpi = k.sb(e2, "halfpi", [64, 1], F32)
                k.op("dve", lambda e: e.memset(halfpi[:], math.pi / 2), w=[halfpi])
                k.op("act", lambda e: e.activation(out=s16[:], in_=th[:], func=AF.Sin, scale=1.0 / 16), r=[th], w=[s16])
                k.op("act", lambda e: e.activation(out=c16[:], in_=th[:], func=AF.Sin, scale=1.0 / 16, bias=halfpi[:]), r=[th, halfpi], w=[c16])
                are, aim = t32("are"), t32("aim")
                k.op("dve", lambda e: e.tensor_tensor(out=are[:], in0=mag[:], in1=c16[:], op=mul), r=[mag, c16], w=[are])
                k.op("dve", lambda e: e.tensor_tensor(out=aim[:], in0=mag[:], in1=s16[:], op=mul), r=[mag, s16], w=[aim])

                def cmul(ore, oim, xr, xi, yr, yi, rk, wk, tA, tB):
                    k.op("dve", lambda e: e.tensor_tensor(out=tA, in0=xr, in1=yr, op=mul), r=rk, w=[wk[2]])
                    k.op("dve", lambda e: e.tensor_tensor(out=tB, in0=xi, in1=yi, op=mul), r=rk, w=[wk[3]])
                    k.op("dve", lambda e: e.tensor_tensor(out=tB, in0=tA, in1=tB, op=sub), r=[wk[2], wk[3]], w=[wk[3]])
                    k.op("dve", lambda e: e.tensor_tensor(out=tA, in0=xr, in1=yi, op=mul), r=rk, w=[wk[2]])
                    k.op("dve", lambda e: e.tensor_tensor(out=oim, in0=xi, in1=yr, op=mul), r=rk, w=[wk[1]])
                    k.op("dve", lambda e: e.tensor_tensor(out=oim, in0=oim, in1=tA, op=add), r=[wk[1], wk[2]], w=[wk[1]])
                    k.op("dve", lambda e: e.tensor_copy(out=ore, in_=tB), r=[wk[3]], w=[wk[0]])

                for _ in range(4):
                    cmul(are[:], aim[:], are[:], aim[:], are[:], aim[:], [are, aim], [are, aim, t1, t2], t1[:], t2[:])
                cfr, cfi, den = t32("cfr"), t32("cfi"), t32("den")
                k.op("dve", lambda e: e.tensor_tensor(out=den[:], in0=lre[:], in1=lre[:], op=mul), r=[lre], w=[den])
                k.op("dve", lambda e: e.tensor_tensor(out=t1[:], in0=lim[:], in1=lim[:], op=mul), r=[lim], w=[t1])
                k.op("dve", lambda e: e.tensor_tensor(out=den[:], in0=den[:], in1=t1[:], op=add), r=[den, t1], w=[den])
                k.op("dve", lambda e: e.reciprocal(out=den[:], in_=den[:]), r=[den], w=[den])
                am1 = t32("am1")
                nlim = t32("nlim")
                k.op("dve", lambda e: e.tensor_scalar(out=am1[:], in0=are[:], scalar1=-1.0, scalar2=None, op0=add), r=[are], w=[am1])
                k.op("dve", lambda e: e.tensor_scalar(out=nlim[:], in0=lim[:], scalar1=-1.0, scalar2=None, op0=mul), r=[lim], w=[nlim])
                cmul(cfr[:], cfi[:], am1[:], aim[:], lre[:], nlim[:], [am1, aim, lre, nlim], [cfr, cfi, t1, t2], t1[:], t2[:])
                k.op("dve", lambda e: e.tensor_tensor(out=cfr[:], in0=cfr[:], in1=den[:], op=mul), r=[cfr, den], w=[cfr])
                k.op("dve", lambda e: e.tensor_tensor(out=cfi[:], in0=cfi[:], in1=den[:], op=mul), r=[cfi, den], w=[cfi])
                ire, iim = t32("ire"), t32("iim")
                k.op("dve", lambda e: e.tensor_tensor(out=den[:], in0=are[:], in1=are[:], op=mul), r=[are], w=[den])
                k.op("dve", lambda e: e.tensor_tensor(out=t1[:], in0=aim[:], in1=aim[:], op=mul), r=[aim], w=[t1])
                k.op("dve", lambda e: e.tensor_tensor(out=den[:], in0=den[:], in1=t1[:], op=add), r=[den, t1], w=[den])
                k.op("dve", lambda e: e.reciprocal(out=den[:], in_=den[:]), r=[den], w=[den])
                k.op("dve", lambda e: e.tensor_tensor(out=ire[:], in0=are[:], in1=den[:], op=mul), r=[are, den], w=[ire])
                k.op("dve", lambda e: e.scalar_tensor_tensor(out=iim[:], in0=aim[:], scalar=-1.0, in1=den[:], op0=mul, op1=mul), r=[aim, den], w=[iim])
                def t512(nm):
                    return k.sb(e2, nm, [64, 32, 16], F32)
                Bbr, Bbi, u1, u2, xr, xi = t512("Bbr"), t512("Bbi"), t512("u1"), t512("u2"), t512("xr"), t512("xi")
                bc = lambda t: t[:].unsqueeze(2).to_broadcast([64, 32, 16])
                cmul(Bbr[:], Bbi[:], bc(cfr), bc(cfi), Bre[:], Bim[:], [cfr, cfi, Bre, Bim], [Bbr, Bbi, u1, u2], u1[:], u2[:])
                Lr = k.sb(e2, "Lr", [64, 32, 8, 16], F32)
                Li = k.sb(e2, "Li", [64, 32, 8, 16], F32)
                Rr = k.sb(e2, "Rr", [64, 32, 8, 16], F32)
                Ri = k.sb(e2, "Ri", [64, 32, 8, 16], F32)
                Gr = k.sb(e2, "Gr", [64, 32, 8, 16], F32)
                Gi = k.sb(e2, "Gi", [64, 32, 8, 16], F32)
                Erv = Er[:].rearrange("p a (t h) -> p a t h", h=16)
                Eiv = Ei[:].rearrange("p a (t h) -> p a t h", h=16)
                pr, pi_, qr, qi = t32("pr"), t32("pi"), t32("qr"), t32("qi")
                k.op("dve", lambda e: e.memset(pr[:], 1.0), w=[pr])
                k.op("dve", lambda e: e.memset(pi_[:], 0.0), w=[pi_])
                k.op("dve", lambda e: e.memset(qr[:], 1.0), w=[qr])
                k.op("dve", lambda e: e.memset(qi[:], 0.0), w=[qi])
                F_, Bk = slice(0, 16), slice(16, 32)
                for kk_ in range(9):
                    if kk_ > 0:
                        cmul(pr[:], pi_[:], pr[:], pi_[:], are[:], aim[:], [pr, pi_, are, aim], [pr, pi_, t1, t2], t1[:], t2[:])
                    if kk_ <= 7:
                        cmul(xr[:], xi[:], bc(pr), bc(pi_), Bbr[:], Bbi[:], [pr, pi_, Bbr, Bbi], [xr, xi, u1, u2], u1[:], u2[:])
                        for (dst, src) in ((Gr, xr), (Gi, xi)):
                            k.op("pool", lambda e: e.tensor_copy(out=dst[:, F_, 7 - kk_, :], in_=src[:, F_, :]), r=[src, dst], w=[dst])
                            k.op("pool", lambda e: e.tensor_copy(out=dst[:, Bk, kk_, :], in_=src[:, Bk, :]), r=[src, dst], w=[dst])
                    cmul(xr[:], xi[:], bc(pr), bc(pi_), Cre[:], Cim[:], [pr, pi_, Cre, Cim], [xr, xi, u1, u2], u1[:], u2[:])
                    if kk_ <= 7:
                        k.op("pool", lambda e: e.tensor_copy(out=Rr[:, F_, kk_, :], in_=xr[:, F_, :]), r=[xr, Rr], w=[Rr])
                        k.op("pool", lambda e: e.tensor_copy(out=Rr[:, Bk, 7 - kk_, :], in_=xr[:, Bk, :]), r=[xr, Rr], w=[Rr])
                        k.op("pool", lambda e: e.tensor_scalar(out=Ri[:, F_, kk_, :], in0=xi[:, F_, :], scalar1=-1.0, scalar2=None, op0=mul), r=[xi, Ri], w=[Ri])
                        k.op("pool", lambda e: e.tensor_scalar(out=Ri[:, Bk, 7 - kk_, :], in0=xi[:, Bk, :], scalar1=-1.0, scalar2=None, op0=mul), r=[xi, Ri], w=[Ri])
                    if kk_ >= 1:
                        k.op("act", lambda e: e.copy(out=Erv[:, F_, kk_ - 1, :], in_=xr[:, F_, :]), r=[xr, Er], w=[Er])
                        k.op("act", lambda e: e.copy(out=Erv[:, Bk, 8 - kk_, :], in_=xr[:, Bk, :]), r=[xr, Er], w=[Er])
                        k.op("act", lambda e: e.activation(out=Eiv[:, F_, kk_ - 1, :], in_=xi[:, F_, :], func=AF.Copy, scale=-1.0), r=[xi, Ei], w=[Ei])
                        k.op("act", lambda e: e.activation(out=Eiv[:, Bk, 8 - kk_, :], in_=xi[:, Bk, :], func=AF.Copy, scale=-1.0), r=[xi, Ei], w=[Ei])
                    if kk_ == 8:
                        k.op("dve", lambda e: e.tensor_copy(out=A8[:, 0, :], in_=pr[:]), r=[pr], w=[A8])
                        k.op("dve", lambda e: e.tensor_copy(out=A8[:, 1, :], in_=pr[:]), r=[pr, A8], w=[A8])
                        k.op("dve", lambda e: e.tensor_scalar(out=A8[:, 2, :], in0=pi_[:], scalar1=-1.0, scalar2=None, op0=mul), r=[pi_, A8], w=[A8])
                        k.op("dve", lambda e: e.tensor_copy(out=A8[:, 3, :], in_=pi_[:]), r=[pi_, A8], w=[A8])
                    if kk_ <= 7:
                        if kk_ > 0:
                            cmul(qr[:], qi[:], qr[:], qi[:], ire[:], iim[:], [qr, qi, ire, iim], [qr, qi, t1, t2], t1[:], t2[:])
                        cmul(xr[:], xi[:], bc(qr), bc(qi), Bbr[:], Bbi[:], [qr, qi, Bbr, Bbi], [xr, xi, u1, u2], u1[:], u2[:])
                        for (dst, src) in ((Lr, xr), (Li, xi)):
                            k.op("pool", lambda e: e.tensor_copy(out=dst[:, F_, kk_, :], in_=src[:, F_, :]), r=[src, dst], w=[dst])
                            k.op("pool", lambda e: e.tensor_copy(out=dst[:, Bk, 7 - kk_, :], in_=src[:, Bk, :]), r=[src, dst], w=[dst])
                fl = lambda t: t[:].rearrange("p a j h -> p a (j h)")
                for dg in range(32):
                    d = dg // 16
                    ps = self.psb[dg % 2]
                    k.op("pe", lambda e: e.matmul(ps[:, 0:128], lhsT=fl(Lr)[:, dg, :], rhs=fl(Rr)[:, dg, :], start=True, stop=False), r=[Lr, Rr], w=[ps])
                    k.op("pe", lambda e: e.matmul(ps[:, 0:128], lhsT=fl(Li)[:, dg, :], rhs=fl(Ri)[:, dg, :], start=False, stop=True), r=[Li, Ri], w=[ps])
                    k.op("dve", lambda e: e.tensor_tensor(out=Tm[:, dg, :], in0=ps[:, 0:128], in1=tmask[:, d, :], op=mul), r=[ps, tmask], w=[Tm])
                    ps2 = self.psb[2 + dg % 2]
                    k.op("pe", lambda e: e.transpose(out=ps2[:, 0:64], in_=fl(Gr)[:, dg, :], identity=self.ident_f[0:64, 0:64]), r=[Gr, self.ident_f], w=[ps2])
                    k.op("pe", lambda e: e.transpose(out=ps2[:, 64:128], in_=fl(Gi)[:, dg, :], identity=self.ident_f[0:64, 0:64]), r=[Gi, self.ident_f], w=[ps2])
                    k.op("act", lambda e: e.copy(out=Gt[:, dg].rearrange("p a b -> p (a b)"), in_=ps2[:, 0:128]), r=[ps2], w=[Gt])
                k.barrier()
            CBM = 64
            eu = contextlib.ExitStack()
            utok = k.sb(eu, "5utok", [128, 8, 256], F32)
            ub = k.sb(eu, "5ub", [128, 8, 256], BF16)
            ublocks = [(0, 32)] + [(32 + 128 * j, 128) for j in range(8)]
            trp = self.psb[4]
            trp_bf = trp[:].bitcast(BF16)
            usrc = self.s_u.rearrange("(c t) f -> c t f", t=8)
            for (c0, cb) in ublocks:
                k.dma(utok[0:cb], usrc[c0:c0 + cb], w=[utok])
                k.op("dve", lambda e: e.tensor_copy(out=ub[0:cb].rearrange("c a b -> c (a b)").rearrange("c (g t h) -> c g t h", g=16, t=8),
                                                    in_=utok[0:cb].rearrange("c t (g h) -> c g t h", g=16)), r=[utok], w=[ub])
                for g8 in range(2):
                    for gi in range(8):
                        g = g8 * 8 + gi
                        k.op("pe", lambda e: e.transpose(out=trp_bf[:, gi * 128:gi * 128 + cb], in_=ub[0:cb].rearrange("c a b -> c (a b)")[:, g * 128:(g + 1) * 128],
                                                         identity=self.ident_bf[0:cb, 0:cb]), r=[ub, self.ident_bf], w=[trp])
                    k.op("dve", lambda e: e.tensor_copy(out=U[:, g8 * 8:(g8 + 1) * 8, c0:c0 + cb],
                                                        in_=trp_bf[:, 0:1024].rearrange("p (g c) -> p g c", g=8)[:, :, 0:cb]), r=[trp], w=[U])
            k.barrier()
            eu.close()
            em = contextlib.ExitStack()
            Wt = k.sb(em, "5W", [64, 2, 32, CBM], F32)
            SP = [k.sb(em, f"5SP{d}", [64, 2, 16, CBM], BF16) for d in range(2)]
            Hist = [k.sb(em, f"5H{d}", [64, CBM + 1, 3, 16], F32) for d in range(2)]
            Tt = [k.sb(em, f"5T{d}", [64, 2, 16], F32) for d in range(2)]
            Vt = [k.sb(em, f"5V{d}", [64, 2, 16], F32) for d in range(2)]
            yev = [k.sb(em, f"5yev{i}", [128, 512], F32) for i in range(2)]
            blocks = [(0, 32)] + [(32 + CBM * j, CBM) for j in range((NC_ - 32) // CBM)]
            border = [blocks, [blocks[0]] + blocks[:0:-1]]
            A8v = [[A8[:, 0:2, d * 16:(d + 1) * 16], A8[:, 2:4, d * 16:(d + 1) * 16]] for d in range(2)]
            engs = ("dve", "pool")
            psS = self.psb[7]
            nbs = len(blocks)
            self.s5_nitems = sum(16 + 8 + b_[1] + 1 for b_ in blocks)
            yield "ready"

            def w_group(bs, d, g4, ri):
                cb = border[0][bs][1]
                cds = [border[0][bs][0], border[1][bs][0]]
                for gi in range(4):
                    g = g4 * 4 + gi
                    k.op("pe", lambda e: e.matmul(psS[0:64, gi * 128:gi * 128 + cb], lhsT=Gt[:, d * 16 + g, ri, :], rhs=U[:, g, cds[d]:cds[d] + cb],
                                                  start=True, stop=True), r=[Gt, U], w=[psS])
                k.op("dve", lambda e: e.tensor_copy(out=Wt[:, ri, d * 16 + g4 * 4:d * 16 + g4 * 4 + 4, 0:cb],
                                                    in_=psS[0:64, 0:512].rearrange("p (g c) -> p g c", g=4)[:, :, 0:cb]), r=[psS], w=[Wt])

            def y_group(bs, d, g4):
                cb = border[0][bs][1]
                cds = [border[0][bs][0], border[1][bs][0]]
                for gi in range(4):
                    g = g4 * 4 + gi
                    o = psS[:, gi * 128:gi * 128 + cb]
                    k.op("pe", lambda e: e.matmul(o, lhsT=Tm[:, d * 16 + g, :], rhs=U[:, g, cds[d]:cds[d] + cb], start=(gi == 0), stop=False,
                                                  skip_group_check=True), r=[Tm, U], w=[psS])
                    k.op("pe", lambda e: e.matmul(o, lhsT=Er[:, d * 16 + g, :], rhs=SP[d][:, 0, g, 0:cb], start=False, stop=False,
                                                  skip_group_check=True), r=[Er, SP[d]], w=[psS])
                    k.op("pe", lambda e: e.matmul(o, lhsT=Ei[:, d * 16 + g, :], rhs=SP[d][:, 1, g, 0:cb], start=False, stop=True,
                                                  skip_group_check=True), r=[Ei, SP[d]], w=[psS])
                ye = yev[g4 % 2]
                k.op("dve", lambda e: e.tensor_copy(out=ye[:], in_=psS[:, 0:512]), r=[psS], w=[ye])
                k.dma(self.s_y[d][:, g4 * 4:(g4 + 1) * 4, cds[d]:cds[d] + cb], ye[:].rearrange("p (g c) -> p g c", g=4)[:, :, 0:cb], r=[ye])

            prev_cb = None
            for bs in range(nbs):
                cb = border[0][bs][1]
                for d in range(2):
                    dst = Hist[d][:, 0] if d == 0 else Hist[d][:, cb]
                    if bs == 0:
                        k.op(engs[d], lambda e: e.memset(dst, 0.0), r=[Hist[d]], w=[Hist[d]])
                    else:
                        src = Hist[d][:, prev_cb] if d == 0 else Hist[d][:, 0]
                        k.op(engs[d], lambda e: e.tensor_copy(out=dst, in_=src), r=[Hist[d]], w=[Hist[d]])
                for d in range(2):
                    for g4 in range(4):
                        for ri in range(2):
                            w_group(bs, d, g4, ri)
                            yield
                if bs > 0:
                    for d in range(2):
                        for g4 in range(4):
                            y_group(bs - 1, d, g4)
                            yield
                prev_cb = cb
                for i in range(cb):
                    for d, eng in ((0, "dve"), (1, "pool")):
                        H, T_, V_ = Hist[d], Tt[d], Vt[d]
                        if d == 0:
                            col, pv, cu = i, i, i + 1
                        else:
                            col, pv, cu = cb - 1 - i, cb - i, cb - 1 - i
                        k.op(eng, lambda e: e.tensor_tensor(out=T_[:], in0=H[:, pv, 0:2, :], in1=A8v[d][0], op=mul), r=[H, A8], w=[T_])
                        k.op(eng, lambda e: e.tensor_tensor(out=V_[:], in0=H[:, pv, 1:3, :], in1=A8v[d][1], op=mul), r=[H, A8], w=[V_])
                        k.op(eng, lambda e: e.tensor_tensor(out=T_[:], in0=T_[:], in1=V_[:], op=add), r=[T_, V_], w=[T_])
                        k.op(eng, lambda e: e.tensor_tensor(out=H[:, cu, 0:2, :], in0=T_[:], in1=Wt[:, :, d * 16:(d + 1) * 16, col], op=add),
                             r=[T_, Wt, H], w=[H])
                        k.op(eng, lambda e: e.tensor_copy(out=H[:, cu, 2, :], in_=H[:, cu, 0, :]), r=[H], w=[H])
                    yield
                for d in range(2):
                    lo = 0 if d == 0 else 1
                    k.op("pool", lambda e: e.tensor_copy(out=SP[d][:, :, :, 0:cb].rearrange("p r g c -> p c r g"), in_=Hist[d][:, lo:lo + cb, 0:2, :]),
                         r=[Hist[d]], w=[SP[d]])
                yield
            for d in range(2):
                for g4 in range(4):
                    y_group(nbs - 1, d, g4)
                    yield
            k.barrier()
            em.close()
            with contextlib.ExitStack() as e3:
                utok = k.sb(e3, "5utok2", [128, 8, 256], F32)
                D8 = k.sb(e3, "5D8", [128, 8, 256], F32)
                wg = k.sb(e3, "5wg", [128, 2, 256], BF16)
                bg = k.sb(e3, "5bg", [128, 2], F32)
                for t in range(8):
                    k.dma(D8[:, t, :], self.s5_d[l:l + 1, :].partition_broadcast(128) if False else self.s5_d[l].partition_broadcast(128), w=[D8])
                k.dma(wg[:], self.s5_w_glu[l].rearrange("(c p) n -> p c n", p=128), w=[wg], q="pool")
                self.load_fm(e3, bg[:], self.s5_b_glu[l].rearrange("(c p) -> c p", p=128), 2, [bg])
                yf = k.sb(e3, "5yf", [128, 16, 128], F32)
                yb = k.sb(e3, "5yb", [128, 16, 128], F32)
                ytok = k.sb(e3, "5ytok", [128, 8, 256], F32)
                ygel = k.sb(e3, "5ygel", [128, 8, 256], BF16)
                yT = k.sb(e3, "5yT", [128, 2, 1024], BF16)
                sgm = [k.sb(e3, f"5sg{i}", [128, 512], F32) for i in range(2)]
                yao = [k.sb(e3, f"5ya{i}", [128, 512], BF16) for i in range(2)]
                for (c0, cb) in blocks:
                    if c0 == 0 and not need_ctx:
                        continue
                    ntok = cb * 8
                    k.dma(utok[0:cb], usrc[c0:c0 + cb], w=[utok])
                    k.dma(yf[:, :, 0:cb], self.s_y[0][:, :, c0:c0 + cb], w=[yf])
                    k.dma(yb[:, :, 0:cb], self.s_y[1][:, :, c0:c0 + cb], w=[yb])
                    k.op("pool", lambda e: e.tensor_tensor(out=yf[:, :, 0:cb], in0=yf[:, :, 0:cb], in1=yb[:, :, 0:cb], op=add), r=[yf, yb], w=[yf])
                    k.op("dve", lambda e: e.tensor_tensor(out=ytok[0:cb], in0=utok[0:cb], in1=D8[0:cb], op=mul), r=[utok, D8], w=[ytok])
                    for g4 in range(4):
                        ps = self.psb[g4 % 2]
                        for gi in range(4):
                            g = g4 * 4 + gi
                            k.op("pe", lambda e: e.transpose(out=ps[0:cb, gi * 128:(gi + 1) * 128], in_=yf[:, g, 0:cb], identity=self.ident_f[:, :]),
                                 r=[yf, self.ident_f], w=[ps])
                        yv = ytok[0:cb, :, g4 * 64:(g4 + 1) * 64].rearrange("c t (g h) -> c g t h", g=4)
                        pv = ps[0:cb, 0:512].rearrange("c (g t h) -> c g t h", g=4, t=8)
                        k.op("dve", lambda e: e.tensor_tensor(out=yv, in0=pv, in1=yv, op=add), r=[ps, ytok], w=[ytok])
                    k.op("act", lambda e: e.activation(out=ygel[0:cb], in_=ytok[0:cb], func=AF.Gelu), r=[ytok], w=[ygel])
                    for kc in range(2):
                        for t in range(8):
                            k.op("pe", lambda e: e.transpose(out=trp_bf[:, t * 128:t * 128 + cb], in_=ygel[0:cb, t, kc * 128:(kc + 1) * 128],
                                                             identity=self.ident_bf[0:cb, 0:cb]), r=[ygel, self.ident_bf], w=[trp])
                        k.op("dve", lambda e: e.tensor_copy(out=yT[:, kc, 0:ntok].rearrange("p (c t) -> p t c", t=8),
                                                            in_=trp_bf[:, 0:1024].rearrange("p (t c) -> p t c", t=8)[:, :, 0:cb]), r=[trp], w=[yT])
                    for r0 in range(0, ntok, 512):
                        n = min(512, ntok - r0)
                        for oc in range(2):
                            ps = self.psb[2 + oc]
                            for kc in range(2):
                                k.op("pe", lambda e: e.matmul(ps[:, 0:n], lhsT=wg[:, kc, oc * 128:(oc + 1) * 128], rhs=yT[:, kc, r0:r0 + n],
                                                              start=(kc == 0), stop=(kc == 1)), r=[wg, yT], w=[ps])
                            k.op("act", lambda e: e.activation(out=sgm[oc][:, 0:n], in_=ps[:, 0:n], func=AF.Sigmoid, bias=bg[:, oc:oc + 1]),
                                 r=[ps, bg], w=[sgm[oc]])
                            k.op("dve", lambda e: e.tensor_tensor(out=yao[oc][:, 0:n], in0=yT[:, oc, r0:r0 + n], in1=sgm[oc][:, 0:n], op=mul),
                                 r=[yT, sgm[oc]], w=[yao[oc]])
                            tok0 = c0 * 8 + r0
                            k.dma(self.ymix[oc * 128:(oc + 1) * 128, tok0:tok0 + n], yao[oc][:, 0:n], r=[yao[oc]])
                k.barrier()

def _prep_inputs(inputs):
    f = lambda a: np.ascontiguousarray(np.asarray(a, dtype=np.float32))
    cols = _win_cols()
    qcols = _wqb_cols()
    rs, rm = _rope_tables()
    shared = {
        "w_mod": f(inputs["w_mod"]), "b_mod": f(inputs["b_mod"]), "norm1_g": f(inputs["norm1_g"]), "norm2_g": f(inputs["norm2_g"]),
        "w_in": f(np.asarray(inputs["w_in"])[:, :, cols]), "w_out": f(inputs["w_out"]),
        "w_qb": f(np.asarray(inputs["mla_w_qb"])[:, :, qcols]), "w_kvb": f(inputs["mla_w_kvb"]),
        "qn_g": f(inputs["mla_q_norm_g"]), "kvn_g": f(inputs["mla_kv_norm_g"]),
        "w_up": f(inputs["ffn_w_up"]), "w_down": f(inputs["ffn_w_down"]), "final_g": f(inputs["final_norm_g"]),
        "rope_s": rs, "rope_m": rm, "ident": np.eye(128, dtype=np.float32),
        "swa_mask": _swa_mask(), "swa_sink": f(inputs["swa_sink"]),
        "hg_lb": f(inputs["hg_lb"]), "s5_tmask": _s5_tmask(),
        **{kk: f(inputs[kk]) for kk in ("s5_lam_re", "s5_lam_im", "s5_log_dt", "s5_b_re", "s5_b_im", "s5_c_re", "s5_c_im", "s5_d", "s5_w_glu", "s5_b_glu")}, "hg_norm_g": f(inputs["hg_norm_g"]), **_hg_consts(),
    }
    x = np.asarray(inputs["x"]); ctx = np.asarray(inputs["ctx"]); c = np.asarray(inputs["c"]); c_ctx = np.asarray(inputs["c_ctx"])
    maps = []
    for b in range(NCORES):
        m = dict(shared)
        m["xin"] = f(np.concatenate([ctx[b], x[b]], 0).T)
        m["cc"] = f(np.stack([c[b], c_ctx], 0))
        maps.append(m)
    return maps


def kernel(**inputs):
    bld = Builder()
    maps = _prep_inputs(inputs)
    res = run_bass_kernel_spmd(bld.nc, maps, core_ids=list(range(NCORES)))
    out = np.stack([np.ascontiguousarray(r["out"].T) for r in res.results], 0)
    return out.astype(np.float32)
```

```python
import contextlib
import math
import os
import numpy as np
import concourse.bass as bass
import concourse.mybir as mybir
from concourse.bass_utils import run_bass_kernel_spmd

F32 = mybir.dt.float32
BF16 = mybir.dt.bfloat16
ALU = mybir.AluOpType
AF = mybir.ActivationFunctionType

D = 1024
SEQ = 8192
CTX = 256
TALL = SEQ + CTX
DEPTH = 4
NCORES = 4
FFH = 2816
EPS = 1e-6
GRID_W = 64
MLA_SCALE = 96 ** -0.5
SWA_SCALE = 64 ** -0.5
SAME_ENGINE_SYNC = True
HG_PREP_ENG = os.environ.get("KHGP", "dve")
HG_SBF_ENG = os.environ.get("KHGS", "act")
OVERLAP_SWA = os.environ.get("KOVS", "0") == "1"
OVERLAP_S5 = os.environ.get("KOVL", "1") == "1"
ATTACH_WAIT = os.environ.get("KATTACH", "1") == "1"


class T:
    _n = 0

    def __init__(self, t, name, psum=False):
        self.t = t
        self.psum = psum
        T._n += 1
        self.key = (name, T._n)

    def __getitem__(self, idx):
        return self.t[idx]


class PV(T):
    def __init__(self, base, off, name):
        T.__init__(self, base.t, name, psum=True)
        self.off = off

    def _c(self, c):
        if isinstance(c, slice):
            a = self.off + (c.start or 0)
            b = self.off + (512 if c.stop is None else c.stop)
            return slice(a, b, c.step)
        return self.off + c

    def __getitem__(self, idx):
        if isinstance(idx, tuple):
            return self.t[(idx[0], self._c(idx[1])) + tuple(idx[2:])]
        return self.t[idx, self.off:self.off + 512]


class KB:
    def __init__(self, nc):
        self.nc = nc
        self.es = contextlib.ExitStack()
        self.eng = {"pe": nc.tensor, "act": nc.scalar, "dve": nc.vector, "pool": nc.gpsimd, "sp": nc.sync}
        self.sem = {e: self.es.enter_context(nc.semaphore("s_" + e)) for e in self.eng}
        self.cnt = {e: 0 for e in self.eng}
        self.lanes = {}
        self.lane_val = {}
        self.lane_rr = {}
        for q, n in (("sp", int(os.environ.get("KLANES", "12"))), ("pool", 8), ("act", 4)):
            self.lanes[q] = [self.es.enter_context(nc.semaphore(f"l_{q}{i}")) for i in range(n)]
            self.lane_rr[q] = 0
            for i in range(n):
                self.lane_val[(q, i)] = 0
        self.seen = {e: {} for e in self.eng}
        self.res = {}
        self.ninst = 0
        self.nwait = 0
        self.uid = 0

    def sb(self, es, name, shape, dtype):
        self.uid += 1
        t = es.enter_context(self.nc.sbuf_tensor(f"{name}_{self.uid}", list(shape), dtype))
        return T(t, name)

    def ps(self, es, name, shape, dtype=F32):
        self.uid += 1
        t = es.enter_context(self.nc.psum_tensor(f"{name}_{self.uid}", list(shape), dtype))
        return T(t, name, psum=True)

    def _semof(self, src):
        if src[0] == "eng":
            return self.sem[src[1]]
        return self.lanes[src[1]][src[2]]

    def _wait(self, engine, dep):
        src, val = dep
        if val <= 0:
            return
        if src[0] == "eng" and src[1] == engine:
            if engine == "pe" or not SAME_ENGINE_SYNC:
                return
        if self.seen[engine].get(src, 0) >= val:
            return
        self.eng[engine].wait_ge(self._semof(src), val)
        self.nwait += 1
        self.seen[engine][src] = val

    def _deps(self, r, w, me=None):
        deps = []
        for t in r:
            st = self.res.get(t.key)
            if st and st["w"]:
                deps.append(st["w"])
            if st and t.psum:
                deps.extend((src, v) for src, v in st["r"].items() if src != me)
        for t in w:
            st = self.res.get(t.key)
            if st:
                if st["w"]:
                    deps.append(st["w"])
                deps.extend(st["r"].items())
        return deps

    def _update(self, r, w, src, val):
        for t in r:
            st = self.res.setdefault(t.key, {"w": None, "r": {}})
            st["r"][src] = val
        for t in w:
            self.res[t.key] = {"w": (src, val), "r": {}}

    def _need(self, engine, dep):
        src, val = dep
        if val <= 0:
            return False
        if src[0] == "eng" and src[1] == engine and (engine == "pe" or not SAME_ENGINE_SYNC):
            return False
        return self.seen[engine].get(src, 0) < val

    def op(self, engine, fn, r=(), w=()):
        deps = [d for d in self._deps(r, w, ("eng", engine))]
        best = {}
        for src, val in deps:
            if self._need(engine, (src, val)) and val > best.get(src, 0):
                best[src] = val
        items = list(best.items())
        attach = None
        if ATTACH_WAIT and items:
            attach = items.pop()
        for dep in items:
            self._wait(engine, dep)
        ins = fn(self.eng[engine])
        if attach is not None:
            ins._wait_ge(self._semof(attach[0]), attach[1])
            self.seen[engine][attach[0]] = attach[1]
            self.nwait += 1
        self.cnt[engine] += 1
        ins.then_inc(self.sem[engine], 1)
        self.ninst += 1
        self._update(r, w, ("eng", engine), self.cnt[engine])
        return ins

    def dma(self, out, in_, r=(), w=(), q="sp", **kw):
        i = self.lane_rr[q]
        self.lane_rr[q] = (i + 1) % len(self.lanes[q])
        src = ("dma", q, i)
        deps = self._deps(r, w)
        deps.append((src, self.lane_val[(q, i)]))
        for dep in deps:
            self._wait(q, dep)
        ins = self.eng[q].dma_start(out=out, in_=in_, **kw)
        self.lane_val[(q, i)] += 16
        ins.then_inc(self.lanes[q][i], 16)
        self.ninst += 1
        self._update(r, w, src, self.lane_val[(q, i)])

    def barrier(self):
        for e in self.eng:
            for e2 in self.eng:
                self._wait(e, (("eng", e2), self.cnt[e2]))
            for (q, i), v in self.lane_val.items():
                self._wait(e, (("dma", q, i), v))
        self.res = {}

    def finish(self):
        for (q, i), v in self.lane_val.items():
            self._wait("sp", (("dma", q, i), v))
        for e2 in self.eng:
            if e2 != "sp":
                self._wait("sp", (("eng", e2), self.cnt[e2]))


def _blocks():
    out = [(0, CTX)]
    for j in range(SEQ // 512):
        out.append((CTX + 512 * j, 512))
    return out


def _win_cols():
    cols = {}
    base = {"u": 0, "sq": 256, "sk": 512, "sv": 640, "hq": 768, "zf": 1024, "zb": 1280, "hi": 1536, "hg": 1792,
            "cq": 2048, "ckv": 2304, "kr": 2432}

    def swap64(off):
        return np.concatenate([off + np.arange(16, 32), off + np.arange(0, 16), off + np.arange(48, 64), off + np.arange(32, 48)])

    def swap32(off):
        return np.concatenate([off + np.arange(8, 16), off + np.arange(0, 8), off + np.arange(24, 32), off + np.arange(16, 24)])

    sq = base["sq"]
    hA = np.concatenate([sq + np.arange(0, 64), sq + np.arange(128, 192)])
    hB = np.concatenate([sq + np.arange(64, 128), sq + np.arange(192, 256)])
    hAs = np.concatenate([swap64(sq + 0), swap64(sq + 128)])
    hBs = np.concatenate([swap64(sq + 64), swap64(sq + 192)])
    sk = base["sk"]
    fm = [hA, hB, hAs, hBs, sk + np.arange(128), np.concatenate([swap64(sk), swap64(sk + 64)]),
          base["hq"] + np.arange(256), base["zf"] + np.arange(256), base["zb"] + np.arange(256),
          base["hg"] + np.arange(256), base["cq"] + np.arange(256), base["ckv"] + np.arange(128),
          base["kr"] + np.arange(32), swap32(base["kr"])]
    tm = [base["u"] + np.arange(256), base["sv"] + np.arange(128), base["hi"] + np.arange(256)]
    return np.concatenate(fm + tm)


C_SQ, C_SQS, C_SK, C_SKS, C_HQ, C_ZF, C_ZB, C_HG, C_CQ, C_CKV, C_KR, C_KRS = 0, 256, 512, 640, 768, 1024, 1280, 1536, 1792, 2048, 2176, 2208
C_TM = 2240
NWIN = C_TM + 640


def _wqb_cols():
    def swap32(off):
        return np.concatenate([off + np.arange(8, 16), off + np.arange(0, 8), off + np.arange(24, 32), off + np.arange(16, 24)])
    cols = []
    for h in range(4):
        cols += [h * 96 + np.arange(96), swap32(h * 96 + 64)]
    return np.concatenate(cols)


def _swa_mask():
    kk = np.arange(128)[:, None]
    qq = np.arange(128)[None, :]
    lo = (qq <= kk).astype(np.float32)
    hi = (kk <= qq).astype(np.float32)
    m = np.stack([np.stack([lo] * 4, 1), np.stack([hi] * 4, 1)], 1)
    return np.ascontiguousarray(m.astype(np.float32))


def _hg_consts():
    t = np.arange(2048)
    rm = np.broadcast_to((t % 32 != 0).astype(np.float32)[None, :], (64, 2048))
    s_ = np.arange(128)[:, None]
    t_ = np.arange(128)[None, :]
    same = (s_ // 32) == (t_ // 32)
    fw = (same & (s_ <= t_)).astype(np.float32)
    bw = (same & (s_ >= t_)).astype(np.float32)
    am = np.stack([np.stack([fw] * 4, 1), np.stack([bw] * 4, 1)], 1)
    cm = (np.arange(128)[:, None] // 32 == np.arange(4)[None, :]).astype(np.float32)
    return {"hg_rmask": np.ascontiguousarray(rm), "hg_amask": np.ascontiguousarray(am.astype(np.float32)), "hg_cmask": cm}


def _s5_tmask():
    j = np.arange(128)[:, None] // 16
    t = np.arange(128)[None, :] // 16
    return np.ascontiguousarray(np.stack([(t >= j), (t <= j)], 1).astype(np.float32))


def _rope_tables():
    def tab(dim):
        rows = SEQ // GRID_W
        row = np.repeat(np.arange(rows, dtype=np.float64), GRID_W)
        col = np.tile(np.arange(GRID_W, dtype=np.float64), rows)
        nf = dim // 4
        inv = 10000.0 ** (-np.arange(nf, dtype=np.float64) / nf)
        ar = row[None, :] * inv[:, None]
        ac = col[None, :] * inv[:, None]
        C = np.concatenate([np.cos(ar), np.cos(ar), np.cos(ac), np.cos(ac)], 0)
        S = np.concatenate([-np.sin(ar), np.sin(ar), -np.sin(ac), np.sin(ac)], 0)
        C = np.concatenate([np.ones((dim, CTX)), C], 1)
        S = np.concatenate([np.zeros((dim, CTX)), S], 1)
        return C.astype(np.float32), S.astype(np.float32)
    c64, s64 = tab(64)
    c32, s32 = tab(32)
    rs = np.stack([np.concatenate([c64, c64], 0), np.concatenate([s64, s64], 0)])
    z = np.zeros((64, TALL), np.float32)
    rm = np.stack([np.concatenate([z, c32], 0), np.concatenate([z, s32], 0)])
    return rs, rm


class Builder:
    def __init__(self, nlayers=DEPTH, debug=None, stop_after=None, only=None):
        self.stop_after = stop_after
        self.only = only
        self.nl = nlayers
        self.debug = debug
        nc = bass.Bass("TRN2", target_bir_lowering=False)
        self.nc = nc
        self.k = KB(nc)
        dt = nc.dram_tensor

        def ext(name, shape, dtype=F32):
            return dt(name, list(shape), dtype, kind="ExternalInput").ap()

        def internal(name, shape, dtype=F32):
            return dt(name, list(shape), dtype, kind="Internal").ap()

        self.xin = ext("xin", [D, TALL])
        self.cc = ext("cc", [2, D])
        self.w_mod = ext("w_mod", [DEPTH, D, 6 * D])
        self.b_mod = ext("b_mod", [DEPTH, 6 * D])
        self.norm1_g = ext("norm1_g", [DEPTH, D])
        self.norm2_g = ext("norm2_g", [DEPTH, D])
        self.w_in = ext("w_in", [DEPTH, D, NWIN])
        self.w_out = ext("w_out", [DEPTH, D, D])
        self.w_qb = ext("w_qb", [DEPTH, 256, 512])
        self.w_kvb = ext("w_kvb", [DEPTH, 128, 512])
        self.qn_g = ext("qn_g", [DEPTH, 256])
        self.kvn_g = ext("kvn_g", [DEPTH, 128])
        self.w_up = ext("w_up", [DEPTH, D, 2 * FFH])
        self.w_down = ext("w_down", [DEPTH, FFH, D])
        self.final_g = ext("final_g", [D])
        self.rope_s = ext("rope_s", [2, 128, TALL])
        self.rope_m = ext("rope_m", [2, 96, TALL])
        self.ident = ext("ident", [128, 128])
        self.swa_mask = ext("swa_mask", [128, 2, 4, 128])
        self.swa_sink = ext("swa_sink", [DEPTH, 4])
        self.hg_lb = ext("hg_lb", [2, DEPTH, 256])
        self.s5_lam_re = ext("s5_lam_re", [DEPTH, 2, 16, 64])
        self.s5_lam_im = ext("s5_lam_im", [DEPTH, 2, 16, 64])
        self.s5_log_dt = ext("s5_log_dt", [DEPTH, 2, 16, 64])
        self.s5_b_re = ext("s5_b_re", [DEPTH, 2, 16, 64, 16])
        self.s5_b_im = ext("s5_b_im", [DEPTH, 2, 16, 64, 16])
        self.s5_c_re = ext("s5_c_re", [DEPTH, 2, 16, 16, 64])
        self.s5_c_im = ext("s5_c_im", [DEPTH, 2, 16, 16, 64])
        self.s5_d = ext("s5_d", [DEPTH, 256])
        self.s5_w_glu = ext("s5_w_glu", [DEPTH, 256, 256])
        self.s5_b_glu = ext("s5_b_glu", [DEPTH, 256])
        self.s5_tmask = ext("s5_tmask", [128, 2, 128])
        self.hg_norm_g = ext("hg_norm_g", [DEPTH, 64])
        self.hg_rmask = ext("hg_rmask", [64, 2048])
        self.hg_amask = ext("hg_amask", [128, 2, 4, 128])
        self.hg_cmask = ext("hg_cmask", [128, 4])
        self.out = dt("out", [D, SEQ], F32, kind="ExternalOutput").ap()
        self.xres = internal("xres", [D, TALL])
        self.s_sq = internal("s_sq", [256, TALL], BF16)
        self.s_sk = internal("s_sk", [128, TALL], BF16)
        self.s_sv = internal("s_sv", [TALL, 130], BF16)
        self.s_u = internal("s_u", [TALL, 256], F32)
        self.s_hq = internal("s_hq", [256, TALL], BF16)
        self.s_zf = internal("s_zf", [256, TALL], F32)
        self.s_zb = internal("s_zb", [256, TALL], F32)
        self.s_hg = internal("s_hg", [256, TALL], F32)
        self.s_hi = internal("s_hi", [TALL, 256], BF16)
        self.s_mq = internal("s_mq", [4, 96, TALL], BF16)
        self.s_mk = internal("s_mk", [4, 96, TALL], BF16)
        self.s_mv = internal("s_mv", [TALL, 260], BF16)
        self.ymix = internal("ymix", [D, TALL], BF16)
        self.s_ob = internal("s_ob", [64, 4, TALL], F32)
        self.s_y = [internal(f"s_y{d}", [128, 16, TALL // 8], F32) for d in range(2)]
        if debug:
            self.dbg = {n: dt("dbg_" + n, list(v[0]), v[1], kind="ExternalOutput").ap() for n, v in debug.items()}
        self.build()

    def build(self):
        k = self.k
        nc = self.nc
        es = k.es
        self.ones_bf = k.sb(es, "ones_bf", [128, 128], BF16)
        self.ident_bf = k.sb(es, "ident_bf", [128, 128], BF16)
        self.ident_f = k.sb(es, "ident_f", [128, 128], F32)
        self.mod = k.sb(es, "mod", [128, DEPTH, 48, 2], F32)
        self.gs = k.sb(es, "gs", [128, DEPTH, 2, 8, 2], F32)
        self.epsb = k.sb(es, "epsb", [128, 1], F32)
        k.op("dve", lambda e: e.memset(self.ones_bf[:], 1.0), w=[self.ones_bf])
        k.op("dve", lambda e: e.memset(self.epsb[:], EPS), w=[self.epsb])
        k.dma(self.ident_f[:], self.ident, w=[self.ident_f])
        k.dma(self.ident_bf[:], self.ident, w=[self.ident_bf], q="pool")
        self.psw = [k.ps(es, f"psw{i}", [128, 1024], F32) for i in range(4)]
        self.psb = [PV(self.psw[i // 2], 512 * (i % 2), f"psb{i}") for i in range(8)]
        self.setup_mod()
        k.barrier()
        self.setup_hglb()
        for l in range(self.nl):
            self.layer(l)
        if self.debug:
            k.barrier()
            for n in self.debug:
                src = self.debug[n][2](self) if len(self.debug[n]) > 2 else getattr(self, n)
                nd = len(src.shape)
                pat = " ".join("abcd"[:nd])
                fl = lambda a: a.rearrange(f"{pat} -> ({pat})").rearrange("(p f) -> p f", p=16)
                k.dma(fl(self.dbg[n]), fl(src))
        k.finish()
        es.close()

    def load_fm(self, es, dst_ap, src2d, n, wkeys, wd=128):
        k = self.k
        stg = k.sb(es, "stg", [128, 128], F32)
        ps = self.psb[7]
        k.dma(stg[0:n, 0:wd], src2d, w=[stg])
        k.op("pe", lambda e: e.transpose(out=ps[0:wd, 0:n], in_=stg[0:n, 0:wd], identity=self.ident_f[0:n, 0:n]),
             r=[stg, self.ident_f], w=[ps])
        k.op("dve", lambda e: e.tensor_copy(out=dst_ap, in_=ps[0:wd, 0:n]), r=[ps], w=wkeys)

    def setup_mod(self):
        k = self.k
        with contextlib.ExitStack() as es:
            craw = k.sb(es, "craw", [128, 8, 2], F32)
            csil = k.sb(es, "csil", [128, 8, 2], F32)
            bm = k.sb(es, "bm", [128, DEPTH, 48], F32)
            ng = k.sb(es, "ng", [128, 2, DEPTH, 8], F32)
            wm = [k.sb(es, f"wm{i}", [128, 8, 768], F32) for i in range(2)]
            self.load_fm(es, craw[:, :, 0], self.cc[0].rearrange("(c p) -> c p", p=128), 8, [craw])
            self.load_fm(es, craw[:, :, 1], self.cc[1].rearrange("(c p) -> c p", p=128), 8, [craw])
            for l in range(DEPTH):
                self.load_fm(es, bm[:, l, :], self.b_mod[l].rearrange("(j p) -> j p", p=128), 48, [bm])
            self.load_fm(es, ng[:, 0], self.norm1_g.rearrange("l (c p) -> (l c) p", p=128), 32, [ng])
            self.load_fm(es, ng[:, 1], self.norm2_g.rearrange("l (c p) -> (l c) p", p=128), 32, [ng])
            k.op("act", lambda e: e.activation(out=csil[:], in_=craw[:], func=AF.Silu), r=[craw], w=[csil])
            it = 0
            for l in range(self.nl):
                for grp in range(8):
                    wt = wm[it % 2]
                    it += 1
                    k.dma(wt[:], self.w_mod[l].rearrange("(c p) n -> p c n", p=128)[:, :, grp * 768:(grp + 1) * 768], w=[wt])
                    for n in range(6):
                        ps = self.psb[n % 4]
                        for kc in range(8):
                            k.op("pe", lambda e: e.matmul(ps[:, 0:2], lhsT=wt[:, kc, n * 128:(n + 1) * 128], rhs=csil[:, kc, :],
                                                          start=(kc == 0), stop=(kc == 7)), r=[wt, csil], w=[ps])
                        j = grp * 6 + n
                        k.op("dve", lambda e: e.tensor_scalar(out=self.mod[:, l, j, :], in0=ps[:, 0:2], scalar1=bm[:, l, j:j + 1],
                                                              scalar2=None, op0=ALU.add), r=[ps, bm], w=[self.mod])
                for which, sci in ((0, 1), (1, 4)):
                    for j in range(2):
                        k.op("dve", lambda e: e.scalar_tensor_tensor(out=self.gs[:, l, which, :, j], in0=self.mod[:, l, sci * 8:(sci + 1) * 8, j],
                                                                     scalar=1.0, in1=ng[:, which, l, :], op0=ALU.add, op1=ALU.mult),
                             r=[self.mod, ng], w=[self.gs])
            k.barrier()

    def layer(self, l):
        k = self.k
        xsrc = self.xin if l == 0 else self.xres
        if self.stop_after == "mod":
            return
        self.phase_proj(l, xsrc)
        k.barrier()
        if self.stop_after == "proj":
            return
        if self.only is None and OVERLAP_S5:
            gen = self.phase_s5_gen(l)
            next(gen)
            need_ctx = l < DEPTH - 1
            npi = (16 * 4 * 33) + (4 if need_ctx else 0)
            rate = self.s5_nitems / float(npi)
            acc = [0.0]

            def filler():
                acc[0] += rate
                while acc[0] >= 1.0:
                    acc[0] -= 1.0
                    next(gen, None)
            self.phase_mla(l, filler=filler)
            for _ in gen:
                pass
        else:
            if self.only in (None, "mla"):
                self.phase_mla(l)
        if self.only is None and OVERLAP_SWA:
            sgen = self.phase_swa_gen(l, corun=True)
            next(sgen)
            self.phase_hg(l, filler=lambda: next(sgen, None))
            for _ in sgen:
                pass
        elif self.only in (None, "swa"):
            for _ in self.phase_swa_gen(l):
                pass
        if self.only == "hg" or (self.only is None and not OVERLAP_SWA):
            self.phase_hg(l)
        if self.only == "s5" or (self.only is None and not OVERLAP_S5):
            for _ in self.phase_s5_gen(l):
                pass
        if self.stop_after == "mix":
            return
        self.phase_ffn(l, xsrc)

    def norm_mod(self, es_tiles, xsrc, t0, n, l, which, ctxflag, load=True, gain=None, bias=None, out_f32=None):
        k = self.k
        xt, sq, rstd, tmps, ht, ps = es_tiles
        shi = 0 if which == 0 else 3
        if load:
            k.dma(xt[:, :, :n], xsrc.rearrange("(c p) t -> p c t", p=128)[:, :, t0:t0 + n], w=[xt])
        for c in range(8):
            k.op("pool", lambda e: e.tensor_tensor(out=sq[:, c, :n], in0=xt[:, c, :n], in1=xt[:, c, :n], op=ALU.mult), r=[xt], w=[sq])
        for c in range(8):
            k.op("pe", lambda e: e.matmul(ps[:, :n], lhsT=self.ones_bf[:], rhs=sq[:, c, :n], start=(c == 0), stop=(c == 7)),
                 r=[sq, self.ones_bf], w=[ps])
        k.op("act", lambda e: e.activation(out=rstd[:, :n], in_=ps[:, :n], func=AF.Sqrt, bias=self.epsb[:], scale=1.0 / D),
             r=[ps, self.epsb], w=[rstd])
        k.op("dve", lambda e: e.reciprocal(out=rstd[:, :n], in_=rstd[:, :n]), r=[rstd], w=[rstd])
        for c in range(8):
            g_ap = gain[:, c:c + 1] if gain is not None else self.gs[:, l, which, c, ctxflag:ctxflag + 1]
            if out_f32 is not None:
                tmp = tmps[c % len(tmps)]
                k.op("dve", lambda e: e.scalar_tensor_tensor(out=tmp[:, :n], in0=xt[:, c, :n], scalar=g_ap,
                                                             in1=rstd[:, :n], op0=ALU.mult, op1=ALU.mult), r=[xt, rstd, self.gs], w=[tmp])
                k.dma(out_f32(c), tmp[:, :n], r=[tmp])
                continue
            tmp = tmps[c % len(tmps)]
            k.op("dve", lambda e: e.scalar_tensor_tensor(out=tmp[:, :n], in0=xt[:, c, :n], scalar=g_ap,
                                                         in1=rstd[:, :n], op0=ALU.mult, op1=ALU.mult), r=[xt, rstd, self.gs], w=[tmp])
            k.op("act", lambda e: e.activation(out=ht[:, c, :n], in_=tmp[:, :n], func=AF.Identity,
                                               bias=self.mod[:, l, shi * 8 + c, ctxflag:ctxflag + 1], scale=1.0),
                 r=[tmp, self.mod], w=[ht])

    def phase_proj(self, l, xsrc):
        k = self.k
        with contextlib.ExitStack() as es:
            win = k.sb(es, "win", [128, 8, NWIN], BF16)
            wqb = k.sb(es, "wqb", [128, 2, 512], BF16)
            wkvb = k.sb(es, "wkvb", [128, 512], BF16)
            qng = k.sb(es, "qng", [128, 2], F32)
            kvng = k.sb(es, "kvng", [128, 1], F32)
            for c in range(8):
                k.dma(win[:, c, :], self.w_in[l, c * 128:(c + 1) * 128, :], w=[win], q="pool")
            k.dma(wqb[:], self.w_qb[l].rearrange("(c p) n -> p c n", p=128), w=[wqb], q="pool")
            k.dma(wkvb[:], self.w_kvb[l], w=[wkvb], q="pool")
            self.load_fm(es, qng[:], self.qn_g[l].rearrange("(c p) -> c p", p=128), 2, [qng])
            self.load_fm(es, kvng[:], self.kvn_g[l].rearrange("(c p) -> c p", p=128), 1, [kvng])
            NB = 2
            xt = [k.sb(es, f"xt{i}", [128, 8, 512], F32) for i in range(NB)]
            sq = [k.sb(es, f"sq{i}", [128, 8, 512], BF16) for i in range(NB)]
            rstd = [k.sb(es, f"rstd{i}", [128, 512], F32) for i in range(NB)]
            tmp = [k.sb(es, f"tmp{i}", [128, 512], F32) for i in range(2)]
            ht = [k.sb(es, f"ht{i}", [128, 8, 512], BF16) for i in range(NB)]
            rs = [k.sb(es, f"rs{i}", [128, 2, 512], F32) for i in range(NB)]
            rm = [k.sb(es, f"rm{i}", [96, 2, 512], F32) for i in range(NB)]
            ob = [k.sb(es, f"ob{i}", [128, 512], BF16) for i in range(4)]
            of = [k.sb(es, f"of{i}", [128, 512], F32) for i in range(4)]
            r1 = [k.sb(es, f"r1{i}", [128, 512], F32) for i in range(2)]
            r2 = [k.sb(es, f"r2{i}", [128, 512], F32) for i in range(2)]
            cqn = [k.sb(es, f"cqn{i}", [128, 2, 512], BF16) for i in range(NB)]
            ckvn = [k.sb(es, f"ckvn{i}", [128, 512], BF16) for i in range(NB)]
            nsq = [k.sb(es, f"nsq{i}", [128, 512], BF16) for i in range(2)]
            nrs = [k.sb(es, f"nrs{i}", [128, 512], F32) for i in range(2)]
            tmo = [k.sb(es, f"tmo{i}", [128, 640], F32) for i in range(2)]
            tmb = [k.sb(es, f"tmb{i}", [128, 256], BF16) for i in range(2)]
            svb = [k.sb(es, f"svb{i}", [128, 2, 65], BF16) for i in range(2)]
            mvb = [k.sb(es, f"mvb{i}", [128, 4, 65], BF16) for i in range(2)]
            for i in range(2):
                k.op("pool", lambda e: e.memset(svb[i][:, :, 64:65], 1.0), w=[svb[i]])
                k.op("pool", lambda e: e.memset(mvb[i][:, :, 64:65], 1.0), w=[mvb[i]])
            state = {"ob": 0, "of": 0, "ps": 0, "r": 0, "n": 0}

            def nxt(lst, key):
                i = state[key]
                state[key] = (i + 1) % len(lst)
                return lst[i]

            def fm_mm(ps, col0, m, hT, n, lhs_w=None, p0=0):
                for kc in range(8):
                    k.op("pe", lambda e: e.matmul(ps[0:m, :n], lhsT=win[:, kc, col0:col0 + m], rhs=hT[:, kc, :n],
                                                  start=(kc == 0), stop=(kc == 7)), r=[win, hT], w=[ps])

            blks = _blocks()

            def do_norm(bj):
                t0_, n_ = blks[bj]
                self.norm_mod((xt[bj % NB], sq[bj % NB], rstd[bj % NB], tmp, ht[bj % NB], self.psb[7]), xsrc, t0_, n_, l, 0, 1 if bj == 0 else 0)
            do_norm(0)
            for bi, (t0, n) in enumerate(blks):
                ctxflag = 1 if bi == 0 else 0
                b = bi % NB
                hT = ht[b]
                CUT = float(os.environ.get("KCUT", "99"))
                if bi >= int(os.environ.get("KBLK", "99")):
                    break
                if CUT < 1:
                    continue
                k.dma(rs[b][:, :, :n], self.rope_s[:, :, t0:t0 + n].rearrange("a p t -> p a t"), w=[rs[b]])
                k.dma(rm[b][64:96, :, :n], self.rope_m[:, 64:96, t0:t0 + n].rearrange("a p t -> p a t"), w=[rm[b]])
                for ci, (c_a, c_b, dst) in enumerate(((C_SQ, C_SQS, self.s_sq[0:128]), (C_SQ + 128, C_SQS + 128, self.s_sq[128:256]),
                                                      (C_SK, C_SKS, self.s_sk))):
                    pa = nxt(self.psb[0:6], "ps")
                    fm_mm(pa, c_a, 128, hT, n)
                    pb = nxt(self.psb[0:6], "ps")
                    fm_mm(pb, c_b, 128, hT, n)
                    a1 = nxt(r1, "r")
                    a2 = r2[r1.index(a1)]
                    o = nxt(ob, "ob")
                    k.op("dve", lambda e: e.tensor_tensor(out=a1[:, :n], in0=pa[:, :n], in1=rs[b][:, 0, :n], op=ALU.mult), r=[pa, rs[b]], w=[a1])
                    k.op("dve", lambda e: e.tensor_tensor(out=a2[:, :n], in0=pb[:, :n], in1=rs[b][:, 1, :n], op=ALU.mult), r=[pb, rs[b]], w=[a2])
                    k.op("pool", lambda e: e.tensor_tensor(out=o[:, :n], in0=a1[:, :n], in1=a2[:, :n], op=ALU.add), r=[a1, a2], w=[o])
                    k.dma(dst[:, t0:t0 + n], o[:, :n], r=[o])
                PJ = int(os.environ.get("KPJ", "2"))
                if PJ == 2 and bi + 1 < len(blks) and bi + 1 < int(os.environ.get("KBLK", "99")):
                    do_norm(bi + 1)
                if CUT < 2:
                    continue
                for c in range(2):
                    pa = nxt(self.psb[0:6], "ps")
                    fm_mm(pa, C_HQ + 128 * c, 128, hT, n)
                    o = nxt(ob, "ob")
                    k.op("act", lambda e: e.copy(out=o[:, :n], in_=pa[:, :n]), r=[pa], w=[o])
                    k.dma(self.s_hq[128 * c:128 * c + 128, t0:t0 + n], o[:, :n], r=[o])
                for (c0, dst, fn) in ((C_ZF, self.s_zf, AF.Copy), (C_ZB, self.s_zb, AF.Copy), (C_HG, self.s_hg, AF.Silu)):
                    for c in range(2):
                        pa = nxt(self.psb[0:6], "ps")
                        fm_mm(pa, c0 + 128 * c, 128, hT, n)
                        o = nxt(of, "of")
                        k.op("act", lambda e: e.activation(out=o[:, :n], in_=pa[:, :n], func=fn), r=[pa], w=[o])
                        k.dma(dst[128 * c:128 * c + 128, t0:t0 + n], o[:, :n], r=[o])
                if PJ == 3 and bi + 1 < len(blks) and bi + 1 < int(os.environ.get("KBLK", "99")):
                    do_norm(bi + 1)
                if CUT < 3:
                    continue
                pcq = [nxt(self.psb[0:6], "ps") for _ in range(2)]
                for c in range(2):
                    fm_mm(pcq[c], C_CQ + 128 * c, 128, hT, n)
                pss = self.psb[6]
                for c in range(2):
                    s_ = nxt(nsq, "n")
                    k.op("act", lambda e: e.activation(out=s_[:, :n], in_=pcq[c][:, :n], func=AF.Square), r=[pcq[c]], w=[s_])
                    k.op("pe", lambda e: e.matmul(pss[:, :n], lhsT=self.ones_bf[:], rhs=s_[:, :n], start=(c == 0), stop=(c == 1)),
                         r=[s_, self.ones_bf], w=[pss])
                nr = nrs[0]
                k.op("act", lambda e: e.activation(out=nr[:, :n], in_=pss[:, :n], func=AF.Sqrt, bias=self.epsb[:], scale=1.0 / 256),
                     r=[pss, self.epsb], w=[nr])
                k.op("dve", lambda e: e.reciprocal(out=nr[:, :n], in_=nr[:, :n]), r=[nr], w=[nr])
                for c in range(2):
                    k.op("dve", lambda e: e.scalar_tensor_tensor(out=cqn[b][:, c, :n], in0=pcq[c][:, :n], scalar=qng[:, c:c + 1], in1=nr[:, :n],
                                                                 op0=ALU.mult, op1=ALU.mult), r=[pcq[c], qng, nr], w=[cqn[b]])
                for h in range(4):
                    pa = nxt(self.psb[0:6], "ps")
                    pb = nxt(self.psb[0:6], "ps")
                    for c in range(2):
                        k.op("pe", lambda e: e.matmul(pa[0:96, :n], lhsT=wqb[:, c, h * 128:h * 128 + 96], rhs=cqn[b][:, c, :n],
                                                      start=(c == 0), stop=(c == 1)), r=[wqb, cqn[b]], w=[pa])
                    for c in range(2):
                        k.op("pe", lambda e: e.matmul(pb[0:96, :n], lhsT=wqb[:, c, h * 128 + 32:h * 128 + 128], rhs=cqn[b][:, c, :n],
                                                      start=(c == 0), stop=(c == 1)), r=[wqb, cqn[b]], w=[pb])
                    o = nxt(ob, "ob")
                    a1 = nxt(r1, "r")
                    a2 = r2[r1.index(a1)]
                    k.op("act", lambda e: e.copy(out=o[0:64, :n], in_=pa[0:64, :n]), r=[pa], w=[o])
                    k.op("dve", lambda e: e.tensor_tensor(out=a1[64:96, :n], in0=pa[64:96, :n], in1=rm[b][64:96, 0, :n], op=ALU.mult),
                         r=[pa, rm[b]], w=[a1])
                    k.op("dve", lambda e: e.tensor_tensor(out=a2[64:96, :n], in0=pb[64:96, :n], in1=rm[b][64:96, 1, :n], op=ALU.mult),
                         r=[pb, rm[b]], w=[a2])
                    k.op("pool", lambda e: e.tensor_tensor(out=o[64:96, :n], in0=a1[64:96, :n], in1=a2[64:96, :n], op=ALU.add),
                         r=[a1, a2, o], w=[o])
                    k.dma(self.s_mq[h, :, t0:t0 + n], o[0:96, :n], r=[o])
                if PJ == 4 and bi + 1 < len(blks) and bi + 1 < int(os.environ.get("KBLK", "99")):
                    do_norm(bi + 1)
                if CUT < 4:
                    continue
                pkv = nxt(self.psb[0:6], "ps")
                fm_mm(pkv, C_CKV, 128, hT, n)
                s_ = nxt(nsq, "n")
                k.op("act", lambda e: e.activation(out=s_[:, :n], in_=pkv[:, :n], func=AF.Square), r=[pkv], w=[s_])
                k.op("pe", lambda e: e.matmul(pss[:, :n], lhsT=self.ones_bf[:], rhs=s_[:, :n], start=True, stop=True),
                     r=[s_, self.ones_bf], w=[pss])
                nr = nrs[1]
                k.op("act", lambda e: e.activation(out=nr[:, :n], in_=pss[:, :n], func=AF.Sqrt, bias=self.epsb[:], scale=1.0 / 128),
                     r=[pss, self.epsb], w=[nr])
                k.op("dve", lambda e: e.reciprocal(out=nr[:, :n], in_=nr[:, :n]), r=[nr], w=[nr])
                k.op("dve", lambda e: e.scalar_tensor_tensor(out=ckvn[b][:, :n], in0=pkv[:, :n], scalar=kvng[:, 0:1], in1=nr[:, :n],
                                                             op0=ALU.mult, op1=ALU.mult), r=[pkv, kvng, nr], w=[ckvn[b]])
                if CUT < 4.2:
                    continue
                pa = nxt(self.psb[0:6], "ps")
                pb = nxt(self.psb[0:6], "ps")
                fm_mm(pa, C_KR - 64, 96, hT, n)
                fm_mm(pb, C_KRS - 64, 96, hT, n)
                a1 = nxt(r1, "r")
                a2 = r2[r1.index(a1)]
                kro = nxt(ob, "ob")
                k.op("dve", lambda e: e.tensor_tensor(out=a1[64:96, :n], in0=pa[64:96, :n], in1=rm[b][64:96, 0, :n], op=ALU.mult),
                     r=[pa, rm[b]], w=[a1])
                k.op("dve", lambda e: e.tensor_tensor(out=a2[64:96, :n], in0=pb[64:96, :n], in1=rm[b][64:96, 1, :n], op=ALU.mult),
                     r=[pb, rm[b]], w=[a2])
                k.op("pool", lambda e: e.tensor_tensor(out=kro[64:96, :n], in0=a1[64:96, :n], in1=a2[64:96, :n], op=ALU.add),
                     r=[a1, a2], w=[kro])
                if CUT < 4.4:
                    continue
                for h in range(4):
                    k.dma(self.s_mk[h, 64:96, t0:t0 + n], kro[64:96, :n], r=[kro])
                    if CUT < 4.6:
                        continue
                    pa = nxt(self.psb[0:6], "ps")
                    if os.environ.get("KVAR") == "A":
                        k.op("pe", lambda e: e.matmul(pa[:, :n], lhsT=wkvb[:, h * 128:h * 128 + 128], rhs=ckvn[b][:, :n], start=True, stop=True),
                             r=[wkvb, ckvn[b]], w=[pa])
                    else:
                        k.op("pe", lambda e: e.matmul(pa[0:64, :n], lhsT=wkvb[:, h * 128:h * 128 + 64], rhs=ckvn[b][:, :n], start=True, stop=True),
                             r=[wkvb, ckvn[b]], w=[pa])
                    if CUT < 4.7:
                        continue
                    o = nxt(ob, "ob")
                    k.op("act", lambda e: e.copy(out=o[0:64, :n], in_=pa[0:64, :n]), r=[pa], w=[o])
                    if CUT < 4.8:
                        continue
                    k.dma(self.s_mk[h, 0:64, t0:t0 + n], o[0:64, :n], r=[o])
                if PJ == 5 and bi + 1 < len(blks) and bi + 1 < int(os.environ.get("KBLK", "99")):
                    do_norm(bi + 1)
                if CUT < 5:
                    continue
                for st in range(n // 128):
                    ts_ = slice(st * 128, st * 128 + 128)
                    pa = nxt(self.psb[0:6], "ps")
                    pb = nxt(self.psb[0:6], "ps")
                    for kc in range(8):
                        k.op("pe", lambda e: e.matmul(pa[:, 0:512], lhsT=hT[:, kc, ts_], rhs=win[:, kc, C_TM:C_TM + 512],
                                                      start=(kc == 0), stop=(kc == 7)), r=[win, hT], w=[pa])
                    for kc in range(8):
                        k.op("pe", lambda e: e.matmul(pb[:, 0:128], lhsT=hT[:, kc, ts_], rhs=win[:, kc, C_TM + 512:C_TM + 640],
                                                      start=(kc == 0), stop=(kc == 7)), r=[win, hT], w=[pb])
                    k.op("pe", lambda e: e.matmul(pb[:, 128:384], lhsT=ckvn[b][:, ts_],
                                                  rhs=wkvb[:].rearrange("p (h x) -> p h x", x=128)[:, :, 64:128],
                                                  start=True, stop=True), r=[wkvb, ckvn[b]], w=[pb])
                    uo = tmo[st % 2]
                    bo = tmb[st % 2]
                    so = svb[st % 2]
                    mo = mvb[st % 2]
                    k.op("act", lambda e: e.copy(out=uo[:, 0:256], in_=pa[:, 0:256]), r=[pa], w=[uo])
                    k.op("dve", lambda e: e.tensor_copy(out=bo[:, 0:128], in_=pa[:, 384:512]), r=[pa], w=[bo])
                    k.op("dve", lambda e: e.tensor_copy(out=bo[:, 128:256], in_=pb[:, 0:128]), r=[pb, bo], w=[bo])
                    k.op("dve", lambda e: e.tensor_copy(out=so[:, :, 0:64], in_=pa[:, 256:384].rearrange("p (g x) -> p g x", g=2)), r=[pa, so], w=[so])
                    k.op("act", lambda e: e.copy(out=mo[:, :, 0:64], in_=pb[:, 128:384].rearrange("p (g x) -> p g x", g=4)), r=[pb, mo], w=[mo])
                    tt = t0 + st * 128
                    k.dma(self.s_u[tt:tt + 128, :], uo[:, 0:256], r=[uo])
                    k.dma(self.s_sv[tt:tt + 128, :], so[:].rearrange("p g x -> p (g x)"), r=[so])
                    k.dma(self.s_hi[tt:tt + 128, :], bo[:, 0:256], r=[bo])
                    k.dma(self.s_mv[tt:tt + 128, :], mo[:].rearrange("p g x -> p (g x)"), r=[mo])
            k.barrier()


    def phase_ffn(self, l, xsrc):
        k = self.k
        last = (l == DEPTH - 1)
        NB = 256
        with contextlib.ExitStack() as es:
            wout = k.sb(es, "wout", [128, 8, D], BF16)
            wup = k.sb(es, "wup", [128, 8, 2 * FFH], BF16)
            wdn = k.sb(es, "wdn", [128, 22, D], BF16)
            for c in range(8):
                k.dma(wout[:, c, :], self.w_out[l, c * 128:(c + 1) * 128, :], w=[wout], q="pool")
                k.dma(wup[:, c, :], self.w_up[l, c * 128:(c + 1) * 128, :], w=[wup], q="pool")
            for j in range(22):
                k.dma(wdn[:, j, :], self.w_down[l, j * 128:(j + 1) * 128, :], w=[wdn], q="pool")
            fg = None
            if last:
                fg = k.sb(es, "fg", [128, 8], F32)
                self.load_fm(es, fg[:], self.final_g.rearrange("(c p) -> c p", p=128), 8, [fg])
            xt = [k.sb(es, f"fxt{i}", [128, 8, NB], F32) for i in range(2)]
            ym = [k.sb(es, f"fym{i}", [128, 8, NB], BF16) for i in range(2)]
            sq = k.sb(es, "fsq", [128, 8, NB], BF16)
            tmp = [k.sb(es, f"ftmp{i}", [128, NB], F32) for i in range(2)]
            ht = [k.sb(es, f"fht{i}", [128, 8, NB], BF16) for i in range(2)]
            rstd = k.sb(es, "frstd", [128, NB], F32)
            sq2, rstd2, tmp2 = sq, rstd, tmp
            aT = k.sb(es, "faT", [128, 22, NB], BF16)
            sg = [k.sb(es, f"fsg{i}", [128, NB], F32) for i in range(2)]
            psi = [0]

            def nps():
                psi[0] = (psi[0] + 1) % 6
                return self.psb[psi[0]]

            t_start = CTX if last else 0
            t0s = list(range(t_start, TALL, NB))
            n = NB

            def stage_a(bi):
                t0 = t0s[bi]
                flag = 1 if t0 < CTX else 0
                b = bi % 2
                k.dma(xt[b][:, :, :n], xsrc.rearrange("(c p) t -> p c t", p=128)[:, :, t0:t0 + n], w=[xt[b]])
                k.dma(ym[b][:, :, :n], self.ymix.rearrange("(c p) t -> p c t", p=128)[:, :, t0:t0 + n], w=[ym[b]])
                for oc in range(8):
                    ps = nps()
                    for kc in range(8):
                        k.op("pe", lambda e: e.matmul(ps[:, :n], lhsT=wout[:, kc, oc * 128:(oc + 1) * 128], rhs=ym[b][:, kc, :n],
                                                      start=(kc == 0), stop=(kc == 7)), r=[wout, ym[b]], w=[ps])
                    k.op("dve", lambda e: e.scalar_tensor_tensor(out=xt[b][:, oc, :n], in0=ps[:, :n], scalar=self.mod[:, l, 16 + oc, flag:flag + 1],
                                                                 in1=xt[b][:, oc, :n], op0=ALU.mult, op1=ALU.add), r=[ps, xt[b], self.mod], w=[xt[b]])
                self.norm_mod((xt[b], sq, rstd, tmp, ht[b], self.psb[7]), None, t0, n, l, 1, flag, load=False)

            stage_a(0)
            for bi, t0 in enumerate(t0s):
                flag = 1 if t0 < CTX else 0
                b = bi % 2
                for j in range(22):
                    if j == int(os.environ.get("KFJ", "16")) and bi + 1 < len(t0s):
                        stage_a(bi + 1)
                    pg = nps()
                    pu = nps()
                    for kc in range(8):
                        k.op("pe", lambda e: e.matmul(pg[:, :n], lhsT=wup[:, kc, j * 128:(j + 1) * 128], rhs=ht[b][:, kc, :n],
                                                      start=(kc == 0), stop=(kc == 7)), r=[wup, ht[b]], w=[pg])
                    for kc in range(8):
                        k.op("pe", lambda e: e.matmul(pu[:, :n], lhsT=wup[:, kc, FFH + j * 128:FFH + (j + 1) * 128], rhs=ht[b][:, kc, :n],
                                                      start=(kc == 0), stop=(kc == 7)), r=[wup, ht[b]], w=[pu])
                    s_ = sg[j % 2]
                    k.op("act", lambda e: e.activation(out=s_[:, :n], in_=pg[:, :n], func=AF.Silu), r=[pg], w=[s_])
                    k.op("dve", lambda e: e.tensor_tensor(out=aT[:, j, :n], in0=s_[:, :n], in1=pu[:, :n], op=ALU.mult), r=[s_, pu], w=[aT])
                for oc in range(8):
                    ps = nps()
                    for j in range(22):
                        k.op("pe", lambda e: e.matmul(ps[:, :n], lhsT=wdn[:, j, oc * 128:(oc + 1) * 128], rhs=aT[:, j, :n],
                                                      start=(j == 0), stop=(j == 21)), r=[wdn, aT], w=[ps])
                    k.op("dve", lambda e: e.scalar_tensor_tensor(out=xt[b][:, oc, :n], in0=ps[:, :n], scalar=self.mod[:, l, 40 + oc, flag:flag + 1],
                                                                 in1=xt[b][:, oc, :n], op0=ALU.mult, op1=ALU.add), r=[ps, xt[b], self.mod], w=[xt[b]])
                if not last:
                    k.dma(self.xres.rearrange("(c p) t -> p c t", p=128)[:, :, t0:t0 + n], xt[b][:, :, :n], r=[xt[b]])
                else:
                    self.norm_mod((xt[b], sq2, rstd2, tmp2, None, self.psb[7]), None, t0, n, l, 1, flag, load=False, gain=fg,
                                  out_f32=lambda c: self.out[c * 128:(c + 1) * 128, t0 - CTX:t0 - CTX + n])
            k.barrier()

    def attn_finish(self, OT, n, Osb, rec, yo, sel, dst_aps, nh=1, bc=None):
        k = self.k
        bc = self.psb[6] if bc is None else bc
        k.op("act", lambda e: e.copy(out=Osb[0:65, :n], in_=OT[0:65, :n]), r=[OT], w=[Osb])
        k.op("pe", lambda e: e.matmul(bc[0:64, :n], lhsT=sel[0:65, :], rhs=Osb[0:65, :n], start=True, stop=True), r=[sel, Osb], w=[bc])
        k.op("dve", lambda e: e.reciprocal(out=rec[0:64, :n], in_=bc[0:64, :n]), r=[bc], w=[rec])
        k.op("dve", lambda e: e.tensor_tensor(out=yo[0:64, :n], in0=Osb[0:64, :n], in1=rec[0:64, :n], op=ALU.mult), r=[Osb, rec], w=[yo])
        w = n // nh
        for j, dst in enumerate(dst_aps):
            k.dma(dst, yo[0:64, j * w:(j + 1) * w], r=[yo])

    def make_sel(self, es):
        k = self.k
        sel = k.sb(es, "sel", [65, 64], F32)
        k.op("dve", lambda e: e.memset(sel[0:64, :], 0.0), w=[sel])
        k.op("dve", lambda e: e.memset(sel[64:65, :], 1.0), r=[sel], w=[sel])
        return sel

    def phase_mla(self, l, filler=None):
        k = self.k
        need_ctx = l < DEPTH - 1
        NT = TALL // 128
        with contextlib.ExitStack() as es:
            KT = k.sb(es, "mKT", [96, 2, TALL], BF16)
            Va = k.sb(es, "mVa", [128, NT, 2, 65], BF16)
            sel = self.make_sel(es)
            QT = [k.sb(es, f"mQT{i}", [96, 2, 512], BF16) for i in range(2)]
            PT = [k.sb(es, f"mPT{i}", [128, 2, 512], BF16) for i in range(2)]
            Osb = [k.sb(es, f"mOsb{i}", [65, 512], F32) for i in range(2)]
            rec = [k.sb(es, f"mrec{i}", [64, 512], F32) for i in range(2)]
            yo = [k.sb(es, f"myo{i}", [64, 512], BF16) for i in range(2)]
            cnt = 0
            vsrc = self.s_mv.rearrange("(t p) (h x) -> p t h x", p=128, h=4)
            qi = 0
            for hp in range(2):
                for j in range(2):
                    k.dma(KT[:, j, :], self.s_mk[2 * hp + j], w=[KT])
                for t in range(0, NT, 6):
                    k.dma(Va[:, t:t + 6], vsrc[:, t:t + 6, 2 * hp:2 * hp + 2, :], w=[Va])
                for bi, (t0, n) in enumerate(_blocks()):
                    if bi == 0 and not need_ctx:
                        continue
                    ktiles = [0, 1] if bi == 0 else list(range(NT))
                    b = qi % 2
                    qi += 1
                    k.dma(QT[b][:, :, :n], self.s_mq[2 * hp:2 * hp + 2, :, t0:t0 + n].rearrange("h p t -> p h t"), w=[QT[b]])
                    for hj in range(2):
                        h = 2 * hp + hj
                        OT = self.psb[4 + cnt % 2]
                        npair = len(ktiles) // 2

                        def st_pair(ip):
                            STw = self.psw[ip % 2]
                            for j in range(2):
                                kt = ktiles[2 * ip + j]
                                k.op("pe", lambda e: e.matmul(STw[:, j * 512:j * 512 + n], lhsT=KT[:, hj, kt * 128:(kt + 1) * 128], rhs=QT[b][:, hj, :n],
                                                              start=True, stop=True), r=[KT, QT[b]], w=[STw])
                        st_pair(0)
                        for ip in range(npair):
                            if ip + 1 < npair:
                                st_pair(ip + 1)
                            STw = self.psw[ip % 2]
                            P = PT[ip % 2]
                            k.op("act", lambda e: e.activation(out=P[:, :, :n], in_=STw[:, :].rearrange("p (j x) -> p j x", j=2)[:, :, :n], func=AF.Exp,
                                                               scale=MLA_SCALE), r=[STw], w=[P])
                            for j in range(2):
                                kt = ktiles[2 * ip + j]
                                k.op("pe", lambda e: e.matmul(OT[0:65, :n], lhsT=Va[:, kt, hj, :], rhs=P[:, j, :n], start=(ip == 0 and j == 0),
                                                              stop=(ip == npair - 1 and j == 1)), r=[Va, P], w=[OT])
                            if filler is not None:
                                filler()
                        self.attn_finish(OT, n, Osb[cnt % 2], rec[cnt % 2], yo[cnt % 2], sel,
                                         [self.ymix[768 + h * 64:768 + (h + 1) * 64, t0:t0 + n]])
                        cnt += 1
            k.barrier()

    def phase_swa_gen(self, l, corun=False):
        k = self.k
        need_ctx = l < DEPTH - 1
        NT = TALL // 128
        with contextlib.ExitStack() as es:
            QT = k.sb(es, "sQT", [128, 2, TALL], BF16)
            KT = k.sb(es, "sKT", [128, TALL], BF16)
            Va = k.sb(es, "sVa", [128, NT, 2, 65], BF16)
            msk = k.sb(es, "smsk", [128, 2, 4, 128], BF16)
            sel = self.make_sel(es)
            sk = k.sb(es, "ssk", [1, 4], F32)
            skrow = k.sb(es, "sskrow", [1, 4, 128], BF16)
            e64 = k.sb(es, "se64", [1, 65], BF16)
            k.dma(msk[:], self.swa_mask, w=[msk], q="pool")
            k.dma(sk[:], self.swa_sink[l:l + 1, :], w=[sk])
            k.op("act", lambda e: e.activation(out=sk[:], in_=sk[:], func=AF.Exp), r=[sk], w=[sk])
            for h in range(4):
                k.op("dve", lambda e: e.tensor_scalar(out=skrow[:, h, :], in0=self.ones_bf[0:1, :], scalar1=sk[0:1, h:h + 1], scalar2=None,
                                                      op0=ALU.mult), r=[sk, self.ones_bf], w=[skrow])
            k.op("dve", lambda e: e.memset(e64[:, 0:64], 0.0), w=[e64])
            k.op("dve", lambda e: e.memset(e64[:, 64:65], 1.0), r=[e64], w=[e64])
            for c in range(2):
                k.dma(QT[:, c, :], self.s_sq[c * 128:(c + 1) * 128, :], w=[QT])
            k.dma(KT[:], self.s_sk, w=[KT])
            vsrc = self.s_sv.rearrange("(t p) f -> p t f", p=128)
            for t in range(0, NT, 6):
                k.dma(Va[:, t:t + 6].rearrange("p t h x -> p t (h x)"), vsrc[:, t:t + 6, :], w=[Va])
            PT = [k.sb(es, f"sPT{i}", [128, 4, 128], BF16) for i in range(3)]
            stb = [self.psb[6]] if corun else [self.psb[0], self.psb[1], self.psb[2]]
            otb = [self.psb[7]] if corun else [self.psb[4], self.psb[5]]
            Osb = [k.sb(es, f"sOsb{i}", [65, 512], F32) for i in range(2)]
            rec = [k.sb(es, f"srec{i}", [64, 512], F32) for i in range(2)]
            yo = [k.sb(es, f"syo{i}", [64, 512], BF16) for i in range(2)]
            cnt = 0
            yield "ready"
            for gt in range(NT):
                if gt < 2:
                    if not need_ctx:
                        continue
                    keys = [(0, None), (1, None)]
                else:
                    keys = []
                    if gt > 2:
                        keys.append((gt - 1, 0))
                    keys.append((gt, None))
                    if gt < NT - 1:
                        keys.append((gt + 1, 1))
                    keys += [(0, None), (1, None)]
                qs = slice(gt * 128, (gt + 1) * 128)
                OT = otb[cnt % len(otb)]
                nk = len(keys)

                assert not corun

                def st_mm(i):
                    STw = self.psw[i % 2]
                    kt = keys[i][0]
                    for g in range(2):
                        p0 = 64 * g
                        k.op("pe", lambda e: e.matmul(STw[:, g * 512:g * 512 + 256], lhsT=KT[p0:p0 + 64, kt * 128:(kt + 1) * 128], rhs=QT[p0:p0 + 64, :, qs],
                                                      start=True, stop=True), r=[KT, QT], w=[STw])
                st_mm(0)
                for i in range(nk):
                    if i + 1 < nk:
                        st_mm(i + 1)
                    ST = self.psw[i % 2]
                    P = PT[i % 3]
                    kt, mk = keys[i]
                    k.op("act", lambda e: e.activation(out=P[:].rearrange("p (g a) b -> p g (a b)", g=2),
                                                       in_=ST[:, :].rearrange("p (g x) -> p g x", g=2)[:, :, 0:256], func=AF.Exp, scale=SWA_SCALE),
                         r=[ST], w=[P])
                    if mk is not None:
                        k.op("dve", lambda e: e.tensor_tensor(out=P[:], in0=P[:], in1=msk[:, mk], op=ALU.mult), r=[P, msk], w=[P])
                    for g in range(2):
                        k.op("pe", lambda e: e.matmul(OT[0:65, g * 256:(g + 1) * 256], lhsT=Va[:, kt, g, :], rhs=P[:, 2 * g:2 * g + 2, :],
                                                      start=(i == 0 and g == 0), stop=False, skip_group_check=True), r=[Va, P], w=[OT])
                k.op("pe", lambda e: e.matmul(OT[0:65, 0:512], lhsT=e64[0:1, :], rhs=skrow[0:1, :, :], start=False, stop=True,
                                              skip_group_check=True), r=[e64, skrow], w=[OT])
                self.attn_finish(OT, 512, Osb[cnt % 2], rec[cnt % 2], yo[cnt % 2], sel,
                                 [self.ymix[256 + h * 64:256 + (h + 1) * 64, qs] for h in range(4)], nh=4,
                                 bc=(OT if corun else None))
                cnt += 1
                yield
            k.barrier()

    def setup_hglb(self):
        k = self.k
        es = k.es
        self.hglb = k.sb(es, "hglb", [64, 2, DEPTH, 4], F32)
        self.hgoml = k.sb(es, "hgoml", [64, 2, DEPTH, 4], F32)
        self.hgnoml = k.sb(es, "hgnoml", [64, 2, DEPTH, 4], F32)
        with contextlib.ExitStack() as es2:
            e_ = k.sb(es2, "lbe", [64, 2, DEPTH, 4], F32)
            s_ = k.sb(es2, "lbs", [64, 2, 4], F32)
            self.load_fm(es2, e_[:].rearrange("p d l h -> p (d l h)"), self.hg_lb.rearrange("d l (h x) -> (d l h) x", x=64), 32, [e_], wd=64)
            k.op("act", lambda e: e.activation(out=e_[:], in_=e_[:], func=AF.Exp), r=[e_], w=[e_])
            k.op("dve", lambda e: e.tensor_tensor(out=s_[:], in0=e_[:, :, 0, :], in1=e_[:, :, 1, :], op=ALU.add), r=[e_], w=[s_])
            for l in (2, 3):
                k.op("dve", lambda e: e.tensor_tensor(out=s_[:], in0=s_[:], in1=e_[:, :, l, :], op=ALU.add), r=[e_, s_], w=[s_])
            k.op("dve", lambda e: e.reciprocal(out=s_[:], in_=s_[:]), r=[s_], w=[s_])
            for l in range(DEPTH):
                k.op("dve", lambda e: e.tensor_tensor(out=e_[:, :, l, :], in0=e_[:, :, l, :], in1=s_[:], op=ALU.mult), r=[e_, s_], w=[e_])
            k.op("dve", lambda e: e.memset(self.hglb[:, :, 0, :], 0.0), w=[self.hglb])
            k.op("dve", lambda e: e.tensor_copy(out=self.hglb[:, :, 1, :], in_=e_[:, :, 1, :]), r=[e_, self.hglb], w=[self.hglb])
            for l in (2, 3):
                k.op("dve", lambda e: e.tensor_tensor(out=self.hglb[:, :, l, :], in0=self.hglb[:, :, l - 1, :], in1=e_[:, :, l, :], op=ALU.add),
                     r=[e_, self.hglb], w=[self.hglb])
            k.op("dve", lambda e: e.tensor_scalar(out=self.hgoml[:], in0=self.hglb[:], scalar1=-1.0, scalar2=1.0, op0=ALU.mult, op1=ALU.add),
                 r=[self.hglb], w=[self.hgoml])
            k.op("dve", lambda e: e.tensor_scalar(out=self.hgnoml[:], in0=self.hglb[:], scalar1=-1.0, scalar2=None, op0=ALU.add),
                 r=[self.hglb], w=[self.hgnoml])
            k.barrier()

    def phase_hg(self, l, filler=None):
        k = self.k
        with contextlib.ExitStack() as es:
            rmask = k.sb(es, "hrm", [64, 2048], F32)
            amask = k.sb(es, "ham", [128, 2, 4, 128], BF16)
            cmask = k.sb(es, "hcm", [128, 4], F32)
            ng = k.sb(es, "hng", [64, 1], F32)
            k.dma(rmask[:], self.hg_rmask, w=[rmask])
            k.dma(amask[:], self.hg_amask, w=[amask], q="pool")
            k.dma(cmask[:], self.hg_cmask, w=[cmask])
            self.load_fm(es, ng[:], self.hg_norm_g[l:l + 1, :], 1, [ng], wd=64)
            S = k.sb(es, "hS", [64, 4, 64], F32)
            St = k.sb(es, "hSt", [64, 4, 64], F32)
            Sbf = [k.sb(es, f"hSbf{j}", [64, 4, 64], BF16) for j in range(8)]
            names = ["z", "sg", "lf", "kk", "P", "eP", "eN"]
            bt = {nm: k.sb(es, "hb_" + nm, [64, 2048], F32) for nm in names}
            bq = k.sb(es, "hb_q", [64, 2048], BF16)
            bqd = k.sb(es, "hb_qd", [64, 2048], BF16)
            bki = k.sb(es, "hb_ki", [64, 2048], BF16)
            dec = k.sb(es, "hdec", [64, 4, 16], F32)
            vt = [k.sb(es, f"hvt{i}", [128, 256], BF16) for i in range(2)]
            kim = k.sb(es, "hkim", [128, 4, 256], BF16)
            attm = k.sb(es, "hattm", [128, 4, 128], BF16)
            obt = [k.sb(es, f"hobt{i}", [64, 4, 128], F32) for i in range(2)]
            gsl = [k.sb(es, f"hgsl{i}", [64, 4, 128], F32) for i in range(2)]
            osum = k.sb(es, "hosum", [64, 4, 128], F32)
            sqo = k.sb(es, "hsqo", [64, 512], BF16)
            rst = k.sb(es, "hrst", [64, 512], F32)
            yo = [k.sb(es, f"hyo{i}", [64, 4, 128], BF16) for i in range(2)]
            Mps = [self.psb[0], self.psb[1]]
            attps = self.psb[2]
            ops = self.psb[3]
            trp = self.psb[4]
            ssps = self.psb[5]
            trp_bf = trp[:].bitcast(BF16)
            ymix_v = self.ymix[512:768, :].rearrange("(h d) t -> d h t", d=64)
            sbi = [0]
            vti = [0]
            bqd2 = [bqd, k.sb(es, "hb_qd2", [64, 2048], BF16)]
            bki2 = [bki, k.sb(es, "hb_ki2", [64, 2048], BF16)]
            dec2 = [dec, k.sb(es, "hdec2", [64, 4, 16], F32)]
            Sx = [S, k.sb(es, "hS2", [64, 4, 64], F32)]
            Stx = [St, k.sb(es, "hSt2", [64, 4, 64], F32)]
            sidx = [0]

            def prep_groups(d, t0, n, bs):
                zsrc = (self.s_zf if d == 0 else self.s_zb).rearrange("(h d) t -> d h t", d=64)
                qsrc = self.s_hq.rearrange("(h d) t -> d h t", d=64)
                bqd_, bki_, dec_ = bqd2[bs], bki2[bs], dec2[bs]

                def V(t):
                    return t[:, 0:4 * n].rearrange("p (h t) -> p h t", h=4)
                f2 = lambda t: t[:, 0:4 * n]
                z, sg, lf, kk, P, eP, eN = [bt[nm] for nm in names]

                def g1():
                    k.dma(V(z), zsrc[:, :, t0:t0 + n], w=[z])
                    k.dma(V(bq), qsrc[:, :, t0:t0 + n], w=[bq])
                    k.op("act", lambda e: e.activation(out=f2(sg), in_=f2(z), func=AF.Sigmoid), r=[z], w=[sg])

                def g2():
                    for h in range(4):
                        k.op(HG_PREP_ENG, lambda e: e.tensor_scalar(out=V(lf)[:, h, :], in0=V(sg)[:, h, :], scalar1=self.hgoml[:, d, l, h:h + 1],
                                                              scalar2=self.hglb[:, d, l, h:h + 1], op0=ALU.mult, op1=ALU.add),
                             r=[sg, self.hgoml, self.hglb], w=[lf])
                        k.op("pool", lambda e: e.tensor_scalar(out=V(kk)[:, h, :], in0=V(sg)[:, h, :], scalar1=self.hgnoml[:, d, l, h:h + 1],
                                                               scalar2=self.hgoml[:, d, l, h:h + 1], op0=ALU.mult, op1=ALU.add),
                             r=[sg, self.hgoml, self.hgnoml], w=[kk])
                    k.op("act", lambda e: e.activation(out=f2(lf), in_=f2(lf), func=AF.Ln), r=[lf], w=[lf])

                def g3():
                    k.op("dve", lambda e: e.tensor_tensor_scan(out=f2(P), data0=f2(rmask), data1=f2(lf), initial=0.0, op0=ALU.mult, op1=ALU.add),
                         r=[rmask, lf], w=[P])
                    k.op("act", lambda e: e.activation(out=dec_[:, :, 0:n // 32], in_=V(P)[:, :, 31:n:32], func=AF.Exp), r=[P], w=[dec_])
                    if d == 1:
                        k.op("pool", lambda e: e.tensor_tensor(out=f2(P), in0=f2(P), in1=f2(lf), op=ALU.subtract), r=[P, lf], w=[P])

                def g4():
                    k.op("act", lambda e: e.activation(out=f2(eP), in_=f2(P), func=AF.Exp), r=[P], w=[eP])
                    k.op("act", lambda e: e.activation(out=f2(eN), in_=f2(P), func=AF.Exp, scale=-1.0), r=[P], w=[eN])
                    eq, ek = (eP, eN) if d == 0 else (eN, eP)
                    k.op(HG_PREP_ENG, lambda e: e.tensor_tensor(out=f2(bqd_), in0=f2(bq), in1=f2(eq), op=ALU.mult), r=[bq, eq], w=[bqd_])
                    k.op("pool", lambda e: e.tensor_tensor(out=f2(bki_), in0=f2(kk), in1=f2(ek), op=ALU.mult), r=[kk, ek], w=[bki_])
                return [g1, g2, g3, g4]

            for d in (1, 0):
                S = Sx[sidx[0] % 2]
                k.op("dve", lambda e: e.memset(S[:], 0.0), r=[S], w=[S])
                blocks = _blocks()
                if d == 1:
                    blocks = [blocks[0]] + blocks[:0:-1]
                for g_ in prep_groups(d, blocks[0][0], blocks[0][1], 0):
                    g_()
                for bix, (t0, n) in enumerate(blocks):
                    bs = bix % 2
                    pending = prep_groups(d, blocks[bix + 1][0], blocks[bix + 1][1], (bix + 1) % 2) if bix + 1 < len(blocks) else []

                    def V(t):
                        return t[:, 0:4 * n].rearrange("p (h t) -> p h t", h=4)
                    bqd, bki, dec = bqd2[bs], bki2[bs], dec2[bs]
                    qd, ki, decv = V(bqd), V(bki), dec
                    ntile = n // 128
                    for tix, ti in enumerate(range(ntile) if d == 0 else range(ntile - 1, -1, -1)):
                        if tix > 0:
                            for _ in range(4 // ntile if ntile < 4 else 1):
                                if pending:
                                    pending.pop(0)()
                        if filler is not None:
                            filler()
                        cols = slice(ti * 128, ti * 128 + 128)
                        gt0 = t0 + ti * 128
                        v = vt[vti[0] % 2]
                        ob_ = obt[vti[0] % 2]
                        gs_ = gsl[vti[0] % 2]
                        yo_ = yo[vti[0] % 2]
                        vti[0] += 1
                        k.dma(v[:], self.s_hi[gt0:gt0 + 128, :], w=[v])
                        if d == 0:
                            k.dma(ob_[:], self.s_ob[:, :, gt0:gt0 + 128], w=[ob_])
                            k.dma(gs_[:], self.s_hg.rearrange("(h d) t -> d h t", d=64)[:, :, gt0:gt0 + 128], w=[gs_])
                        for h in range(4):
                            k.op("pe", lambda e: e.transpose(out=trp_bf[:, h * 64:(h + 1) * 64], in_=ki[:, h, cols], identity=self.ident_bf[0:64, 0:64]),
                                 r=[bki, self.ident_bf], w=[trp])
                        for j in range(4):
                            k.op("dve" if j % 2 == 0 else "act", (lambda e: e.tensor_scalar(out=kim[:, j, :], in0=trp_bf[:, 0:256], scalar1=cmask[:, j:j + 1], scalar2=None, op0=ALU.mult))
                                 if j % 2 == 0 else (lambda e: e.activation(out=kim[:, j, :], in_=trp_bf[:, 0:256], func=AF.Copy, scale=cmask[:, j:j + 1])),
                                 r=[trp, cmask], w=[kim])
                        for j in range(4):
                            for h in range(4):
                                k.op("pe", lambda e: e.matmul(Mps[j // 2][0:64, (j % 2) * 256 + h * 64:(j % 2) * 256 + (h + 1) * 64],
                                                              lhsT=kim[:, j, h * 64:(h + 1) * 64], rhs=v[:, h * 64:(h + 1) * 64], start=True, stop=True),
                                     r=[kim, v], w=[Mps[j // 2]])
                        for h in range(4):
                            k.op("pe", lambda e: e.matmul(attps[:, h * 128:(h + 1) * 128], lhsT=ki[:, h, cols], rhs=qd[:, h, cols], start=True, stop=True),
                                 r=[bki, bqd], w=[attps])
                        k.op("dve", lambda e: e.tensor_tensor(out=attm[:].rearrange("p a b -> p (a b)"), in0=attps[:, 0:512],
                                                              in1=amask[:, d].rearrange("p a b -> p (a b)"), op=ALU.mult), r=[attps, amask], w=[attm])
                        for h in range(4):
                            k.op("pe", lambda e: e.matmul(ops[0:64, h * 128:(h + 1) * 128], lhsT=v[:, h * 64:(h + 1) * 64], rhs=attm[:, h, :],
                                                          start=(h == 0), stop=False, skip_group_check=True), r=[v, attm], w=[ops])
                        order = range(4) if d == 0 else range(3, -1, -1)
                        for ji, j in enumerate(order):
                            ce = ti * 4 + j
                            Mj = Mps[j // 2][0:64, (j % 2) * 256:(j % 2) * 256 + 256].rearrange("p (h x) -> p h x", h=4)
                            dec_bc = decv[:, :, ce:ce + 1].to_broadcast([64, 4, 64])
                            sb_ = Sbf[sbi[0] % 8]
                            sbi[0] += 1
                            S = Sx[sidx[0] % 2]
                            Sn = Sx[(sidx[0] + 1) % 2]
                            St = Stx[sidx[0] % 2]
                            sidx[0] += 1
                            if d == 0:
                                k.op(HG_SBF_ENG, (lambda e: e.copy(out=sb_[:], in_=S[:])) if HG_SBF_ENG == "act" else (lambda e: e.tensor_copy(out=sb_[:], in_=S[:])), r=[S], w=[sb_])
                                k.op("dve", lambda e: e.tensor_tensor(out=St[:], in0=S[:], in1=Mj, op=ALU.add), r=[S, Mps[j // 2]], w=[St])
                                k.op("dve", lambda e: e.tensor_tensor(out=Sn[:], in0=St[:], in1=dec_bc, op=ALU.mult), r=[St, dec], w=[Sn])
                            else:
                                k.op("dve", lambda e: e.tensor_tensor(out=St[:], in0=S[:], in1=dec_bc, op=ALU.mult), r=[S, dec], w=[St])
                                k.op(HG_SBF_ENG, (lambda e: e.copy(out=sb_[:], in_=St[:])) if HG_SBF_ENG == "act" else (lambda e: e.tensor_copy(out=sb_[:], in_=St[:])), r=[St], w=[sb_])
                                k.op("dve", lambda e: e.tensor_tensor(out=Sn[:], in0=St[:], in1=Mj, op=ALU.add), r=[St, Mps[j // 2]], w=[Sn])
                            for h in range(4):
                                last = (ji == 3 and h == 3)
                                k.op("pe", lambda e: e.matmul(ops[0:64, h * 128 + j * 32:h * 128 + (j + 1) * 32], lhsT=sb_[:, h, :],
                                                              rhs=qd[:, h, ti * 128 + j * 32:ti * 128 + (j + 1) * 32], start=False, stop=last,
                                                              skip_group_check=True), r=[sb_, bqd], w=[ops])
                        opsv = ops[0:64, 0:512].rearrange("p (h t) -> p h t", h=4)
                        if d == 1:
                            k.op("act", lambda e: e.copy(out=ob_[:], in_=opsv), r=[ops], w=[ob_])
                            k.dma(self.s_ob[:, :, gt0:gt0 + 128], ob_[:], r=[ob_])
                        else:
                            k.op("dve", lambda e: e.tensor_tensor(out=osum[:], in0=opsv, in1=ob_[:], op=ALU.add), r=[ops, ob_], w=[osum])
                            o2 = osum[:].rearrange("p h t -> p (h t)")
                            k.op("pool", lambda e: e.tensor_tensor(out=sqo[:], in0=o2, in1=o2, op=ALU.mult), r=[osum], w=[sqo])
                            k.op("pe", lambda e: e.matmul(ssps[0:64, 0:512], lhsT=self.ones_bf[0:64, 0:64], rhs=sqo[:], start=True, stop=True),
                                 r=[sqo, self.ones_bf], w=[ssps])
                            k.op("act", lambda e: e.activation(out=rst[:], in_=ssps[0:64, 0:512], func=AF.Sqrt, bias=self.epsb[0:64, :], scale=1.0 / 64),
                                 r=[ssps, self.epsb], w=[rst])
                            k.op("dve", lambda e: e.reciprocal(out=rst[:], in_=rst[:]), r=[rst], w=[rst])
                            k.op("dve", lambda e: e.scalar_tensor_tensor(out=o2, in0=o2, scalar=ng[:, 0:1], in1=rst[:], op0=ALU.mult, op1=ALU.mult),
                                 r=[osum, ng, rst], w=[osum])
                            k.op("pool", lambda e: e.tensor_tensor(out=yo_[:], in0=osum[:], in1=gs_[:], op=ALU.mult), r=[osum, gs_], w=[yo_])
                            k.dma(ymix_v[:, :, gt0:gt0 + 128], yo_[:], r=[yo_])
                    while pending:
                        pending.pop(0)()
                k.barrier()

    def phase_s5_gen(self, l):
        k = self.k
        need_ctx = l < DEPTH - 1
        NC_ = TALL // 8
        mul, add, sub = ALU.mult, ALU.add, ALU.subtract
        with contextlib.ExitStack() as es:
            Tm = k.sb(es, "5Tm", [128, 32, 128], BF16)
            Gt = k.sb(es, "5Gt", [128, 32, 2, 64], BF16)
            Er = k.sb(es, "5Er", [64, 32, 128], BF16)
            Ei = k.sb(es, "5Ei", [64, 32, 128], BF16)
            A8 = k.sb(es, "5A8", [64, 4, 32], F32)
            U = k.sb(es, "5U", [128, 16, NC_], BF16)
            tmask = k.sb(es, "5tmask", [128, 2, 128], F32)
            k.dma(tmask[:], self.s5_tmask, w=[tmask])
            with contextlib.ExitStack() as e2:
                def t32(nm):
                    return k.sb(e2, nm, [64, 32], F32)
                lre, lim, ldt = t32("lre"), t32("lim"), t32("ldt")
                for dst, src in ((lre, self.s5_lam_re), (lim, self.s5_lam_im), (ldt, self.s5_log_dt)):
                    self.load_fm(e2, dst[:], src[l].rearrange("d g p -> (d g) p"), 32, [dst], wd=64)
                Bre = k.sb(e2, "Bre", [64, 32, 16], F32)
                Bim = k.sb(e2, "Bim", [64, 32, 16], F32)
                Cre = k.sb(e2, "Cre", [64, 32, 16], F32)
                Cim = k.sb(e2, "Cim", [64, 32, 16], F32)
                for dst, src in ((Bre, self.s5_b_re), (Bim, self.s5_b_im)):
                    for d in range(2):
                        k.dma(dst[:, d * 16:(d + 1) * 16, :], src[l, d].rearrange("g p h -> p g h"), w=[dst], allow_slow_non_contiguous=True)
                for dst, src in ((Cre, self.s5_c_re), (Cim, self.s5_c_im)):
                    for q4 in range(4):
                        self.load_fm(e2, dst[:].rearrange("p a h -> p (a h)")[:, q4 * 128:(q4 + 1) * 128],
                                     src[l].rearrange("d g h p -> (d g h) p")[q4 * 128:(q4 + 1) * 128, :], 128, [dst], wd=64)
                dt_, mag, th, c16, s16 = t32("dt"), t32("mag"), t32("th"), t32("c16"), t32("s16")
                t1, t2 = t32("t1"), t32("t2")
                k.op("act", lambda e: e.activation(out=dt_[:], in_=ldt[:], func=AF.Exp), r=[ldt], w=[dt_])
                k.op("dve", lambda e: e.tensor_tensor(out=mag[:], in0=lre[:], in1=dt_[:], op=mul), r=[lre, dt_], w=[mag])
                k.op("dve", lambda e: e.tensor_tensor(out=th[:], in0=lim[:], in1=dt_[:], op=mul), r=[lim, dt_], w=[th])
                k.op("act", lambda e: e.activation(out=mag[:], in_=mag[:], func=AF.Exp, scale=1.0 / 16), r=[mag], w=[mag])
                halfpi = k.sb(e2, "halfpi", [64, 1], F32)
                k.op("dve", lambda e: e.memset(halfpi[:], math.pi / 2), w=[halfpi])
                k.op("act", lambda e: e.activation(out=s16[:], in_=th[:], func=AF.Sin, scale=1.0 / 16), r=[th], w=[s16])
                k.op("act", lambda e: e.activation(out=c16[:], in_=th[:], func=AF.Sin, scale=1.0 / 16, bias=halfpi[:]), r=[th, halfpi], w=[c16])
                are, aim = t32("are"), t32("aim")
                k.op("dve", lambda e: e.tensor_tensor(out=are[:], in0=mag[:], in1=c16[:], op=mul), r=[mag, c16], w=[are])
                k.op("dve", lambda e: e.tensor_tensor(out=aim[:], in0=mag[:], in1=s16[:], op=mul), r=[mag, s16], w=[aim])

                def cmul(ore, oim, xr, xi, yr, yi, rk, wk, tA, tB):
                    k.op("dve", lambda e: e.tensor_tensor(out=tA, in0=xr, in1=yr, op=mul), r=rk, w=[wk[2]])
                    k.op("dve", lambda e: e.tensor_tensor(out=tB, in0=xi, in1=yi, op=mul), r=rk, w=[wk[3]])
                    k.op("dve", lambda e: e.tensor_tensor(out=tB, in0=tA, in1=tB, op=sub), r=[wk[2], wk[3]], w=[wk[3]])
                    k.op("dve", lambda e: e.tensor_tensor(out=tA, in0=xr, in1=yi, op=mul), r=rk, w=[wk[2]])
                    k.op("dve", lambda e: e.tensor_tensor(out=oim, in0=xi, in1=yr, op=mul), r=rk, w=[wk[1]])
                    k.op("dve", lambda e: e.tensor_tensor(out=oim, in0=oim, in1=tA, op=add), r=[wk[1], wk[2]], w=[wk[1]])
                    k.op("dve", lambda e: e.tensor_copy(out=ore, in_=tB), r=[wk[3]], w=[wk[0]])

                for _ in range(4):
                    cmul(are[:], aim[:], are[:], aim[:], are[:], aim[:], [are, aim], [are, aim, t1, t2], t1[:], t2[:])
                cfr, cfi, den = t32("cfr"), t32("cfi"), t32("den")
                k.op("dve", lambda e: e.tensor_tensor(out=den[:], in0=lre[:], in1=lre[:], op=mul), r=[lre], w=[den])
                k.op("dve", lambda e: e.tensor_tensor(out=t1[:], in0=lim[:], in1=lim[:], op=mul), r=[lim], w=[t1])
                k.op("dve", lambda e: e.tensor_tensor(out=den[:], in0=den[:], in1=t1[:], op=add), r=[den, t1], w=[den])
                k.op("dve", lambda e: e.reciprocal(out=den[:], in_=den[:]), r=[den], w=[den])
                am1 = t32("am1")
                nlim = t32("nlim")
                k.op("dve", lambda e: e.tensor_scalar(out=am1[:], in0=are[:], scalar1=-1.0, scalar2=None, op0=add), r=[are], w=[am1])
                k.op("dve", lambda e: e.tensor_scalar(out=nlim[:], in0=lim[:], scalar1=-1.0, scalar2=None, op0=mul), r=[lim], w=[nlim])
                cmul(cfr[:], cfi[:], am1[:], aim[:], lre[:], nlim[:], [am1, aim, lre, nlim], [cfr, cfi, t1, t2], t1[:], t2[:])
                k.op("dve", lambda e: e.tensor_tensor(out=cfr[:], in0=cfr[:], in1=den[:], op=mul), r=[cfr, den], w=[cfr])
                k.op("dve", lambda e: e.tensor_tensor(out=cfi[:], in0=cfi[:], in1=den[:], op=mul), r=[cfi, den], w=[cfi])
                ire, iim = t32("ire"), t32("iim")
                k.op("dve", lambda e: e.tensor_tensor(out=den[:], in0=are[:], in1=are[:], op=mul), r=[are], w=[den])
                k.op("dve", lambda e: e.tensor_tensor(out=t1[:], in0=aim[:], in1=aim[:], op=mul), r=[aim], w=[t1])
                k.op("dve", lambda e: e.tensor_tensor(out=den[:], in0=den[:], in1=t1[:], op=add), r=[den, t1], w=[den])
                k.op("dve", lambda e: e.reciprocal(out=den[:], in_=den[:]), r=[den], w=[den])
                k.op("dve", lambda e: e.tensor_tensor(out=ire[:], in0=are[:], in1=den[:], op=mul), r=[are, den], w=[ire])
                k.op("dve", lambda e: e.scalar_tensor_tensor(out=iim[:], in0=aim[:], scalar=-1.0, in1=den[:], op0=mul, op1=mul), r=[aim, den], w=[iim])
                def t512(nm):
                    return k.sb(e2, nm, [64, 32, 16], F32)
                Bbr, Bbi, u1, u2, xr, xi = t512("Bbr"), t512("Bbi"), t512("u1"), t512("u2"), t512("xr"), t512("xi")
                bc = lambda t: t[:].unsqueeze(2).to_broadcast([64, 32, 16])
                cmul(Bbr[:], Bbi[:], bc(cfr), bc(cfi), Bre[:], Bim[:], [cfr, cfi, Bre, Bim], [Bbr, Bbi, u1, u2], u1[:], u2[:])
                Lr = k.sb(e2, "Lr", [64, 32, 8, 16], F32)
                Li = k.sb(e2, "Li", [64, 32, 8, 16], F32)
                Rr = k.sb(e2, "Rr", [64, 32, 8, 16], F32)
                Ri = k.sb(e2, "Ri", [64, 32, 8, 16], F32)
                Gr = k.sb(e2, "Gr", [64, 32, 8, 16], F32)
                Gi = k.sb(e2, "Gi", [64, 32, 8, 16], F32)
                Erv = Er[:].rearrange("p a (t h) -> p a t h", h=16)
                Eiv = Ei[:].rearrange("p a (t h) -> p a t h", h=16)
                pr, pi_, qr, qi = t32("pr"), t32("pi"), t32("qr"), t32("qi")
                k.op("dve", lambda e: e.memset(pr[:], 1.0), w=[pr])
                k.op("dve", lambda e: e.memset(pi_[:], 0.0), w=[pi_])
                k.op("dve", lambda e: e.memset(qr[:], 1.0), w=[qr])
                k.op("dve", lambda e: e.memset(qi[:], 0.0), w=[qi])
                F_, Bk = slice(0, 16), slice(16, 32)
                for kk_ in range(9):
                    if kk_ > 0:
                        cmul(pr[:], pi_[:], pr[:], pi_[:], are[:], aim[:], [pr, pi_, are, aim], [pr, pi_, t1, t2], t1[:], t2[:])
                    if kk_ <= 7:
                        cmul(xr[:], xi[:], bc(pr), bc(pi_), Bbr[:], Bbi[:], [pr, pi_, Bbr, Bbi], [xr, xi, u1, u2], u1[:], u2[:])
                        for (dst, src) in ((Gr, xr), (Gi, xi)):
                            k.op("pool", lambda e: e.tensor_copy(out=dst[:, F_, 7 - kk_, :], in_=src[:, F_, :]), r=[src, dst], w=[dst])
                            k.op("pool", lambda e: e.tensor_copy(out=dst[:, Bk, kk_, :], in_=src[:, Bk, :]), r=[src, dst], w=[dst])
                    cmul(xr[:], xi[:], bc(pr), bc(pi_), Cre[:], Cim[:], [pr, pi_, Cre, Cim], [xr, xi, u1, u2], u1[:], u2[:])
                    if kk_ <= 7:
                        k.op("pool", lambda e: e.tensor_copy(out=Rr[:, F_, kk_, :], in_=xr[:, F_, :]), r=[xr, Rr], w=[Rr])
                        k.op("pool", lambda e: e.tensor_copy(out=Rr[:, Bk, 7 - kk_, :], in_=xr[:, Bk, :]), r=[xr, Rr], w=[Rr])
                        k.op("pool", lambda e: e.tensor_scalar(out=Ri[:, F_, kk_, :], in0=xi[:, F_, :], scalar1=-1.0, scalar2=None, op0=mul), r=[xi, Ri], w=[Ri])
                        k.op("pool", lambda e: e.tensor_scalar(out=Ri[:, Bk, 7 - kk_, :], in0=xi[:, Bk, :], scalar1=-1.0, scalar2=None, op0=mul), r=[xi, Ri], w=[Ri])
                    if kk_ >= 1:
                        k.op("act", lambda e: e.copy(out=Erv[:, F_, kk_ - 1, :], in_=xr[:, F_, :]), r=[xr, Er], w=[Er])
                        k.op("act", lambda e: e.copy(out=Erv[:, Bk, 8 - kk_, :], in_=xr[:, Bk, :]), r=[xr, Er], w=[Er])
                        k.op("act", lambda e: e.activation(out=Eiv[:, F_, kk_ - 1, :], in_=xi[:, F_, :], func=AF.Copy, scale=-1.0), r=[xi, Ei], w=[Ei])
                        k.op("act", lambda e: e.activation(out=Eiv[:, Bk, 8 - kk_, :], in_=xi[:, Bk, :], func=AF.Copy, scale=-1.0), r=[xi, Ei], w=[Ei])
                    if kk_ == 8:
                        k.op("dve", lambda e: e.tensor_copy(out=A8[:, 0, :], in_=pr[:]), r=[pr], w=[A8])
                        k.op("dve", lambda e: e.tensor_copy(out=A8[:, 1, :], in_=pr[:]), r=[pr, A8], w=[A8])
                        k.op("dve", lambda e: e.tensor_scalar(out=A8[:, 2, :], in0=pi_[:], scalar1=-1.0, scalar2=None, op0=mul), r=[pi_, A8], w=[A8])
                        k.op("dve", lambda e: e.tensor_copy(out=A8[:, 3, :], in_=pi_[:]), r=[pi_, A8], w=[A8])
                    if kk_ <= 7:
                        if kk_ > 0:
                            cmul(qr[:], qi[:], qr[:], qi[:], ire[:], iim[:], [qr, qi, ire, iim], [qr, qi, t1, t2], t1[:], t2[:])
                        cmul(xr[:], xi[:], bc(qr), bc(qi), Bbr[:], Bbi[:], [qr, qi, Bbr, Bbi], [xr, xi, u1, u2], u1[:], u2[:])
                        for (dst, src) in ((Lr, xr), (Li, xi)):
                            k.op("pool", lambda e: e.tensor_copy(out=dst[:, F_, kk_, :], in_=src[:, F_, :]), r=[src, dst], w=[dst])
                            k.op("pool", lambda e: e.tensor_copy(out=dst[:, Bk, 7 - kk_, :], in_=src[:, Bk, :]), r=[src, dst], w=[dst])
                fl = lambda t: t[:].rearrange("p a j h -> p a (j h)")
                for dg in range(32):
                    d = dg // 16
                    ps = self.psb[dg % 2]
                    k.op("pe", lambda e: e.matmul(ps[:, 0:128], lhsT=fl(Lr)[:, dg, :], rhs=fl(Rr)[:, dg, :], start=True, stop=False), r=[Lr, Rr], w=[ps])
                    k.op("pe", lambda e: e.matmul(ps[:, 0:128], lhsT=fl(Li)[:, dg, :], rhs=fl(Ri)[:, dg, :], start=False, stop=True), r=[Li, Ri], w=[ps])
                    k.op("dve", lambda e: e.tensor_tensor(out=Tm[:, dg, :], in0=ps[:, 0:128], in1=tmask[:, d, :], op=mul), r=[ps, tmask], w=[Tm])
                    ps2 = self.psb[2 + dg % 2]
                    k.op("pe", lambda e: e.transpose(out=ps2[:, 0:64], in_=fl(Gr)[:, dg, :], identity=self.ident_f[0:64, 0:64]), r=[Gr, self.ident_f], w=[ps2])
                    k.op("pe", lambda e: e.transpose(out=ps2[:, 64:128], in_=fl(Gi)[:, dg, :], identity=self.ident_f[0:64, 0:64]), r=[Gi, self.ident_f], w=[ps2])
                    k.op("act", lambda e: e.copy(out=Gt[:, dg].rearrange("p a b -> p (a b)"), in_=ps2[:, 0:128]), r=[ps2], w=[Gt])
                k.barrier()
            CBM = 64
            eu = contextlib.ExitStack()
            utok = k.sb(eu, "5utok", [128, 8, 256], F32)
            ub = k.sb(eu, "5ub", [128, 8, 256], BF16)
            ublocks = [(0, 32)] + [(32 + 128 * j, 128) for j in range(8)]
            trp = self.psb[4]
            trp_bf = trp[:].bitcast(BF16)
            usrc = self.s_u.rearrange("(c t) f -> c t f", t=8)
            for (c0, cb) in ublocks:
                k.dma(utok[0:cb], usrc[c0:c0 + cb], w=[utok])
                k.op("dve", lambda e: e.tensor_copy(out=ub[0:cb].rearrange("c a b -> c (a b)").rearrange("c (g t h) -> c g t h", g=16, t=8),
                                                    in_=utok[0:cb].rearrange("c t (g h) -> c g t h", g=16)), r=[utok], w=[ub])
                for g8 in range(2):
                    for gi in range(8):
                        g = g8 * 8 + gi
                        k.op("pe", lambda e: e.transpose(out=trp_bf[:, gi * 128:gi * 128 + cb], in_=ub[0:cb].rearrange("c a b -> c (a b)")[:, g * 128:(g + 1) * 128],
                                                         identity=self.ident_bf[0:cb, 0:cb]), r=[ub, self.ident_bf], w=[trp])
                    k.op("dve", lambda e: e.tensor_copy(out=U[:, g8 * 8:(g8 + 1) * 8, c0:c0 + cb],
                                                        in_=trp_bf[:, 0:1024].rearrange("p (g c) -> p g c", g=8)[:, :, 0:cb]), r=[trp], w=[U])
            k.barrier()
            eu.close()
            em = contextlib.ExitStack()
            Wt = k.sb(em, "5W", [64, 2, 32, CBM], F32)
            SP = [k.sb(em, f"5SP{d}", [64, 2, 16, CBM], BF16) for d in range(2)]
            Hist = [k.sb(em, f"5H{d}", [64, CBM + 1, 3, 16], F32) for d in range(2)]
            Tt = [k.sb(em, f"5T{d}", [64, 2, 16], F32) for d in range(2)]
            Vt = [k.sb(em, f"5V{d}", [64, 2, 16], F32) for d in range(2)]
            yev = [k.sb(em, f"5yev{i}", [128, 512], F32) for i in range(2)]
            blocks = [(0, 32)] + [(32 + CBM * j, CBM) for j in range((NC_ - 32) // CBM)]
            border = [blocks, [blocks[0]] + blocks[:0:-1]]
            A8v = [[A8[:, 0:2, d * 16:(d + 1) * 16], A8[:, 2:4, d * 16:(d + 1) * 16]] for d in range(2)]
            engs = ("dve", "pool")
            psS = self.psb[7]
            nbs = len(blocks)
            self.s5_nitems = sum(16 + 8 + b_[1] + 1 for b_ in blocks)
            yield "ready"

            def w_group(bs, d, g4, ri):
                cb = border[0][bs][1]
                cds = [border[0][bs][0], border[1][bs][0]]
                for gi in range(4):
                    g = g4 * 4 + gi
                    k.op("pe", lambda e: e.matmul(psS[0:64, gi * 128:gi * 128 + cb], lhsT=Gt[:, d * 16 + g, ri, :], rhs=U[:, g, cds[d]:cds[d] + cb],
                                                  start=True, stop=True), r=[Gt, U], w=[psS])
                k.op("dve", lambda e: e.tensor_copy(out=Wt[:, ri, d * 16 + g4 * 4:d * 16 + g4 * 4 + 4, 0:cb],
                                                    in_=psS[0:64, 0:512].rearrange("p (g c) -> p g c", g=4)[:, :, 0:cb]), r=[psS], w=[Wt])

            def y_group(bs, d, g4):
                cb = border[0][bs][1]
                cds = [border[0][bs][0], border[1][bs][0]]
                for gi in range(4):
                    g = g4 * 4 + gi
                    o = psS[:, gi * 128:gi * 128 + cb]
                    k.op("pe", lambda e: e.matmul(o, lhsT=Tm[:, d * 16 + g, :], rhs=U[:, g, cds[d]:cds[d] + cb], start=(gi == 0), stop=False,
                                                  skip_group_check=True), r=[Tm, U], w=[psS])
                    k.op("pe", lambda e: e.matmul(o, lhsT=Er[:, d * 16 + g, :], rhs=SP[d][:, 0, g, 0:cb], start=False, stop=False,
                                                  skip_group_check=True), r=[Er, SP[d]], w=[psS])
                    k.op("pe", lambda e: e.matmul(o, lhsT=Ei[:, d * 16 + g, :], rhs=SP[d][:, 1, g, 0:cb], start=False, stop=True,
                                                  skip_group_check=True), r=[Ei, SP[d]], w=[psS])
                ye = yev[g4 % 2]
                k.op("dve", lambda e: e.tensor_copy(out=ye[:], in_=psS[:, 0:512]), r=[psS], w=[ye])
                k.dma(self.s_y[d][:, g4 * 4:(g4 + 1) * 4, cds[d]:cds[d] + cb], ye[:].rearrange("p (g c) -> p g c", g=4)[:, :, 0:cb], r=[ye])

            prev_cb = None
            for bs in range(nbs):
                cb = border[0][bs][1]
                for d in range(2):
                    dst = Hist[d][:, 0] if d == 0 else Hist[d][:, cb]
                    if bs == 0:
                        k.op(engs[d], lambda e: e.memset(dst, 0.0), r=[Hist[d]], w=[Hist[d]])
                    else:
                        src = Hist[d][:, prev_cb] if d == 0 else Hist[d][:, 0]
                        k.op(engs[d], lambda e: e.tensor_copy(out=dst, in_=src), r=[Hist[d]], w=[Hist[d]])
                for d in range(2):
                    for g4 in range(4):
                        for ri in range(2):
                            w_group(bs, d, g4, ri)
                            yield
                if bs > 0:
                    for d in range(2):
                        for g4 in range(4):
                            y_group(bs - 1, d, g4)
                            yield
                prev_cb = cb
                for i in range(cb):
                    for d, eng in ((0, "dve"), (1, "pool")):
                        H, T_, V_ = Hist[d], Tt[d], Vt[d]
                        if d == 0:
                            col, pv, cu = i, i, i + 1
                        else:
                            col, pv, cu = cb - 1 - i, cb - i, cb - 1 - i
                        k.op(eng, lambda e: e.tensor_tensor(out=T_[:], in0=H[:, pv, 0:2, :], in1=A8v[d][0], op=mul), r=[H, A8], w=[T_])
                        k.op(eng, lambda e: e.tensor_tensor(out=V_[:], in0=H[:, pv, 1:3, :], in1=A8v[d][1], op=mul), r=[H, A8], w=[V_])
                        k.op(eng, lambda e: e.tensor_tensor(out=T_[:], in0=T_[:], in1=V_[:], op=add), r=[T_, V_], w=[T_])
                        k.op(eng, lambda e: e.tensor_tensor(out=H[:, cu, 0:2, :], in0=T_[:], in1=Wt[:, :, d * 16:(d + 1) * 16, col], op=add),
                             r=[T_, Wt, H], w=[H])
                        k.op(eng, lambda e: e.tensor_copy(out=H[:, cu, 2, :], in_=H[:, cu, 0, :]), r=[H], w=[H])
                    yield
                for d in range(2):
                    lo = 0 if d == 0 else 1
                    k.op("pool", lambda e: e.tensor_copy(out=SP[d][:, :, :, 0:cb].rearrange("p r g c -> p c r g"), in_=Hist[d][:, lo:lo + cb, 0:2, :]),
                         r=[Hist[d]], w=[SP[d]])
                yield
            for d in range(2):
                for g4 in range(4):
                    y_group(nbs - 1, d, g4)
                    yield
            k.barrier()
            em.close()
            with contextlib.ExitStack() as e3:
                utok = k.sb(e3, "5utok2", [128, 8, 256], F32)
                D8 = k.sb(e3, "5D8", [128, 8, 256], F32)
                wg = k.sb(e3, "5wg", [128, 2, 256], BF16)
                bg = k.sb(e3, "5bg", [128, 2], F32)
                for t in range(8):
                    k.dma(D8[:, t, :], self.s5_d[l:l + 1, :].partition_broadcast(128) if False else self.s5_d[l].partition_broadcast(128), w=[D8])
                k.dma(wg[:], self.s5_w_glu[l].rearrange("(c p) n -> p c n", p=128), w=[wg], q="pool")
                self.load_fm(e3, bg[:], self.s5_b_glu[l].rearrange("(c p) -> c p", p=128), 2, [bg])
                yf = k.sb(e3, "5yf", [128, 16, 128], F32)
                yb = k.sb(e3, "5yb", [128, 16, 128], F32)
                ytok = k.sb(e3, "5ytok", [128, 8, 256], F32)
                ygel = k.sb(e3, "5ygel", [128, 8, 256], BF16)
                yT = k.sb(e3, "5yT", [128, 2, 1024], BF16)
                sgm = [k.sb(e3, f"5sg{i}", [128, 512], F32) for i in range(2)]
                yao = [k.sb(e3, f"5ya{i}", [128, 512], BF16) for i in range(2)]
                for (c0, cb) in blocks:
                    if c0 == 0 and not need_ctx:
                        continue
                    ntok = cb * 8
                    k.dma(utok[0:cb], usrc[c0:c0 + cb], w=[utok])
                    k.dma(yf[:, :, 0:cb], self.s_y[0][:, :, c0:c0 + cb], w=[yf])
                    k.dma(yb[:, :, 0:cb], self.s_y[1][:, :, c0:c0 + cb], w=[yb])
                    k.op("pool", lambda e: e.tensor_tensor(out=yf[:, :, 0:cb], in0=yf[:, :, 0:cb], in1=yb[:, :, 0:cb], op=add), r=[yf, yb], w=[yf])
                    k.op("dve", lambda e: e.tensor_tensor(out=ytok[0:cb], in0=utok[0:cb], in1=D8[0:cb], op=mul), r=[utok, D8], w=[ytok])
                    for g4 in range(4):
                        ps = self.psb[g4 % 2]
                        for gi in range(4):
                            g = g4 * 4 + gi
                            k.op("pe", lambda e: e.transpose(out=ps[0:cb, gi * 128:(gi + 1) * 128], in_=yf[:, g, 0:cb], identity=self.ident_f[:, :]),
                                 r=[yf, self.ident_f], w=[ps])
                        yv = ytok[0:cb, :, g4 * 64:(g4 + 1) * 64].rearrange("c t (g h) -> c g t h", g=4)
                        pv = ps[0:cb, 0:512].rearrange("c (g t h) -> c g t h", g=4, t=8)
                        k.op("dve", lambda e: e.tensor_tensor(out=yv, in0=pv, in1=yv, op=add), r=[ps, ytok], w=[ytok])
                    k.op("act", lambda e: e.activation(out=ygel[0:cb], in_=ytok[0:cb], func=AF.Gelu), r=[ytok], w=[ygel])
                    for kc in range(2):
                        for t in range(8):
                            k.op("pe", lambda e: e.transpose(out=trp_bf[:, t * 128:t * 128 + cb], in_=ygel[0:cb, t, kc * 128:(kc + 1) * 128],
                                                             identity=self.ident_bf[0:cb, 0:cb]), r=[ygel, self.ident_bf], w=[trp])
                        k.op("dve", lambda e: e.tensor_copy(out=yT[:, kc, 0:ntok].rearrange("p (c t) -> p t c", t=8),
                                                            in_=trp_bf[:, 0:1024].rearrange("p (t c) -> p t c", t=8)[:, :, 0:cb]), r=[trp], w=[yT])
                    for r0 in range(0, ntok, 512):
                        n = min(512, ntok - r0)
                        for oc in range(2):
                            ps = self.psb[2 + oc]
                            for kc in range(2):
                                k.op("pe", lambda e: e.matmul(ps[:, 0:n], lhsT=wg[:, kc, oc * 128:(oc + 1) * 128], rhs=yT[:, kc, r0:r0 + n],
                                                              start=(kc == 0), stop=(kc == 1)), r=[wg, yT], w=[ps])
                            k.op("act", lambda e: e.activation(out=sgm[oc][:, 0:n], in_=ps[:, 0:n], func=AF.Sigmoid, bias=bg[:, oc:oc + 1]),
                                 r=[ps, bg], w=[sgm[oc]])
                            k.op("dve", lambda e: e.tensor_tensor(out=yao[oc][:, 0:n], in0=yT[:, oc, r0:r0 + n], in1=sgm[oc][:, 0:n], op=mul),
                                 r=[yT, sgm[oc]], w=[yao[oc]])
                            tok0 = c0 * 8 + r0
                            k.dma(self.ymix[oc * 128:(oc + 1) * 128, tok0:tok0 + n], yao[oc][:, 0:n], r=[yao[oc]])
                k.barrier()

def _prep_inputs(inputs):
    f = lambda a: np.ascontiguousarray(np.asarray(a, dtype=np.float32))
    cols = _win_cols()
    qcols = _wqb_cols()
    rs, rm = _rope_tables()
    shared = {
        "w_mod": f(inputs["w_mod"]), "b_mod": f(inputs["b_mod"]), "norm1_g": f(inputs["norm1_g"]), "norm2_g": f(inputs["norm2_g"]),
        "w_in": f(np.asarray(inputs["w_in"])[:, :, cols]), "w_out": f(inputs["w_out"]),
        "w_qb": f(np.asarray(inputs["mla_w_qb"])[:, :, qcols]), "w_kvb": f(inputs["mla_w_kvb"]),
        "qn_g": f(inputs["mla_q_norm_g"]), "kvn_g": f(inputs["mla_kv_norm_g"]),
        "w_up": f(inputs["ffn_w_up"]), "w_down": f(inputs["ffn_w_down"]), "final_g": f(inputs["final_norm_g"]),
        "rope_s": rs, "rope_m": rm, "ident": np.eye(128, dtype=np.float32),
        "swa_mask": _swa_mask(), "swa_sink": f(inputs["swa_sink"]),
        "hg_lb": f(inputs["hg_lb"]), "s5_tmask": _s5_tmask(),
        **{kk: f(inputs[kk]) for kk in ("s5_lam_re", "s5_lam_im", "s5_log_dt", "s5_b_re", "s5_b_im", "s5_c_re", "s5_c_im", "s5_d", "s5_w_glu", "s5_b_glu")}, "hg_norm_g": f(inputs["hg_norm_g"]), **_hg_consts(),
    }
    x = np.asarray(inputs["x"]); ctx = np.asarray(inputs["ctx"]); c = np.asarray(inputs["c"]); c_ctx = np.asarray(inputs["c_ctx"])
    maps = []
    for b in range(NCORES):
        m = dict(shared)
        m["xin"] = f(np.concatenate([ctx[b], x[b]], 0).T)
        m["cc"] = f(np.stack([c[b], c_ctx], 0))
        maps.append(m)
    return maps


def kernel(**inputs):
    bld = Builder()
    maps = _prep_inputs(inputs)
    res = run_bass_kernel_spmd(bld.nc, maps, core_ids=list(range(NCORES)))
    out = np.stack([np.ascontiguousarray(r["out"].T) for r in res.results], 0)
    return out.astype(np.float32)
```

```python
import contextlib
import math
import os
import numpy as np
import concourse.bass as bass
import concourse.mybir as mybir
from concourse.bass_utils import run_bass_kernel_spmd

F32 = mybir.dt.float32
BF16 = mybir.dt.bfloat16
ALU = mybir.AluOpType
AF = mybir.ActivationFunctionType

D = 1024
SEQ = 8192
CTX = 256
TALL = SEQ + CTX
DEPTH = 4
NCORES = 4
FFH = 2816
EPS = 1e-6
GRID_W = 64
MLA_SCALE = 96 ** -0.5
SWA_SCALE = 64 ** -0.5
SAME_ENGINE_SYNC = True
HG_PREP_ENG = os.environ.get("KHGP", "dve")
HG_SBF_ENG = os.environ.get("KHGS", "act")
OVERLAP_SWA = os.environ.get("KOVS", "0") == "1"
OVERLAP_S5 = os.environ.get("KOVL", "1") == "1"
ATTACH_WAIT = os.environ.get("KATTACH", "1") == "1"


class T:
    _n = 0

    def __init__(self, t, name, psum=False):
        self.t = t
        self.psum = psum
        T._n += 1
        self.key = (name, T._n)

    def __getitem__(self, idx):
        return self.t[idx]


class PV(T):
    def __init__(self, base, off, name):
        T.__init__(self, base.t, name, psum=True)
        self.off = off

    def _c(self, c):
        if isinstance(c, slice):
            a = self.off + (c.start or 0)
            b = self.off + (512 if c.stop is None else c.stop)
            return slice(a, b, c.step)
        return self.off + c

    def __getitem__(self, idx):
        if isinstance(idx, tuple):
            return self.t[(idx[0], self._c(idx[1])) + tuple(idx[2:])]
        return self.t[idx, self.off:self.off + 512]


class KB:
    def __init__(self, nc):
        self.nc = nc
        self.es = contextlib.ExitStack()
        self.eng = {"pe": nc.tensor, "act": nc.scalar, "dve": nc.vector, "pool": nc.gpsimd, "sp": nc.sync}
        self.sem = {e: self.es.enter_context(nc.semaphore("s_" + e)) for e in self.eng}
        self.cnt = {e: 0 for e in self.eng}
        self.lanes = {}
        self.lane_val = {}
        self.lane_rr = {}
        for q, n in (("sp", int(os.environ.get("KLANES", "12"))), ("pool", 8), ("act", 4)):
            self.lanes[q] = [self.es.enter_context(nc.semaphore(f"l_{q}{i}")) for i in range(n)]
            self.lane_rr[q] = 0
            for i in range(n):
                self.lane_val[(q, i)] = 0
        self.seen = {e: {} for e in self.eng}
        self.res = {}
        self.ninst = 0
        self.nwait = 0
        self.uid = 0

    def sb(self, es, name, shape, dtype):
        self.uid += 1
        t = es.enter_context(self.nc.sbuf_tensor(f"{name}_{self.uid}", list(shape), dtype))
        return T(t, name)

    def ps(self, es, name, shape, dtype=F32):
        self.uid += 1
        t = es.enter_context(self.nc.psum_tensor(f"{name}_{self.uid}", list(shape), dtype))
        return T(t, name, psum=True)

    def _semof(self, src):
        if src[0] == "eng":
            return self.sem[src[1]]
        return self.lanes[src[1]][src[2]]

    def _wait(self, engine, dep):
        src, val = dep
        if val <= 0:
            return
        if src[0] == "eng" and src[1] == engine:
            if engine == "pe" or not SAME_ENGINE_SYNC:
                return
        if self.seen[engine].get(src, 0) >= val:
            return
        self.eng[engine].wait_ge(self._semof(src), val)
        self.nwait += 1
        self.seen[engine][src] = val

    def _deps(self, r, w, me=None):
        deps = []
        for t in r:
            st = self.res.get(t.key)
            if st and st["w"]:
                deps.append(st["w"])
            if st and t.psum:
                deps.extend((src, v) for src, v in st["r"].items() if src != me)
        for t in w:
            st = self.res.get(t.key)
            if st:
                if st["w"]:
                    deps.append(st["w"])
                deps.extend(st["r"].items())
        return deps

    def _update(self, r, w, src, val):
        for t in r:
            st = self.res.setdefault(t.key, {"w": None, "r": {}})
            st["r"][src] = val
        for t in w:
            self.res[t.key] = {"w": (src, val), "r": {}}

    def _need(self, engine, dep):
        src, val = dep
        if val <= 0:
            return False
        if src[0] == "eng" and src[1] == engine and (engine == "pe" or not SAME_ENGINE_SYNC):
            return False
        return self.seen[engine].get(src, 0) < val

    def op(self, engine, fn, r=(), w=()):
        deps = [d for d in self._deps(r, w, ("eng", engine))]
        best = {}
        for src, val in deps:
            if self._need(engine, (src, val)) and val > best.get(src, 0):
                best[src] = val
        items = list(best.items())
        attach = None
        if ATTACH_WAIT and items:
            attach = items.pop()
        for dep in items:
            self._wait(engine, dep)
        ins = fn(self.eng[engine])
        if attach is not None:
            ins._wait_ge(self._semof(attach[0]), attach[1])
            self.seen[engine][attach[0]] = attach[1]
            self.nwait += 1
        self.cnt[engine] += 1
        ins.then_inc(self.sem[engine], 1)
        self.ninst += 1
        self._update(r, w, ("eng", engine), self.cnt[engine])
        return ins

    def dma(self, out, in_, r=(), w=(), q="sp", **kw):
        i = self.lane_rr[q]
        self.lane_rr[q] = (i + 1) % len(self.lanes[q])
        src = ("dma", q, i)
        deps = self._deps(r, w)
        deps.append((src, self.lane_val[(q, i)]))
        for dep in deps:
            self._wait(q, dep)
        ins = self.eng[q].dma_start(out=out, in_=in_, **kw)
        self.lane_val[(q, i)] += 16
        ins.then_inc(self.lanes[q][i], 16)
        self.ninst += 1
        self._update(r, w, src, self.lane_val[(q, i)])

    def barrier(self):
        for e in self.eng:
            for e2 in self.eng:
                self._wait(e, (("eng", e2), self.cnt[e2]))
            for (q, i), v in self.lane_val.items():
                self._wait(e, (("dma", q, i), v))
        self.res = {}

    def finish(self):
        for (q, i), v in self.lane_val.items():
            self._wait("sp", (("dma", q, i), v))
        for e2 in self.eng:
            if e2 != "sp":
                self._wait("sp", (("eng", e2), self.cnt[e2]))


def _blocks():
    out = [(0, CTX)]
    for j in range(SEQ // 512):
        out.append((CTX + 512 * j, 512))
    return out


def _win_cols():
    cols = {}
    base = {"u": 0, "sq": 256, "sk": 512, "sv": 640, "hq": 768, "zf": 1024, "zb": 1280, "hi": 1536, "hg": 1792,
            "cq": 2048, "ckv": 2304, "kr": 2432}

    def swap64(off):
        return np.concatenate([off + np.arange(16, 32), off + np.arange(0, 16), off + np.arange(48, 64), off + np.arange(32, 48)])

    def swap32(off):
        return np.concatenate([off + np.arange(8, 16), off + np.arange(0, 8), off + np.arange(24, 32), off + np.arange(16, 24)])

    sq = base["sq"]
    hA = np.concatenate([sq + np.arange(0, 64), sq + np.arange(128, 192)])
    hB = np.concatenate([sq + np.arange(64, 128), sq + np.arange(192, 256)])
    hAs = np.concatenate([swap64(sq + 0), swap64(sq + 128)])
    hBs = np.concatenate([swap64(sq + 64), swap64(sq + 192)])
    sk = base["sk"]
    fm = [hA, hB, hAs, hBs, sk + np.arange(128), np.concatenate([swap64(sk), swap64(sk + 64)]),
          base["hq"] + np.arange(256), base["zf"] + np.arange(256), base["zb"] + np.arange(256),
          base["hg"] + np.arange(256), base["cq"] + np.arange(256), base["ckv"] + np.arange(128),
          base["kr"] + np.arange(32), swap32(base["kr"])]
    tm = [base["u"] + np.arange(256), base["sv"] + np.arange(128), base["hi"] + np.arange(256)]
    return np.concatenate(fm + tm)


C_SQ, C_SQS, C_SK, C_SKS, C_HQ, C_ZF, C_ZB, C_HG, C_CQ, C_CKV, C_KR, C_KRS = 0, 256, 512, 640, 768, 1024, 1280, 1536, 1792, 2048, 2176, 2208
C_TM = 2240
NWIN = C_TM + 640


def _wqb_cols():
    def swap32(off):
        return np.concatenate([off + np.arange(8, 16), off + np.arange(0, 8), off + np.arange(24, 32), off + np.arange(16, 24)])
    cols = []
    for h in range(4):
        cols += [h * 96 + np.arange(96), swap32(h * 96 + 64)]
    return np.concatenate(cols)


def _swa_mask():
    kk = np.arange(128)[:, None]
    qq = np.arange(128)[None, :]
    lo = (qq <= kk).astype(np.float32)
    hi = (kk <= qq).astype(np.float32)
    m = np.stack([np.stack([lo] * 4, 1), np.stack([hi] * 4, 1)], 1)
    return np.ascontiguousarray(m.astype(np.float32))


def _hg_consts():
    t = np.arange(2048)
    rm = np.broadcast_to((t % 32 != 0).astype(np.float32)[None, :], (64, 2048))
    s_ = np.arange(128)[:, None]
    t_ = np.arange(128)[None, :]
    same = (s_ // 32) == (t_ // 32)
    fw = (same & (s_ <= t_)).astype(np.float32)
    bw = (same & (s_ >= t_)).astype(np.float32)
    am = np.stack([np.stack([fw] * 4, 1), np.stack([bw] * 4, 1)], 1)
    cm = (np.arange(128)[:, None] // 32 == np.arange(4)[None, :]).astype(np.float32)
    return {"hg_rmask": np.ascontiguousarray(rm), "hg_amask": np.ascontiguousarray(am.astype(np.float32)), "hg_cmask": cm}


def _s5_tmask():
    j = np.arange(128)[:, None] // 16
    t = np.arange(128)[None, :] // 16
    return np.ascontiguousarray(np.stack([(t >= j), (t <= j)], 1).astype(np.float32))


def _rope_tables():
    def tab(dim):
        rows = SEQ // GRID_W
        row = np.repeat(np.arange(rows, dtype=np.float64), GRID_W)
        col = np.tile(np.arange(GRID_W, dtype=np.float64), rows)
        nf = dim // 4
        inv = 10000.0 ** (-np.arange(nf, dtype=np.float64) / nf)
        ar = row[None, :] * inv[:, None]
        ac = col[None, :] * inv[:, None]
        C = np.concatenate([np.cos(ar), np.cos(ar), np.cos(ac), np.cos(ac)], 0)
        S = np.concatenate([-np.sin(ar), np.sin(ar), -np.sin(ac), np.sin(ac)], 0)
        C = np.concatenate([np.ones((dim, CTX)), C], 1)
        S = np.concatenate([np.zeros((dim, CTX)), S], 1)
        return C.astype(np.float32), S.astype(np.float32)
    c64, s64 = tab(64)
    c32, s32 = tab(32)
    rs = np.stack([np.concatenate([c64, c64], 0), np.concatenate([s64, s64], 0)])
    z = np.zeros((64, TALL), np.float32)
    rm = np.stack([np.concatenate([z, c32], 0), np.concatenate([z, s32], 0)])
    return rs, rm


class Builder:
    def __init__(self, nlayers=DEPTH, debug=None, stop_after=None, only=None):
        self.stop_after = stop_after
        self.only = only
        self.nl = nlayers
        self.debug = debug
        nc = bass.Bass("TRN2", target_bir_lowering=False)
        self.nc = nc
        self.k = KB(nc)
        dt = nc.dram_tensor

        def ext(name, shape, dtype=F32):
            return dt(name, list(shape), dtype, kind="ExternalInput").ap()

        def internal(name, shape, dtype=F32):
            return dt(name, list(shape), dtype, kind="Internal").ap()

        self.xin = ext("xin", [D, TALL])
        self.cc = ext("cc", [2, D])
        self.w_mod = ext("w_mod", [DEPTH, D, 6 * D])
        self.b_mod = ext("b_mod", [DEPTH, 6 * D])
        self.norm1_g = ext("norm1_g", [DEPTH, D])
        self.norm2_g = ext("norm2_g", [DEPTH, D])
        self.w_in = ext("w_in", [DEPTH, D, NWIN])
        self.w_out = ext("w_out", [DEPTH, D, D])
        self.w_qb = ext("w_qb", [DEPTH, 256, 512])
        self.w_kvb = ext("w_kvb", [DEPTH, 128, 512])
        self.qn_g = ext("qn_g", [DEPTH, 256])
        self.kvn_g = ext("kvn_g", [DEPTH, 128])
        self.w_up = ext("w_up", [DEPTH, D, 2 * FFH])
        self.w_down = ext("w_down", [DEPTH, FFH, D])
        self.final_g = ext("final_g", [D])
        self.rope_s = ext("rope_s", [2, 128, TALL])
        self.rope_m = ext("rope_m", [2, 96, TALL])
        self.ident = ext("ident", [128, 128])
        self.swa_mask = ext("swa_mask", [128, 2, 4, 128])
        self.swa_sink = ext("swa_sink", [DEPTH, 4])
        self.hg_lb = ext("hg_lb", [2, DEPTH, 256])
        self.s5_lam_re = ext("s5_lam_re", [DEPTH, 2, 16, 64])
        self.s5_lam_im = ext("s5_lam_im", [DEPTH, 2, 16, 64])
        self.s5_log_dt = ext("s5_log_dt", [DEPTH, 2, 16, 64])
        self.s5_b_re = ext("s5_b_re", [DEPTH, 2, 16, 64, 16])
        self.s5_b_im = ext("s5_b_im", [DEPTH, 2, 16, 64, 16])
        self.s5_c_re = ext("s5_c_re", [DEPTH, 2, 16, 16, 64])
        self.s5_c_im = ext("s5_c_im", [DEPTH, 2, 16, 16, 64])
        self.s5_d = ext("s5_d", [DEPTH, 256])
        self.s5_w_glu = ext("s5_w_glu", [DEPTH, 256, 256])
        self.s5_b_glu = ext("s5_b_glu", [DEPTH, 256])
        self.s5_tmask = ext("s5_tmask", [128, 2, 128])
        self.hg_norm_g = ext("hg_norm_g", [DEPTH, 64])
        self.hg_rmask = ext("hg_rmask", [64, 2048])
        self.hg_amask = ext("hg_amask", [128, 2, 4, 128])
        self.hg_cmask = ext("hg_cmask", [128, 4])
        self.out = dt("out", [D, SEQ], F32, kind="ExternalOutput").ap()
        self.xres = internal("xres", [D, TALL])
        self.s_sq = internal("s_sq", [256, TALL], BF16)
        self.s_sk = internal("s_sk", [128, TALL], BF16)
        self.s_sv = internal("s_sv", [TALL, 130], BF16)
        self.s_u = internal("s_u", [TALL, 256], F32)
        self.s_hq = internal("s_hq", [256, TALL], BF16)
        self.s_zf = internal("s_zf", [256, TALL], F32)
        self.s_zb = internal("s_zb", [256, TALL], F32)
        self.s_hg = internal("s_hg", [256, TALL], F32)
        self.s_hi = internal("s_hi", [TALL, 256], BF16)
        self.s_mq = internal("s_mq", [4, 96, TALL], BF16)
        self.s_mk = internal("s_mk", [4, 96, TALL], BF16)
        self.s_mv = internal("s_mv", [TALL, 260], BF16)
        self.ymix = internal("ymix", [D, TALL], BF16)
        self.s_ob = internal("s_ob", [64, 4, TALL], F32)
        self.s_y = [internal(f"s_y{d}", [128, 16, TALL // 8], F32) for d in range(2)]
        if debug:
            self.dbg = {n: dt("dbg_" + n, list(v[0]), v[1], kind="ExternalOutput").ap() for n, v in debug.items()}
        self.build()

    def build(self):
        k = self.k
        nc = self.nc
        es = k.es
        self.ones_bf = k.sb(es, "ones_bf", [128, 128], BF16)
        self.ident_bf = k.sb(es, "ident_bf", [128, 128], BF16)
        self.ident_f = k.sb(es, "ident_f", [128, 128], F32)
        self.mod = k.sb(es, "mod", [128, DEPTH, 48, 2], F32)
        self.gs = k.sb(es, "gs", [128, DEPTH, 2, 8, 2], F32)
        self.epsb = k.sb(es, "epsb", [128, 1], F32)
        k.op("dve", lambda e: e.memset(self.ones_bf[:], 1.0), w=[self.ones_bf])
        k.op("dve", lambda e: e.memset(self.epsb[:], EPS), w=[self.epsb])
        k.dma(self.ident_f[:], self.ident, w=[self.ident_f])
        k.dma(self.ident_bf[:], self.ident, w=[self.ident_bf], q="pool")
        self.psw = [k.ps(es, f"psw{i}", [128, 1024], F32) for i in range(4)]
        self.psb = [PV(self.psw[i // 2], 512 * (i % 2), f"psb{i}") for i in range(8)]
        self.setup_mod()
        k.barrier()
        self.setup_hglb()
        for l in range(self.nl):
            self.layer(l)
        if self.debug:
            k.barrier()
            for n in self.debug:
                src = self.debug[n][2](self) if len(self.debug[n]) > 2 else getattr(self, n)
                nd = len(src.shape)
                pat = " ".join("abcd"[:nd])
                fl = lambda a: a.rearrange(f"{pat} -> ({pat})").rearrange("(p f) -> p f", p=16)
                k.dma(fl(self.dbg[n]), fl(src))
        k.finish()
        es.close()

    def load_fm(self, es, dst_ap, src2d, n, wkeys, wd=128):
        k = self.k
        stg = k.sb(es, "stg", [128, 128], F32)
        ps = self.psb[7]
        k.dma(stg[0:n, 0:wd], src2d, w=[stg])
        k.op("pe", lambda e: e.transpose(out=ps[0:wd, 0:n], in_=stg[0:n, 0:wd], identity=self.ident_f[0:n, 0:n]),
             r=[stg, self.ident_f], w=[ps])
        k.op("dve", lambda e: e.tensor_copy(out=dst_ap, in_=ps[0:wd, 0:n]), r=[ps], w=wkeys)

    def setup_mod(self):
        k = self.k
        with contextlib.ExitStack() as es:
            craw = k.sb(es, "craw", [128, 8, 2], F32)
            csil = k.sb(es, "csil", [128, 8, 2], F32)
            bm = k.sb(es, "bm", [128, DEPTH, 48], F32)
            ng = k.sb(es, "ng", [128, 2, DEPTH, 8], F32)
            wm = [k.sb(es, f"wm{i}", [128, 8, 768], F32) for i in range(2)]
            self.load_fm(es, craw[:, :, 0], self.cc[0].rearrange("(c p) -> c p", p=128), 8, [craw])
            self.load_fm(es, craw[:, :, 1], self.cc[1].rearrange("(c p) -> c p", p=128), 8, [craw])
            for l in range(DEPTH):
                self.load_fm(es, bm[:, l, :], self.b_mod[l].rearrange("(j p) -> j p", p=128), 48, [bm])
            self.load_fm(es, ng[:, 0], self.norm1_g.rearrange("l (c p) -> (l c) p", p=128), 32, [ng])
            self.load_fm(es, ng[:, 1], self.norm2_g.rearrange("l (c p) -> (l c) p", p=128), 32, [ng])
            k.op("act", lambda e: e.activation(out=csil[:], in_=craw[:], func=AF.Silu), r=[craw], w=[csil])
            it = 0
            for l in range(self.nl):
                for grp in range(8):
                    wt = wm[it % 2]
                    it += 1
                    k.dma(wt[:], self.w_mod[l].rearrange("(c p) n -> p c n", p=128)[:, :, grp * 768:(grp + 1) * 768], w=[wt])
                    for n in range(6):
                        ps = self.psb[n % 4]
                        for kc in range(8):
                            k.op("pe", lambda e: e.matmul(ps[:, 0:2], lhsT=wt[:, kc, n * 128:(n + 1) * 128], rhs=csil[:, kc, :],
                                                          start=(kc == 0), stop=(kc == 7)), r=[wt, csil], w=[ps])
                        j = grp * 6 + n
                        k.op("dve", lambda e: e.tensor_scalar(out=self.mod[:, l, j, :], in0=ps[:, 0:2], scalar1=bm[:, l, j:j + 1],
                                                              scalar2=None, op0=ALU.add), r=[ps, bm], w=[self.mod])
                for which, sci in ((0, 1), (1, 4)):
                    for j in range(2):
                        k.op("dve", lambda e: e.scalar_tensor_tensor(out=self.gs[:, l, which, :, j], in0=self.mod[:, l, sci * 8:(sci + 1) * 8, j],
                                                                     scalar=1.0, in1=ng[:, which, l, :], op0=ALU.add, op1=ALU.mult),
                             r=[self.mod, ng], w=[self.gs])
            k.barrier()

    def layer(self, l):
        k = self.k
        xsrc = self.xin if l == 0 else self.xres
        if self.stop_after == "mod":
            return
        self.phase_proj(l, xsrc)
        k.barrier()
        if self.stop_after == "proj":
            return
        if self.only is None and OVERLAP_S5:
            gen = self.phase_s5_gen(l)
            next(gen)
            need_ctx = l < DEPTH - 1
            npi = (16 * 4 * 33) + (4 if need_ctx else 0)
            rate = self.s5_nitems / float(npi)
            acc = [0.0]

            def filler():
                acc[0] += rate
                while acc[0] >= 1.0:
                    acc[0] -= 1.0
                    next(gen, None)
            self.phase_mla(l, filler=filler)
            for _ in gen:
                pass
        else:
            if self.only in (None, "mla"):
                self.phase_mla(l)
        if self.only is None and OVERLAP_SWA:
            sgen = self.phase_swa_gen(l, corun=True)
            next(sgen)
            self.phase_hg(l, filler=lambda: next(sgen, None))
            for _ in sgen:
                pass
        elif self.only in (None, "swa"):
            for _ in self.phase_swa_gen(l):
                pass
        if self.only == "hg" or (self.only is None and not OVERLAP_SWA):
            self.phase_hg(l)
        if self.only == "s5" or (self.only is None and not OVERLAP_S5):
            for _ in self.phase_s5_gen(l):
                pass
        if self.stop_after == "mix":
            return
        self.phase_ffn(l, xsrc)

    def norm_mod(self, es_tiles, xsrc, t0, n, l, which, ctxflag, load=True, gain=None, bias=None, out_f32=None):
        k = self.k
        xt, sq, rstd, tmps, ht, ps = es_tiles
        shi = 0 if which == 0 else 3
        if load:
            k.dma(xt[:, :, :n], xsrc.rearrange("(c p) t -> p c t", p=128)[:, :, t0:t0 + n], w=[xt])
        for c in range(8):
            k.op("pool", lambda e: e.tensor_tensor(out=sq[:, c, :n], in0=xt[:, c, :n], in1=xt[:, c, :n], op=ALU.mult), r=[xt], w=[sq])
        for c in range(8):
            k.op("pe", lambda e: e.matmul(ps[:, :n], lhsT=self.ones_bf[:], rhs=sq[:, c, :n], start=(c == 0), stop=(c == 7)),
                 r=[sq, self.ones_bf], w=[ps])
        k.op("act", lambda e: e.activation(out=rstd[:, :n], in_=ps[:, :n], func=AF.Sqrt, bias=self.epsb[:], scale=1.0 / D),
             r=[ps, self.epsb], w=[rstd])
        k.op("dve", lambda e: e.reciprocal(out=rstd[:, :n], in_=rstd[:, :n]), r=[rstd], w=[rstd])
        for c in range(8):
            g_ap = gain[:, c:c + 1] if gain is not None else self.gs[:, l, which, c, ctxflag:ctxflag + 1]
            if out_f32 is not None:
                tmp = tmps[c % len(tmps)]
                k.op("dve", lambda e: e.scalar_tensor_tensor(out=tmp[:, :n], in0=xt[:, c, :n], scalar=g_ap,
                                                             in1=rstd[:, :n], op0=ALU.mult, op1=ALU.mult), r=[xt, rstd, self.gs], w=[tmp])
                k.dma(out_f32(c), tmp[:, :n], r=[tmp])
                continue
            tmp = tmps[c % len(tmps)]
            k.op("dve", lambda e: e.scalar_tensor_tensor(out=tmp[:, :n], in0=xt[:, c, :n], scalar=g_ap,
                                                         in1=rstd[:, :n], op0=ALU.mult, op1=ALU.mult), r=[xt, rstd, self.gs], w=[tmp])
            k.op("act", lambda e: e.activation(out=ht[:, c, :n], in_=tmp[:, :n], func=AF.Identity,
                                               bias=self.mod[:, l, shi * 8 + c, ctxflag:ctxflag + 1], scale=1.0),
                 r=[tmp, self.mod], w=[ht])

    def phase_proj(self, l, xsrc):
        k = self.k
        with contextlib.ExitStack() as es:
            win = k.sb(es, "win", [128, 8, NWIN], BF16)
            wqb = k.sb(es, "wqb", [128, 2, 512], BF16)
            wkvb = k.sb(es, "wkvb", [128, 512], BF16)
            qng = k.sb(es, "qng", [128, 2], F32)
            kvng = k.sb(es, "kvng", [128, 1], F32)
            for c in range(8):
                k.dma(win[:, c, :], self.w_in[l, c * 128:(c + 1) * 128, :], w=[win], q="pool")
            k.dma(wqb[:], self.w_qb[l].rearrange("(c p) n -> p c n", p=128), w=[wqb], q="pool")
            k.dma(wkvb[:], self.w_kvb[l], w=[wkvb], q="pool")
            self.load_fm(es, qng[:], self.qn_g[l].rearrange("(c p) -> c p", p=128), 2, [qng])
            self.load_fm(es, kvng[:], self.kvn_g[l].rearrange("(c p) -> c p", p=128), 1, [kvng])
            NB = 2
            xt = [k.sb(es, f"xt{i}", [128, 8, 512], F32) for i in range(NB)]
            sq = [k.sb(es, f"sq{i}", [128, 8, 512], BF16) for i in range(NB)]
            rstd = [k.sb(es, f"rstd{i}", [128, 512], F32) for i in range(NB)]
            tmp = [k.sb(es, f"tmp{i}", [128, 512], F32) for i in range(2)]
            ht = [k.sb(es, f"ht{i}", [128, 8, 512], BF16) for i in range(NB)]
            rs = [k.sb(es, f"rs{i}", [128, 2, 512], F32) for i in range(NB)]
            rm = [k.sb(es, f"rm{i}", [96, 2, 512], F32) for i in range(NB)]
            ob = [k.sb(es, f"ob{i}", [128, 512], BF16) for i in range(4)]
            of = [k.sb(es, f"of{i}", [128, 512], F32) for i in range(4)]
            r1 = [k.sb(es, f"r1{i}", [128, 512], F32) for i in range(2)]
            r2 = [k.sb(es, f"r2{i}", [128, 512], F32) for i in range(2)]
            cqn = [k.sb(es, f"cqn{i}", [128, 2, 512], BF16) for i in range(NB)]
            ckvn = [k.sb(es, f"ckvn{i}", [128, 512], BF16) for i in range(NB)]
            nsq = [k.sb(es, f"nsq{i}", [128, 512], BF16) for i in range(2)]
            nrs = [k.sb(es, f"nrs{i}", [128, 512], F32) for i in range(2)]
            tmo = [k.sb(es, f"tmo{i}", [128, 640], F32) for i in range(2)]
            tmb = [k.sb(es, f"tmb{i}", [128, 256], BF16) for i in range(2)]
            svb = [k.sb(es, f"svb{i}", [128, 2, 65], BF16) for i in range(2)]
            mvb = [k.sb(es, f"mvb{i}", [128, 4, 65], BF16) for i in range(2)]
            for i in range(2):
                k.op("pool", lambda e: e.memset(svb[i][:, :, 64:65], 1.0), w=[svb[i]])
                k.op("pool", lambda e: e.memset(mvb[i][:, :, 64:65], 1.0), w=[mvb[i]])
            state = {"ob": 0, "of": 0, "ps": 0, "r": 0, "n": 0}

            def nxt(lst, key):
                i = state[key]
                state[key] = (i + 1) % len(lst)
                return lst[i]

            def fm_mm(ps, col0, m, hT, n, lhs_w=None, p0=0):
                for kc in range(8):
                    k.op("pe", lambda e: e.matmul(ps[0:m, :n], lhsT=win[:, kc, col0:col0 + m], rhs=hT[:, kc, :n],
                                                  start=(kc == 0), stop=(kc == 7)), r=[win, hT], w=[ps])

            blks = _blocks()

            def do_norm(bj):
                t0_, n_ = blks[bj]
                self.norm_mod((xt[bj % NB], sq[bj % NB], rstd[bj % NB], tmp, ht[bj % NB], self.psb[7]), xsrc, t0_, n_, l, 0, 1 if bj == 0 else 0)
            do_norm(0)
            for bi, (t0, n) in enumerate(blks):
                ctxflag = 1 if bi == 0 else 0
                b = bi % NB
                hT = ht[b]
                CUT = float(os.environ.get("KCUT", "99"))
                if bi >= int(os.environ.get("KBLK", "99")):
                    break
                if CUT < 1:
                    continue
                k.dma(rs[b][:, :, :n], self.rope_s[:, :, t0:t0 + n].rearrange("a p t -> p a t"), w=[rs[b]])
                k.dma(rm[b][64:96, :, :n], self.rope_m[:, 64:96, t0:t0 + n].rearrange("a p t -> p a t"), w=[rm[b]])
                for ci, (c_a, c_b, dst) in enumerate(((C_SQ, C_SQS, self.s_sq[0:128]), (C_SQ + 128, C_SQS + 128, self.s_sq[128:256]),
                                                      (C_SK, C_SKS, self.s_sk))):
                    pa = nxt(self.psb[0:6], "ps")
                    fm_mm(pa, c_a, 128, hT, n)
                    pb = nxt(self.psb[0:6], "ps")
                    fm_mm(pb, c_b, 128, hT, n)
                    a1 = nxt(r1, "r")
                    a2 = r2[r1.index(a1)]
                    o = nxt(ob, "ob")
                    k.op("dve", lambda e: e.tensor_tensor(out=a1[:, :n], in0=pa[:, :n], in1=rs[b][:, 0, :n], op=ALU.mult), r=[pa, rs[b]], w=[a1])
                    k.op("dve", lambda e: e.tensor_tensor(out=a2[:, :n], in0=pb[:, :n], in1=rs[b][:, 1, :n], op=ALU.mult), r=[pb, rs[b]], w=[a2])
                    k.op("pool", lambda e: e.tensor_tensor(out=o[:, :n], in0=a1[:, :n], in1=a2[:, :n], op=ALU.add), r=[a1, a2], w=[o])
                    k.dma(dst[:, t0:t0 + n], o[:, :n], r=[o])
                PJ = int(os.environ.get("KPJ", "2"))
                if PJ == 2 and bi + 1 < len(blks) and bi + 1 < int(os.environ.get("KBLK", "99")):
                    do_norm(bi + 1)
                if CUT < 2:
                    continue
                for c in range(2):
                    pa = nxt(self.psb[0:6], "ps")
                    fm_mm(pa, C_HQ + 128 * c, 128, hT, n)
                    o = nxt(ob, "ob")
                    k.op("act", lambda e: e.copy(out=o[:, :n], in_=pa[:, :n]), r=[pa], w=[o])
                    k.dma(self.s_hq[128 * c:128 * c + 128, t0:t0 + n], o[:, :n], r=[o])
                for (c0, dst, fn) in ((C_ZF, self.s_zf, AF.Copy), (C_ZB, self.s_zb, AF.Copy), (C_HG, self.s_hg, AF.Silu)):
                    for c in range(2):
                        pa = nxt(self.psb[0:6], "ps")
                        fm_mm(pa, c0 + 128 * c, 128, hT, n)
                        o = nxt(of, "of")
                        k.op("act", lambda e: e.activation(out=o[:, :n], in_=pa[:, :n], func=fn), r=[pa], w=[o])
                        k.dma(dst[128 * c:128 * c + 128, t0:t0 + n], o[:, :n], r=[o])
                if PJ == 3 and bi + 1 < len(blks) and bi + 1 < int(os.environ.get("KBLK", "99")):
                    do_norm(bi + 1)
                if CUT < 3:
                    continue
                pcq = [nxt(self.psb[0:6], "ps") for _ in range(2)]
                for c in range(2):
                    fm_mm(pcq[c], C_CQ + 128 * c, 128, hT, n)
                pss = self.psb[6]
                for c in range(2):
                    s_ = nxt(nsq, "n")
                    k.op("act", lambda e: e.activation(out=s_[:, :n], in_=pcq[c][:, :n], func=AF.Square), r=[pcq[c]], w=[s_])
                    k.op("pe", lambda e: e.matmul(pss[:, :n], lhsT=self.ones_bf[:], rhs=s_[:, :n], start=(c == 0), stop=(c == 1)),
                         r=[s_, self.ones_bf], w=[pss])
                nr = nrs[0]
                k.op("act", lambda e: e.activation(out=nr[:, :n], in_=pss[:, :n], func=AF.Sqrt, bias=self.epsb[:], scale=1.0 / 256),
                     r=[pss, self.epsb], w=[nr])
                k.op("dve", lambda e: e.reciprocal(out=nr[:, :n], in_=nr[:, :n]), r=[nr], w=[nr])
                for c in range(2):
                    k.op("dve", lambda e: e.scalar_tensor_tensor(out=cqn[b][:, c, :n], in0=pcq[c][:, :n], scalar=qng[:, c:c + 1], in1=nr[:, :n],
                                                                 op0=ALU.mult, op1=ALU.mult), r=[pcq[c], qng, nr], w=[cqn[b]])
                for h in range(4):
                    pa = nxt(self.psb[0:6], "ps")
                    pb = nxt(self.psb[0:6], "ps")
                    for c in range(2):
                        k.op("pe", lambda e: e.matmul(pa[0:96, :n], lhsT=wqb[:, c, h * 128:h * 128 + 96], rhs=cqn[b][:, c, :n],
                                                      start=(c == 0), stop=(c == 1)), r=[wqb, cqn[b]], w=[pa])
                    for c in range(2):
                        k.op("pe", lambda e: e.matmul(pb[0:96, :n], lhsT=wqb[:, c, h * 128 + 32:h * 128 + 128], rhs=cqn[b][:, c, :n],
                                                      start=(c == 0), stop=(c == 1)), r=[wqb, cqn[b]], w=[pb])
                    o = nxt(ob, "ob")
                    a1 = nxt(r1, "r")
                    a2 = r2[r1.index(a1)]
                    k.op("act", lambda e: e.copy(out=o[0:64, :n], in_=pa[0:64, :n]), r=[pa], w=[o])
                    k.op("dve", lambda e: e.tensor_tensor(out=a1[64:96, :n], in0=pa[64:96, :n], in1=rm[b][64:96, 0, :n], op=ALU.mult),
                         r=[pa, rm[b]], w=[a1])
                    k.op("dve", lambda e: e.tensor_tensor(out=a2[64:96, :n], in0=pb[64:96, :n], in1=rm[b][64:96, 1, :n], op=ALU.mult),
                         r=[pb, rm[b]], w=[a2])
                    k.op("pool", lambda e: e.tensor_tensor(out=o[64:96, :n], in0=a1[64:96, :n], in1=a2[64:96, :n], op=ALU.add),
                         r=[a1, a2, o], w=[o])
                    k.dma(self.s_mq[h, :, t0:t0 + n], o[0:96, :n], r=[o])
                if PJ == 4 and bi + 1 < len(blks) and bi + 1 < int(os.environ.get("KBLK", "99")):
                    do_norm(bi + 1)
                if CUT < 4:
                    continue
                pkv = nxt(self.psb[0:6], "ps")
                fm_mm(pkv, C_CKV, 128, hT, n)
                s_ = nxt(nsq, "n")
                k.op("act", lambda e: e.activation(out=s_[:, :n], in_=pkv[:, :n], func=AF.Square), r=[pkv], w=[s_])
                k.op("pe", lambda e: e.matmul(pss[:, :n], lhsT=self.ones_bf[:], rhs=s_[:, :n], start=True, stop=True),
                     r=[s_, self.ones_bf], w=[pss])
                nr = nrs[1]
                k.op("act", lambda e: e.activation(out=nr[:, :n], in_=pss[:, :n], func=AF.Sqrt, bias=self.epsb[:], scale=1.0 / 128),
                     r=[pss, self.epsb], w=[nr])
                k.op("dve", lambda e: e.reciprocal(out=nr[:, :n], in_=nr[:, :n]), r=[nr], w=[nr])
                k.op("dve", lambda e: e.scalar_tensor_tensor(out=ckvn[b][:, :n], in0=pkv[:, :n], scalar=kvng[:, 0:1], in1=nr[:, :n],
                                                             op0=ALU.mult, op1=ALU.mult), r=[pkv, kvng, nr], w=[ckvn[b]])
                if CUT < 4.2:
                    continue
                pa = nxt(self.psb[0:6], "ps")
                pb = nxt(self.psb[0:6], "ps")
                fm_mm(pa, C_KR - 64, 96, hT, n)
                fm_mm(pb, C_KRS - 64, 96, hT, n)
                a1 = nxt(r1, "r")
                a2 = r2[r1.index(a1)]
                kro = nxt(ob, "ob")
                k.op("dve", lambda e: e.tensor_tensor(out=a1[64:96, :n], in0=pa[64:96, :n], in1=rm[b][64:96, 0, :n], op=ALU.mult),
                     r=[pa, rm[b]], w=[a1])
                k.op("dve", lambda e: e.tensor_tensor(out=a2[64:96, :n], in0=pb[64:96, :n], in1=rm[b][64:96, 1, :n], op=ALU.mult),
                     r=[pb, rm[b]], w=[a2])
                k.op("pool", lambda e: e.tensor_tensor(out=kro[64:96, :n], in0=a1[64:96, :n], in1=a2[64:96, :n], op=ALU.add),
                     r=[a1, a2], w=[kro])
                if CUT < 4.4:
                    continue
                for h in range(4):
                    k.dma(self.s_mk[h, 64:96, t0:t0 + n], kro[64:96, :n], r=[kro])
                    if CUT < 4.6:
                        continue
                    pa = nxt(self.psb[0:6], "ps")
                    if os.environ.get("KVAR") == "A":
                        k.op("pe", lambda e: e.matmul(pa[:, :n], lhsT=wkvb[:, h * 128:h * 128 + 128], rhs=ckvn[b][:, :n], start=True, stop=True),
                             r=[wkvb, ckvn[b]], w=[pa])
                    else:
                        k.op("pe", lambda e: e.matmul(pa[0:64, :n], lhsT=wkvb[:, h * 128:h * 128 + 64], rhs=ckvn[b][:, :n], start=True, stop=True),
                             r=[wkvb, ckvn[b]], w=[pa])
                    if CUT < 4.7:
                        continue
                    o = nxt(ob, "ob")
                    k.op("act", lambda e: e.copy(out=o[0:64, :n], in_=pa[0:64, :n]), r=[pa], w=[o])
                    if CUT < 4.8:
                        continue
                    k.dma(self.s_mk[h, 0:64, t0:t0 + n], o[0:64, :n], r=[o])
                if PJ == 5 and bi + 1 < len(blks) and bi + 1 < int(os.environ.get("KBLK", "99")):
                    do_norm(bi + 1)
                if CUT < 5:
                    continue
                for st in range(n // 128):
                    ts_ = slice(st * 128, st * 128 + 128)
                    pa = nxt(self.psb[0:6], "ps")
                    pb = nxt(self.psb[0:6], "ps")
                    for kc in range(8):
                        k.op("pe", lambda e: e.matmul(pa[:, 0:512], lhsT=hT[:, kc, ts_], rhs=win[:, kc, C_TM:C_TM + 512],
                                                      start=(kc == 0), stop=(kc == 7)), r=[win, hT], w=[pa])
                    for kc in range(8):
                        k.op("pe", lambda e: e.matmul(pb[:, 0:128], lhsT=hT[:, kc, ts_], rhs=win[:, kc, C_TM + 512:C_TM + 640],
                                                      start=(kc == 0), stop=(kc == 7)), r=[win, hT], w=[pb])
                    k.op("pe", lambda e: e.matmul(pb[:, 128:384], lhsT=ckvn[b][:, ts_],
                                                  rhs=wkvb[:].rearrange("p (h x) -> p h x", x=128)[:, :, 64:128],
                                                  start=True, stop=True), r=[wkvb, ckvn[b]], w=[pb])
                    uo = tmo[st % 2]
                    bo = tmb[st % 2]
                    so = svb[st % 2]
                    mo = mvb[st % 2]
                    k.op("act", lambda e: e.copy(out=uo[:, 0:256], in_=pa[:, 0:256]), r=[pa], w=[uo])
                    k.op("dve", lambda e: e.tensor_copy(out=bo[:, 0:128], in_=pa[:, 384:512]), r=[pa], w=[bo])
                    k.op("dve", lambda e: e.tensor_copy(out=bo[:, 128:256], in_=pb[:, 0:128]), r=[pb, bo], w=[bo])
                    k.op("dve", lambda e: e.tensor_copy(out=so[:, :, 0:64], in_=pa[:, 256:384].rearrange("p (g x) -> p g x", g=2)), r=[pa, so], w=[so])
                    k.op("act", lambda e: e.copy(out=mo[:, :, 0:64], in_=pb[:, 128:384].rearrange("p (g x) -> p g x", g=4)), r=[pb, mo], w=[mo])
                    tt = t0 + st * 128
                    k.dma(self.s_u[tt:tt + 128, :], uo[:, 0:256], r=[uo])
                    k.dma(self.s_sv[tt:tt + 128, :], so[:].rearrange("p g x -> p (g x)"), r=[so])
                    k.dma(self.s_hi[tt:tt + 128, :], bo[:, 0:256], r=[bo])
                    k.dma(self.s_mv[tt:tt + 128, :], mo[:].rearrange("p g x -> p (g x)"), r=[mo])
            k.barrier()


    def phase_ffn(self, l, xsrc):
        k = self.k
        last = (l == DEPTH - 1)
        NB = 256
        with contextlib.ExitStack() as es:
            wout = k.sb(es, "wout", [128, 8, D], BF16)
            wup = k.sb(es, "wup", [128, 8, 2 * FFH], BF16)
            wdn = k.sb(es, "wdn", [128, 22, D], BF16)
            for c in range(8):
                k.dma(wout[:, c, :], self.w_out[l, c * 128:(c + 1) * 128, :], w=[wout], q="pool")
                k.dma(wup[:, c, :], self.w_up[l, c * 128:(c + 1) * 128, :], w=[wup], q="pool")
            for j in range(22):
                k.dma(wdn[:, j, :], self.w_down[l, j * 128:(j + 1) * 128, :], w=[wdn], q="pool")
            fg = None
            if last:
                fg = k.sb(es, "fg", [128, 8], F32)
                self.load_fm(es, fg[:], self.final_g.rearrange("(c p) -> c p", p=128), 8, [fg])
            xt = [k.sb(es, f"fxt{i}", [128, 8, NB], F32) for i in range(2)]
            ym = [k.sb(es, f"fym{i}", [128, 8, NB], BF16) for i in range(2)]
            sq = k.sb(es, "fsq", [128, 8, NB], BF16)
            tmp = [k.sb(es, f"ftmp{i}", [128, NB], F32) for i in range(2)]
            ht = [k.sb(es, f"fht{i}", [128, 8, NB], BF16) for i in range(2)]
            rstd = k.sb(es, "frstd", [128, NB], F32)
            sq2, rstd2, tmp2 = sq, rstd, tmp
            aT = k.sb(es, "faT", [128, 22, NB], BF16)
            sg = [k.sb(es, f"fsg{i}", [128, NB], F32) for i in range(2)]
            psi = [0]

            def nps():
                psi[0] = (psi[0] + 1) % 6
                return self.psb[psi[0]]

            t_start = CTX if last else 0
            t0s = list(range(t_start, TALL, NB))
            n = NB

            def stage_a(bi):
                t0 = t0s[bi]
                flag = 1 if t0 < CTX else 0
                b = bi % 2
                k.dma(xt[b][:, :, :n], xsrc.rearrange("(c p) t -> p c t", p=128)[:, :, t0:t0 + n], w=[xt[b]])
                k.dma(ym[b][:, :, :n], self.ymix.rearrange("(c p) t -> p c t", p=128)[:, :, t0:t0 + n], w=[ym[b]])
                for oc in range(8):
                    ps = nps()
                    for kc in range(8):
                        k.op("pe", lambda e: e.matmul(ps[:, :n], lhsT=wout[:, kc, oc * 128:(oc + 1) * 128], rhs=ym[b][:, kc, :n],
                                                      start=(kc == 0), stop=(kc == 7)), r=[wout, ym[b]], w=[ps])
                    k.op("dve", lambda e: e.scalar_tensor_tensor(out=xt[b][:, oc, :n], in0=ps[:, :n], scalar=self.mod[:, l, 16 + oc, flag:flag + 1],
                                                                 in1=xt[b][:, oc, :n], op0=ALU.mult, op1=ALU.add), r=[ps, xt[b], self.mod], w=[xt[b]])
                self.norm_mod((xt[b], sq, rstd, tmp, ht[b], self.psb[7]), None, t0, n, l, 1, flag, load=False)

            stage_a(0)
            for bi, t0 in enumerate(t0s):
                flag = 1 if t0 < CTX else 0
                b = bi % 2
                for j in range(22):
                    if j == int(os.environ.get("KFJ", "16")) and bi + 1 < len(t0s):
                        stage_a(bi + 1)
                    pg = nps()
                    pu = nps()
                    for kc in range(8):
                        k.op("pe", lambda e: e.matmul(pg[:, :n], lhsT=wup[:, kc, j * 128:(j + 1) * 128], rhs=ht[b][:, kc, :n],
                                                      start=(kc == 0), stop=(kc == 7)), r=[wup, ht[b]], w=[pg])
                    for kc in range(8):
                        k.op("pe", lambda e: e.matmul(pu[:, :n], lhsT=wup[:, kc, FFH + j * 128:FFH + (j + 1) * 128], rhs=ht[b][:, kc, :n],
                                                      start=(kc == 0), stop=(kc == 7)), r=[wup, ht[b]], w=[pu])
                    s_ = sg[j % 2]
                    k.op("act", lambda e: e.activation(out=s_[:, :n], in_=pg[:, :n], func=AF.Silu), r=[pg], w=[s_])
                    k.op("dve", lambda e: e.tensor_tensor(out=aT[:, j, :n], in0=s_[:, :n], in1=pu[:, :n], op=ALU.mult), r=[s_, pu], w=[aT])
                for oc in range(8):
                    ps = nps()
                    for j in range(22):
                        k.op("pe", lambda e: e.matmul(ps[:, :n], lhsT=wdn[:, j, oc * 128:(oc + 1) * 128], rhs=aT[:, j, :n],
                                                      start=(j == 0), stop=(j == 21)), r=[wdn, aT], w=[ps])
                    k.op("dve", lambda e: e.scalar_tensor_tensor(out=xt[b][:, oc, :n], in0=ps[:, :n], scalar=self.mod[:, l, 40 + oc, flag:flag + 1],
                                                                 in1=xt[b][:, oc, :n], op0=ALU.mult, op1=ALU.add), r=[ps, xt[b], self.mod], w=[xt[b]])
                if not last:
                    k.dma(self.xres.rearrange("(c p) t -> p c t", p=128)[:, :, t0:t0 + n], xt[b][:, :, :n], r=[xt[b]])
                else:
                    self.norm_mod((xt[b], sq2, rstd2, tmp2, None, self.psb[7]), None, t0, n, l, 1, flag, load=False, gain=fg,
                                  out_f32=lambda c: self.out[c * 128:(c + 1) * 128, t0 - CTX:t0 - CTX + n])
            k.barrier()

    def attn_finish(self, OT, n, Osb, rec, yo, sel, dst_aps, nh=1, bc=None):
        k = self.k
        bc = self.psb[6] if bc is None else bc
        k.op("act", lambda e: e.copy(out=Osb[0:65, :n], in_=OT[0:65, :n]), r=[OT], w=[Osb])
        k.op("pe", lambda e: e.matmul(bc[0:64, :n], lhsT=sel[0:65, :], rhs=Osb[0:65, :n], start=True, stop=True), r=[sel, Osb], w=[bc])
        k.op("dve", lambda e: e.reciprocal(out=rec[0:64, :n], in_=bc[0:64, :n]), r=[bc], w=[rec])
        k.op("dve", lambda e: e.tensor_tensor(out=yo[0:64, :n], in0=Osb[0:64, :n], in1=rec[0:64, :n], op=ALU.mult), r=[Osb, rec], w=[yo])
        w = n // nh
        for j, dst in enumerate(dst_aps):
            k.dma(dst, yo[0:64, j * w:(j + 1) * w], r=[yo])

    def make_sel(self, es):
        k = self.k
        sel = k.sb(es, "sel", [65, 64], F32)
        k.op("dve", lambda e: e.memset(sel[0:64, :], 0.0), w=[sel])
        k.op("dve", lambda e: e.memset(sel[64:65, :], 1.0), r=[sel], w=[sel])
        return sel

    def phase_mla(self, l, filler=None):
        k = self.k
        need_ctx = l < DEPTH - 1
        NT = TALL // 128
        with contextlib.ExitStack() as es:
            KT = k.sb(es, "mKT", [96, 2, TALL], BF16)
            Va = k.sb(es, "mVa", [128, NT, 2, 65], BF16)
            sel = self.make_sel(es)
            QT = [k.sb(es, f"mQT{i}", [96, 2, 512], BF16) for i in range(2)]
            PT = [k.sb(es, f"mPT{i}", [128, 2, 512], BF16) for i in range(2)]
            Osb = [k.sb(es, f"mOsb{i}", [65, 512], F32) for i in range(2)]
            rec = [k.sb(es, f"mrec{i}", [64, 512], F32) for i in range(2)]
            yo = [k.sb(es, f"myo{i}", [64, 512], BF16) for i in range(2)]
            cnt = 0
            vsrc = self.s_mv.rearrange("(t p) (h x) -> p t h x", p=128, h=4)
            units = []
            qi = 0
            for hp in range(2):
                for bi, (t0, n) in enumerate(_blocks()):
                    if bi == 0 and not need_ctx:
                        continue
                    ktiles = [0, 1] if bi == 0 else list(range(NT))
                    for hj in range(2):
                        units.append(dict(hp=hp, bi=bi, t0=t0, n=n, hj=hj, ktiles=ktiles, b=qi % 2, first=(hj == 0), newhp=(hj == 0 and len([u for u in units if u["hp"] == hp]) == 0)))
                    qi += 1

            def prologue(u):
                if u["newhp"]:
                    for j in range(2):
                        k.dma(KT[:, j, :], self.s_mk[2 * u["hp"] + j], w=[KT])
                    for t in range(0, NT, 6):
                        k.dma(Va[:, t:t + 6], vsrc[:, t:t + 6, 2 * u["hp"]:2 * u["hp"] + 2, :], w=[Va])
                if u["first"]:
                    k.dma(QT[u["b"]][:, :, :u["n"]], self.s_mq[2 * u["hp"]:2 * u["hp"] + 2, :, u["t0"]:u["t0"] + u["n"]].rearrange("h p t -> p h t"),
                          w=[QT[u["b"]]])

            gpair = [0]

            def st_pair(u, ip, par):
                STw = self.psw[par % 2]
                n = u["n"]
                for j in range(2):
                    kt = u["ktiles"][2 * ip + j]
                    k.op("pe", lambda e: e.matmul(STw[:, j * 512:j * 512 + n], lhsT=KT[:, u["hj"], kt * 128:(kt + 1) * 128], rhs=QT[u["b"]][:, u["hj"], :n],
                                                  start=True, stop=True), r=[KT, QT[u["b"]]], w=[STw])

            prologue(units[0])
            st_pair(units[0], 0, gpair[0])
            for ui, u in enumerate(units):
                n, hj, ktiles = u["n"], u["hj"], u["ktiles"]
                h = 2 * u["hp"] + hj
                OT = self.psb[4 + cnt % 2]
                npair = len(ktiles) // 2
                if ui > 0 and u["newhp"]:
                    prologue(u)
                    st_pair(u, 0, gpair[0])
                for ip in range(npair):
                    par = gpair[0]
                    gpair[0] += 1
                    if ip + 1 < npair:
                        st_pair(u, ip + 1, par + 1)
                    elif ui + 1 < len(units) and not units[ui + 1]["newhp"]:
                        prologue(units[ui + 1])
                        st_pair(units[ui + 1], 0, par + 1)
                    STw = self.psw[par % 2]
                    P = PT[par % 2]
                    k.op("act", lambda e: e.activation(out=P[:, :, :n], in_=STw[:, :].rearrange("p (j x) -> p j x", j=2)[:, :, :n], func=AF.Exp,
                                                       scale=MLA_SCALE), r=[STw], w=[P])
                    for j in range(2):
                        kt = ktiles[2 * ip + j]
                        k.op("pe", lambda e: e.matmul(OT[0:65, :n], lhsT=Va[:, kt, hj, :], rhs=P[:, j, :n], start=(ip == 0 and j == 0),
                                                      stop=(ip == npair - 1 and j == 1)), r=[Va, P], w=[OT])
                    if filler is not None:
                        filler()
                self.attn_finish(OT, n, Osb[cnt % 2], rec[cnt % 2], yo[cnt % 2], sel,
                                 [self.ymix[768 + h * 64:768 + (h + 1) * 64, u["t0"]:u["t0"] + n]])
                cnt += 1
            k.barrier()

    def phase_swa_gen(self, l, corun=False):
        k = self.k
        need_ctx = l < DEPTH - 1
        NT = TALL // 128
        with contextlib.ExitStack() as es:
            QT = k.sb(es, "sQT", [128, 2, TALL], BF16)
            KT = k.sb(es, "sKT", [128, TALL], BF16)
            Va = k.sb(es, "sVa", [128, NT, 2, 65], BF16)
            msk = k.sb(es, "smsk", [128, 2, 4, 128], BF16)
            sel = self.make_sel(es)
            sk = k.sb(es, "ssk", [1, 4], F32)
            skrow = k.sb(es, "sskrow", [1, 4, 128], BF16)
            e64 = k.sb(es, "se64", [1, 65], BF16)
            k.dma(msk[:], self.swa_mask, w=[msk], q="pool")
            k.dma(sk[:], self.swa_sink[l:l + 1, :], w=[sk])
            k.op("act", lambda e: e.activation(out=sk[:], in_=sk[:], func=AF.Exp), r=[sk], w=[sk])
            for h in range(4):
                k.op("dve", lambda e: e.tensor_scalar(out=skrow[:, h, :], in0=self.ones_bf[0:1, :], scalar1=sk[0:1, h:h + 1], scalar2=None,
                                                      op0=ALU.mult), r=[sk, self.ones_bf], w=[skrow])
            k.op("dve", lambda e: e.memset(e64[:, 0:64], 0.0), w=[e64])
            k.op("dve", lambda e: e.memset(e64[:, 64:65], 1.0), r=[e64], w=[e64])
            for c in range(2):
                k.dma(QT[:, c, :], self.s_sq[c * 128:(c + 1) * 128, :], w=[QT])
            k.dma(KT[:], self.s_sk, w=[KT])
            vsrc = self.s_sv.rearrange("(t p) f -> p t f", p=128)
            for t in range(0, NT, 6):
                k.dma(Va[:, t:t + 6].rearrange("p t h x -> p t (h x)"), vsrc[:, t:t + 6, :], w=[Va])
            PT = [k.sb(es, f"sPT{i}", [128, 4, 128], BF16) for i in range(3)]
            stb = [self.psb[6]] if corun else [self.psb[0], self.psb[1], self.psb[2]]
            otb = [self.psb[7]] if corun else [self.psb[4], self.psb[5]]
            Osb = [k.sb(es, f"sOsb{i}", [65, 512], F32) for i in range(2)]
            rec = [k.sb(es, f"srec{i}", [64, 512], F32) for i in range(2)]
            yo = [k.sb(es, f"syo{i}", [64, 512], BF16) for i in range(2)]
            cnt = 0
            yield "ready"
            for gt in range(NT):
                if gt < 2:
                    if not need_ctx:
                        continue
                    keys = [(0, None), (1, None)]
                else:
                    keys = []
                    if gt > 2:
                        keys.append((gt - 1, 0))
                    keys.append((gt, None))
                    if gt < NT - 1:
                        keys.append((gt + 1, 1))
                    keys += [(0, None), (1, None)]
                qs = slice(gt * 128, (gt + 1) * 128)
                OT = otb[cnt % len(otb)]
                nk = len(keys)

                assert not corun

                def st_mm(i):
                    STw = self.psw[i % 2]
                    kt = keys[i][0]
                    for g in range(2):
                        p0 = 64 * g
                        k.op("pe", lambda e: e.matmul(STw[:, g * 512:g * 512 + 256], lhsT=KT[p0:p0 + 64, kt * 128:(kt + 1) * 128], rhs=QT[p0:p0 + 64, :, qs],
                                                      start=True, stop=True), r=[KT, QT], w=[STw])
                st_mm(0)
                for i in range(nk):
                    if i + 1 < nk:
                        st_mm(i + 1)
                    ST = self.psw[i % 2]
                    P = PT[i % 3]
                    kt, mk = keys[i]
                    k.op("act", lambda e: e.activation(out=P[:].rearrange("p (g a) b -> p g (a b)", g=2),
                                                       in_=ST[:, :].rearrange("p (g x) -> p g x", g=2)[:, :, 0:256], func=AF.Exp, scale=SWA_SCALE),
                         r=[ST], w=[P])
                    if mk is not None:
                        k.op("dve", lambda e: e.tensor_tensor(out=P[:], in0=P[:], in1=msk[:, mk], op=ALU.mult), r=[P, msk], w=[P])
                    for g in range(2):
                        k.op("pe", lambda e: e.matmul(OT[0:65, g * 256:(g + 1) * 256], lhsT=Va[:, kt, g, :], rhs=P[:, 2 * g:2 * g + 2, :],
                                                      start=(i == 0 and g == 0), stop=False, skip_group_check=True), r=[Va, P], w=[OT])
                k.op("pe", lambda e: e.matmul(OT[0:65, 0:512], lhsT=e64[0:1, :], rhs=skrow[0:1, :, :], start=False, stop=True,
                                              skip_group_check=True), r=[e64, skrow], w=[OT])
                self.attn_finish(OT, 512, Osb[cnt % 2], rec[cnt % 2], yo[cnt % 2], sel,
                                 [self.ymix[256 + h * 64:256 + (h + 1) * 64, qs] for h in range(4)], nh=4,
                                 bc=(OT if corun else None))
                cnt += 1
                yield
            k.barrier()

    def setup_hglb(self):
        k = self.k
        es = k.es
        self.hglb = k.sb(es, "hglb", [64, 2, DEPTH, 4], F32)
        self.hgoml = k.sb(es, "hgoml", [64, 2, DEPTH, 4], F32)
        self.hgnoml = k.sb(es, "hgnoml", [64, 2, DEPTH, 4], F32)
        with contextlib.ExitStack() as es2:
            e_ = k.sb(es2, "lbe", [64, 2, DEPTH, 4], F32)
            s_ = k.sb(es2, "lbs", [64, 2, 4], F32)
            self.load_fm(es2, e_[:].rearrange("p d l h -> p (d l h)"), self.hg_lb.rearrange("d l (h x) -> (d l h) x", x=64), 32, [e_], wd=64)
            k.op("act", lambda e: e.activation(out=e_[:], in_=e_[:], func=AF.Exp), r=[e_], w=[e_])
            k.op("dve", lambda e: e.tensor_tensor(out=s_[:], in0=e_[:, :, 0, :], in1=e_[:, :, 1, :], op=ALU.add), r=[e_], w=[s_])
            for l in (2, 3):
                k.op("dve", lambda e: e.tensor_tensor(out=s_[:], in0=s_[:], in1=e_[:, :, l, :], op=ALU.add), r=[e_, s_], w=[s_])
            k.op("dve", lambda e: e.reciprocal(out=s_[:], in_=s_[:]), r=[s_], w=[s_])
            for l in range(DEPTH):
                k.op("dve", lambda e: e.tensor_tensor(out=e_[:, :, l, :], in0=e_[:, :, l, :], in1=s_[:], op=ALU.mult), r=[e_, s_], w=[e_])
            k.op("dve", lambda e: e.memset(self.hglb[:, :, 0, :], 0.0), w=[self.hglb])
            k.op("dve", lambda e: e.tensor_copy(out=self.hglb[:, :, 1, :], in_=e_[:, :, 1, :]), r=[e_, self.hglb], w=[self.hglb])
            for l in (2, 3):
                k.op("dve", lambda e: e.tensor_tensor(out=self.hglb[:, :, l, :], in0=self.hglb[:, :, l - 1, :], in1=e_[:, :, l, :], op=ALU.add),
                     r=[e_, self.hglb], w=[self.hglb])
            k.op("dve", lambda e: e.tensor_scalar(out=self.hgoml[:], in0=self.hglb[:], scalar1=-1.0, scalar2=1.0, op0=ALU.mult, op1=ALU.add),
                 r=[self.hglb], w=[self.hgoml])
            k.op("dve", lambda e: e.tensor_scalar(out=self.hgnoml[:], in0=self.hglb[:], scalar1=-1.0, scalar2=None, op0=ALU.add),
                 r=[self.hglb], w=[self.hgnoml])
            k.barrier()

    def phase_hg(self, l, filler=None):
        k = self.k
        with contextlib.ExitStack() as es:
            rmask = k.sb(es, "hrm", [64, 2048], F32)
            amask = k.sb(es, "ham", [128, 2, 4, 128], BF16)
            cmask = k.sb(es, "hcm", [128, 4], F32)
            ng = k.sb(es, "hng", [64, 1], F32)
            k.dma(rmask[:], self.hg_rmask, w=[rmask])
            k.dma(amask[:], self.hg_amask, w=[amask], q="pool")
            k.dma(cmask[:], self.hg_cmask, w=[cmask])
            self.load_fm(es, ng[:], self.hg_norm_g[l:l + 1, :], 1, [ng], wd=64)
            S = k.sb(es, "hS", [64, 4, 64], F32)
            St = k.sb(es, "hSt", [64, 4, 64], F32)
            Sbf = [k.sb(es, f"hSbf{j}", [64, 4, 64], BF16) for j in range(8)]
            names = ["z", "sg", "lf", "kk", "P", "eP", "eN"]
            bt = {nm: k.sb(es, "hb_" + nm, [64, 2048], F32) for nm in names}
            bq = k.sb(es, "hb_q", [64, 2048], BF16)
            bqd = k.sb(es, "hb_qd", [64, 2048], BF16)
            bki = k.sb(es, "hb_ki", [64, 2048], BF16)
            dec = k.sb(es, "hdec", [64, 4, 16], F32)
            vt = [k.sb(es, f"hvt{i}", [128, 256], BF16) for i in range(2)]
            kim = k.sb(es, "hkim", [128, 4, 256], BF16)
            attm = k.sb(es, "hattm", [128, 4, 128], BF16)
            obt = [k.sb(es, f"hobt{i}", [64, 4, 128], F32) for i in range(2)]
            gsl = [k.sb(es, f"hgsl{i}", [64, 4, 128], F32) for i in range(2)]
            osum = k.sb(es, "hosum", [64, 4, 128], F32)
            sqo = k.sb(es, "hsqo", [64, 512], BF16)
            rst = k.sb(es, "hrst", [64, 512], F32)
            yo = [k.sb(es, f"hyo{i}", [64, 4, 128], BF16) for i in range(2)]
            Mps = [self.psb[0], self.psb[1]]
            attps = self.psb[2]
            ops = self.psb[3]
            trp = self.psb[4]
            ssps = self.psb[5]
            trp_bf = trp[:].bitcast(BF16)
            ymix_v = self.ymix[512:768, :].rearrange("(h d) t -> d h t", d=64)
            sbi = [0]
            vti = [0]
            bqd2 = [bqd, k.sb(es, "hb_qd2", [64, 2048], BF16)]
            bki2 = [bki, k.sb(es, "hb_ki2", [64, 2048], BF16)]
            dec2 = [dec, k.sb(es, "hdec2", [64, 4, 16], F32)]
            Sx = [S, k.sb(es, "hS2", [64, 4, 64], F32)]
            Stx = [St, k.sb(es, "hSt2", [64, 4, 64], F32)]
            sidx = [0]

            def prep_groups(d, t0, n, bs):
                zsrc = (self.s_zf if d == 0 else self.s_zb).rearrange("(h d) t -> d h t", d=64)
                qsrc = self.s_hq.rearrange("(h d) t -> d h t", d=64)
                bqd_, bki_, dec_ = bqd2[bs], bki2[bs], dec2[bs]

                def V(t):
                    return t[:, 0:4 * n].rearrange("p (h t) -> p h t", h=4)
                f2 = lambda t: t[:, 0:4 * n]
                z, sg, lf, kk, P, eP, eN = [bt[nm] for nm in names]

                def g1():
                    k.dma(V(z), zsrc[:, :, t0:t0 + n], w=[z])
                    k.dma(V(bq), qsrc[:, :, t0:t0 + n], w=[bq])
                    k.op("act", lambda e: e.activation(out=f2(sg), in_=f2(z), func=AF.Sigmoid), r=[z], w=[sg])

                def g2():
                    for h in range(4):
                        k.op(HG_PREP_ENG, lambda e: e.tensor_scalar(out=V(lf)[:, h, :], in0=V(sg)[:, h, :], scalar1=self.hgoml[:, d, l, h:h + 1],
                                                              scalar2=self.hglb[:, d, l, h:h + 1], op0=ALU.mult, op1=ALU.add),
                             r=[sg, self.hgoml, self.hglb], w=[lf])
                        k.op("pool", lambda e: e.tensor_scalar(out=V(kk)[:, h, :], in0=V(sg)[:, h, :], scalar1=self.hgnoml[:, d, l, h:h + 1],
                                                               scalar2=self.hgoml[:, d, l, h:h + 1], op0=ALU.mult, op1=ALU.add),
                             r=[sg, self.hgoml, self.hgnoml], w=[kk])
                    k.op("act", lambda e: e.activation(out=f2(lf), in_=f2(lf), func=AF.Ln), r=[lf], w=[lf])

                def g3():
                    k.op("dve", lambda e: e.tensor_tensor_scan(out=f2(P), data0=f2(rmask), data1=f2(lf), initial=0.0, op0=ALU.mult, op1=ALU.add),
                         r=[rmask, lf], w=[P])
                    k.op("act", lambda e: e.activation(out=dec_[:, :, 0:n // 32], in_=V(P)[:, :, 31:n:32], func=AF.Exp), r=[P], w=[dec_])
                    if d == 1:
                        k.op("pool", lambda e: e.tensor_tensor(out=f2(P), in0=f2(P), in1=f2(lf), op=ALU.subtract), r=[P, lf], w=[P])

                def g4():
                    k.op("act", lambda e: e.activation(out=f2(eP), in_=f2(P), func=AF.Exp), r=[P], w=[eP])
                    k.op("act", lambda e: e.activation(out=f2(eN), in_=f2(P), func=AF.Exp, scale=-1.0), r=[P], w=[eN])
                    eq, ek = (eP, eN) if d == 0 else (eN, eP)
                    k.op(HG_PREP_ENG, lambda e: e.tensor_tensor(out=f2(bqd_), in0=f2(bq), in1=f2(eq), op=ALU.mult), r=[bq, eq], w=[bqd_])
                    k.op("pool", lambda e: e.tensor_tensor(out=f2(bki_), in0=f2(kk), in1=f2(ek), op=ALU.mult), r=[kk, ek], w=[bki_])
                return [g1, g2, g3, g4]

            for d in (1, 0):
                S = Sx[sidx[0] % 2]
                k.op("dve", lambda e: e.memset(S[:], 0.0), r=[S], w=[S])
                blocks = _blocks()
                if d == 1:
                    blocks = [blocks[0]] + blocks[:0:-1]
                for g_ in prep_groups(d, blocks[0][0], blocks[0][1], 0):
                    g_()
                for bix, (t0, n) in enumerate(blocks):
                    bs = bix % 2
                    pending = prep_groups(d, blocks[bix + 1][0], blocks[bix + 1][1], (bix + 1) % 2) if bix + 1 < len(blocks) else []

                    def V(t):
                        return t[:, 0:4 * n].rearrange("p (h t) -> p h t", h=4)
                    bqd, bki, dec = bqd2[bs], bki2[bs], dec2[bs]
                    qd, ki, decv = V(bqd), V(bki), dec
                    ntile = n // 128
                    for tix, ti in enumerate(range(ntile) if d == 0 else range(ntile - 1, -1, -1)):
                        if tix > 0:
                            for _ in range(4 // ntile if ntile < 4 else 1):
                                if pending:
                                    pending.pop(0)()
                        if filler is not None:
                            filler()
                        cols = slice(ti * 128, ti * 128 + 128)
                        gt0 = t0 + ti * 128
                        v = vt[vti[0] % 2]
                        ob_ = obt[vti[0] % 2]
                        gs_ = gsl[vti[0] % 2]
                        yo_ = yo[vti[0] % 2]
                        vti[0] += 1
                        k.dma(v[:], self.s_hi[gt0:gt0 + 128, :], w=[v])
                        if d == 0:
                            k.dma(ob_[:], self.s_ob[:, :, gt0:gt0 + 128], w=[ob_])
                            k.dma(gs_[:], self.s_hg.rearrange("(h d) t -> d h t", d=64)[:, :, gt0:gt0 + 128], w=[gs_])
                        for h in range(4):
                            k.op("pe", lambda e: e.transpose(out=trp_bf[:, h * 64:(h + 1) * 64], in_=ki[:, h, cols], identity=self.ident_bf[0:64, 0:64]),
                                 r=[bki, self.ident_bf], w=[trp])
                        for j in range(4):
                            k.op("dve" if j % 2 == 0 else "act", (lambda e: e.tensor_scalar(out=kim[:, j, :], in0=trp_bf[:, 0:256], scalar1=cmask[:, j:j + 1], scalar2=None, op0=ALU.mult))
                                 if j % 2 == 0 else (lambda e: e.activation(out=kim[:, j, :], in_=trp_bf[:, 0:256], func=AF.Copy, scale=cmask[:, j:j + 1])),
                                 r=[trp, cmask], w=[kim])
                        for j in range(4):
                            for h in range(4):
                                k.op("pe", lambda e: e.matmul(Mps[j // 2][0:64, (j % 2) * 256 + h * 64:(j % 2) * 256 + (h + 1) * 64],
                                                              lhsT=kim[:, j, h * 64:(h + 1) * 64], rhs=v[:, h * 64:(h + 1) * 64], start=True, stop=True),
                                     r=[kim, v], w=[Mps[j // 2]])
                        for h in range(4):
                            k.op("pe", lambda e: e.matmul(attps[:, h * 128:(h + 1) * 128], lhsT=ki[:, h, cols], rhs=qd[:, h, cols], start=True, stop=True),
                                 r=[bki, bqd], w=[attps])
                        k.op("dve", lambda e: e.tensor_tensor(out=attm[:].rearrange("p a b -> p (a b)"), in0=attps[:, 0:512],
                                                              in1=amask[:, d].rearrange("p a b -> p (a b)"), op=ALU.mult), r=[attps, amask], w=[attm])
                        for h in range(4):
                            k.op("pe", lambda e: e.matmul(ops[0:64, h * 128:(h + 1) * 128], lhsT=v[:, h * 64:(h + 1) * 64], rhs=attm[:, h, :],
                                                          start=(h == 0), stop=False, skip_group_check=True), r=[v, attm], w=[ops])
                        order = range(4) if d == 0 else range(3, -1, -1)
                        for ji, j in enumerate(order):
                            ce = ti * 4 + j
                            Mj = Mps[j // 2][0:64, (j % 2) * 256:(j % 2) * 256 + 256].rearrange("p (h x) -> p h x", h=4)
                            dec_bc = decv[:, :, ce:ce + 1].to_broadcast([64, 4, 64])
                            sb_ = Sbf[sbi[0] % 8]
                            sbi[0] += 1
                            S = Sx[sidx[0] % 2]
                            Sn = Sx[(sidx[0] + 1) % 2]
                            St = Stx[sidx[0] % 2]
                            sidx[0] += 1
                            if d == 0:
                                k.op(HG_SBF_ENG, (lambda e: e.copy(out=sb_[:], in_=S[:])) if HG_SBF_ENG == "act" else (lambda e: e.tensor_copy(out=sb_[:], in_=S[:])), r=[S], w=[sb_])
                                k.op("dve", lambda e: e.tensor_tensor(out=St[:], in0=S[:], in1=Mj, op=ALU.add), r=[S, Mps[j // 2]], w=[St])
                                k.op("dve", lambda e: e.tensor_tensor(out=Sn[:], in0=St[:], in1=dec_bc, op=ALU.mult), r=[St, dec], w=[Sn])
                            else:
                                k.op("dve", lambda e: e.tensor_tensor(out=St[:], in0=S[:], in1=dec_bc, op=ALU.mult), r=[S, dec], w=[St])
                                k.op(HG_SBF_ENG, (lambda e: e.copy(out=sb_[:], in_=St[:])) if HG_SBF_ENG == "act" else (lambda e: e.tensor_copy(out=sb_[:], in_=St[:])), r=[St], w=[sb_])
                                k.op("dve", lambda e: e.tensor_tensor(out=Sn[:], in0=St[:], in1=Mj, op=ALU.add), r=[St, Mps[j // 2]], w=[Sn])
                            for h in range(4):
                                last = (ji == 3 and h == 3)
                                k.op("pe", lambda e: e.matmul(ops[0:64, h * 128 + j * 32:h * 128 + (j + 1) * 32], lhsT=sb_[:, h, :],
                                                              rhs=qd[:, h, ti * 128 + j * 32:ti * 128 + (j + 1) * 32], start=False, stop=last,
                                                              skip_group_check=True), r=[sb_, bqd], w=[ops])
                        opsv = ops[0:64, 0:512].rearrange("p (h t) -> p h t", h=4)
                        if d == 1:
                            k.op("act", lambda e: e.copy(out=ob_[:], in_=opsv), r=[ops], w=[ob_])
                            k.dma(self.s_ob[:, :, gt0:gt0 + 128], ob_[:], r=[ob_])
                        else:
                            k.op("dve", lambda e: e.tensor_tensor(out=osum[:], in0=opsv, in1=ob_[:], op=ALU.add), r=[ops, ob_], w=[osum])
                            o2 = osum[:].rearrange("p h t -> p (h t)")
                            k.op("pool", lambda e: e.tensor_tensor(out=sqo[:], in0=o2, in1=o2, op=ALU.mult), r=[osum], w=[sqo])
                            k.op("pe", lambda e: e.matmul(ssps[0:64, 0:512], lhsT=self.ones_bf[0:64, 0:64], rhs=sqo[:], start=True, stop=True),
                                 r=[sqo, self.ones_bf], w=[ssps])
                            k.op("act", lambda e: e.activation(out=rst[:], in_=ssps[0:64, 0:512], func=AF.Sqrt, bias=self.epsb[0:64, :], scale=1.0 / 64),
                                 r=[ssps, self.epsb], w=[rst])
                            k.op("dve", lambda e: e.reciprocal(out=rst[:], in_=rst[:]), r=[rst], w=[rst])
                            k.op("dve", lambda e: e.scalar_tensor_tensor(out=o2, in0=o2, scalar=ng[:, 0:1], in1=rst[:], op0=ALU.mult, op1=ALU.mult),
                                 r=[osum, ng, rst], w=[osum])
                            k.op("pool", lambda e: e.tensor_tensor(out=yo_[:], in0=osum[:], in1=gs_[:], op=ALU.mult), r=[osum, gs_], w=[yo_])
                            k.dma(ymix_v[:, :, gt0:gt0 + 128], yo_[:], r=[yo_])
                    while pending:
                        pending.pop(0)()
                k.barrier()

    def phase_s5_gen(self, l):
        k = self.k
        need_ctx = l < DEPTH - 1
        NC_ = TALL // 8
        mul, add, sub = ALU.mult, ALU.add, ALU.subtract
        with contextlib.ExitStack() as es:
            Tm = k.sb(es, "5Tm", [128, 32, 128], BF16)
            Gt = k.sb(es, "5Gt", [128, 32, 2, 64], BF16)
            Er = k.sb(es, "5Er", [64, 32, 128], BF16)
            Ei = k.sb(es, "5Ei", [64, 32, 128], BF16)
            A8 = k.sb(es, "5A8", [64, 4, 32], F32)
            U = k.sb(es, "5U", [128, 16, NC_], BF16)
            tmask = k.sb(es, "5tmask", [128, 2, 128], F32)
            k.dma(tmask[:], self.s5_tmask, w=[tmask])
            with contextlib.ExitStack() as e2:
                def t32(nm):
                    return k.sb(e2, nm, [64, 32], F32)
                lre, lim, ldt = t32("lre"), t32("lim"), t32("ldt")
                for dst, src in ((lre, self.s5_lam_re), (lim, self.s5_lam_im), (ldt, self.s5_log_dt)):
                    self.load_fm(e2, dst[:], src[l].rearrange("d g p -> (d g) p"), 32, [dst], wd=64)
                Bre = k.sb(e2, "Bre", [64, 32, 16], F32)
                Bim = k.sb(e2, "Bim", [64, 32, 16], F32)
                Cre = k.sb(e2, "Cre", [64, 32, 16], F32)
                Cim = k.sb(e2, "Cim", [64, 32, 16], F32)
                for dst, src in ((Bre, self.s5_b_re), (Bim, self.s5_b_im)):
                    for d in range(2):
                        k.dma(dst[:, d * 16:(d + 1) * 16, :], src[l, d].rearrange("g p h -> p g h"), w=[dst], allow_slow_non_contiguous=True)
                for dst, src in ((Cre, self.s5_c_re), (Cim, self.s5_c_im)):
                    for q4 in range(4):
                        self.load_fm(e2, dst[:].rearrange("p a h -> p (a h)")[:, q4 * 128:(q4 + 1) * 128],
                                     src[l].rearrange("d g h p -> (d g h) p")[q4 * 128:(q4 + 1) * 128, :], 128, [dst], wd=64)
                dt_, mag, th, c16, s16 = t32("dt"), t32("mag"), t32("th"), t32("c16"), t32("s16")
                t1, t2 = t32("t1"), t32("t2")
                k.op("act", lambda e: e.activation(out=dt_[:], in_=ldt[:], func=AF.Exp), r=[ldt], w=[dt_])
                k.op("dve", lambda e: e.tensor_tensor(out=mag[:], in0=lre[:], in1=dt_[:], op=mul), r=[lre, dt_], w=[mag])
                k.op("dve", lambda e: e.tensor_tensor(out=th[:], in0=lim[:], in1=dt_[:], op=mul), r=[lim, dt_], w=[th])
                k.op("act", lambda e: e.activation(out=mag[:], in_=mag[:], func=AF.Exp, scale=1.0 / 16), r=[mag], w=[mag])
                halfpi = k.sb(e2, "halfpi", [64, 1], F32)
                k.op("dve", lambda e: e.memset(halfpi[:], math.pi / 2), w=[halfpi])
                k.op("act", lambda e: e.activation(out=s16[:], in_=th[:], func=AF.Sin, scale=1.0 / 16), r=[th], w=[s16])
                k.op("act", lambda e: e.activation(out=c16[:], in_=th[:], func=AF.Sin, scale=1.0 / 16, bias=halfpi[:]), r=[th, halfpi], w=[c16])
                are, aim = t32("are"), t32("aim")
                k.op("dve", lambda e: e.tensor_tensor(out=are[:], in0=mag[:], in1=c16[:], op=mul), r=[mag, c16], w=[are])
                k.op("dve", lambda e: e.tensor_tensor(out=aim[:], in0=mag[:], in1=s16[:], op=mul), r=[mag, s16], w=[aim])

                def cmul(ore, oim, xr, xi, yr, yi, rk, wk, tA, tB):
                    k.op("dve", lambda e: e.tensor_tensor(out=tA, in0=xr, in1=yr, op=mul), r=rk, w=[wk[2]])
                    k.op("dve", lambda e: e.tensor_tensor(out=tB, in0=xi, in1=yi, op=mul), r=rk, w=[wk[3]])
                    k.op("dve", lambda e: e.tensor_tensor(out=tB, in0=tA, in1=tB, op=sub), r=[wk[2], wk[3]], w=[wk[3]])
                    k.op("dve", lambda e: e.tensor_tensor(out=tA, in0=xr, in1=yi, op=mul), r=rk, w=[wk[2]])
                    k.op("dve", lambda e: e.tensor_tensor(out=oim, in0=xi, in1=yr, op=mul), r=rk, w=[wk[1]])
                    k.op("dve", lambda e: e.tensor_tensor(out=oim, in0=oim, in1=tA, op=add), r=[wk[1], wk[2]], w=[wk[1]])
                    k.op("dve", lambda e: e.tensor_copy(out=ore, in_=tB), r=[wk[3]], w=[wk[0]])

                for _ in range(4):
                    cmul(are[:], aim[:], are[:], aim[:], are[:], aim[:], [are, aim], [are, aim, t1, t2], t1[:], t2[:])
                cfr, cfi, den = t32("cfr"), t32("cfi"), t32("den")
                k.op("dve", lambda e: e.tensor_tensor(out=den[:], in0=lre[:], in1=lre[:], op=mul), r=[lre], w=[den])
                k.op("dve", lambda e: e.tensor_tensor(out=t1[:], in0=lim[:], in1=lim[:], op=mul), r=[lim], w=[t1])
                k.op("dve", lambda e: e.tensor_tensor(out=den[:], in0=den[:], in1=t1[:], op=add), r=[den, t1], w=[den])
                k.op("dve", lambda e: e.reciprocal(out=den[:], in_=den[:]), r=[den], w=[den])
                am1 = t32("am1")
                nlim = t32("nlim")
                k.op("dve", lambda e: e.tensor_scalar(out=am1[:], in0=are[:], scalar1=-1.0, scalar2=None, op0=add), r=[are], w=[am1])
                k.op("dve", lambda e: e.tensor_scalar(out=nlim[:], in0=lim[:], scalar1=-1.0, scalar2=None, op0=mul), r=[lim], w=[nlim])
                cmul(cfr[:], cfi[:], am1[:], aim[:], lre[:], nlim[:], [am1, aim, lre, nlim], [cfr, cfi, t1, t2], t1[:], t2[:])
                k.op("dve", lambda e: e.tensor_tensor(out=cfr[:], in0=cfr[:], in1=den[:], op=mul), r=[cfr, den], w=[cfr])
                k.op("dve", lambda e: e.tensor_tensor(out=cfi[:], in0=cfi[:], in1=den[:], op=mul), r=[cfi, den], w=[cfi])
                ire, iim = t32("ire"), t32("iim")
                k.op("dve", lambda e: e.tensor_tensor(out=den[:], in0=are[:], in1=are[:], op=mul), r=[are], w=[den])
                k.op("dve", lambda e: e.tensor_tensor(out=t1[:], in0=aim[:], in1=aim[:], op=mul), r=[aim], w=[t1])
                k.op("dve", lambda e: e.tensor_tensor(out=den[:], in0=den[:], in1=t1[:], op=add), r=[den, t1], w=[den])
                k.op("dve", lambda e: e.reciprocal(out=den[:], in_=den[:]), r=[den], w=[den])
                k.op("dve", lambda e: e.tensor_tensor(out=ire[:], in0=are[:], in1=den[:], op=mul), r=[are, den], w=[ire])
                k.op("dve", lambda e: e.scalar_tensor_tensor(out=iim[:], in0=aim[:], scalar=-1.0, in1=den[:], op0=mul, op1=mul), r=[aim, den], w=[iim])
                def t512(nm):
                    return k.sb(e2, nm, [64, 32, 16], F32)
                Bbr, Bbi, u1, u2, xr, xi = t512("Bbr"), t512("Bbi"), t512("u1"), t512("u2"), t512("xr"), t512("xi")
                bc = lambda t: t[:].unsqueeze(2).to_broadcast([64, 32, 16])
                cmul(Bbr[:], Bbi[:], bc(cfr), bc(cfi), Bre[:], Bim[:], [cfr, cfi, Bre, Bim], [Bbr, Bbi, u1, u2], u1[:], u2[:])
                Lr = k.sb(e2, "Lr", [64, 32, 8, 16], F32)
                Li = k.sb(e2, "Li", [64, 32, 8, 16], F32)
                Rr = k.sb(e2, "Rr", [64, 32, 8, 16], F32)
                Ri = k.sb(e2, "Ri", [64, 32, 8, 16], F32)
                Gr = k.sb(e2, "Gr", [64, 32, 8, 16], F32)
                Gi = k.sb(e2, "Gi", [64, 32, 8, 16], F32)
                Erv = Er[:].rearrange("p a (t h) -> p a t h", h=16)
                Eiv = Ei[:].rearrange("p a (t h) -> p a t h", h=16)
                pr, pi_, qr, qi = t32("pr"), t32("pi"), t32("qr"), t32("qi")
                k.op("dve", lambda e: e.memset(pr[:], 1.0), w=[pr])
                k.op("dve", lambda e: e.memset(pi_[:], 0.0), w=[pi_])
                k.op("dve", lambda e: e.memset(qr[:], 1.0), w=[qr])
                k.op("dve", lambda e: e.memset(qi[:], 0.0), w=[qi])
                F_, Bk = slice(0, 16), slice(16, 32)
                for kk_ in range(9):
                    if kk_ > 0:
                        cmul(pr[:], pi_[:], pr[:], pi_[:], are[:], aim[:], [pr, pi_, are, aim], [pr, pi_, t1, t2], t1[:], t2[:])
                    if kk_ <= 7:
                        cmul(xr[:], xi[:], bc(pr), bc(pi_), Bbr[:], Bbi[:], [pr, pi_, Bbr, Bbi], [xr, xi, u1, u2], u1[:], u2[:])
                        for (dst, src) in ((Gr, xr), (Gi, xi)):
                            k.op("pool", lambda e: e.tensor_copy(out=dst[:, F_, 7 - kk_, :], in_=src[:, F_, :]), r=[src, dst], w=[dst])
                            k.op("pool", lambda e: e.tensor_copy(out=dst[:, Bk, kk_, :], in_=src[:, Bk, :]), r=[src, dst], w=[dst])
                    cmul(xr[:], xi[:], bc(pr), bc(pi_), Cre[:], Cim[:], [pr, pi_, Cre, Cim], [xr, xi, u1, u2], u1[:], u2[:])
                    if kk_ <= 7:
                        k.op("pool", lambda e: e.tensor_copy(out=Rr[:, F_, kk_, :], in_=xr[:, F_, :]), r=[xr, Rr], w=[Rr])
                        k.op("pool", lambda e: e.tensor_copy(out=Rr[:, Bk, 7 - kk_, :], in_=xr[:, Bk, :]), r=[xr, Rr], w=[Rr])
                        k.op("pool", lambda e: e.tensor_scalar(out=Ri[:, F_, kk_, :], in0=xi[:, F_, :], scalar1=-1.0, scalar2=None, op0=mul), r=[xi, Ri], w=[Ri])
                        k.op("pool", lambda e: e.tensor_scalar(out=Ri[:, Bk, 7 - kk_, :], in0=xi[:, Bk, :], scalar1=-1.0, scalar2=None, op0=mul), r=[xi, Ri], w=[Ri])
                    if kk_ >= 1:
                        k.op("act", lambda e: e.copy(out=Erv[:, F_, kk_ - 1, :], in_=xr[:, F_, :]), r=[xr, Er], w=[Er])
                        k.op("act", lambda e: e.copy(out=Erv[:, Bk, 8 - kk_, :], in_=xr[:, Bk, :]), r=[xr, Er], w=[Er])
                        k.op("act", lambda e: e.activation(out=Eiv[:, F_, kk_ - 1, :], in_=xi[:, F_, :], func=AF.Copy, scale=-1.0), r=[xi, Ei], w=[Ei])
                        k.op("act", lambda e: e.activation(out=Eiv[:, Bk, 8 - kk_, :], in_=xi[:, Bk, :], func=AF.Copy, scale=-1.0), r=[xi, Ei], w=[Ei])
                    if kk_ == 8:
                        k.op("dve", lambda e: e.tensor_copy(out=A8[:, 0, :], in_=pr[:]), r=[pr], w=[A8])
                        k.op("dve", lambda e: e.tensor_copy(out=A8[:, 1, :], in_=pr[:]), r=[pr, A8], w=[A8])
                        k.op("dve", lambda e: e.tensor_scalar(out=A8[:, 2, :], in0=pi_[:], scalar1=-1.0, scalar2=None, op0=mul), r=[pi_, A8], w=[A8])
                        k.op("dve", lambda e: e.tensor_copy(out=A8[:, 3, :], in_=pi_[:]), r=[pi_, A8], w=[A8])
                    if kk_ <= 7:
                        if kk_ > 0:
                            cmul(qr[:], qi[:], qr[:], qi[:], ire[:], iim[:], [qr, qi, ire, iim], [qr, qi, t1, t2], t1[:], t2[:])
                        cmul(xr[:], xi[:], bc(qr), bc(qi), Bbr[:], Bbi[:], [qr, qi, Bbr, Bbi], [xr, xi, u1, u2], u1[:], u2[:])
                        for (dst, src) in ((Lr, xr), (Li, xi)):
                            k.op("pool", lambda e: e.tensor_copy(out=dst[:, F_, kk_, :], in_=src[:, F_, :]), r=[src, dst], w=[dst])
                            k.op("pool", lambda e: e.tensor_copy(out=dst[:, Bk, 7 - kk_, :], in_=src[:, Bk, :]), r=[src, dst], w=[dst])
                fl = lambda t: t[:].rearrange("p a j h -> p a (j h)")
                for dg in range(32):
                    d = dg // 16
                    ps = self.psb[dg % 2]
                    k.op("pe", lambda e: e.matmul(ps[:, 0:128], lhsT=fl(Lr)[:, dg, :], rhs=fl(Rr)[:, dg, :], start=True, stop=False), r=[Lr, Rr], w=[ps])
                    k.op("pe", lambda e: e.matmul(ps[:, 0:128], lhsT=fl(Li)[:, dg, :], rhs=fl(Ri)[:, dg, :], start=False, stop=True), r=[Li, Ri], w=[ps])
                    k.op("dve", lambda e: e.tensor_tensor(out=Tm[:, dg, :], in0=ps[:, 0:128], in1=tmask[:, d, :], op=mul), r=[ps, tmask], w=[Tm])
                    ps2 = self.psb[2 + dg % 2]
                    k.op("pe", lambda e: e.transpose(out=ps2[:, 0:64], in_=fl(Gr)[:, dg, :], identity=self.ident_f[0:64, 0:64]), r=[Gr, self.ident_f], w=[ps2])
                    k.op("pe", lambda e: e.transpose(out=ps2[:, 64:128], in_=fl(Gi)[:, dg, :], identity=self.ident_f[0:64, 0:64]), r=[Gi, self.ident_f], w=[ps2])
                    k.op("act", lambda e: e.copy(out=Gt[:, dg].rearrange("p a b -> p (a b)"), in_=ps2[:, 0:128]), r=[ps2], w=[Gt])
                k.barrier()
            CBM = 64
            eu = contextlib.ExitStack()
            utok = k.sb(eu, "5utok", [128, 8, 256], F32)
            ub = k.sb(eu, "5ub", [128, 8, 256], BF16)
            ublocks = [(0, 32)] + [(32 + 128 * j, 128) for j in range(8)]
            trp = self.psb[4]
            trp_bf = trp[:].bitcast(BF16)
            usrc = self.s_u.rearrange("(c t) f -> c t f", t=8)
            for (c0, cb) in ublocks:
                k.dma(utok[0:cb], usrc[c0:c0 + cb], w=[utok])
                k.op("dve", lambda e: e.tensor_copy(out=ub[0:cb].rearrange("c a b -> c (a b)").rearrange("c (g t h) -> c g t h", g=16, t=8),
                                                    in_=utok[0:cb].rearrange("c t (g h) -> c g t h", g=16)), r=[utok], w=[ub])
                for g8 in range(2):
                    for gi in range(8):
                        g = g8 * 8 + gi
                        k.op("pe", lambda e: e.transpose(out=trp_bf[:, gi * 128:gi * 128 + cb], in_=ub[0:cb].rearrange("c a b -> c (a b)")[:, g * 128:(g + 1) * 128],
                                                         identity=self.ident_bf[0:cb, 0:cb]), r=[ub, self.ident_bf], w=[trp])
                    k.op("dve", lambda e: e.tensor_copy(out=U[:, g8 * 8:(g8 + 1) * 8, c0:c0 + cb],
                                                        in_=trp_bf[:, 0:1024].rearrange("p (g c) -> p g c", g=8)[:, :, 0:cb]), r=[trp], w=[U])
            k.barrier()
            eu.close()
            em = contextlib.ExitStack()
            Wt = k.sb(em, "5W", [64, 2, 32, CBM], F32)
            SP = [k.sb(em, f"5SP{d}", [64, 2, 16, CBM], BF16) for d in range(2)]
            Hist = [k.sb(em, f"5H{d}", [64, CBM + 1, 3, 16], F32) for d in range(2)]
            Tt = [k.sb(em, f"5T{d}", [64, 2, 16], F32) for d in range(2)]
            Vt = [k.sb(em, f"5V{d}", [64, 2, 16], F32) for d in range(2)]
            yev = [k.sb(em, f"5yev{i}", [128, 512], F32) for i in range(2)]
            blocks = [(0, 32)] + [(32 + CBM * j, CBM) for j in range((NC_ - 32) // CBM)]
            border = [blocks, [blocks[0]] + blocks[:0:-1]]
            A8v = [[A8[:, 0:2, d * 16:(d + 1) * 16], A8[:, 2:4, d * 16:(d + 1) * 16]] for d in range(2)]
            engs = ("dve", "pool")
            psS = self.psb[7]
            nbs = len(blocks)
            self.s5_nitems = sum(16 + 8 + b_[1] + 1 for b_ in blocks)
            yield "ready"

            def w_group(bs, d, g4, ri):
                cb = border[0][bs][1]
                cds = [border[0][bs][0], border[1][bs][0]]
                for gi in range(4):
                    g = g4 * 4 + gi
                    k.op("pe", lambda e: e.matmul(psS[0:64, gi * 128:gi * 128 + cb], lhsT=Gt[:, d * 16 + g, ri, :], rhs=U[:, g, cds[d]:cds[d] + cb],
                                                  start=True, stop=True), r=[Gt, U], w=[psS])
                k.op("dve", lambda e: e.tensor_copy(out=Wt[:, ri, d * 16 + g4 * 4:d * 16 + g4 * 4 + 4, 0:cb],
                                                    in_=psS[0:64, 0:512].rearrange("p (g c) -> p g c", g=4)[:, :, 0:cb]), r=[psS], w=[Wt])

            def y_group(bs, d, g4):
                cb = border[0][bs][1]
                cds = [border[0][bs][0], border[1][bs][0]]
                for gi in range(4):
                    g = g4 * 4 + gi
                    o = psS[:, gi * 128:gi * 128 + cb]
                    k.op("pe", lambda e: e.matmul(o, lhsT=Tm[:, d * 16 + g, :], rhs=U[:, g, cds[d]:cds[d] + cb], start=(gi == 0), stop=False,
                                                  skip_group_check=True), r=[Tm, U], w=[psS])
                    k.op("pe", lambda e: e.matmul(o, lhsT=Er[:, d * 16 + g, :], rhs=SP[d][:, 0, g, 0:cb], start=False, stop=False,
                                                  skip_group_check=True), r=[Er, SP[d]], w=[psS])
                    k.op("pe", lambda e: e.matmul(o, lhsT=Ei[:, d * 16 + g, :], rhs=SP[d][:, 1, g, 0:cb], start=False, stop=True,
                                                  skip_group_check=True), r=[Ei, SP[d]], w=[psS])
                ye = yev[g4 % 2]
                k.op("dve", lambda e: e.tensor_copy(out=ye[:], in_=psS[:, 0:512]), r=[psS], w=[ye])
                k.dma(self.s_y[d][:, g4 * 4:(g4 + 1) * 4, cds[d]:cds[d] + cb], ye[:].rearrange("p (g c) -> p g c", g=4)[:, :, 0:cb], r=[ye])

            prev_cb = None
            for bs in range(nbs):
                cb = border[0][bs][1]
                for d in range(2):
                    dst = Hist[d][:, 0] if d == 0 else Hist[d][:, cb]
                    if bs == 0:
                        k.op(engs[d], lambda e: e.memset(dst, 0.0), r=[Hist[d]], w=[Hist[d]])
                    else:
                        src = Hist[d][:, prev_cb] if d == 0 else Hist[d][:, 0]
                        k.op(engs[d], lambda e: e.tensor_copy(out=dst, in_=src), r=[Hist[d]], w=[Hist[d]])
                for d in range(2):
                    for g4 in range(4):
                        for ri in range(2):
                            w_group(bs, d, g4, ri)
                            yield
                if bs > 0:
                    for d in range(2):
                        for g4 in range(4):
                            y_group(bs - 1, d, g4)
                            yield
                prev_cb = cb
                for i in range(cb):
                    for d, eng in ((0, "dve"), (1, "pool")):
                        H, T_, V_ = Hist[d], Tt[d], Vt[d]
                        if d == 0:
                            col, pv, cu = i, i, i + 1
                        else:
                            col, pv, cu = cb - 1 - i, cb - i, cb - 1 - i
                        k.op(eng, lambda e: e.tensor_tensor(out=T_[:], in0=H[:, pv, 0:2, :], in1=A8v[d][0], op=mul), r=[H, A8], w=[T_])
                        k.op(eng, lambda e: e.tensor_tensor(out=V_[:], in0=H[:, pv, 1:3, :], in1=A8v[d][1], op=mul), r=[H, A8], w=[V_])
                        k.op(eng, lambda e: e.tensor_tensor(out=T_[:], in0=T_[:], in1=V_[:], op=add), r=[T_, V_], w=[T_])
                        k.op(eng, lambda e: e.tensor_tensor(out=H[:, cu, 0:2, :], in0=T_[:], in1=Wt[:, :, d * 16:(d + 1) * 16, col], op=add),
                             r=[T_, Wt, H], w=[H])
                        k.op(eng, lambda e: e.tensor_copy(out=H[:, cu, 2, :], in_=H[:, cu, 0, :]), r=[H], w=[H])
                    yield
                for d in range(2):
                    lo = 0 if d == 0 else 1
                    k.op("pool", lambda e: e.tensor_copy(out=SP[d][:, :, :, 0:cb].rearrange("p r g c -> p c r g"), in_=Hist[d][:, lo:lo + cb, 0:2, :]),
                         r=[Hist[d]], w=[SP[d]])
                yield
            for d in range(2):
                for g4 in range(4):
                    y_group(nbs - 1, d, g4)
                    yield
            k.barrier()
            em.close()
            with contextlib.ExitStack() as e3:
                utok = k.sb(e3, "5utok2", [128, 8, 256], F32)
                D8 = k.sb(e3, "5D8", [128, 8, 256], F32)
                wg = k.sb(e3, "5wg", [128, 2, 256], BF16)
                bg = k.sb(e3, "5bg", [128, 2], F32)
                for t in range(8):
                    k.dma(D8[:, t, :], self.s5_d[l:l + 1, :].partition_broadcast(128) if False else self.s5_d[l].partition_broadcast(128), w=[D8])
                k.dma(wg[:], self.s5_w_glu[l].rearrange("(c p) n -> p c n", p=128), w=[wg], q="pool")
                self.load_fm(e3, bg[:], self.s5_b_glu[l].rearrange("(c p) -> c p", p=128), 2, [bg])
                yf = k.sb(e3, "5yf", [128, 16, 128], F32)
                yb = k.sb(e3, "5yb", [128, 16, 128], F32)
                ytok = k.sb(e3, "5ytok", [128, 8, 256], F32)
                ygel = k.sb(e3, "5ygel", [128, 8, 256], BF16)
                yT = k.sb(e3, "5yT", [128, 2, 1024], BF16)
                sgm = [k.sb(e3, f"5sg{i}", [128, 512], F32) for i in range(2)]
                yao = [k.sb(e3, f"5ya{i}", [128, 512], BF16) for i in range(2)]
                for (c0, cb) in blocks:
                    if c0 == 0 and not need_ctx:
                        continue
                    ntok = cb * 8
                    k.dma(utok[0:cb], usrc[c0:c0 + cb], w=[utok])
                    k.dma(yf[:, :, 0:cb], self.s_y[0][:, :, c0:c0 + cb], w=[yf])
                    k.dma(yb[:, :, 0:cb], self.s_y[1][:, :, c0:c0 + cb], w=[yb])
                    k.op("pool", lambda e: e.tensor_tensor(out=yf[:, :, 0:cb], in0=yf[:, :, 0:cb], in1=yb[:, :, 0:cb], op=add), r=[yf, yb], w=[yf])
                    k.op("dve", lambda e: e.tensor_tensor(out=ytok[0:cb], in0=utok[0:cb], in1=D8[0:cb], op=mul), r=[utok, D8], w=[ytok])
                    for g4 in range(4):
                        ps = self.psb[g4 % 2]
                        for gi in range(4):
                            g = g4 * 4 + gi
                            k.op("pe", lambda e: e.transpose(out=ps[0:cb, gi * 128:(gi + 1) * 128], in_=yf[:, g, 0:cb], identity=self.ident_f[:, :]),
                                 r=[yf, self.ident_f], w=[ps])
                        yv = ytok[0:cb, :, g4 * 64:(g4 + 1) * 64].rearrange("c t (g h) -> c g t h", g=4)
                        pv = ps[0:cb, 0:512].rearrange("c (g t h) -> c g t h", g=4, t=8)
                        k.op("dve", lambda e: e.tensor_tensor(out=yv, in0=pv, in1=yv, op=add), r=[ps, ytok], w=[ytok])
                    k.op("act", lambda e: e.activation(out=ygel[0:cb], in_=ytok[0:cb], func=AF.Gelu), r=[ytok], w=[ygel])
                    for kc in range(2):
                        for t in range(8):
                            k.op("pe", lambda e: e.transpose(out=trp_bf[:, t * 128:t * 128 + cb], in_=ygel[0:cb, t, kc * 128:(kc + 1) * 128],
                                                             identity=self.ident_bf[0:cb, 0:cb]), r=[ygel, self.ident_bf], w=[trp])
                        k.op("dve", lambda e: e.tensor_copy(out=yT[:, kc, 0:ntok].rearrange("p (c t) -> p t c", t=8),
                                                            in_=trp_bf[:, 0:1024].rearrange("p (t c) -> p t c", t=8)[:, :, 0:cb]), r=[trp], w=[yT])
                    for r0 in range(0, ntok, 512):
                        n = min(512, ntok - r0)
                        for oc in range(2):
                            ps = self.psb[2 + oc]
                            for kc in range(2):
                                k.op("pe", lambda e: e.matmul(ps[:, 0:n], lhsT=wg[:, kc, oc * 128:(oc + 1) * 128], rhs=yT[:, kc, r0:r0 + n],
                                                              start=(kc == 0), stop=(kc == 1)), r=[wg, yT], w=[ps])
                            k.op("act", lambda e: e.activation(out=sgm[oc][:, 0:n], in_=ps[:, 0:n], func=AF.Sigmoid, bias=bg[:, oc:oc + 1]),
                                 r=[ps, bg], w=[sgm[oc]])
                            k.op("dve", lambda e: e.tensor_tensor(out=yao[oc][:, 0:n], in0=yT[:, oc, r0:r0 + n], in1=sgm[oc][:, 0:n], op=mul),
                                 r=[yT, sgm[oc]], w=[yao[oc]])
                            tok0 = c0 * 8 + r0
                            k.dma(self.ymix[oc * 128:(oc + 1) * 128, tok0:tok0 + n], yao[oc][:, 0:n], r=[yao[oc]])
                k.barrier()

def _prep_inputs(inputs):
    f = lambda a: np.ascontiguousarray(np.asarray(a, dtype=np.float32))
    cols = _win_cols()
    qcols = _wqb_cols()
    rs, rm = _rope_tables()
    shared = {
        "w_mod": f(inputs["w_mod"]), "b_mod": f(inputs["b_mod"]), "norm1_g": f(inputs["norm1_g"]), "norm2_g": f(inputs["norm2_g"]),
        "w_in": f(np.asarray(inputs["w_in"])[:, :, cols]), "w_out": f(inputs["w_out"]),
        "w_qb": f(np.asarray(inputs["mla_w_qb"])[:, :, qcols]), "w_kvb": f(inputs["mla_w_kvb"]),
        "qn_g": f(inputs["mla_q_norm_g"]), "kvn_g": f(inputs["mla_kv_norm_g"]),
        "w_up": f(inputs["ffn_w_up"]), "w_down": f(inputs["ffn_w_down"]), "final_g": f(inputs["final_norm_g"]),
        "rope_s": rs, "rope_m": rm, "ident": np.eye(128, dtype=np.float32),
        "swa_mask": _swa_mask(), "swa_sink": f(inputs["swa_sink"]),
        "hg_lb": f(inputs["hg_lb"]), "s5_tmask": _s5_tmask(),
        **{kk: f(inputs[kk]) for kk in ("s5_lam_re", "s5_lam_im", "s5_log_dt", "s5_b_re", "s5_b_im", "s5_c_re", "s5_c_im", "s5_d", "s5_w_glu", "s5_b_glu")}, "hg_norm_g": f(inputs["hg_norm_g"]), **_hg_consts(),
    }
    x = np.asarray(inputs["x"]); ctx = np.asarray(inputs["ctx"]); c = np.asarray(inputs["c"]); c_ctx = np.asarray(inputs["c_ctx"])
    maps = []
    for b in range(NCORES):
        m = dict(shared)
        m["xin"] = f(np.concatenate([ctx[b], x[b]], 0).T)
        m["cc"] = f(np.stack([c[b], c_ctx], 0))
        maps.append(m)
    return maps


def kernel(**inputs):
    bld = Builder()
    maps = _prep_inputs(inputs)
    res = run_bass_kernel_spmd(bld.nc, maps, core_ids=list(range(NCORES)))
    out = np.stack([np.ascontiguousarray(r["out"].T) for r in res.results], 0)
    return out.astype(np.float32)
```

```python
import contextlib
import math
import os
import numpy as np
import concourse.bass as bass
import concourse.mybir as mybir
from concourse.bass_utils import run_bass_kernel_spmd

F32 = mybir.dt.float32
BF16 = mybir.dt.bfloat16
ALU = mybir.AluOpType
AF = mybir.ActivationFunctionType

D = 1024
SEQ = 8192
CTX = 256
TALL = SEQ + CTX
DEPTH = 4
NCORES = 4
FFH = 2816
EPS = 1e-6
GRID_W = 64
MLA_SCALE = 96 ** -0.5
SWA_SCALE = 64 ** -0.5
SAME_ENGINE_SYNC = True
HG_PREP_ENG = os.environ.get("KHGP", "dve")
HG_SBF_ENG = os.environ.get("KHGS", "act")
OVERLAP_SWA = os.environ.get("KOVS", "0") == "1"
OVERLAP_S5 = os.environ.get("KOVL", "1") == "1"
ATTACH_WAIT = os.environ.get("KATTACH", "1") == "1"


class T:
    _n = 0

    def __init__(self, t, name, psum=False):
        self.t = t
        self.psum = psum
        T._n += 1
        self.key = (name, T._n)

    def __getitem__(self, idx):
        return self.t[idx]


class PV(T):
    def __init__(self, base, off, name):
        T.__init__(self, base.t, name, psum=True)
        self.off = off

    def _c(self, c):
        if isinstance(c, slice):
            a = self.off + (c.start or 0)
            b = self.off + (512 if c.stop is None else c.stop)
            return slice(a, b, c.step)
        return self.off + c

    def __getitem__(self, idx):
        if isinstance(idx, tuple):
            return self.t[(idx[0], self._c(idx[1])) + tuple(idx[2:])]
        return self.t[idx, self.off:self.off + 512]


class KB:
    def __init__(self, nc):
        self.nc = nc
        self.es = contextlib.ExitStack()
        self.eng = {"pe": nc.tensor, "act": nc.scalar, "dve": nc.vector, "pool": nc.gpsimd, "sp": nc.sync}
        self.sem = {e: self.es.enter_context(nc.semaphore("s_" + e)) for e in self.eng}
        self.cnt = {e: 0 for e in self.eng}
        self.lanes = {}
        self.lane_val = {}
        self.lane_rr = {}
        for q, n in (("sp", int(os.environ.get("KLANES", "12"))), ("pool", 8), ("act", 4)):
            self.lanes[q] = [self.es.enter_context(nc.semaphore(f"l_{q}{i}")) for i in range(n)]
            self.lane_rr[q] = 0
            for i in range(n):
                self.lane_val[(q, i)] = 0
        self.seen = {e: {} for e in self.eng}
        self.res = {}
        self.ninst = 0
        self.nwait = 0
        self.uid = 0

    def sb(self, es, name, shape, dtype):
        self.uid += 1
        t = es.enter_context(self.nc.sbuf_tensor(f"{name}_{self.uid}", list(shape), dtype))
        return T(t, name)

    def ps(self, es, name, shape, dtype=F32):
        self.uid += 1
        t = es.enter_context(self.nc.psum_tensor(f"{name}_{self.uid}", list(shape), dtype))
        return T(t, name, psum=True)

    def _semof(self, src):
        if src[0] == "eng":
            return self.sem[src[1]]
        return self.lanes[src[1]][src[2]]

    def _wait(self, engine, dep):
        src, val = dep
        if val <= 0:
            return
        if src[0] == "eng" and src[1] == engine:
            if engine == "pe" or not SAME_ENGINE_SYNC:
                return
        if self.seen[engine].get(src, 0) >= val:
            return
        self.eng[engine].wait_ge(self._semof(src), val)
        self.nwait += 1
        self.seen[engine][src] = val

    def _deps(self, r, w, me=None):
        deps = []
        for t in r:
            st = self.res.get(t.key)
            if st and st["w"]:
                deps.append(st["w"])
            if st and t.psum:
                deps.extend((src, v) for src, v in st["r"].items() if src != me)
        for t in w:
            st = self.res.get(t.key)
            if st:
                if st["w"]:
                    deps.append(st["w"])
                deps.extend(st["r"].items())
        return deps

    def _update(self, r, w, src, val):
        for t in r:
            st = self.res.setdefault(t.key, {"w": None, "r": {}})
            st["r"][src] = val
        for t in w:
            self.res[t.key] = {"w": (src, val), "r": {}}

    def _need(self, engine, dep):
        src, val = dep
        if val <= 0:
            return False
        if src[0] == "eng" and src[1] == engine and (engine == "pe" or not SAME_ENGINE_SYNC):
            return False
        return self.seen[engine].get(src, 0) < val

    def op(self, engine, fn, r=(), w=()):
        deps = [d for d in self._deps(r, w, ("eng", engine))]
        best = {}
        for src, val in deps:
            if self._need(engine, (src, val)) and val > best.get(src, 0):
                best[src] = val
        items = list(best.items())
        attach = None
        if ATTACH_WAIT and items:
            attach = items.pop()
        for dep in items:
            self._wait(engine, dep)
        ins = fn(self.eng[engine])
        if attach is not None:
            ins._wait_ge(self._semof(attach[0]), attach[1])
            self.seen[engine][attach[0]] = attach[1]
            self.nwait += 1
        self.cnt[engine] += 1
        ins.then_inc(self.sem[engine], 1)
        self.ninst += 1
        self._update(r, w, ("eng", engine), self.cnt[engine])
        return ins

    def dma(self, out, in_, r=(), w=(), q="sp", **kw):
        i = self.lane_rr[q]
        self.lane_rr[q] = (i + 1) % len(self.lanes[q])
        src = ("dma", q, i)
        deps = self._deps(r, w)
        deps.append((src, self.lane_val[(q, i)]))
        for dep in deps:
            self._wait(q, dep)
        ins = self.eng[q].dma_start(out=out, in_=in_, **kw)
        self.lane_val[(q, i)] += 16
        ins.then_inc(self.lanes[q][i], 16)
        self.ninst += 1
        self._update(r, w, src, self.lane_val[(q, i)])

    def barrier(self):
        for e in self.eng:
            for e2 in self.eng:
                self._wait(e, (("eng", e2), self.cnt[e2]))
            for (q, i), v in self.lane_val.items():
                self._wait(e, (("dma", q, i), v))
        self.res = {}

    def finish(self):
        for (q, i), v in self.lane_val.items():
            self._wait("sp", (("dma", q, i), v))
        for e2 in self.eng:
            if e2 != "sp":
                self._wait("sp", (("eng", e2), self.cnt[e2]))


def _blocks():
    out = [(0, CTX)]
    for j in range(SEQ // 512):
        out.append((CTX + 512 * j, 512))
    return out


def _win_cols():
    cols = {}
    base = {"u": 0, "sq": 256, "sk": 512, "sv": 640, "hq": 768, "zf": 1024, "zb": 1280, "hi": 1536, "hg": 1792,
            "cq": 2048, "ckv": 2304, "kr": 2432}

    def swap64(off):
        return np.concatenate([off + np.arange(16, 32), off + np.arange(0, 16), off + np.arange(48, 64), off + np.arange(32, 48)])

    def swap32(off):
        return np.concatenate([off + np.arange(8, 16), off + np.arange(0, 8), off + np.arange(24, 32), off + np.arange(16, 24)])

    sq = base["sq"]
    hA = np.concatenate([sq + np.arange(0, 64), sq + np.arange(128, 192)])
    hB = np.concatenate([sq + np.arange(64, 128), sq + np.arange(192, 256)])
    hAs = np.concatenate([swap64(sq + 0), swap64(sq + 128)])
    hBs = np.concatenate([swap64(sq + 64), swap64(sq + 192)])
    sk = base["sk"]
    fm = [hA, hB, hAs, hBs, sk + np.arange(128), np.concatenate([swap64(sk), swap64(sk + 64)]),
          base["hq"] + np.arange(256), base["zf"] + np.arange(256), base["zb"] + np.arange(256),
          base["hg"] + np.arange(256), base["cq"] + np.arange(256), base["ckv"] + np.arange(128),
          base["kr"] + np.arange(32), swap32(base["kr"])]
    tm = [base["u"] + np.arange(256), base["sv"] + np.arange(128), base["hi"] + np.arange(256)]
    return np.concatenate(fm + tm)


C_SQ, C_SQS, C_SK, C_SKS, C_HQ, C_ZF, C_ZB, C_HG, C_CQ, C_CKV, C_KR, C_KRS = 0, 256, 512, 640, 768, 1024, 1280, 1536, 1792, 2048, 2176, 2208
C_TM = 2240
NWIN = C_TM + 640


def _wqb_cols():
    def swap32(off):
        return np.concatenate([off + np.arange(8, 16), off + np.arange(0, 8), off + np.arange(24, 32), off + np.arange(16, 24)])
    cols = []
    for h in range(4):
        cols += [h * 96 + np.arange(96), swap32(h * 96 + 64)]
    return np.concatenate(cols)


def _swa_mask():
    kk = np.arange(128)[:, None]
    qq = np.arange(128)[None, :]
    lo = (qq <= kk).astype(np.float32)
    hi = (kk <= qq).astype(np.float32)
    m = np.stack([np.stack([lo] * 4, 1), np.stack([hi] * 4, 1)], 1)
    return np.ascontiguousarray(m.astype(np.float32))


def _hg_consts():
    t = np.arange(2048)
    rm = np.broadcast_to((t % 32 != 0).astype(np.float32)[None, :], (64, 2048))
    s_ = np.arange(128)[:, None]
    t_ = np.arange(128)[None, :]
    same = (s_ // 32) == (t_ // 32)
    fw = (same & (s_ <= t_)).astype(np.float32)
    bw = (same & (s_ >= t_)).astype(np.float32)
    am = np.stack([np.stack([fw] * 4, 1), np.stack([bw] * 4, 1)], 1)
    cm = (np.arange(128)[:, None] // 32 == np.arange(4)[None, :]).astype(np.float32)
    return {"hg_rmask": np.ascontiguousarray(rm), "hg_amask": np.ascontiguousarray(am.astype(np.float32)), "hg_cmask": cm}


def _s5_tmask():
    j = np.arange(128)[:, None] // 16
    t = np.arange(128)[None, :] // 16
    return np.ascontiguousarray(np.stack([(t >= j), (t <= j)], 1).astype(np.float32))


def _rope_tables():
    def tab(dim):
        rows = SEQ // GRID_W
        row = np.repeat(np.arange(rows, dtype=np.float64), GRID_W)
        col = np.tile(np.arange(GRID_W, dtype=np.float64), rows)
        nf = dim // 4
        inv = 10000.0 ** (-np.arange(nf, dtype=np.float64) / nf)
        ar = row[None, :] * inv[:, None]
        ac = col[None, :] * inv[:, None]
        C = np.concatenate([np.cos(ar), np.cos(ar), np.cos(ac), np.cos(ac)], 0)
        S = np.concatenate([-np.sin(ar), np.sin(ar), -np.sin(ac), np.sin(ac)], 0)
        C = np.concatenate([np.ones((dim, CTX)), C], 1)
        S = np.concatenate([np.zeros((dim, CTX)), S], 1)
        return C.astype(np.float32), S.astype(np.float32)
    c64, s64 = tab(64)
    c32, s32 = tab(32)
    rs = np.stack([np.concatenate([c64, c64], 0), np.concatenate([s64, s64], 0)])
    z = np.zeros((64, TALL), np.float32)
    rm = np.stack([np.concatenate([z, c32], 0), np.concatenate([z, s32], 0)])
    return rs, rm


class Builder:
    def __init__(self, nlayers=DEPTH, debug=None, stop_after=None, only=None):
        self.stop_after = stop_after
        self.only = only
        self.nl = nlayers
        self.debug = debug
        nc = bass.Bass("TRN2", target_bir_lowering=False)
        self.nc = nc
        self.k = KB(nc)
        dt = nc.dram_tensor

        def ext(name, shape, dtype=F32):
            return dt(name, list(shape), dtype, kind="ExternalInput").ap()

        def internal(name, shape, dtype=F32):
            return dt(name, list(shape), dtype, kind="Internal").ap()

        self.xin = ext("xin", [D, TALL])
        self.cc = ext("cc", [2, D])
        self.w_mod = ext("w_mod", [DEPTH, D, 6 * D])
        self.b_mod = ext("b_mod", [DEPTH, 6 * D])
        self.norm1_g = ext("norm1_g", [DEPTH, D])
        self.norm2_g = ext("norm2_g", [DEPTH, D])
        self.w_in = ext("w_in", [DEPTH, D, NWIN])
        self.w_out = ext("w_out", [DEPTH, D, D])
        self.w_qb = ext("w_qb", [DEPTH, 256, 512])
        self.w_kvb = ext("w_kvb", [DEPTH, 128, 512])
        self.qn_g = ext("qn_g", [DEPTH, 256])
        self.kvn_g = ext("kvn_g", [DEPTH, 128])
        self.w_up = ext("w_up", [DEPTH, D, 2 * FFH])
        self.w_down = ext("w_down", [DEPTH, FFH, D])
        self.final_g = ext("final_g", [D])
        self.rope_s = ext("rope_s", [2, 128, TALL])
        self.rope_m = ext("rope_m", [2, 96, TALL])
        self.ident = ext("ident", [128, 128])
        self.swa_mask = ext("swa_mask", [128, 2, 4, 128])
        self.swa_sink = ext("swa_sink", [DEPTH, 4])
        self.hg_lb = ext("hg_lb", [2, DEPTH, 256])
        self.s5_lam_re = ext("s5_lam_re", [DEPTH, 2, 16, 64])
        self.s5_lam_im = ext("s5_lam_im", [DEPTH, 2, 16, 64])
        self.s5_log_dt = ext("s5_log_dt", [DEPTH, 2, 16, 64])
        self.s5_b_re = ext("s5_b_re", [DEPTH, 2, 16, 64, 16])
        self.s5_b_im = ext("s5_b_im", [DEPTH, 2, 16, 64, 16])
        self.s5_c_re = ext("s5_c_re", [DEPTH, 2, 16, 16, 64])
        self.s5_c_im = ext("s5_c_im", [DEPTH, 2, 16, 16, 64])
        self.s5_d = ext("s5_d", [DEPTH, 256])
        self.s5_w_glu = ext("s5_w_glu", [DEPTH, 256, 256])
        self.s5_b_glu = ext("s5_b_glu", [DEPTH, 256])
        self.s5_tmask = ext("s5_tmask", [128, 2, 128])
        self.hg_norm_g = ext("hg_norm_g", [DEPTH, 64])
        self.hg_rmask = ext("hg_rmask", [64, 2048])
        self.hg_amask = ext("hg_amask", [128, 2, 4, 128])
        self.hg_cmask = ext("hg_cmask", [128, 4])
        self.out = dt("out", [D, SEQ], F32, kind="ExternalOutput").ap()
        self.xres = internal("xres", [D, TALL])
        self.s_sq = internal("s_sq", [256, TALL], BF16)
        self.s_sk = internal("s_sk", [128, TALL], BF16)
        self.s_sv = internal("s_sv", [TALL, 130], BF16)
        self.s_u = internal("s_u", [TALL, 256], F32)
        self.s_hq = internal("s_hq", [256, TALL], BF16)
        self.s_zf = internal("s_zf", [256, TALL], F32)
        self.s_zb = internal("s_zb", [256, TALL], F32)
        self.s_hg = internal("s_hg", [256, TALL], F32)
        self.s_hi = internal("s_hi", [TALL, 256], BF16)
        self.s_mq = internal("s_mq", [4, 96, TALL], BF16)
        self.s_mk = internal("s_mk", [4, 96, TALL], BF16)
        self.s_mv = internal("s_mv", [TALL, 260], BF16)
        self.ymix = internal("ymix", [D, TALL], BF16)
        self.s_ob = internal("s_ob", [64, 4, TALL], F32)
        self.s_y = [internal(f"s_y{d}", [128, 16, TALL // 8], F32) for d in range(2)]
        if debug:
            self.dbg = {n: dt("dbg_" + n, list(v[0]), v[1], kind="ExternalOutput").ap() for n, v in debug.items()}
        self.build()

    def build(self):
        k = self.k
        nc = self.nc
        es = k.es
        self.ones_bf = k.sb(es, "ones_bf", [128, 128], BF16)
        self.ident_bf = k.sb(es, "ident_bf", [128, 128], BF16)
        self.ident_f = k.sb(es, "ident_f", [128, 128], F32)
        self.mod = k.sb(es, "mod", [128, DEPTH, 48, 2], F32)
        self.gs = k.sb(es, "gs", [128, DEPTH, 2, 8, 2], F32)
        self.epsb = k.sb(es, "epsb", [128, 1], F32)
        k.op("dve", lambda e: e.memset(self.ones_bf[:], 1.0), w=[self.ones_bf])
        k.op("dve", lambda e: e.memset(self.epsb[:], EPS), w=[self.epsb])
        k.dma(self.ident_f[:], self.ident, w=[self.ident_f])
        k.dma(self.ident_bf[:], self.ident, w=[self.ident_bf], q="pool")
        self.psw = [k.ps(es, f"psw{i}", [128, 1024], F32) for i in range(4)]
        self.psb = [PV(self.psw[i // 2], 512 * (i % 2), f"psb{i}") for i in range(8)]
        self.setup_mod()
        k.barrier()
        self.setup_hglb()
        for l in range(self.nl):
            self.layer(l)
        if self.debug:
            k.barrier()
            for n in self.debug:
                src = self.debug[n][2](self) if len(self.debug[n]) > 2 else getattr(self, n)
                nd = len(src.shape)
                pat = " ".join("abcd"[:nd])
                fl = lambda a: a.rearrange(f"{pat} -> ({pat})").rearrange("(p f) -> p f", p=16)
                k.dma(fl(self.dbg[n]), fl(src))
        k.finish()
        es.close()

    def load_fm(self, es, dst_ap, src2d, n, wkeys, wd=128):
        k = self.k
        stg = k.sb(es, "stg", [128, 128], F32)
        ps = self.psb[7]
        k.dma(stg[0:n, 0:wd], src2d, w=[stg])
        k.op("pe", lambda e: e.transpose(out=ps[0:wd, 0:n], in_=stg[0:n, 0:wd], identity=self.ident_f[0:n, 0:n]),
             r=[stg, self.ident_f], w=[ps])
        k.op("dve", lambda e: e.tensor_copy(out=dst_ap, in_=ps[0:wd, 0:n]), r=[ps], w=wkeys)

    def setup_mod(self):
        k = self.k
        with contextlib.ExitStack() as es:
            craw = k.sb(es, "craw", [128, 8, 2], F32)
            csil = k.sb(es, "csil", [128, 8, 2], F32)
            bm = k.sb(es, "bm", [128, DEPTH, 48], F32)
            ng = k.sb(es, "ng", [128, 2, DEPTH, 8], F32)
            wm = [k.sb(es, f"wm{i}", [128, 8, 768], F32) for i in range(2)]
            self.load_fm(es, craw[:, :, 0], self.cc[0].rearrange("(c p) -> c p", p=128), 8, [craw])
            self.load_fm(es, craw[:, :, 1], self.cc[1].rearrange("(c p) -> c p", p=128), 8, [craw])
            for l in range(DEPTH):
                self.load_fm(es, bm[:, l, :], self.b_mod[l].rearrange("(j p) -> j p", p=128), 48, [bm])
            self.load_fm(es, ng[:, 0], self.norm1_g.rearrange("l (c p) -> (l c) p", p=128), 32, [ng])
            self.load_fm(es, ng[:, 1], self.norm2_g.rearrange("l (c p) -> (l c) p", p=128), 32, [ng])
            k.op("act", lambda e: e.activation(out=csil[:], in_=craw[:], func=AF.Silu), r=[craw], w=[csil])
            it = 0
            for l in range(self.nl):
                for grp in range(8):
                    wt = wm[it % 2]
                    it += 1
                    k.dma(wt[:], self.w_mod[l].rearrange("(c p) n -> p c n", p=128)[:, :, grp * 768:(grp + 1) * 768], w=[wt])
                    for n in range(6):
                        ps = self.psb[n % 4]
                        for kc in range(8):
                            k.op("pe", lambda e: e.matmul(ps[:, 0:2], lhsT=wt[:, kc, n * 128:(n + 1) * 128], rhs=csil[:, kc, :],
                                                          start=(kc == 0), stop=(kc == 7)), r=[wt, csil], w=[ps])
                        j = grp * 6 + n
                        k.op("dve", lambda e: e.tensor_scalar(out=self.mod[:, l, j, :], in0=ps[:, 0:2], scalar1=bm[:, l, j:j + 1],
                                                              scalar2=None, op0=ALU.add), r=[ps, bm], w=[self.mod])
                for which, sci in ((0, 1), (1, 4)):
                    for j in range(2):
                        k.op("dve", lambda e: e.scalar_tensor_tensor(out=self.gs[:, l, which, :, j], in0=self.mod[:, l, sci * 8:(sci + 1) * 8, j],
                                                                     scalar=1.0, in1=ng[:, which, l, :], op0=ALU.add, op1=ALU.mult),
                             r=[self.mod, ng], w=[self.gs])
            k.barrier()

    def layer(self, l):
        k = self.k
        xsrc = self.xin if l == 0 else self.xres
        if self.stop_after == "mod":
            return
        self.phase_proj(l, xsrc)
        k.barrier()
        if self.stop_after == "proj":
            return
        if self.only is None and OVERLAP_S5:
            gen = self.phase_s5_gen(l)
            next(gen)
            need_ctx = l < DEPTH - 1
            npi = (16 * 4 * 33) + (4 if need_ctx else 0)
            rate = self.s5_nitems / float(npi)
            acc = [0.0]

            def filler():
                acc[0] += rate
                while acc[0] >= 1.0:
                    acc[0] -= 1.0
                    next(gen, None)
            self.phase_mla(l, filler=filler)
            for _ in gen:
                pass
        else:
            if self.only in (None, "mla"):
                self.phase_mla(l)
        if self.only is None and OVERLAP_SWA:
            sgen = self.phase_swa_gen(l, corun=True)
            next(sgen)
            self.phase_hg(l, filler=lambda: next(sgen, None))
            for _ in sgen:
                pass
        elif self.only in (None, "swa"):
            for _ in self.phase_swa_gen(l):
                pass
        if self.only == "hg" or (self.only is None and not OVERLAP_SWA):
            self.phase_hg(l)
        if self.only == "s5" or (self.only is None and not OVERLAP_S5):
            for _ in self.phase_s5_gen(l):
                pass
        if self.stop_after == "mix":
            return
        self.phase_ffn(l, xsrc)

    def norm_mod(self, es_tiles, xsrc, t0, n, l, which, ctxflag, load=True, gain=None, bias=None, out_f32=None):
        k = self.k
        xt, sq, rstd, tmps, ht, ps = es_tiles
        shi = 0 if which == 0 else 3
        if load:
            k.dma(xt[:, :, :n], xsrc.rearrange("(c p) t -> p c t", p=128)[:, :, t0:t0 + n], w=[xt])
        for c in range(8):
            k.op("pool", lambda e: e.tensor_tensor(out=sq[:, c, :n], in0=xt[:, c, :n], in1=xt[:, c, :n], op=ALU.mult), r=[xt], w=[sq])
        for c in range(8):
            k.op("pe", lambda e: e.matmul(ps[:, :n], lhsT=self.ones_bf[:], rhs=sq[:, c, :n], start=(c == 0), stop=(c == 7)),
                 r=[sq, self.ones_bf], w=[ps])
        k.op("act", lambda e: e.activation(out=rstd[:, :n], in_=ps[:, :n], func=AF.Sqrt, bias=self.epsb[:], scale=1.0 / D),
             r=[ps, self.epsb], w=[rstd])
        k.op("dve", lambda e: e.reciprocal(out=rstd[:, :n], in_=rstd[:, :n]), r=[rstd], w=[rstd])
        for c in range(8):
            g_ap = gain[:, c:c + 1] if gain is not None else self.gs[:, l, which, c, ctxflag:ctxflag + 1]
            if out_f32 is not None:
                tmp = tmps[c % len(tmps)]
                k.op("dve", lambda e: e.scalar_tensor_tensor(out=tmp[:, :n], in0=xt[:, c, :n], scalar=g_ap,
                                                             in1=rstd[:, :n], op0=ALU.mult, op1=ALU.mult), r=[xt, rstd, self.gs], w=[tmp])
                k.dma(out_f32(c), tmp[:, :n], r=[tmp])
                continue
            tmp = tmps[c % len(tmps)]
            k.op("dve", lambda e: e.scalar_tensor_tensor(out=tmp[:, :n], in0=xt[:, c, :n], scalar=g_ap,
                                                         in1=rstd[:, :n], op0=ALU.mult, op1=ALU.mult), r=[xt, rstd, self.gs], w=[tmp])
            k.op("act", lambda e: e.activation(out=ht[:, c, :n], in_=tmp[:, :n], func=AF.Identity,
                                               bias=self.mod[:, l, shi * 8 + c, ctxflag:ctxflag + 1], scale=1.0),
                 r=[tmp, self.mod], w=[ht])

    def phase_proj(self, l, xsrc):
        k = self.k
        with contextlib.ExitStack() as es:
            win = k.sb(es, "win", [128, 8, NWIN], BF16)
            wqb = k.sb(es, "wqb", [128, 2, 512], BF16)
            wkvb = k.sb(es, "wkvb", [128, 512], BF16)
            qng = k.sb(es, "qng", [128, 2], F32)
            kvng = k.sb(es, "kvng", [128, 1], F32)
            for c in range(8):
                k.dma(win[:, c, :], self.w_in[l, c * 128:(c + 1) * 128, :], w=[win], q="pool")
            k.dma(wqb[:], self.w_qb[l].rearrange("(c p) n -> p c n", p=128), w=[wqb], q="pool")
            k.dma(wkvb[:], self.w_kvb[l], w=[wkvb], q="pool")
            self.load_fm(es, qng[:], self.qn_g[l].rearrange("(c p) -> c p", p=128), 2, [qng])
            self.load_fm(es, kvng[:], self.kvn_g[l].rearrange("(c p) -> c p", p=128), 1, [kvng])
            NB = 2
            xt = [k.sb(es, f"xt{i}", [128, 8, 512], F32) for i in range(NB)]
            sq = [k.sb(es, f"sq{i}", [128, 8, 512], BF16) for i in range(NB)]
            rstd = [k.sb(es, f"rstd{i}", [128, 512], F32) for i in range(NB)]
            tmp = [k.sb(es, f"tmp{i}", [128, 512], F32) for i in range(2)]
            ht = [k.sb(es, f"ht{i}", [128, 8, 512], BF16) for i in range(NB)]
            rs = [k.sb(es, f"rs{i}", [128, 2, 512], F32) for i in range(NB)]
            rm = [k.sb(es, f"rm{i}", [96, 2, 512], F32) for i in range(NB)]
            ob = [k.sb(es, f"ob{i}", [128, 512], BF16) for i in range(4)]
            of = [k.sb(es, f"of{i}", [128, 512], F32) for i in range(4)]
            r1 = [k.sb(es, f"r1{i}", [128, 512], F32) for i in range(2)]
            r2 = [k.sb(es, f"r2{i}", [128, 512], F32) for i in range(2)]
            cqn = [k.sb(es, f"cqn{i}", [128, 2, 512], BF16) for i in range(NB)]
            ckvn = [k.sb(es, f"ckvn{i}", [128, 512], BF16) for i in range(NB)]
            nsq = [k.sb(es, f"nsq{i}", [128, 512], BF16) for i in range(2)]
            nrs = [k.sb(es, f"nrs{i}", [128, 512], F32) for i in range(2)]
            tmo = [k.sb(es, f"tmo{i}", [128, 640], F32) for i in range(2)]
            tmb = [k.sb(es, f"tmb{i}", [128, 256], BF16) for i in range(2)]
            svb = [k.sb(es, f"svb{i}", [128, 2, 65], BF16) for i in range(2)]
            mvb = [k.sb(es, f"mvb{i}", [128, 4, 65], BF16) for i in range(2)]
            for i in range(2):
                k.op("pool", lambda e: e.memset(svb[i][:, :, 64:65], 1.0), w=[svb[i]])
                k.op("pool", lambda e: e.memset(mvb[i][:, :, 64:65], 1.0), w=[mvb[i]])
            state = {"ob": 0, "of": 0, "ps": 0, "r": 0, "n": 0}

            def nxt(lst, key):
                i = state[key]
                state[key] = (i + 1) % len(lst)
                return lst[i]

            def fm_mm(ps, col0, m, hT, n, lhs_w=None, p0=0):
                for kc in range(8):
                    k.op("pe", lambda e: e.matmul(ps[0:m, :n], lhsT=win[:, kc, col0:col0 + m], rhs=hT[:, kc, :n],
                                                  start=(kc == 0), stop=(kc == 7)), r=[win, hT], w=[ps])

            blks = _blocks()

            def do_norm(bj):
                t0_, n_ = blks[bj]
                self.norm_mod((xt[bj % NB], sq[bj % NB], rstd[bj % NB], tmp, ht[bj % NB], self.psb[7]), xsrc, t0_, n_, l, 0, 1 if bj == 0 else 0)
            do_norm(0)
            for bi, (t0, n) in enumerate(blks):
                ctxflag = 1 if bi == 0 else 0
                b = bi % NB
                hT = ht[b]
                CUT = float(os.environ.get("KCUT", "99"))
                if bi >= int(os.environ.get("KBLK", "99")):
                    break
                if CUT < 1:
                    continue
                k.dma(rs[b][:, :, :n], self.rope_s[:, :, t0:t0 + n].rearrange("a p t -> p a t"), w=[rs[b]])
                k.dma(rm[b][64:96, :, :n], self.rope_m[:, 64:96, t0:t0 + n].rearrange("a p t -> p a t"), w=[rm[b]])
                for ci, (c_a, c_b, dst) in enumerate(((C_SQ, C_SQS, self.s_sq[0:128]), (C_SQ + 128, C_SQS + 128, self.s_sq[128:256]),
                                                      (C_SK, C_SKS, self.s_sk))):
                    pa = nxt(self.psb[0:6], "ps")
                    fm_mm(pa, c_a, 128, hT, n)
                    pb = nxt(self.psb[0:6], "ps")
                    fm_mm(pb, c_b, 128, hT, n)
                    a1 = nxt(r1, "r")
                    a2 = r2[r1.index(a1)]
                    o = nxt(ob, "ob")
                    k.op("dve", lambda e: e.tensor_tensor(out=a1[:, :n], in0=pa[:, :n], in1=rs[b][:, 0, :n], op=ALU.mult), r=[pa, rs[b]], w=[a1])
                    k.op("dve", lambda e: e.tensor_tensor(out=a2[:, :n], in0=pb[:, :n], in1=rs[b][:, 1, :n], op=ALU.mult), r=[pb, rs[b]], w=[a2])
                    k.op("pool", lambda e: e.tensor_tensor(out=o[:, :n], in0=a1[:, :n], in1=a2[:, :n], op=ALU.add), r=[a1, a2], w=[o])
                    k.dma(dst[:, t0:t0 + n], o[:, :n], r=[o])
                PJ = int(os.environ.get("KPJ", "2"))
                if PJ == 2 and bi + 1 < len(blks) and bi + 1 < int(os.environ.get("KBLK", "99")):
                    do_norm(bi + 1)
                if CUT < 2:
                    continue
                for c in range(2):
                    pa = nxt(self.psb[0:6], "ps")
                    fm_mm(pa, C_HQ + 128 * c, 128, hT, n)
                    o = nxt(ob, "ob")
                    k.op("act", lambda e: e.copy(out=o[:, :n], in_=pa[:, :n]), r=[pa], w=[o])
                    k.dma(self.s_hq[128 * c:128 * c + 128, t0:t0 + n], o[:, :n], r=[o])
                for (c0, dst, fn) in ((C_ZF, self.s_zf, AF.Copy), (C_ZB, self.s_zb, AF.Copy), (C_HG, self.s_hg, AF.Silu)):
                    for c in range(2):
                        pa = nxt(self.psb[0:6], "ps")
                        fm_mm(pa, c0 + 128 * c, 128, hT, n)
                        o = nxt(of, "of")
                        k.op("act", lambda e: e.activation(out=o[:, :n], in_=pa[:, :n], func=fn), r=[pa], w=[o])
                        k.dma(dst[128 * c:128 * c + 128, t0:t0 + n], o[:, :n], r=[o])
                if PJ == 3 and bi + 1 < len(blks) and bi + 1 < int(os.environ.get("KBLK", "99")):
                    do_norm(bi + 1)
                if CUT < 3:
                    continue
                pcq = [nxt(self.psb[0:6], "ps") for _ in range(2)]
                for c in range(2):
                    fm_mm(pcq[c], C_CQ + 128 * c, 128, hT, n)
                pss = self.psb[6]
                for c in range(2):
                    s_ = nxt(nsq, "n")
                    k.op("act", lambda e: e.activation(out=s_[:, :n], in_=pcq[c][:, :n], func=AF.Square), r=[pcq[c]], w=[s_])
                    k.op("pe", lambda e: e.matmul(pss[:, :n], lhsT=self.ones_bf[:], rhs=s_[:, :n], start=(c == 0), stop=(c == 1)),
                         r=[s_, self.ones_bf], w=[pss])
                nr = nrs[0]
                k.op("act", lambda e: e.activation(out=nr[:, :n], in_=pss[:, :n], func=AF.Sqrt, bias=self.epsb[:], scale=1.0 / 256),
                     r=[pss, self.epsb], w=[nr])
                k.op("dve", lambda e: e.reciprocal(out=nr[:, :n], in_=nr[:, :n]), r=[nr], w=[nr])
                for c in range(2):
                    k.op("dve", lambda e: e.scalar_tensor_tensor(out=cqn[b][:, c, :n], in0=pcq[c][:, :n], scalar=qng[:, c:c + 1], in1=nr[:, :n],
                                                                 op0=ALU.mult, op1=ALU.mult), r=[pcq[c], qng, nr], w=[cqn[b]])
                for h in range(4):
                    pa = nxt(self.psb[0:6], "ps")
                    pb = nxt(self.psb[0:6], "ps")
                    for c in range(2):
                        k.op("pe", lambda e: e.matmul(pa[0:96, :n], lhsT=wqb[:, c, h * 128:h * 128 + 96], rhs=cqn[b][:, c, :n],
                                                      start=(c == 0), stop=(c == 1)), r=[wqb, cqn[b]], w=[pa])
                    for c in range(2):
                        k.op("pe", lambda e: e.matmul(pb[0:96, :n], lhsT=wqb[:, c, h * 128 + 32:h * 128 + 128], rhs=cqn[b][:, c, :n],
                                                      start=(c == 0), stop=(c == 1)), r=[wqb, cqn[b]], w=[pb])
                    o = nxt(ob, "ob")
                    a1 = nxt(r1, "r")
                    a2 = r2[r1.index(a1)]
                    k.op("act", lambda e: e.copy(out=o[0:64, :n], in_=pa[0:64, :n]), r=[pa], w=[o])
                    k.op("dve", lambda e: e.tensor_tensor(out=a1[64:96, :n], in0=pa[64:96, :n], in1=rm[b][64:96, 0, :n], op=ALU.mult),
                         r=[pa, rm[b]], w=[a1])
                    k.op("dve", lambda e: e.tensor_tensor(out=a2[64:96, :n], in0=pb[64:96, :n], in1=rm[b][64:96, 1, :n], op=ALU.mult),
                         r=[pb, rm[b]], w=[a2])
                    k.op("pool", lambda e: e.tensor_tensor(out=o[64:96, :n], in0=a1[64:96, :n], in1=a2[64:96, :n], op=ALU.add),
                         r=[a1, a2, o], w=[o])
                    k.dma(self.s_mq[h, :, t0:t0 + n], o[0:96, :n], r=[o])
                if PJ == 4 and bi + 1 < len(blks) and bi + 1 < int(os.environ.get("KBLK", "99")):
                    do_norm(bi + 1)
                if CUT < 4:
                    continue
                pkv = nxt(self.psb[0:6], "ps")
                fm_mm(pkv, C_CKV, 128, hT, n)
                s_ = nxt(nsq, "n")
                k.op("act", lambda e: e.activation(out=s_[:, :n], in_=pkv[:, :n], func=AF.Square), r=[pkv], w=[s_])
                k.op("pe", lambda e: e.matmul(pss[:, :n], lhsT=self.ones_bf[:], rhs=s_[:, :n], start=True, stop=True),
                     r=[s_, self.ones_bf], w=[pss])
                nr = nrs[1]
                k.op("act", lambda e: e.activation(out=nr[:, :n], in_=pss[:, :n], func=AF.Sqrt, bias=self.epsb[:], scale=1.0 / 128),
                     r=[pss, self.epsb], w=[nr])
                k.op("dve", lambda e: e.reciprocal(out=nr[:, :n], in_=nr[:, :n]), r=[nr], w=[nr])
                k.op("dve", lambda e: e.scalar_tensor_tensor(out=ckvn[b][:, :n], in0=pkv[:, :n], scalar=kvng[:, 0:1], in1=nr[:, :n],
                                                             op0=ALU.mult, op1=ALU.mult), r=[pkv, kvng, nr], w=[ckvn[b]])
                if CUT < 4.2:
                    continue
                pa = nxt(self.psb[0:6], "ps")
                pb = nxt(self.psb[0:6], "ps")
                fm_mm(pa, C_KR - 64, 96, hT, n)
                fm_mm(pb, C_KRS - 64, 96, hT, n)
                a1 = nxt(r1, "r")
                a2 = r2[r1.index(a1)]
                kro = nxt(ob, "ob")
                k.op("dve", lambda e: e.tensor_tensor(out=a1[64:96, :n], in0=pa[64:96, :n], in1=rm[b][64:96, 0, :n], op=ALU.mult),
                     r=[pa, rm[b]], w=[a1])
                k.op("dve", lambda e: e.tensor_tensor(out=a2[64:96, :n], in0=pb[64:96, :n], in1=rm[b][64:96, 1, :n], op=ALU.mult),
                     r=[pb, rm[b]], w=[a2])
                k.op("pool", lambda e: e.tensor_tensor(out=kro[64:96, :n], in0=a1[64:96, :n], in1=a2[64:96, :n], op=ALU.add),
                     r=[a1, a2], w=[kro])
                if CUT < 4.4:
                    continue
                for h in range(4):
                    k.dma(self.s_mk[h, 64:96, t0:t0 + n], kro[64:96, :n], r=[kro])
                    if CUT < 4.6:
                        continue
                    pa = nxt(self.psb[0:6], "ps")
                    if os.environ.get("KVAR") == "A":
                        k.op("pe", lambda e: e.matmul(pa[:, :n], lhsT=wkvb[:, h * 128:h * 128 + 128], rhs=ckvn[b][:, :n], start=True, stop=True),
                             r=[wkvb, ckvn[b]], w=[pa])
                    else:
                        k.op("pe", lambda e: e.matmul(pa[0:64, :n], lhsT=wkvb[:, h * 128:h * 128 + 64], rhs=ckvn[b][:, :n], start=True, stop=True),
                             r=[wkvb, ckvn[b]], w=[pa])
                    if CUT < 4.7:
                        continue
                    o = nxt(ob, "ob")
                    k.op("act", lambda e: e.copy(out=o[0:64, :n], in_=pa[0:64, :n]), r=[pa], w=[o])
                    if CUT < 4.8:
                        continue
                    k.dma(self.s_mk[h, 0:64, t0:t0 + n], o[0:64, :n], r=[o])
                if PJ == 5 and bi + 1 < len(blks) and bi + 1 < int(os.environ.get("KBLK", "99")):
                    do_norm(bi + 1)
                if CUT < 5:
                    continue
                for st in range(n // 128):
                    ts_ = slice(st * 128, st * 128 + 128)
                    pa = nxt(self.psb[0:6], "ps")
                    pb = nxt(self.psb[0:6], "ps")
                    for kc in range(8):
                        k.op("pe", lambda e: e.matmul(pa[:, 0:512], lhsT=hT[:, kc, ts_], rhs=win[:, kc, C_TM:C_TM + 512],
                                                      start=(kc == 0), stop=(kc == 7)), r=[win, hT], w=[pa])
                    for kc in range(8):
                        k.op("pe", lambda e: e.matmul(pb[:, 0:128], lhsT=hT[:, kc, ts_], rhs=win[:, kc, C_TM + 512:C_TM + 640],
                                                      start=(kc == 0), stop=(kc == 7)), r=[win, hT], w=[pb])
                    k.op("pe", lambda e: e.matmul(pb[:, 128:384], lhsT=ckvn[b][:, ts_],
                                                  rhs=wkvb[:].rearrange("p (h x) -> p h x", x=128)[:, :, 64:128],
                                                  start=True, stop=True), r=[wkvb, ckvn[b]], w=[pb])
                    uo = tmo[st % 2]
                    bo = tmb[st % 2]
                    so = svb[st % 2]
                    mo = mvb[st % 2]
                    k.op("act", lambda e: e.copy(out=uo[:, 0:256], in_=pa[:, 0:256]), r=[pa], w=[uo])
                    k.op("dve", lambda e: e.tensor_copy(out=bo[:, 0:128], in_=pa[:, 384:512]), r=[pa], w=[bo])
                    k.op("dve", lambda e: e.tensor_copy(out=bo[:, 128:256], in_=pb[:, 0:128]), r=[pb, bo], w=[bo])
                    k.op("dve", lambda e: e.tensor_copy(out=so[:, :, 0:64], in_=pa[:, 256:384].rearrange("p (g x) -> p g x", g=2)), r=[pa, so], w=[so])
                    k.op("act", lambda e: e.copy(out=mo[:, :, 0:64], in_=pb[:, 128:384].rearrange("p (g x) -> p g x", g=4)), r=[pb, mo], w=[mo])
                    tt = t0 + st * 128
                    k.dma(self.s_u[tt:tt + 128, :], uo[:, 0:256], r=[uo])
                    k.dma(self.s_sv[tt:tt + 128, :], so[:].rearrange("p g x -> p (g x)"), r=[so])
                    k.dma(self.s_hi[tt:tt + 128, :], bo[:, 0:256], r=[bo])
                    k.dma(self.s_mv[tt:tt + 128, :], mo[:].rearrange("p g x -> p (g x)"), r=[mo])
            k.barrier()


    def phase_ffn(self, l, xsrc):
        k = self.k
        last = (l == DEPTH - 1)
        NB = 256
        with contextlib.ExitStack() as es:
            wout = k.sb(es, "wout", [128, 8, D], BF16)
            wup = k.sb(es, "wup", [128, 8, 2 * FFH], BF16)
            wdn = k.sb(es, "wdn", [128, 22, D], BF16)
            for c in range(8):
                k.dma(wout[:, c, :], self.w_out[l, c * 128:(c + 1) * 128, :], w=[wout], q="pool")
                k.dma(wup[:, c, :], self.w_up[l, c * 128:(c + 1) * 128, :], w=[wup], q="pool")
            for j in range(22):
                k.dma(wdn[:, j, :], self.w_down[l, j * 128:(j + 1) * 128, :], w=[wdn], q="pool")
            fg = None
            if last:
                fg = k.sb(es, "fg", [128, 8], F32)
                self.load_fm(es, fg[:], self.final_g.rearrange("(c p) -> c p", p=128), 8, [fg])
            xt = [k.sb(es, f"fxt{i}", [128, 8, NB], F32) for i in range(2)]
            ym = [k.sb(es, f"fym{i}", [128, 8, NB], BF16) for i in range(2)]
            sq = k.sb(es, "fsq", [128, 8, NB], BF16)
            tmp = [k.sb(es, f"ftmp{i}", [128, NB], F32) for i in range(2)]
            ht = [k.sb(es, f"fht{i}", [128, 8, NB], BF16) for i in range(2)]
            rstd = k.sb(es, "frstd", [128, NB], F32)
            sq2, rstd2, tmp2 = sq, rstd, tmp
            aT = k.sb(es, "faT", [128, 22, NB], BF16)
            sg = [k.sb(es, f"fsg{i}", [128, NB], F32) for i in range(2)]
            psi = [0]

            def nps():
                psi[0] = (psi[0] + 1) % 6
                return self.psb[psi[0]]

            t_start = CTX if last else 0
            t0s = list(range(t_start, TALL, NB))
            n = NB

            def stage_a(bi):
                t0 = t0s[bi]
                flag = 1 if t0 < CTX else 0
                b = bi % 2
                k.dma(xt[b][:, :, :n], xsrc.rearrange("(c p) t -> p c t", p=128)[:, :, t0:t0 + n], w=[xt[b]])
                k.dma(ym[b][:, :, :n], self.ymix.rearrange("(c p) t -> p c t", p=128)[:, :, t0:t0 + n], w=[ym[b]])
                for oc in range(8):
                    ps = nps()
                    for kc in range(8):
                        k.op("pe", lambda e: e.matmul(ps[:, :n], lhsT=wout[:, kc, oc * 128:(oc + 1) * 128], rhs=ym[b][:, kc, :n],
                                                      start=(kc == 0), stop=(kc == 7)), r=[wout, ym[b]], w=[ps])
                    k.op("dve", lambda e: e.scalar_tensor_tensor(out=xt[b][:, oc, :n], in0=ps[:, :n], scalar=self.mod[:, l, 16 + oc, flag:flag + 1],
                                                                 in1=xt[b][:, oc, :n], op0=ALU.mult, op1=ALU.add), r=[ps, xt[b], self.mod], w=[xt[b]])
                self.norm_mod((xt[b], sq, rstd, tmp, ht[b], self.psb[7]), None, t0, n, l, 1, flag, load=False)

            stage_a(0)
            for bi, t0 in enumerate(t0s):
                flag = 1 if t0 < CTX else 0
                b = bi % 2
                for j in range(22):
                    if j == int(os.environ.get("KFJ", "16")) and bi + 1 < len(t0s):
                        stage_a(bi + 1)
                    pg = nps()
                    pu = nps()
                    for kc in range(8):
                        k.op("pe", lambda e: e.matmul(pg[:, :n], lhsT=wup[:, kc, j * 128:(j + 1) * 128], rhs=ht[b][:, kc, :n],
                                                      start=(kc == 0), stop=(kc == 7)), r=[wup, ht[b]], w=[pg])
                    for kc in range(8):
                        k.op("pe", lambda e: e.matmul(pu[:, :n], lhsT=wup[:, kc, FFH + j * 128:FFH + (j + 1) * 128], rhs=ht[b][:, kc, :n],
                                                      start=(kc == 0), stop=(kc == 7)), r=[wup, ht[b]], w=[pu])
                    s_ = sg[j % 2]
                    k.op("act", lambda e: e.activation(out=s_[:, :n], in_=pg[:, :n], func=AF.Silu), r=[pg], w=[s_])
                    k.op("dve", lambda e: e.tensor_tensor(out=aT[:, j, :n], in0=s_[:, :n], in1=pu[:, :n], op=ALU.mult), r=[s_, pu], w=[aT])
                for oc in range(8):
                    ps = nps()
                    for j in range(22):
                        k.op("pe", lambda e: e.matmul(ps[:, :n], lhsT=wdn[:, j, oc * 128:(oc + 1) * 128], rhs=aT[:, j, :n],
                                                      start=(j == 0), stop=(j == 21)), r=[wdn, aT], w=[ps])
                    k.op("dve", lambda e: e.scalar_tensor_tensor(out=xt[b][:, oc, :n], in0=ps[:, :n], scalar=self.mod[:, l, 40 + oc, flag:flag + 1],
                                                                 in1=xt[b][:, oc, :n], op0=ALU.mult, op1=ALU.add), r=[ps, xt[b], self.mod], w=[xt[b]])
                if not last:
                    k.dma(self.xres.rearrange("(c p) t -> p c t", p=128)[:, :, t0:t0 + n], xt[b][:, :, :n], r=[xt[b]])
                else:
                    self.norm_mod((xt[b], sq2, rstd2, tmp2, None, self.psb[7]), None, t0, n, l, 1, flag, load=False, gain=fg,
                                  out_f32=lambda c: self.out[c * 128:(c + 1) * 128, t0 - CTX:t0 - CTX + n])
            k.barrier()

    def attn_finish(self, OT, n, Osb, rec, yo, sel, dst_aps, nh=1, bc=None):
        k = self.k
        bc = self.psb[6] if bc is None else bc
        k.op("act", lambda e: e.copy(out=Osb[0:65, :n], in_=OT[0:65, :n]), r=[OT], w=[Osb])
        k.op("pe", lambda e: e.matmul(bc[0:64, :n], lhsT=sel[0:65, :], rhs=Osb[0:65, :n], start=True, stop=True), r=[sel, Osb], w=[bc])
        k.op("dve", lambda e: e.reciprocal(out=rec[0:64, :n], in_=bc[0:64, :n]), r=[bc], w=[rec])
        k.op("dve", lambda e: e.tensor_tensor(out=yo[0:64, :n], in0=Osb[0:64, :n], in1=rec[0:64, :n], op=ALU.mult), r=[Osb, rec], w=[yo])
        w = n // nh
        for j, dst in enumerate(dst_aps):
            k.dma(dst, yo[0:64, j * w:(j + 1) * w], r=[yo])

    def make_sel(self, es):
        k = self.k
        sel = k.sb(es, "sel", [65, 64], F32)
        k.op("dve", lambda e: e.memset(sel[0:64, :], 0.0), w=[sel])
        k.op("dve", lambda e: e.memset(sel[64:65, :], 1.0), r=[sel], w=[sel])
        return sel

    def phase_mla(self, l, filler=None):
        k = self.k
        need_ctx = l < DEPTH - 1
        NT = TALL // 128
        with contextlib.ExitStack() as es:
            KT = k.sb(es, "mKT", [96, 2, TALL], BF16)
            Va = k.sb(es, "mVa", [128, NT, 2, 65], BF16)
            sel = self.make_sel(es)
            QT = [k.sb(es, f"mQT{i}", [96, 2, 512], BF16) for i in range(2)]
            PT = [k.sb(es, f"mPT{i}", [128, 2, 512], BF16) for i in range(2)]
            Osb = [k.sb(es, f"mOsb{i}", [65, 512], F32) for i in range(2)]
            rec = [k.sb(es, f"mrec{i}", [64, 512], F32) for i in range(2)]
            yo = [k.sb(es, f"myo{i}", [64, 512], BF16) for i in range(2)]
            cnt = 0
            vsrc = self.s_mv.rearrange("(t p) (h x) -> p t h x", p=128, h=4)
            units = []
            qi = 0
            for hp in range(2):
                for bi, (t0, n) in enumerate(_blocks()):
                    if bi == 0 and not need_ctx:
                        continue
                    ktiles = [0, 1] if bi == 0 else list(range(NT))
                    for hj in range(2):
                        units.append(dict(hp=hp, bi=bi, t0=t0, n=n, hj=hj, ktiles=ktiles, b=qi % 2, first=(hj == 0), newhp=(hj == 0 and len([u for u in units if u["hp"] == hp]) == 0)))
                    qi += 1

            def prologue(u):
                if u["newhp"]:
                    for j in range(2):
                        k.dma(KT[:, j, :], self.s_mk[2 * u["hp"] + j], w=[KT])
                    for t in range(0, NT, 6):
                        k.dma(Va[:, t:t + 6], vsrc[:, t:t + 6, 2 * u["hp"]:2 * u["hp"] + 2, :], w=[Va])
                if u["first"]:
                    k.dma(QT[u["b"]][:, :, :u["n"]], self.s_mq[2 * u["hp"]:2 * u["hp"] + 2, :, u["t0"]:u["t0"] + u["n"]].rearrange("h p t -> p h t"),
                          w=[QT[u["b"]]])

            gpair = [0]

            def st_pair(u, ip, par):
                STw = self.psw[par % 2]
                n = u["n"]
                for j in range(2):
                    kt = u["ktiles"][2 * ip + j]
                    k.op("pe", lambda e: e.matmul(STw[:, j * 512:j * 512 + n], lhsT=KT[:, u["hj"], kt * 128:(kt + 1) * 128], rhs=QT[u["b"]][:, u["hj"], :n],
                                                  start=True, stop=True), r=[KT, QT[u["b"]]], w=[STw])

            prologue(units[0])
            st_pair(units[0], 0, gpair[0])
            for ui, u in enumerate(units):
                n, hj, ktiles = u["n"], u["hj"], u["ktiles"]
                h = 2 * u["hp"] + hj
                OT = self.psb[4 + cnt % 2]
                npair = len(ktiles) // 2
                if ui > 0 and u["newhp"]:
                    prologue(u)
                    st_pair(u, 0, gpair[0])
                for ip in range(npair):
                    par = gpair[0]
                    gpair[0] += 1
                    if ip + 1 < npair:
                        st_pair(u, ip + 1, par + 1)
                    elif ui + 1 < len(units) and not units[ui + 1]["newhp"]:
                        prologue(units[ui + 1])
                        st_pair(units[ui + 1], 0, par + 1)
                    STw = self.psw[par % 2]
                    P = PT[par % 2]
                    k.op("act", lambda e: e.activation(out=P[:, :, :n], in_=STw[:, :].rearrange("p (j x) -> p j x", j=2)[:, :, :n], func=AF.Exp,
                                                       scale=MLA_SCALE), r=[STw], w=[P])
                    for j in range(2):
                        kt = ktiles[2 * ip + j]
                        k.op("pe", lambda e: e.matmul(OT[0:65, :n], lhsT=Va[:, kt, hj, :], rhs=P[:, j, :n], start=(ip == 0 and j == 0),
                                                      stop=(ip == npair - 1 and j == 1)), r=[Va, P], w=[OT])
                    if filler is not None:
                        filler()
                self.attn_finish(OT, n, Osb[cnt % 2], rec[cnt % 2], yo[cnt % 2], sel,
                                 [self.ymix[768 + h * 64:768 + (h + 1) * 64, u["t0"]:u["t0"] + n]])
                cnt += 1
            k.barrier()

    def phase_swa_gen(self, l, corun=False):
        k = self.k
        need_ctx = l < DEPTH - 1
        NT = TALL // 128
        with contextlib.ExitStack() as es:
            QT = k.sb(es, "sQT", [128, 2, TALL], BF16)
            KT = k.sb(es, "sKT", [128, TALL], BF16)
            Va = k.sb(es, "sVa", [128, NT, 2, 65], BF16)
            msk = k.sb(es, "smsk", [128, 2, 4, 128], BF16)
            sel = self.make_sel(es)
            sk = k.sb(es, "ssk", [1, 4], F32)
            skrow = k.sb(es, "sskrow", [1, 4, 128], BF16)
            e64 = k.sb(es, "se64", [1, 65], BF16)
            k.dma(msk[:], self.swa_mask, w=[msk], q="pool")
            k.dma(sk[:], self.swa_sink[l:l + 1, :], w=[sk])
            k.op("act", lambda e: e.activation(out=sk[:], in_=sk[:], func=AF.Exp), r=[sk], w=[sk])
            for h in range(4):
                k.op("dve", lambda e: e.tensor_scalar(out=skrow[:, h, :], in0=self.ones_bf[0:1, :], scalar1=sk[0:1, h:h + 1], scalar2=None,
                                                      op0=ALU.mult), r=[sk, self.ones_bf], w=[skrow])
            k.op("dve", lambda e: e.memset(e64[:, 0:64], 0.0), w=[e64])
            k.op("dve", lambda e: e.memset(e64[:, 64:65], 1.0), r=[e64], w=[e64])
            for c in range(2):
                k.dma(QT[:, c, :], self.s_sq[c * 128:(c + 1) * 128, :], w=[QT])
            k.dma(KT[:], self.s_sk, w=[KT])
            vsrc = self.s_sv.rearrange("(t p) f -> p t f", p=128)
            for t in range(0, NT, 6):
                k.dma(Va[:, t:t + 6].rearrange("p t h x -> p t (h x)"), vsrc[:, t:t + 6, :], w=[Va])
            PT = [k.sb(es, f"sPT{i}", [128, 4, 128], BF16) for i in range(3)]
            stb = [self.psb[6]] if corun else [self.psb[0], self.psb[1], self.psb[2]]
            otb = [self.psb[7]] if corun else [self.psb[4], self.psb[5]]
            Osb = [k.sb(es, f"sOsb{i}", [65, 512], F32) for i in range(2)]
            rec = [k.sb(es, f"srec{i}", [64, 512], F32) for i in range(2)]
            yo = [k.sb(es, f"syo{i}", [64, 512], BF16) for i in range(2)]
            cnt = 0
            yield "ready"
            for gt in range(NT):
                if gt < 2:
                    if not need_ctx:
                        continue
                    keys = [(0, None), (1, None)]
                else:
                    keys = []
                    if gt > 2:
                        keys.append((gt - 1, 0))
                    keys.append((gt, None))
                    if gt < NT - 1:
                        keys.append((gt + 1, 1))
                    keys += [(0, None), (1, None)]
                qs = slice(gt * 128, (gt + 1) * 128)
                OT = otb[cnt % len(otb)]
                nk = len(keys)

                assert not corun

                def st_mm(i):
                    STw = self.psw[i % 2]
                    kt = keys[i][0]
                    for g in range(2):
                        p0 = 64 * g
                        k.op("pe", lambda e: e.matmul(STw[:, g * 512:g * 512 + 256], lhsT=KT[p0:p0 + 64, kt * 128:(kt + 1) * 128], rhs=QT[p0:p0 + 64, :, qs],
                                                      start=True, stop=True), r=[KT, QT], w=[STw])
                st_mm(0)
                for i in range(nk):
                    if i + 1 < nk:
                        st_mm(i + 1)
                    ST = self.psw[i % 2]
                    P = PT[i % 3]
                    kt, mk = keys[i]
                    k.op("act", lambda e: e.activation(out=P[:].rearrange("p (g a) b -> p g (a b)", g=2),
                                                       in_=ST[:, :].rearrange("p (g x) -> p g x", g=2)[:, :, 0:256], func=AF.Exp, scale=SWA_SCALE),
                         r=[ST], w=[P])
                    if mk is not None:
                        k.op("dve", lambda e: e.tensor_tensor(out=P[:], in0=P[:], in1=msk[:, mk], op=ALU.mult), r=[P, msk], w=[P])
                    for g in range(2):
                        k.op("pe", lambda e: e.matmul(OT[0:65, g * 256:(g + 1) * 256], lhsT=Va[:, kt, g, :], rhs=P[:, 2 * g:2 * g + 2, :],
                                                      start=(i == 0 and g == 0), stop=False, skip_group_check=True), r=[Va, P], w=[OT])
                k.op("pe", lambda e: e.matmul(OT[0:65, 0:512], lhsT=e64[0:1, :], rhs=skrow[0:1, :, :], start=False, stop=True,
                                              skip_group_check=True), r=[e64, skrow], w=[OT])
                self.attn_finish(OT, 512, Osb[cnt % 2], rec[cnt % 2], yo[cnt % 2], sel,
                                 [self.ymix[256 + h * 64:256 + (h + 1) * 64, qs] for h in range(4)], nh=4,
                                 bc=(OT if corun else None))
                cnt += 1
                yield
            k.barrier()

    def setup_hglb(self):
        k = self.k
        es = k.es
        self.hglb = k.sb(es, "hglb", [64, 2, DEPTH, 4], F32)
        self.hgoml = k.sb(es, "hgoml", [64, 2, DEPTH, 4], F32)
        self.hgnoml = k.sb(es, "hgnoml", [64, 2, DEPTH, 4], F32)
        with contextlib.ExitStack() as es2:
            e_ = k.sb(es2, "lbe", [64, 2, DEPTH, 4], F32)
            s_ = k.sb(es2, "lbs", [64, 2, 4], F32)
            self.load_fm(es2, e_[:].rearrange("p d l h -> p (d l h)"), self.hg_lb.rearrange("d l (h x) -> (d l h) x", x=64), 32, [e_], wd=64)
            k.op("act", lambda e: e.activation(out=e_[:], in_=e_[:], func=AF.Exp), r=[e_], w=[e_])
            k.op("dve", lambda e: e.tensor_tensor(out=s_[:], in0=e_[:, :, 0, :], in1=e_[:, :, 1, :], op=ALU.add), r=[e_], w=[s_])
            for l in (2, 3):
                k.op("dve", lambda e: e.tensor_tensor(out=s_[:], in0=s_[:], in1=e_[:, :, l, :], op=ALU.add), r=[e_, s_], w=[s_])
            k.op("dve", lambda e: e.reciprocal(out=s_[:], in_=s_[:]), r=[s_], w=[s_])
            for l in range(DEPTH):
                k.op("dve", lambda e: e.tensor_tensor(out=e_[:, :, l, :], in0=e_[:, :, l, :], in1=s_[:], op=ALU.mult), r=[e_, s_], w=[e_])
            k.op("dve", lambda e: e.memset(self.hglb[:, :, 0, :], 0.0), w=[self.hglb])
            k.op("dve", lambda e: e.tensor_copy(out=self.hglb[:, :, 1, :], in_=e_[:, :, 1, :]), r=[e_, self.hglb], w=[self.hglb])
            for l in (2, 3):
                k.op("dve", lambda e: e.tensor_tensor(out=self.hglb[:, :, l, :], in0=self.hglb[:, :, l - 1, :], in1=e_[:, :, l, :], op=ALU.add),
                     r=[e_, self.hglb], w=[self.hglb])
            k.op("dve", lambda e: e.tensor_scalar(out=self.hgoml[:], in0=self.hglb[:], scalar1=-1.0, scalar2=1.0, op0=ALU.mult, op1=ALU.add),
                 r=[self.hglb], w=[self.hgoml])
            k.op("dve", lambda e: e.tensor_scalar(out=self.hgnoml[:], in0=self.hglb[:], scalar1=-1.0, scalar2=None, op0=ALU.add),
                 r=[self.hglb], w=[self.hgnoml])
            k.barrier()

    def phase_hg(self, l, filler=None):
        k = self.k
        with contextlib.ExitStack() as es:
            rmask = k.sb(es, "hrm", [64, 2048], F32)
            amask = k.sb(es, "ham", [128, 2, 4, 128], BF16)
            cmask = k.sb(es, "hcm", [128, 4], F32)
            ng = k.sb(es, "hng", [64, 1], F32)
            k.dma(rmask[:], self.hg_rmask, w=[rmask])
            k.dma(amask[:], self.hg_amask, w=[amask], q="pool")
            k.dma(cmask[:], self.hg_cmask, w=[cmask])
            self.load_fm(es, ng[:], self.hg_norm_g[l:l + 1, :], 1, [ng], wd=64)
            S = k.sb(es, "hS", [64, 4, 64], F32)
            St = k.sb(es, "hSt", [64, 4, 64], F32)
            Sbf = [k.sb(es, f"hSbf{j}", [64, 4, 64], BF16) for j in range(8)]
            names = ["z", "sg", "lf", "kk", "P", "eP", "eN"]
            bt = {nm: k.sb(es, "hb_" + nm, [64, 2048], F32) for nm in names}
            bq = k.sb(es, "hb_q", [64, 2048], BF16)
            bqd = k.sb(es, "hb_qd", [64, 2048], BF16)
            bki = k.sb(es, "hb_ki", [64, 2048], BF16)
            dec = k.sb(es, "hdec", [64, 4, 16], F32)
            vt = [k.sb(es, f"hvt{i}", [128, 256], BF16) for i in range(2)]
            kim = k.sb(es, "hkim", [128, 4, 256], BF16)
            attm = k.sb(es, "hattm", [128, 4, 128], BF16)
            obt = [k.sb(es, f"hobt{i}", [64, 4, 128], F32) for i in range(2)]
            gsl = [k.sb(es, f"hgsl{i}", [64, 4, 128], F32) for i in range(2)]
            osum = k.sb(es, "hosum", [64, 4, 128], F32)
            sqo = k.sb(es, "hsqo", [64, 512], BF16)
            rst = k.sb(es, "hrst", [64, 512], F32)
            yo = [k.sb(es, f"hyo{i}", [64, 4, 128], BF16) for i in range(2)]
            Mps = [self.psb[0], self.psb[1]]
            attps = self.psb[2]
            ops = self.psb[3]
            trp = self.psb[4]
            ssps = self.psb[5]
            trp_bf = trp[:].bitcast(BF16)
            ymix_v = self.ymix[512:768, :].rearrange("(h d) t -> d h t", d=64)
            sbi = [0]
            vti = [0]
            bqd2 = [bqd, k.sb(es, "hb_qd2", [64, 2048], BF16)]
            bki2 = [bki, k.sb(es, "hb_ki2", [64, 2048], BF16)]
            dec2 = [dec, k.sb(es, "hdec2", [64, 4, 16], F32)]
            Sx = [S, k.sb(es, "hS2", [64, 4, 64], F32)]
            Stx = [St, k.sb(es, "hSt2", [64, 4, 64], F32)]
            sidx = [0]

            def prep_groups(d, t0, n, bs):
                zsrc = (self.s_zf if d == 0 else self.s_zb).rearrange("(h d) t -> d h t", d=64)
                qsrc = self.s_hq.rearrange("(h d) t -> d h t", d=64)
                bqd_, bki_, dec_ = bqd2[bs], bki2[bs], dec2[bs]

                def V(t):
                    return t[:, 0:4 * n].rearrange("p (h t) -> p h t", h=4)
                f2 = lambda t: t[:, 0:4 * n]
                z, sg, lf, kk, P, eP, eN = [bt[nm] for nm in names]

                def g1():
                    k.dma(V(z), zsrc[:, :, t0:t0 + n], w=[z])
                    k.dma(V(bq), qsrc[:, :, t0:t0 + n], w=[bq])
                    k.op("act", lambda e: e.activation(out=f2(sg), in_=f2(z), func=AF.Sigmoid), r=[z], w=[sg])

                def g2():
                    for h in range(4):
                        k.op(HG_PREP_ENG, lambda e: e.tensor_scalar(out=V(lf)[:, h, :], in0=V(sg)[:, h, :], scalar1=self.hgoml[:, d, l, h:h + 1],
                                                              scalar2=self.hglb[:, d, l, h:h + 1], op0=ALU.mult, op1=ALU.add),
                             r=[sg, self.hgoml, self.hglb], w=[lf])
                        k.op("pool", lambda e: e.tensor_scalar(out=V(kk)[:, h, :], in0=V(sg)[:, h, :], scalar1=self.hgnoml[:, d, l, h:h + 1],
                                                               scalar2=self.hgoml[:, d, l, h:h + 1], op0=ALU.mult, op1=ALU.add),
                             r=[sg, self.hgoml, self.hgnoml], w=[kk])
                    k.op("act", lambda e: e.activation(out=f2(lf), in_=f2(lf), func=AF.Ln), r=[lf], w=[lf])

                def g3():
                    k.op("dve", lambda e: e.tensor_tensor_scan(out=f2(P), data0=f2(rmask), data1=f2(lf), initial=0.0, op0=ALU.mult, op1=ALU.add),
                         r=[rmask, lf], w=[P])
                    k.op("act", lambda e: e.activation(out=dec_[:, :, 0:n // 32], in_=V(P)[:, :, 31:n:32], func=AF.Exp), r=[P], w=[dec_])
                    if d == 1:
                        k.op("pool", lambda e: e.tensor_tensor(out=f2(P), in0=f2(P), in1=f2(lf), op=ALU.subtract), r=[P, lf], w=[P])

                def g4():
                    k.op("act", lambda e: e.activation(out=f2(eP), in_=f2(P), func=AF.Exp), r=[P], w=[eP])
                    k.op("act", lambda e: e.activation(out=f2(eN), in_=f2(P), func=AF.Exp, scale=-1.0), r=[P], w=[eN])
                    eq, ek = (eP, eN) if d == 0 else (eN, eP)
                    k.op(HG_PREP_ENG, lambda e: e.tensor_tensor(out=f2(bqd_), in0=f2(bq), in1=f2(eq), op=ALU.mult), r=[bq, eq], w=[bqd_])
                    k.op("pool", lambda e: e.tensor_tensor(out=f2(bki_), in0=f2(kk), in1=f2(ek), op=ALU.mult), r=[kk, ek], w=[bki_])
                return [g1, g2, g3, g4]

            for d in (1, 0):
                S = Sx[sidx[0] % 2]
                k.op("dve", lambda e: e.memset(S[:], 0.0), r=[S], w=[S])
                blocks = _blocks()
                if d == 1:
                    blocks = [blocks[0]] + blocks[:0:-1]
                for g_ in prep_groups(d, blocks[0][0], blocks[0][1], 0):
                    g_()
                for bix, (t0, n) in enumerate(blocks):
                    bs = bix % 2
                    pending = prep_groups(d, blocks[bix + 1][0], blocks[bix + 1][1], (bix + 1) % 2) if bix + 1 < len(blocks) else []

                    def V(t):
                        return t[:, 0:4 * n].rearrange("p (h t) -> p h t", h=4)
                    bqd, bki, dec = bqd2[bs], bki2[bs], dec2[bs]
                    qd, ki, decv = V(bqd), V(bki), dec
                    ntile = n // 128
                    for tix, ti in enumerate(range(ntile) if d == 0 else range(ntile - 1, -1, -1)):
                        for _ in range(4 // ntile if ntile < 4 else 1):
                            if pending:
                                pending.pop(0)()
                        if filler is not None:
                            filler()
                        cols = slice(ti * 128, ti * 128 + 128)
                        gt0 = t0 + ti * 128
                        v = vt[vti[0] % 2]
                        ob_ = obt[vti[0] % 2]
                        gs_ = gsl[vti[0] % 2]
                        yo_ = yo[vti[0] % 2]
                        vti[0] += 1
                        k.dma(v[:], self.s_hi[gt0:gt0 + 128, :], w=[v])
                        if d == 0:
                            k.dma(ob_[:], self.s_ob[:, :, gt0:gt0 + 128], w=[ob_])
                            k.dma(gs_[:], self.s_hg.rearrange("(h d) t -> d h t", d=64)[:, :, gt0:gt0 + 128], w=[gs_])
                        for h in range(4):
                            k.op("pe", lambda e: e.transpose(out=trp_bf[:, h * 64:(h + 1) * 64], in_=ki[:, h, cols], identity=self.ident_bf[0:64, 0:64]),
                                 r=[bki, self.ident_bf], w=[trp])
                        for j in range(4):
                            k.op("dve" if j % 2 == 0 else "act", (lambda e: e.tensor_scalar(out=kim[:, j, :], in0=trp_bf[:, 0:256], scalar1=cmask[:, j:j + 1], scalar2=None, op0=ALU.mult))
                                 if j % 2 == 0 else (lambda e: e.activation(out=kim[:, j, :], in_=trp_bf[:, 0:256], func=AF.Copy, scale=cmask[:, j:j + 1])),
                                 r=[trp, cmask], w=[kim])
                        for j in range(4):
                            for h in range(4):
                                k.op("pe", lambda e: e.matmul(Mps[j // 2][0:64, (j % 2) * 256 + h * 64:(j % 2) * 256 + (h + 1) * 64],
                                                              lhsT=kim[:, j, h * 64:(h + 1) * 64], rhs=v[:, h * 64:(h + 1) * 64], start=True, stop=True),
                                     r=[kim, v], w=[Mps[j // 2]])
                        for h in range(4):
                            k.op("pe", lambda e: e.matmul(attps[:, h * 128:(h + 1) * 128], lhsT=ki[:, h, cols], rhs=qd[:, h, cols], start=True, stop=True),
                                 r=[bki, bqd], w=[attps])
                        k.op("dve", lambda e: e.tensor_tensor(out=attm[:].rearrange("p a b -> p (a b)"), in0=attps[:, 0:512],
                                                              in1=amask[:, d].rearrange("p a b -> p (a b)"), op=ALU.mult), r=[attps, amask], w=[attm])
                        for h in range(4):
                            k.op("pe", lambda e: e.matmul(ops[0:64, h * 128:(h + 1) * 128], lhsT=v[:, h * 64:(h + 1) * 64], rhs=attm[:, h, :],
                                                          start=(h == 0), stop=False, skip_group_check=True), r=[v, attm], w=[ops])
                        order = range(4) if d == 0 else range(3, -1, -1)
                        for ji, j in enumerate(order):
                            ce = ti * 4 + j
                            Mj = Mps[j // 2][0:64, (j % 2) * 256:(j % 2) * 256 + 256].rearrange("p (h x) -> p h x", h=4)
                            dec_bc = decv[:, :, ce:ce + 1].to_broadcast([64, 4, 64])
                            sb_ = Sbf[sbi[0] % 8]
                            sbi[0] += 1
                            S = Sx[sidx[0] % 2]
                            Sn = Sx[(sidx[0] + 1) % 2]
                            St = Stx[sidx[0] % 2]
                            sidx[0] += 1
                            if d == 0:
                                k.op(HG_SBF_ENG, (lambda e: e.copy(out=sb_[:], in_=S[:])) if HG_SBF_ENG == "act" else (lambda e: e.tensor_copy(out=sb_[:], in_=S[:])), r=[S], w=[sb_])
                                k.op("dve", lambda e: e.tensor_tensor(out=St[:], in0=S[:], in1=Mj, op=ALU.add), r=[S, Mps[j // 2]], w=[St])
                                k.op("dve", lambda e: e.tensor_tensor(out=Sn[:], in0=St[:], in1=dec_bc, op=ALU.mult), r=[St, dec], w=[Sn])
                            else:
                                k.op("dve", lambda e: e.tensor_tensor(out=St[:], in0=S[:], in1=dec_bc, op=ALU.mult), r=[S, dec], w=[St])
                                k.op(HG_SBF_ENG, (lambda e: e.copy(out=sb_[:], in_=St[:])) if HG_SBF_ENG == "act" else (lambda e: e.tensor_copy(out=sb_[:], in_=St[:])), r=[St], w=[sb_])
                                k.op("dve", lambda e: e.tensor_tensor(out=Sn[:], in0=St[:], in1=Mj, op=ALU.add), r=[St, Mps[j // 2]], w=[Sn])
                            for h in range(4):
                                last = (ji == 3 and h == 3)
                                k.op("pe", lambda e: e.matmul(ops[0:64, h * 128 + j * 32:h * 128 + (j + 1) * 32], lhsT=sb_[:, h, :],
                                                              rhs=qd[:, h, ti * 128 + j * 32:ti * 128 + (j + 1) * 32], start=False, stop=last,
                                                              skip_group_check=True), r=[sb_, bqd], w=[ops])
                        opsv = ops[0:64, 0:512].rearrange("p (h t) -> p h t", h=4)
                        if d == 1:
                            k.op("act", lambda e: e.copy(out=ob_[:], in_=opsv), r=[ops], w=[ob_])
                            k.dma(self.s_ob[:, :, gt0:gt0 + 128], ob_[:], r=[ob_])
                        else:
                            k.op("dve", lambda e: e.tensor_tensor(out=osum[:], in0=opsv, in1=ob_[:], op=ALU.add), r=[ops, ob_], w=[osum])
                            o2 = osum[:].rearrange("p h t -> p (h t)")
                            k.op("pool", lambda e: e.tensor_tensor(out=sqo[:], in0=o2, in1=o2, op=ALU.mult), r=[osum], w=[sqo])
                            k.op("pe", lambda e: e.matmul(ssps[0:64, 0:512], lhsT=self.ones_bf[0:64, 0:64], rhs=sqo[:], start=True, stop=True),
                                 r=[sqo, self.ones_bf], w=[ssps])
                            k.op("act", lambda e: e.activation(out=rst[:], in_=ssps[0:64, 0:512], func=AF.Sqrt, bias=self.epsb[0:64, :], scale=1.0 / 64),
                                 r=[ssps, self.epsb], w=[rst])
                            k.op("dve", lambda e: e.reciprocal(out=rst[:], in_=rst[:]), r=[rst], w=[rst])
                            k.op("dve", lambda e: e.scalar_tensor_tensor(out=o2, in0=o2, scalar=ng[:, 0:1], in1=rst[:], op0=ALU.mult, op1=ALU.mult),
                                 r=[osum, ng, rst], w=[osum])
                            k.op("pool", lambda e: e.tensor_tensor(out=yo_[:], in0=osum[:], in1=gs_[:], op=ALU.mult), r=[osum, gs_], w=[yo_])
                            k.dma(ymix_v[:, :, gt0:gt0 + 128], yo_[:], r=[yo_])
                    while pending:
                        pending.pop(0)()
                k.barrier()

    def phase_s5_gen(self, l):
        k = self.k
        need_ctx = l < DEPTH - 1
        NC_ = TALL // 8
        mul, add, sub = ALU.mult, ALU.add, ALU.subtract
        with contextlib.ExitStack() as es:
            Tm = k.sb(es, "5Tm", [128, 32, 128], BF16)
            Gt = k.sb(es, "5Gt", [128, 32, 2, 64], BF16)
            Er = k.sb(es, "5Er", [64, 32, 128], BF16)
            Ei = k.sb(es, "5Ei", [64, 32, 128], BF16)
            A8 = k.sb(es, "5A8", [64, 4, 32], F32)
            U = k.sb(es, "5U", [128, 16, NC_], BF16)
            tmask = k.sb(es, "5tmask", [128, 2, 128], F32)
            k.dma(tmask[:], self.s5_tmask, w=[tmask])
            with contextlib.ExitStack() as e2:
                def t32(nm):
                    return k.sb(e2, nm, [64, 32], F32)
                lre, lim, ldt = t32("lre"), t32("lim"), t32("ldt")
                for dst, src in ((lre, self.s5_lam_re), (lim, self.s5_lam_im), (ldt, self.s5_log_dt)):
                    self.load_fm(e2, dst[:], src[l].rearrange("d g p -> (d g) p"), 32, [dst], wd=64)
                Bre = k.sb(e2, "Bre", [64, 32, 16], F32)
                Bim = k.sb(e2, "Bim", [64, 32, 16], F32)
                Cre = k.sb(e2, "Cre", [64, 32, 16], F32)
                Cim = k.sb(e2, "Cim", [64, 32, 16], F32)
                for dst, src in ((Bre, self.s5_b_re), (Bim, self.s5_b_im)):
                    for d in range(2):
                        k.dma(dst[:, d * 16:(d + 1) * 16, :], src[l, d].rearrange("g p h -> p g h"), w=[dst], allow_slow_non_contiguous=True)
                for dst, src in ((Cre, self.s5_c_re), (Cim, self.s5_c_im)):
                    for q4 in range(4):
                        self.load_fm(e2, dst[:].rearrange("p a h -> p (a h)")[:, q4 * 128:(q4 + 1) * 128],
                                     src[l].rearrange("d g h p -> (d g h) p")[q4 * 128:(q4 + 1) * 128, :], 128, [dst], wd=64)
                dt_, mag, th, c16, s16 = t32("dt"), t32("mag"), t32("th"), t32("c16"), t32("s16")
                t1, t2 = t32("t1"), t32("t2")
                k.op("act", lambda e: e.activation(out=dt_[:], in_=ldt[:], func=AF.Exp), r=[ldt], w=[dt_])
                k.op("dve", lambda e: e.tensor_tensor(out=mag[:], in0=lre[:], in1=dt_[:], op=mul), r=[lre, dt_], w=[mag])
                k.op("dve", lambda e: e.tensor_tensor(out=th[:], in0=lim[:], in1=dt_[:], op=mul), r=[lim, dt_], w=[th])
                k.op("act", lambda e: e.activation(out=mag[:], in_=mag[:], func=AF.Exp, scale=1.0 / 16), r=[mag], w=[mag])
                halfpi = k.sb(e2, "halfpi", [64, 1], F32)
                k.op("dve", lambda e: e.memset(halfpi[:], math.pi / 2), w=[halfpi])
                k.op("act", lambda e: e.activation(out=s16[:], in_=th[:], func=AF.Sin, scale=1.0 / 16), r=[th], w=[s16])
                k.op("act", lambda e: e.activation(out=c16[:], in_=th[:], func=AF.Sin, scale=1.0 / 16, bias=halfpi[:]), r=[th, halfpi], w=[c16])
                are, aim = t32("are"), t32("aim")
                k.op("dve", lambda e: e.tensor_tensor(out=are[:], in0=mag[:], in1=c16[:], op=mul), r=[mag, c16], w=[are])
                k.op("dve", lambda e: e.tensor_tensor(out=aim[:], in0=mag[:], in1=s16[:], op=mul), r=[mag, s16], w=[aim])

                def cmul(ore, oim, xr, xi, yr, yi, rk, wk, tA, tB):
                    k.op("dve", lambda e: e.tensor_tensor(out=tA, in0=xr, in1=yr, op=mul), r=rk, w=[wk[2]])
                    k.op("dve", lambda e: e.tensor_tensor(out=tB, in0=xi, in1=yi, op=mul), r=rk, w=[wk[3]])
                    k.op("dve", lambda e: e.tensor_tensor(out=tB, in0=tA, in1=tB, op=sub), r=[wk[2], wk[3]], w=[wk[3]])
                    k.op("dve", lambda e: e.tensor_tensor(out=tA, in0=xr, in1=yi, op=mul), r=rk, w=[wk[2]])
                    k.op("dve", lambda e: e.tensor_tensor(out=oim, in0=xi, in1=yr, op=mul), r=rk, w=[wk[1]])
                    k.op("dve", lambda e: e.tensor_tensor(out=oim, in0=oim, in1=tA, op=add), r=[wk[1], wk[2]], w=[wk[1]])
                    k.op("dve", lambda e: e.tensor_copy(out=ore, in_=tB), r=[wk[3]], w=[wk[0]])

                for _ in range(4):
                    cmul(are[:], aim[:], are[:], aim[:], are[:], aim[:], [are, aim], [are, aim, t1, t2], t1[:], t2[:])
                cfr, cfi, den = t32("cfr"), t32("cfi"), t32("den")
                k.op("dve", lambda e: e.tensor_tensor(out=den[:], in0=lre[:], in1=lre[:], op=mul), r=[lre], w=[den])
                k.op("dve", lambda e: e.tensor_tensor(out=t1[:], in0=lim[:], in1=lim[:], op=mul), r=[lim], w=[t1])
                k.op("dve", lambda e: e.tensor_tensor(out=den[:], in0=den[:], in1=t1[:], op=add), r=[den, t1], w=[den])
                k.op("dve", lambda e: e.reciprocal(out=den[:], in_=den[:]), r=[den], w=[den])
                am1 = t32("am1")
                nlim = t32("nlim")
                k.op("dve", lambda e: e.tensor_scalar(out=am1[:], in0=are[:], scalar1=-1.0, scalar2=None, op0=add), r=[are], w=[am1])
                k.op("dve", lambda e: e.tensor_scalar(out=nlim[:], in0=lim[:], scalar1=-1.0, scalar2=None, op0=mul), r=[lim], w=[nlim])
                cmul(cfr[:], cfi[:], am1[:], aim[:], lre[:], nlim[:], [am1, aim, lre, nlim], [cfr, cfi, t1, t2], t1[:], t2[:])
                k.op("dve", lambda e: e.tensor_tensor(out=cfr[:], in0=cfr[:], in1=den[:], op=mul), r=[cfr, den], w=[cfr])
                k.op("dve", lambda e: e.tensor_tensor(out=cfi[:], in0=cfi[:], in1=den[:], op=mul), r=[cfi, den], w=[cfi])
                ire, iim = t32("ire"), t32("iim")
                k.op("dve", lambda e: e.tensor_tensor(out=den[:], in0=are[:], in1=are[:], op=mul), r=[are], w=[den])
                k.op("dve", lambda e: e.tensor_tensor(out=t1[:], in0=aim[:], in1=aim[:], op=mul), r=[aim], w=[t1])
                k.op("dve", lambda e: e.tensor_tensor(out=den[:], in0=den[:], in1=t1[:], op=add), r=[den, t1], w=[den])
                k.op("dve", lambda e: e.reciprocal(out=den[:], in_=den[:]), r=[den], w=[den])
                k.op("dve", lambda e: e.tensor_tensor(out=ire[:], in0=are[:], in1=den[:], op=mul), r=[are, den], w=[ire])
                k.op("dve", lambda e: e.scalar_tensor_tensor(out=iim[:], in0=aim[:], scalar=-1.0, in1=den[:], op0=mul, op1=mul), r=[aim, den], w=[iim])
                def t512(nm):
                    return k.sb(e2, nm, [64, 32, 16], F32)
                Bbr, Bbi, u1, u2, xr, xi = t512("Bbr"), t512("Bbi"), t512("u1"), t512("u2"), t512("xr"), t512("xi")
                bc = lambda t: t[:].unsqueeze(2).to_broadcast([64, 32, 16])
                cmul(Bbr[:], Bbi[:], bc(cfr), bc(cfi), Bre[:], Bim[:], [cfr, cfi, Bre, Bim], [Bbr, Bbi, u1, u2], u1[:], u2[:])
                Lr = k.sb(e2, "Lr", [64, 32, 8, 16], F32)
                Li = k.sb(e2, "Li", [64, 32, 8, 16], F32)
                Rr = k.sb(e2, "Rr", [64, 32, 8, 16], F32)
                Ri = k.sb(e2, "Ri", [64, 32, 8, 16], F32)
                Gr = k.sb(e2, "Gr", [64, 32, 8, 16], F32)
                Gi = k.sb(e2, "Gi", [64, 32, 8, 16], F32)
                Erv = Er[:].rearrange("p a (t h) -> p a t h", h=16)
                Eiv = Ei[:].rearrange("p a (t h) -> p a t h", h=16)
                pr, pi_, qr, qi = t32("pr"), t32("pi"), t32("qr"), t32("qi")
                k.op("dve", lambda e: e.memset(pr[:], 1.0), w=[pr])
                k.op("dve", lambda e: e.memset(pi_[:], 0.0), w=[pi_])
                k.op("dve", lambda e: e.memset(qr[:], 1.0), w=[qr])
                k.op("dve", lambda e: e.memset(qi[:], 0.0), w=[qi])
                F_, Bk = slice(0, 16), slice(16, 32)
                for kk_ in range(9):
                    if kk_ > 0:
                        cmul(pr[:], pi_[:], pr[:], pi_[:], are[:], aim[:], [pr, pi_, are, aim], [pr, pi_, t1, t2], t1[:], t2[:])
                    if kk_ <= 7:
                        cmul(xr[:], xi[:], bc(pr), bc(pi_), Bbr[:], Bbi[:], [pr, pi_, Bbr, Bbi], [xr, xi, u1, u2], u1[:], u2[:])
                        for (dst, src) in ((Gr, xr), (Gi, xi)):
                            k.op("pool", lambda e: e.tensor_copy(out=dst[:, F_, 7 - kk_, :], in_=src[:, F_, :]), r=[src, dst], w=[dst])
                            k.op("pool", lambda e: e.tensor_copy(out=dst[:, Bk, kk_, :], in_=src[:, Bk, :]), r=[src, dst], w=[dst])
                    cmul(xr[:], xi[:], bc(pr), bc(pi_), Cre[:], Cim[:], [pr, pi_, Cre, Cim], [xr, xi, u1, u2], u1[:], u2[:])
                    if kk_ <= 7:
                        k.op("pool", lambda e: e.tensor_copy(out=Rr[:, F_, kk_, :], in_=xr[:, F_, :]), r=[xr, Rr], w=[Rr])
                        k.op("pool", lambda e: e.tensor_copy(out=Rr[:, Bk, 7 - kk_, :], in_=xr[:, Bk, :]), r=[xr, Rr], w=[Rr])
                        k.op("pool", lambda e: e.tensor_scalar(out=Ri[:, F_, kk_, :], in0=xi[:, F_, :], scalar1=-1.0, scalar2=None, op0=mul), r=[xi, Ri], w=[Ri])
                        k.op("pool", lambda e: e.tensor_scalar(out=Ri[:, Bk, 7 - kk_, :], in0=xi[:, Bk, :], scalar1=-1.0, scalar2=None, op0=mul), r=[xi, Ri], w=[Ri])
                    if kk_ >= 1:
                        k.op("act", lambda e: e.copy(out=Erv[:, F_, kk_ - 1, :], in_=xr[:, F_, :]), r=[xr, Er], w=[Er])
                        k.op("act", lambda e: e.copy(out=Erv[:, Bk, 8 - kk_, :], in_=xr[:, Bk, :]), r=[xr, Er], w=[Er])
                        k.op("act", lambda e: e.activation(out=Eiv[:, F_, kk_ - 1, :], in_=xi[:, F_, :], func=AF.Copy, scale=-1.0), r=[xi, Ei], w=[Ei])
                        k.op("act", lambda e: e.activation(out=Eiv[:, Bk, 8 - kk_, :], in_=xi[:, Bk, :], func=AF.Copy, scale=-1.0), r=[xi, Ei], w=[Ei])
                    if kk_ == 8:
                        k.op("dve", lambda e: e.tensor_copy(out=A8[:, 0, :], in_=pr[:]), r=[pr], w=[A8])
                        k.op("dve", lambda e: e.tensor_copy(out=A8[:, 1, :], in_=pr[:]), r=[pr, A8], w=[A8])
                        k.op("dve", lambda e: e.tensor_scalar(out=A8[:, 2, :], in0=pi_[:], scalar1=-1.0, scalar2=None, op0=mul), r=[pi_, A8], w=[A8])
                        k.op("dve", lambda e: e.tensor_copy(out=A8[:, 3, :], in_=pi_[:]), r=[pi_, A8], w=[A8])
                    if kk_ <= 7:
                        if kk_ > 0:
                            cmul(qr[:], qi[:], qr[:], qi[:], ire[:], iim[:], [qr, qi, ire, iim], [qr, qi, t1, t2], t1[:], t2[:])
                        cmul(xr[:], xi[:], bc(qr), bc(qi), Bbr[:], Bbi[:], [qr, qi, Bbr, Bbi], [xr, xi, u1, u2], u1[:], u2[:])
                        for (dst, src) in ((Lr, xr), (Li, xi)):
                            k.op("pool", lambda e: e.tensor_copy(out=dst[:, F_, kk_, :], in_=src[:, F_, :]), r=[src, dst], w=[dst])
                            k.op("pool", lambda e: e.tensor_copy(out=dst[:, Bk, 7 - kk_, :], in_=src[:, Bk, :]), r=[src, dst], w=[dst])
                fl = lambda t: t[:].rearrange("p a j h -> p a (j h)")
                for dg in range(32):
                    d = dg // 16
                    ps = self.psb[dg % 2]
                    k.op("pe", lambda e: e.matmul(ps[:, 0:128], lhsT=fl(Lr)[:, dg, :], rhs=fl(Rr)[:, dg, :], start=True, stop=False), r=[Lr, Rr], w=[ps])
                    k.op("pe", lambda e: e.matmul(ps[:, 0:128], lhsT=fl(Li)[:, dg, :], rhs=fl(Ri)[:, dg, :], start=False, stop=True), r=[Li, Ri], w=[ps])
                    k.op("dve", lambda e: e.tensor_tensor(out=Tm[:, dg, :], in0=ps[:, 0:128], in1=tmask[:, d, :], op=mul), r=[ps, tmask], w=[Tm])
                    ps2 = self.psb[2 + dg % 2]
                    k.op("pe", lambda e: e.transpose(out=ps2[:, 0:64], in_=fl(Gr)[:, dg, :], identity=self.ident_f[0:64, 0:64]), r=[Gr, self.ident_f], w=[ps2])
                    k.op("pe", lambda e: e.transpose(out=ps2[:, 64:128], in_=fl(Gi)[:, dg, :], identity=self.ident_f[0:64, 0:64]), r=[Gi, self.ident_f], w=[ps2])
                    k.op("act", lambda e: e.copy(out=Gt[:, dg].rearrange("p a b -> p (a b)"), in_=ps2[:, 0:128]), r=[ps2], w=[Gt])
                k.barrier()
            CBM = 64
            eu = contextlib.ExitStack()
            utok = k.sb(eu, "5utok", [128, 8, 256], F32)
            ub = k.sb(eu, "5ub", [128, 8, 256], BF16)
            ublocks = [(0, 32)] + [(32 + 128 * j, 128) for j in range(8)]
            trp = self.psb[4]
            trp_bf = trp[:].bitcast(BF16)
            usrc = self.s_u.rearrange("(c t) f -> c t f", t=8)
            for (c0, cb) in ublocks:
                k.dma(utok[0:cb], usrc[c0:c0 + cb], w=[utok])
                k.op("dve", lambda e: e.tensor_copy(out=ub[0:cb].rearrange("c a b -> c (a b)").rearrange("c (g t h) -> c g t h", g=16, t=8),
                                                    in_=utok[0:cb].rearrange("c t (g h) -> c g t h", g=16)), r=[utok], w=[ub])
                for g8 in range(2):
                    for gi in range(8):
                        g = g8 * 8 + gi
                        k.op("pe", lambda e: e.transpose(out=trp_bf[:, gi * 128:gi * 128 + cb], in_=ub[0:cb].rearrange("c a b -> c (a b)")[:, g * 128:(g + 1) * 128],
                                                         identity=self.ident_bf[0:cb, 0:cb]), r=[ub, self.ident_bf], w=[trp])
                    k.op("dve", lambda e: e.tensor_copy(out=U[:, g8 * 8:(g8 + 1) * 8, c0:c0 + cb],
                                                        in_=trp_bf[:, 0:1024].rearrange("p (g c) -> p g c", g=8)[:, :, 0:cb]), r=[trp], w=[U])
            k.barrier()
            eu.close()
            em = contextlib.ExitStack()
            Wt = k.sb(em, "5W", [64, 2, 32, CBM], F32)
            SP = [k.sb(em, f"5SP{d}", [64, 2, 16, CBM], BF16) for d in range(2)]
            Hist = [k.sb(em, f"5H{d}", [64, CBM + 1, 3, 16], F32) for d in range(2)]
            Tt = [k.sb(em, f"5T{d}", [64, 2, 16], F32) for d in range(2)]
            Vt = [k.sb(em, f"5V{d}", [64, 2, 16], F32) for d in range(2)]
            yev = [k.sb(em, f"5yev{i}", [128, 512], F32) for i in range(2)]
            blocks = [(0, 32)] + [(32 + CBM * j, CBM) for j in range((NC_ - 32) // CBM)]
            border = [blocks, [blocks[0]] + blocks[:0:-1]]
            A8v = [[A8[:, 0:2, d * 16:(d + 1) * 16], A8[:, 2:4, d * 16:(d + 1) * 16]] for d in range(2)]
            engs = ("dve", "pool")
            psS = self.psb[7]
            nbs = len(blocks)
            self.s5_nitems = sum(16 + 8 + b_[1] + 1 for b_ in blocks)
            yield "ready"

            def w_group(bs, d, g4, ri):
                cb = border[0][bs][1]
                cds = [border[0][bs][0], border[1][bs][0]]
                for gi in range(4):
                    g = g4 * 4 + gi
                    k.op("pe", lambda e: e.matmul(psS[0:64, gi * 128:gi * 128 + cb], lhsT=Gt[:, d * 16 + g, ri, :], rhs=U[:, g, cds[d]:cds[d] + cb],
                                                  start=True, stop=True), r=[Gt, U], w=[psS])
                k.op("dve", lambda e: e.tensor_copy(out=Wt[:, ri, d * 16 + g4 * 4:d * 16 + g4 * 4 + 4, 0:cb],
                                                    in_=psS[0:64, 0:512].rearrange("p (g c) -> p g c", g=4)[:, :, 0:cb]), r=[psS], w=[Wt])

            def y_group(bs, d, g4):
                cb = border[0][bs][1]
                cds = [border[0][bs][0], border[1][bs][0]]
                for gi in range(4):
                    g = g4 * 4 + gi
                    o = psS[:, gi * 128:gi * 128 + cb]
                    k.op("pe", lambda e: e.matmul(o, lhsT=Tm[:, d * 16 + g, :], rhs=U[:, g, cds[d]:cds[d] + cb], start=(gi == 0), stop=False,
                                                  skip_group_check=True), r=[Tm, U], w=[psS])
                    k.op("pe", lambda e: e.matmul(o, lhsT=Er[:, d * 16 + g, :], rhs=SP[d][:, 0, g, 0:cb], start=False, stop=False,
                                                  skip_group_check=True), r=[Er, SP[d]], w=[psS])
                    k.op("pe", lambda e: e.matmul(o, lhsT=Ei[:, d * 16 + g, :], rhs=SP[d][:, 1, g, 0:cb], start=False, stop=True,
                                                  skip_group_check=True), r=[Ei, SP[d]], w=[psS])
                ye = yev[g4 % 2]
                k.op("dve", lambda e: e.tensor_copy(out=ye[:], in_=psS[:, 0:512]), r=[psS], w=[ye])
                k.dma(self.s_y[d][:, g4 * 4:(g4 + 1) * 4, cds[d]:cds[d] + cb], ye[:].rearrange("p (g c) -> p g c", g=4)[:, :, 0:cb], r=[ye])

            prev_cb = None
            for bs in range(nbs):
                cb = border[0][bs][1]
                for d in range(2):
                    dst = Hist[d][:, 0] if d == 0 else Hist[d][:, cb]
                    if bs == 0:
                        k.op(engs[d], lambda e: e.memset(dst, 0.0), r=[Hist[d]], w=[Hist[d]])
                    else:
                        src = Hist[d][:, prev_cb] if d == 0 else Hist[d][:, 0]
                        k.op(engs[d], lambda e: e.tensor_copy(out=dst, in_=src), r=[Hist[d]], w=[Hist[d]])
                for d in range(2):
                    for g4 in range(4):
                        for ri in range(2):
                            w_group(bs, d, g4, ri)
                            yield
                if bs > 0:
                    for d in range(2):
                        for g4 in range(4):
                            y_group(bs - 1, d, g4)
                            yield
                prev_cb = cb
                for i in range(cb):
                    for d, eng in ((0, "dve"), (1, "pool")):
                        H, T_, V_ = Hist[d], Tt[d], Vt[d]
                        if d == 0:
                            col, pv, cu = i, i, i + 1
                        else:
                            col, pv, cu = cb - 1 - i, cb - i, cb - 1 - i
                        k.op(eng, lambda e: e.tensor_tensor(out=T_[:], in0=H[:, pv, 0:2, :], in1=A8v[d][0], op=mul), r=[H, A8], w=[T_])
                        k.op(eng, lambda e: e.tensor_tensor(out=V_[:], in0=H[:, pv, 1:3, :], in1=A8v[d][1], op=mul), r=[H, A8], w=[V_])
                        k.op(eng, lambda e: e.tensor_tensor(out=T_[:], in0=T_[:], in1=V_[:], op=add), r=[T_, V_], w=[T_])
                        k.op(eng, lambda e: e.tensor_tensor(out=H[:, cu, 0:2, :], in0=T_[:], in1=Wt[:, :, d * 16:(d + 1) * 16, col], op=add),
                             r=[T_, Wt, H], w=[H])
                        k.op(eng, lambda e: e.tensor_copy(out=H[:, cu, 2, :], in_=H[:, cu, 0, :]), r=[H], w=[H])
                    yield
                for d in range(2):
                    lo = 0 if d == 0 else 1
                    k.op("pool", lambda e: e.tensor_copy(out=SP[d][:, :, :, 0:cb].rearrange("p r g c -> p c r g"), in_=Hist[d][:, lo:lo + cb, 0:2, :]),
                         r=[Hist[d]], w=[SP[d]])
                yield
            for d in range(2):
                for g4 in range(4):
                    y_group(nbs - 1, d, g4)
                    yield
            k.barrier()
            em.close()
            with contextlib.ExitStack() as e3:
                utok = k.sb(e3, "5utok2", [128, 8, 256], F32)
                D8 = k.sb(e3, "5D8", [128, 8, 256], F32)
                wg = k.sb(e3, "5wg", [128, 2, 256], BF16)
                bg = k.sb(e3, "5bg", [128, 2], F32)
                for t in range(8):
                    k.dma(D8[:, t, :], self.s5_d[l:l + 1, :].partition_broadcast(128) if False else self.s5_d[l].partition_broadcast(128), w=[D8])
                k.dma(wg[:], self.s5_w_glu[l].rearrange("(c p) n -> p c n", p=128), w=[wg], q="pool")
                self.load_fm(e3, bg[:], self.s5_b_glu[l].rearrange("(c p) -> c p", p=128), 2, [bg])
                yf = k.sb(e3, "5yf", [128, 16, 128], F32)
                yb = k.sb(e3, "5yb", [128, 16, 128], F32)
                ytok = k.sb(e3, "5ytok", [128, 8, 256], F32)
                ygel = k.sb(e3, "5ygel", [128, 8, 256], BF16)
                yT = k.sb(e3, "5yT", [128, 2, 1024], BF16)
                sgm = [k.sb(e3, f"5sg{i}", [128, 512], F32) for i in range(2)]
                yao = [k.sb(e3, f"5ya{i}", [128, 512], BF16) for i in range(2)]
                for (c0, cb) in blocks:
                    if c0 == 0 and not need_ctx:
                        continue
                    ntok = cb * 8
                    k.dma(utok[0:cb], usrc[c0:c0 + cb], w=[utok])
                    k.dma(yf[:, :, 0:cb], self.s_y[0][:, :, c0:c0 + cb], w=[yf])
                    k.dma(yb[:, :, 0:cb], self.s_y[1][:, :, c0:c0 + cb], w=[yb])
                    k.op("pool", lambda e: e.tensor_tensor(out=yf[:, :, 0:cb], in0=yf[:, :, 0:cb], in1=yb[:, :, 0:cb], op=add), r=[yf, yb], w=[yf])
                    k.op("dve", lambda e: e.tensor_tensor(out=ytok[0:cb], in0=utok[0:cb], in1=D8[0:cb], op=mul), r=[utok, D8], w=[ytok])
                    for g4 in range(4):
                        ps = self.psb[g4 % 2]
                        for gi in range(4):
                            g = g4 * 4 + gi
                            k.op("pe", lambda e: e.transpose(out=ps[0:cb, gi * 128:(gi + 1) * 128], in_=yf[:, g, 0:cb], identity=self.ident_f[:, :]),
                                 r=[yf, self.ident_f], w=[ps])
                        yv = ytok[0:cb, :, g4 * 64:(g4 + 1) * 64].rearrange("c t (g h) -> c g t h", g=4)
                        pv = ps[0:cb, 0:512].rearrange("c (g t h) -> c g t h", g=4, t=8)
                        k.op("dve", lambda e: e.tensor_tensor(out=yv, in0=pv, in1=yv, op=add), r=[ps, ytok], w=[ytok])
                    k.op("act", lambda e: e.activation(out=ygel[0:cb], in_=ytok[0:cb], func=AF.Gelu), r=[ytok], w=[ygel])
                    for kc in range(2):
                        for t in range(8):
                            k.op("pe", lambda e: e.transpose(out=trp_bf[:, t * 128:t * 128 + cb], in_=ygel[0:cb, t, kc * 128:(kc + 1) * 128],
                                                             identity=self.ident_bf[0:cb, 0:cb]), r=[ygel, self.ident_bf], w=[trp])
                        k.op("dve", lambda e: e.tensor_copy(out=yT[:, kc, 0:ntok].rearrange("p (c t) -> p t c", t=8),
                                                            in_=trp_bf[:, 0:1024].rearrange("p (t c) -> p t c", t=8)[:, :, 0:cb]), r=[trp], w=[yT])
                    for r0 in range(0, ntok, 512):
                        n = min(512, ntok - r0)
                        for oc in range(2):
                            ps = self.psb[2 + oc]
                            for kc in range(2):
                                k.op("pe", lambda e: e.matmul(ps[:, 0:n], lhsT=wg[:, kc, oc * 128:(oc + 1) * 128], rhs=yT[:, kc, r0:r0 + n],
                                                              start=(kc == 0), stop=(kc == 1)), r=[wg, yT], w=[ps])
                            k.op("act", lambda e: e.activation(out=sgm[oc][:, 0:n], in_=ps[:, 0:n], func=AF.Sigmoid, bias=bg[:, oc:oc + 1]),
                                 r=[ps, bg], w=[sgm[oc]])
                            k.op("dve", lambda e: e.tensor_tensor(out=yao[oc][:, 0:n], in0=yT[:, oc, r0:r0 + n], in1=sgm[oc][:, 0:n], op=mul),
                                 r=[yT, sgm[oc]], w=[yao[oc]])
                            tok0 = c0 * 8 + r0
                            k.dma(self.ymix[oc * 128:(oc + 1) * 128, tok0:tok0 + n], yao[oc][:, 0:n], r=[yao[oc]])
                k.barrier()

def _prep_inputs(inputs):
    f = lambda a: np.ascontiguousarray(np.asarray(a, dtype=np.float32))
    cols = _win_cols()
    qcols = _wqb_cols()
    rs, rm = _rope_tables()
    shared = {
        "w_mod": f(inputs["w_mod"]), "b_mod": f(inputs["b_mod"]), "norm1_g": f(inputs["norm1_g"]), "norm2_g": f(inputs["norm2_g"]),
        "w_in": f(np.asarray(inputs["w_in"])[:, :, cols]), "w_out": f(inputs["w_out"]),
        "w_qb": f(np.asarray(inputs["mla_w_qb"])[:, :, qcols]), "w_kvb": f(inputs["mla_w_kvb"]),
        "qn_g": f(inputs["mla_q_norm_g"]), "kvn_g": f(inputs["mla_kv_norm_g"]),
        "w_up": f(inputs["ffn_w_up"]), "w_down": f(inputs["ffn_w_down"]), "final_g": f(inputs["final_norm_g"]),
        "rope_s": rs, "rope_m": rm, "ident": np.eye(128, dtype=np.float32),
        "swa_mask": _swa_mask(), "swa_sink": f(inputs["swa_sink"]),
        "hg_lb": f(inputs["hg_lb"]), "s5_tmask": _s5_tmask(),
        **{kk: f(inputs[kk]) for kk in ("s5_lam_re", "s5_lam_im", "s5_log_dt", "s5_b_re", "s5_b_im", "s5_c_re", "s5_c_im", "s5_d", "s5_w_glu", "s5_b_glu")}, "hg_norm_g": f(inputs["hg_norm_g"]), **_hg_consts(),
    }
    x = np.asarray(inputs["x"]); ctx = np.asarray(inputs["ctx"]); c = np.asarray(inputs["c"]); c_ctx = np.asarray(inputs["c_ctx"])
    maps = []
    for b in range(NCORES):
        m = dict(shared)
        m["xin"] = f(np.concatenate([ctx[b], x[b]], 0).T)
        m["cc"] = f(np.stack([c[b], c_ctx], 0))
        maps.append(m)
    return maps


def kernel(**inputs):
    bld = Builder()
    maps = _prep_inputs(inputs)
    res = run_bass_kernel_spmd(bld.nc, maps, core_ids=list(range(NCORES)))
    out = np.stack([np.ascontiguousarray(r["out"].T) for r in res.results], 0)
    return out.astype(np.float32)
```
